# Optimizing a Trainium2 kernel written in Bass

```python
import math
import jax, jax.numpy as jnp
from jax import lax
import numpy as np

D_MODEL = 1024
BATCH = 16
SEQ = 2048
DEPTH = 2

N_A_LAYERS = DEPTH // 2
N_B_LAYERS = DEPTH - N_A_LAYERS
S5_GROUP = 16
S5_GROUPS = D_MODEL // S5_GROUP
S5_STATE = 64
SB_HEAD_DIM = 64
SB_HEADS = D_MODEL // SB_HEAD_DIM
D_FF = 4 * D_MODEL
Q_BLOCK = 128
EPS = 1e-6
DT_MIN = 1e-3
DT_MAX = 1e-1

kernel_name = "yoco_s5_stickbreaking_hybrid"


def rms_norm(x, g):
    xf = x.astype(jnp.float32)
    y = xf * lax.rsqrt(jnp.mean(xf * xf, axis=-1, keepdims=True) + EPS)
    return (y * g.astype(jnp.float32)).astype(x.dtype)


def modulate(h, shift, scale):
    return h * (1 + scale[:, None, :]) + shift[:, None, :]


def ada_chunks(c, w, b, n):
    m = jnp.einsum('bd,de->be', jax.nn.silu(c), w) + b
    return jnp.split(m, n, axis=-1)


def _ssm_combine(left, right):
    a1r, a1i, b1r, b1i = left
    a2r, a2i, b2r, b2i = right
    ar = a2r * a1r - a2i * a1i
    ai = a2r * a1i + a2i * a1r
    br = a2r * b1r - a2i * b1i + b2r
    bi = a2r * b1i + a2i * b1r + b2i
    return ar, ai, br, bi


def s5_mixer(u, a_re, a_im, log_dt, b_re, b_im, c_re, c_im, d_skip):
    bsz, seq, _ = u.shape
    f32 = jnp.float32
    uf = u.astype(f32).reshape(bsz, seq, S5_GROUPS, S5_GROUP)
    lam_re = a_re.astype(f32)
    lam_im = a_im.astype(f32)
    dt = jnp.exp(log_dt.astype(f32))[:, None]
    mag = jnp.exp(lam_re * dt)
    ab_re = mag * jnp.cos(lam_im * dt)
    ab_im = mag * jnp.sin(lam_im * dt)
    den = lam_re * lam_re + lam_im * lam_im
    nr = ab_re - 1
    ni = ab_im
    f_re = (nr * lam_re + ni * lam_im) / den
    f_im = (ni * lam_re - nr * lam_im) / den
    br = b_re.astype(f32)
    bi = b_im.astype(f32)
    bb_re = f_re[..., None] * br - f_im[..., None] * bi
    bb_im = f_re[..., None] * bi + f_im[..., None] * br
    bu_re = jnp.einsum('bsgh,gph->bsgp', uf, bb_re)
    bu_im = jnp.einsum('bsgh,gph->bsgp', uf, bb_im)
    a_seq_re = jnp.broadcast_to(ab_re, (1, seq) + ab_re.shape)
    a_seq_im = jnp.broadcast_to(ab_im, (1, seq) + ab_im.shape)
    _, _, st_re, st_im = lax.associative_scan(
        _ssm_combine, (a_seq_re, a_seq_im, bu_re, bu_im), axis=1)
    y = (jnp.einsum('bsgp,ghp->bsgh', st_re, c_re.astype(f32))
         - jnp.einsum('bsgp,ghp->bsgh', st_im, c_im.astype(f32)))
    y = y.reshape(bsz, seq, D_MODEL) + d_skip.astype(f32) * u.astype(f32)
    return y.astype(u.dtype)


def stick_breaking_attention(q, k, v):
    seq = q.shape[2]
    scale = 1.0 / math.sqrt(SB_HEAD_DIM)
    outs = []
    for t0 in range(0, seq, Q_BLOCK):
        t1 = t0 + Q_BLOCK
        qb = q[:, :, t0:t1]
        kb = k[:, :, :t1]
        vb = v[:, :, :t1]
        z = jnp.einsum('bhtd,bhsd->bhts', qb, kb).astype(jnp.float32) * scale
        t_idx = jnp.arange(t0, t1)[:, None]
        s_idx = jnp.arange(t1)[None, :]
        strict = s_idx < t_idx
        log_fail = jnp.where(strict, jax.nn.log_sigmoid(-z), 0.0)
        rev = lax.cumsum(log_fail, axis=3, reverse=True)
        after = jnp.concatenate([rev[..., 1:], jnp.zeros_like(rev[..., :1])], axis=-1)
        w = jnp.where(strict, jnp.exp(jax.nn.log_sigmoid(z) + after), 0.0)
        outs.append(jnp.einsum('bhts,bhsd->bhtd', w.astype(v.dtype), vb))
    return jnp.concatenate(outs, axis=2)


def split_heads(t):
    bsz, seq, _ = t.shape
    return t.reshape(bsz, seq, SB_HEADS, SB_HEAD_DIM)


def setup_inputs(seed: int = 0) -> dict:
    key = jax.random.key(seed)
    ks = jax.random.split(key, 32)
    f32 = jnp.float32
    D = D_MODEL
    G, P, H = S5_GROUPS, S5_STATE, S5_GROUP
    nrm = lambda k, shape, std: jax.random.normal(k, shape, f32) * std
    x = jax.random.normal(ks[0], (BATCH, SEQ, D), f32)
    c = jax.random.normal(ks[1], (BATCH, D), f32)
    ada_w = nrm(ks[2], (DEPTH, D, 6 * D), 0.5 * D ** -0.5)
    ada_b = nrm(ks[3], (DEPTH, 6 * D), 0.02)
    mix_norm_g = 1.0 + nrm(ks[4], (DEPTH, D), 0.02)
    mlp_norm_g = 1.0 + nrm(ks[5], (DEPTH, D), 0.02)
    mlp_w1 = nrm(ks[6], (DEPTH, D, D_FF), D ** -0.5)
    mlp_w2 = nrm(ks[7], (DEPTH, D_FF, D), D_FF ** -0.5)
    s5_a_re = -0.5 + nrm(ks[8], (N_A_LAYERS, G, P), 0.01)
    s5_a_im = (jnp.float32(math.pi) * jnp.arange(P, dtype=f32))[None, None, :] + nrm(ks[9], (N_A_LAYERS, G, P), 0.01)
    s5_log_dt = jax.random.uniform(ks[10], (N_A_LAYERS, G), f32, math.log(DT_MIN), math.log(DT_MAX))
    s5_b_re = nrm(ks[11], (N_A_LAYERS, G, P, H), (2 * H) ** -0.5)
    s5_b_im = nrm(ks[12], (N_A_LAYERS, G, P, H), (2 * H) ** -0.5)
    s5_c_re = nrm(ks[13], (N_A_LAYERS, G, H, P), P ** -0.5)
    s5_c_im = nrm(ks[14], (N_A_LAYERS, G, H, P), P ** -0.5)
    s5_d = nrm(ks[15], (N_A_LAYERS, D), 1.0)
    s5_w_glu = nrm(ks[16], (N_A_LAYERS, D, 2 * D), D ** -0.5)
    kv_ada_w = nrm(ks[17], (D, 2 * D), 0.5 * D ** -0.5)
    kv_ada_b = nrm(ks[18], (2 * D,), 0.02)
    kv_norm_g = 1.0 + nrm(ks[19], (D,), 0.02)
    w_kv = nrm(ks[20], (D, 2 * D), D ** -0.5)
    k_norm_g = 1.0 + nrm(ks[21], (SB_HEAD_DIM,), 0.02)
    sb_w_q = nrm(ks[22], (N_B_LAYERS, D, D), D ** -0.5)
    q_norm_g = 1.0 + nrm(ks[23], (N_B_LAYERS, SB_HEAD_DIM), 0.02)
    sb_w_o = nrm(ks[24], (N_B_LAYERS, D, D), D ** -0.5)
    return {"x": x, "c": c, "ada_w": ada_w, "ada_b": ada_b,
            "mix_norm_g": mix_norm_g, "mlp_norm_g": mlp_norm_g,
            "mlp_w1": mlp_w1, "mlp_w2": mlp_w2,
            "s5_a_re": s5_a_re, "s5_a_im": s5_a_im, "s5_log_dt": s5_log_dt,
            "s5_b_re": s5_b_re, "s5_b_im": s5_b_im, "s5_c_re": s5_c_re, "s5_c_im": s5_c_im,
            "s5_d": s5_d, "s5_w_glu": s5_w_glu,
            "kv_ada_w": kv_ada_w, "kv_ada_b": kv_ada_b, "kv_norm_g": kv_norm_g,
            "w_kv": w_kv, "k_norm_g": k_norm_g,
            "sb_w_q": sb_w_q, "q_norm_g": q_norm_g, "sb_w_o": sb_w_o}


def reference(x, c, ada_w, ada_b, mix_norm_g, mlp_norm_g, mlp_w1, mlp_w2,
              s5_a_re, s5_a_im, s5_log_dt, s5_b_re, s5_b_im, s5_c_re, s5_c_im,
              s5_d, s5_w_glu, kv_ada_w, kv_ada_b, kv_norm_g, w_kv, k_norm_g,
              sb_w_q, q_norm_g, sb_w_o):
    bsz, seq, _ = x.shape
    k_sh = None
    v_sh = None
    for i in range(DEPTH):
        sh_a, sc_a, g_a, sh_m, sc_m, g_m = ada_chunks(c, ada_w[i], ada_b[i], 6)
        if i < N_A_LAYERS:
            j = i
            h = modulate(rms_norm(x, mix_norm_g[i]), sh_a, sc_a)
            y = s5_mixer(h, s5_a_re[j], s5_a_im[j], s5_log_dt[j], s5_b_re[j], s5_b_im[j],
                         s5_c_re[j], s5_c_im[j], s5_d[j])
            val, gate = jnp.split(jnp.einsum('bsd,de->bse', jax.nn.gelu(y), s5_w_glu[j]), 2, axis=-1)
            mix = val * jax.nn.sigmoid(gate)
        else:
            j = i - N_A_LAYERS
            if j == 0:
                kv_shift, kv_scale = ada_chunks(c, kv_ada_w, kv_ada_b, 2)
                hkv = modulate(rms_norm(x, kv_norm_g), kv_shift, kv_scale)
                k_flat, v_flat = jnp.split(jnp.einsum('bsd,de->bse', hkv, w_kv), 2, axis=-1)
                k_sh = rms_norm(split_heads(k_flat), k_norm_g).transpose(0, 2, 1, 3)
                v_sh = split_heads(v_flat).transpose(0, 2, 1, 3)
            h = modulate(rms_norm(x, mix_norm_g[i]), sh_a, sc_a)
            q = jnp.einsum('bsd,de->bse', h, sb_w_q[j])
            q = rms_norm(split_heads(q), q_norm_g[j]).transpose(0, 2, 1, 3)
            o = stick_breaking_attention(q, k_sh, v_sh)
            o = o.transpose(0, 2, 1, 3).reshape(bsz, seq, D_MODEL)
            mix = jnp.einsum('bsd,de->bse', o, sb_w_o[j])
        x = x + g_a[:, None, :] * mix
        h = modulate(rms_norm(x, mlp_norm_g[i]), sh_m, sc_m)
        ff = jnp.einsum('bsf,fd->bsd', jnp.square(jax.nn.relu(jnp.einsum('bsd,df->bsf', h, mlp_w1[i]))), mlp_w2[i])
        x = x + g_m[:, None, :] * ff
    return x
```

```python
import math
from contextlib import ExitStack

import numpy as np
import concourse.bass as bass
import concourse.mybir as mybir
from concourse.bass_utils import run_bass_kernel_spmd

F32 = mybir.dt.float32
BF16 = mybir.dt.bfloat16
U8 = mybir.dt.uint8
AF = mybir.ActivationFunctionType
ALU = mybir.AluOpType
PE, DVE, ACT, POOL, SP = "tensor", "vector", "scalar", "gpsimd", "sync"
ENGS = [PE, DVE, ACT, POOL, SP]

D = 1024
T = 2048
NS = 2
FT = 8
TT = 512
NTT = T // TT
DFF = 4096
EPS = 1e-6
NCORES = 8
EPOCH_MAX = 12000


class Tok:
    __slots__ = ("w", "r", "sem", "semcnt", "name")

    def __init__(self, name=""):
        self.w = None
        self.r = []
        self.sem = None
        self.semcnt = 0
        self.name = name


class Op:
    __slots__ = ("eng", "fn", "deps", "dma", "sig", "semref", "semval", "inc", "id", "sigok")


class Prog:
    def __init__(self, nc, stack):
        self.nc = nc
        self.stack = stack
        self.ops = []
        self.last = {e: None for e in ENGS}
        self.barrier_deps = {e: [] for e in ENGS}
        self.dmas = []
        self.nsem = 0

    def new_sem(self, name):
        self.nsem += 1
        return self.stack.enter_context(self.nc.semaphore(f"{name}_{self.nsem}"))

    def op(self, eng, fn, reads=(), writes=(), dma=None, sigok=True):
        o = Op()
        o.sigok = sigok
        o.eng = eng
        o.fn = fn
        o.dma = dma
        o.sig = False
        o.semref = None
        o.semval = 0
        o.inc = 0
        o.id = len(self.ops)
        deps = {}

        def add(d, war=False):
            if d is None:
                return
            if d.dma is None and d.eng == eng:
                if eng == PE:
                    return
            deps[d.id] = d

        for t in reads:
            add(t.w)
        for t in writes:
            add(t.w)
            for r in t.r:
                add(r, war=True)
        for d in self.barrier_deps[eng]:
            add(d)
        self.barrier_deps[eng] = []
        o.deps = list(deps.values())
        for t in reads:
            t.r.append(o)
        for t in writes:
            t.w = o
            t.r = []
        self.ops.append(o)
        self.last[eng] = o
        if dma is not None:
            self.dmas.append(o)
        return o

    def dma(self, eng, out, in_, reads=(), writes=(), tok=None, **kw):
        if tok is None:
            tok = writes[0]

        def fn(e):
            return e.dma_start(out=out, in_=in_, **kw)
        return self.op(eng, fn, reads=reads, writes=writes, dma=tok)

    def barrier(self):
        deps = [o for o in self.last.values() if o is not None] + list(self.dmas)
        self.dmas = []
        for e in ENGS:
            self.barrier_deps[e] = list(self.barrier_deps[e]) + deps

    def emit(self):
        nc = self.nc
        pe_ops = [o for o in self.ops if o.eng == PE and o.dma is None]
        if pe_ops:
            pe_ops[-1].sigok = True
        nxt = {}
        cur = None
        for o in reversed(pe_ops):
            if o.sigok:
                cur = o
            nxt[o.id] = cur
        nfwd = 0
        for o in self.ops:
            nd = {}
            for d in o.deps:
                if d.eng == PE and d.dma is None and not d.sigok:
                    d = nxt[d.id]
                    if d.id > o.id:
                        nfwd += 1
                nd[d.id] = d
            o.deps = list(nd.values())
        self.nfwd = nfwd
        for o in self.ops:
            for d in o.deps:
                d.sig = True
        cnt = {e: 0 for e in ENGS}
        cursem = {e: None for e in ENGS}
        for o in self.ops:
            if o.dma is not None:
                t = o.dma
                if t.sem is None or t.semcnt >= 16 * 3000:
                    t.sem = self.new_sem("d")
                    t.semcnt = 0
                t.semcnt += 16
                o.semref = t.sem
                o.semval = t.semcnt
                o.inc = 16
            elif o.sig:
                e = o.eng
                if cursem[e] is None or cnt[e] >= EPOCH_MAX:
                    cursem[e] = self.new_sem("e" + e[:2])
                    cnt[e] = 0
                cnt[e] += 1
                o.semref = cursem[e]
                o.semval = cnt[e]
                o.inc = 1
        with nc.Block() as block:
            for e in ENGS:
                oplist = [o for o in self.ops if o.eng == e]

                def body(eh, oplist=oplist):
                    waited = {}
                    for o in oplist:
                        need = {}
                        for d in o.deps:
                            k = id(d.semref)
                            if k not in need or need[k][1] < d.semval:
                                need[k] = (d.semref, d.semval)
                        for k, (s, v) in need.items():
                            if waited.get(k, 0) < v:
                                eh.wait_ge(s, v)
                                waited[k] = v
                        if o.fn is not None:
                            inst = o.fn(eh)
                            if o.semref is not None:
                                inst.then_inc(o.semref, o.inc)

                getattr(block, e)(body)


class Buf:
    def __init__(self, ap, name=""):
        self.ap = ap
        self.name = name
        self.toks = {}

    def t(self, key=0):
        if key not in self.toks:
            self.toks[key] = Tok(f"{self.name}{key}")
        return self.toks[key]

    def ts(self, keys):
        return [self.t(k) for k in keys]


class Arena:
    def __init__(self, ap_u8):
        self.ap = ap_u8
        self.size = ap_u8.shape[1]

    def view(self, off, shape, dt, name=""):
        esz = 4 if dt == F32 else 2
        n = 1
        for s in shape[1:]:
            n *= s
        nbytes = n * esz
        assert off % 4 == 0 and off + nbytes <= self.size, (name, off, nbytes, self.size)
        v = self.ap[0:shape[0], off:off + nbytes].bitcast(dt)
        if len(shape) == 3:
            v = v.rearrange("p (a b) -> p a b", b=shape[2])
        elif len(shape) == 4:
            v = v.rearrange("p (a b c) -> p a b c", b=shape[2], c=shape[3])
        elif len(shape) == 5:
            v = v.rearrange("p (a b c d) -> p a b c d", b=shape[2], c=shape[3], d=shape[4])
        return Buf(v, name), off + nbytes


class Builder:
    def __init__(self, debug=None, stop_after=None, nseq=NS):
        self.debug = debug or []
        self.stop_after = stop_after
        self.nseq = nseq
        self.stack = ExitStack()
        self.nc = bass.Bass("TRN2", target_bir_lowering=False)
        self.P = Prog(self.nc, self.stack)
        self.dbg_out = {}

    def dram_in(self, name, shape):
        return self.nc.dram_tensor(name, list(shape), F32, kind="ExternalInput").ap()

    def sb(self, name, shape, dt):
        return self.stack.enter_context(self.nc.sbuf_tensor(name, list(shape), dt))

    def act(self, out, in_, func, reads, writes, bias=None, scale=None, accum_out=None):
        kw = {}
        if bias is not None:
            kw["bias"] = bias
        if scale is not None:
            kw["scale"] = scale
        if accum_out is not None:
            kw["accum_out"] = accum_out
        return self.P.op(ACT, lambda e: e.activation(out=out, in_=in_, func=func, **kw), reads, writes)

    def tt(self, eng, out, in0, in1, op, reads, writes):
        return self.P.op(eng, lambda e: e.tensor_tensor(out, in0, in1, op), reads, writes)

    def stt(self, out, in0, scalar, in1, op0, op1, reads, writes):
        return self.P.op(DVE, lambda e: e.scalar_tensor_tensor(out, in0, scalar, in1, op0, op1), reads, writes)

    def ts(self, eng, out, in0, s1, s2, op0, op1, reads, writes):
        if op1 is None:
            return self.P.op(eng, lambda e: e.tensor_scalar(out, in0, s1, None, op0), reads, writes)
        return self.P.op(eng, lambda e: e.tensor_scalar(out, in0, s1, s2, op0, op1), reads, writes)

    def copy(self, eng, out, in_, reads, writes):
        if eng == ACT:
            return self.P.op(ACT, lambda e: e.activation(out=out, in_=in_, func=AF.Identity), reads, writes)
        return self.P.op(eng, lambda e: e.tensor_copy(out, in_), reads, writes)

    def mm(self, out, lhsT, rhs, start, stop, reads, writes, sig=None):
        return self.P.op(PE, lambda e: e.matmul(out, lhsT=lhsT, rhs=rhs, start=start, stop=stop), reads, writes,
                         sigok=(stop if sig is None else sig))

    def tr(self, out, in_, ident, reads, writes, sig=True):
        return self.P.op(PE, lambda e: e.transpose(out, in_, ident), reads, writes, sigok=sig)

    def dbg(self, name, buf_ap, shape, reads, dt=F32):
        if name not in self.debug:
            return
        o = self.nc.dram_tensor("dbg_" + name, list(shape), dt, kind="ExternalOutput").ap()
        tok = Tok("dbg" + name)
        self.P.dma(SP, o, buf_ap, reads=reads, writes=[tok], tok=tok)
        self.dbg_out[name] = tok

    def build(self):
        nc, P = self.nc, self.P
        I = {}
        I["x"] = self.dram_in("x", [NS, T, D])
        I["c"] = self.dram_in("c", [NS, D])
        I["ada_w"] = self.dram_in("ada_w", [2, D, 6 * D])
        I["ada_b"] = self.dram_in("ada_b", [2, 6 * D])
        I["mix_norm_g"] = self.dram_in("mix_norm_g", [2, D])
        I["mlp_norm_g"] = self.dram_in("mlp_norm_g", [2, D])
        I["mlp_w1"] = self.dram_in("mlp_w1", [2, D, DFF])
        I["mlp_w2"] = self.dram_in("mlp_w2", [2, DFF, D])
        I["s5_a_re"] = self.dram_in("s5_a_re", [1, 64, 64])
        I["s5_a_im"] = self.dram_in("s5_a_im", [1, 64, 64])
        I["s5_log_dt"] = self.dram_in("s5_log_dt", [1, 64])
        I["s5_b_re"] = self.dram_in("s5_b_re", [1, 64, 64, 16])
        I["s5_b_im"] = self.dram_in("s5_b_im", [1, 64, 64, 16])
        I["s5_c_re"] = self.dram_in("s5_c_re", [1, 64, 16, 64])
        I["s5_c_im"] = self.dram_in("s5_c_im", [1, 64, 16, 64])
        I["s5_d"] = self.dram_in("s5_d", [1, D])
        I["s5_w_glu"] = self.dram_in("s5_w_glu", [1, D, 2 * D])
        I["kv_ada_w"] = self.dram_in("kv_ada_w", [D, 2 * D])
        I["kv_ada_b"] = self.dram_in("kv_ada_b", [2 * D])
        I["kv_norm_g"] = self.dram_in("kv_norm_g", [D])
        I["w_kv"] = self.dram_in("w_kv", [D, 2 * D])
        I["k_norm_g"] = self.dram_in("k_norm_g", [64])
        I["sb_w_q"] = self.dram_in("sb_w_q", [1, D, D])
        I["q_norm_g"] = self.dram_in("q_norm_g", [1, 64])
        I["sb_w_o"] = self.dram_in("sb_w_o", [1, D, D])
        self.I = I
        self.out = nc.dram_tensor("out", [NS, T, D], F32, kind="ExternalOutput").ap()
        self.s5w = nc.dram_tensor("s5w_scr", [32, 128, 768], BF16, kind="Internal").ap()
        self.s5w_tok = [Tok(f"s5w{i}") for i in range(8)]
        self.modrow = nc.dram_tensor("modrow_scr", [NS, 2, D], F32, kind="Internal").ap()
        self.modrow_tok = Tok("modrow")

        self.consts()
        ARENA = 194 * 1024
        self.arena = Arena(self.sb("arena", [128, ARENA], U8)[:])
        self.psall = self.stack.enter_context(nc.psum_tensor("psall", [128, 4096], F32))
        self.ps = [Buf(self.psall[:, 512 * i:512 * i + 512], f"ps{i}") for i in range(8)]

        self.setup_phase()
        if self.stop_after is not None and (self.stop_after == "setup" or self.stop_after.startswith("s5") or self.stop_after == "ada"):
            return self.finish()
        for s in range(self.nseq):
            self.seq_pipeline(s)
        return self.finish()

    def finish(self):
        P = self.P
        outs = [o for o in P.ops if o.dma is not None]
        fin = P.op(SP, None)
        fin.deps = list({o.id: o for o in outs}.values())
        P.emit()
        return self.nc

    def consts(self):
        P = self.P
        c = {}

        def mk(name, shape, dt):
            c[name] = Buf(self.sb("c_" + name, shape, dt)[:], name)
            return c[name]

        ident_f = mk("ident_f", [128, 128], F32)
        ident_b = mk("ident_b", [128, 128], BF16)
        onesm_b = mk("onesm_b", [128, 128], BF16)
        blk_b = mk("blk_b", [128, 128], BF16)
        negtri_b = mk("negtri_b", [128, 128], BF16)
        negones_b = mk("negones_b", [128, 128], BF16)
        negmask_b = mk("negmask_b", [128, 128], BF16)
        zeros_b = mk("zeros_b", [128, 64], BF16)
        onesrow_f = mk("onesrow_f", [1, 128], F32)
        sel2 = mk("sel2", [2, 128], F32)
        bmask_f = mk("bmask_f", [128, 128], F32)
        tmp_f = mk("ctmp_f", [128, 128], F32)
        hm = mk("hm", [128, 2], F32)
        zeros512_b = mk("zeros512_b", [128, 512], BF16)

        def pool(fn, reads, writes):
            return P.op(POOL, fn, reads, writes)

        pool(lambda e: e.memset(ident_f.ap, 1.0), [], [ident_f.t()])
        pool(lambda e: e.affine_select(out=ident_f.ap, in_=ident_f.ap, pattern=[[-1, 128]], compare_op=ALU.is_equal,
                                       fill=0.0, base=0, channel_multiplier=1), [ident_f.t()], [ident_f.t()])
        pool(lambda e: e.tensor_copy(ident_b.ap, ident_f.ap), [ident_f.t()], [ident_b.t()])
        pool(lambda e: e.memset(onesm_b.ap, 1.0 / 1024.0), [], [onesm_b.t()])
        pool(lambda e: e.memset(blk_b.ap, 0.0), [], [blk_b.t()])
        pool(lambda e: e.memset(blk_b.ap[0:64, 0:64], 1.0 / 64.0), [blk_b.t()], [blk_b.t()])
        pool(lambda e: e.memset(blk_b.ap[64:128, 64:128], 1.0 / 64.0), [blk_b.t()], [blk_b.t()])
        pool(lambda e: e.memset(tmp_f.ap, -1.0), [], [tmp_f.t()])
        pool(lambda e: e.affine_select(out=tmp_f.ap, in_=tmp_f.ap, pattern=[[-1, 128]], compare_op=ALU.is_ge,
                                       fill=0.0, base=0, channel_multiplier=1), [tmp_f.t()], [tmp_f.t()])
        pool(lambda e: e.tensor_copy(negtri_b.ap, tmp_f.ap), [tmp_f.t()], [negtri_b.t()])
        pool(lambda e: e.memset(negones_b.ap, -1.0), [], [negones_b.t()])
        pool(lambda e: e.tensor_scalar(negmask_b.ap, negtri_b.ap, 30000.0, None, ALU.mult), [negtri_b.t()], [negmask_b.t()])
        pool(lambda e: e.memset(zeros_b.ap, 0.0), [], [zeros_b.t()])
        pool(lambda e: e.memset(onesrow_f.ap, 1.0), [], [onesrow_f.t()])
        pool(lambda e: e.memset(sel2.ap, 1.0), [], [sel2.t()])
        pool(lambda e: e.affine_select(out=sel2.ap, in_=sel2.ap, pattern=[[1, 128]], compare_op=ALU.is_ge,
                                       fill=0.0, base=0, channel_multiplier=-64), [sel2.t()], [sel2.t()])
        pool(lambda e: e.affine_select(out=sel2.ap, in_=sel2.ap, pattern=[[-1, 128]], compare_op=ALU.is_ge,
                                       fill=0.0, base=63, channel_multiplier=64), [sel2.t()], [sel2.t()])
        pool(lambda e: e.memset(bmask_f.ap, 1.0), [], [bmask_f.t()])
        pool(lambda e: e.affine_select(out=bmask_f.ap.rearrange("p (i h) -> p i h", h=16), in_=bmask_f.ap.rearrange("p (i h) -> p i h", h=16),
                                       pattern=[[16, 8], [0, 16]], compare_op=ALU.is_ge,
                                       fill=0.0, base=15, channel_multiplier=-1), [bmask_f.t()], [bmask_f.t()])
        pool(lambda e: e.memset(zeros512_b.ap, 0.0), [], [zeros512_b.t()])
        pool(lambda e: e.memset(hm.ap, 0.0), [], [hm.t()])
        pool(lambda e: e.memset(hm.ap[0:64, 0:1], 1.0), [hm.t()], [hm.t()])
        pool(lambda e: e.memset(hm.ap[64:128, 1:2], 1.0), [hm.t()], [hm.t()])
        self.c = c
        self.VT = Buf(self.sb("VT", [128, 160], F32)[:], "VT")
        self.adaT = Buf(self.sb("adaT", [128, 112, 2], F32)[:], "adaT")
        self.coef = Buf(self.sb("coef", [128, 32, 8, 3], F32)[:], "coef")
        self.modsc = Buf(self.sb("modsc", [128, NS, 5, 2, 8], F32)[:], "modsc")
        self.qkg = Buf(self.sb("qkg", [128, 2], F32)[:], "qkg")
        self.epsc = Buf(self.sb("epsc", [128, 1], F32)[:], "epsc")
        self.onec = Buf(self.sb("onec", [128, 1], F32)[:], "onec")
        pool(lambda e: e.memset(self.epsc.ap, EPS), [], [self.epsc.t()])
        pool(lambda e: e.memset(self.onec.ap, 1.0), [], [self.onec.t()])

    def ring_init(self, off, nslots=2):
        self.ring = []
        for i in range(nslots):
            b, off = self.arena.view(off, [128, 4096], BF16, f"ring{i}")
            self.ring.append(b)
        self.ring_i = 0
        return off

    def wload(self, srcs):
        slot = self.ring[self.ring_i % len(self.ring)]
        self.ring_i += 1
        for dstf, src in srcs:
            self.P.dma(POOL, dstf(slot.ap), src, reads=[], writes=[slot.t()], tok=slot.t())
        return slot

    def wload_k8(self, w2d, col0, ncols=512):
        src = w2d.rearrange("(k p) n -> p k n", p=128)[:, :, col0:col0 + ncols]
        slot = self.wload([(lambda a: a[:, 0:8 * ncols].rearrange("p (k n) -> p k n", n=ncols), src)])
        return slot, slot.ap[:, 0:8 * ncols].rearrange("p (k n) -> p k n", n=ncols)

    def setup_phase(self):
        P, I, c = self.P, self.I, self.c
        A = self.arena
        ps = self.ps
        off = 0
        off = self.ring_init(off, 2)
        self.ring_end = off
        off = A.size - 42 * 1024
        self.s5_limit = off
        vrA, off = A.view(off, [128, 128], F32, "vrA")
        vrB, off = A.view(off, [128, 128], F32, "vrB")
        cs, off = A.view(off, [2, 1024], F32, "cs")
        sT, off = A.view(off, [128, 8, 2], BF16, "sT")
        rowb = []
        for b in range(2):
            r, off = A.view(off, [1, 2048], F32, f"rowb{b}")
            rowb.append(r)
        biasrow, off = A.view(off, [1, 2048], F32, "biasrow")
        grow, off = A.view(off, [1, 1024], F32, "grow")
        abrow, off = A.view(off, [1, 2, 1024], F32, "abrow")
        ld = Tok("setup_ld")

        def ldma(out, in_, wtoks):
            P.dma(SP, out, in_, reads=[], writes=wtoks)

        ldma(vrA.ap[0:16, :], I["mix_norm_g"].rearrange("l (k p) -> (l k) p", p=128), [vrA.t()])
        ldma(vrA.ap[16:32, :], I["mlp_norm_g"].rearrange("l (k p) -> (l k) p", p=128), [vrA.t()])
        ldma(vrA.ap[32:40, :], I["kv_norm_g"].rearrange("(k p) -> k p", p=128), [vrA.t()])
        adab = I["ada_b"].rearrange("l (k p) -> (l k) p", p=128)
        ldma(vrA.ap[40:128, :], adab[0:88, :], [vrA.t()])
        ldma(vrB.ap[0:8, :], adab[88:96, :], [vrB.t()])
        ldma(vrB.ap[8:24, :], I["kv_ada_b"].rearrange("(k p) -> k p", p=128), [vrB.t()])
        for hh in range(2):
            ldma(vrB.ap[24:25, 64 * hh:64 * hh + 64], I["q_norm_g"], [vrB.t()])
            ldma(vrB.ap[25:26, 64 * hh:64 * hh + 64], I["k_norm_g"].rearrange("(o d) -> o d", o=1), [vrB.t()])
        ldma(cs.ap, I["c"], [cs.t()])
        ldma(biasrow.ap, I["ada_b"][0:1, 0:2048], [biasrow.t()])
        ldma(grow.ap, I["mix_norm_g"][0:1, :], [grow.t()])
        VT = self.VT
        self.tr(ps[0].ap[:, 0:128], vrA.ap, c["ident_f"].ap, [vrA.t(), c["ident_f"].t()], [ps[0].t()])
        self.tr(ps[0].ap[:, 128:154], vrB.ap[0:26, :], c["ident_f"].ap[0:26, 0:26], [vrB.t(), c["ident_f"].t()], [ps[0].t()])
        self.copy(DVE, VT.ap[:, 0:154], ps[0].ap[:, 0:154], [ps[0].t()], [VT.t()])
        self.ts(DVE, self.qkg.ap[:, 0:1], VT.ap[:, 152:153], 0.125, None, ALU.mult, None, [VT.t()], [self.qkg.t()])
        self.copy(DVE, self.qkg.ap[:, 1:2], VT.ap[:, 153:154], [VT.t()], [self.qkg.t()])
        self.act(cs.ap, cs.ap, AF.Silu, [cs.t()], [cs.t()])
        for k in range(8):
            self.tr(ps[1].ap[:, 2 * k:2 * k + 2], cs.ap[0:2, 128 * k:128 * k + 128], c["ident_f"].ap[0:2, 0:2],
                    [cs.t(), c["ident_f"].t()], [ps[1].t()])
        self.copy(DVE, sT.ap, ps[1].ap[:, 0:16].rearrange("p (k b) -> p k b", b=2), [ps[1].t()], [sT.t()])
        s5_rest = self.s5_setup_early()
        adaps = ps[5]
        chunks = [(I["ada_w"][0], j * 512) for j in range(12)] + [(I["ada_w"][1], j * 512) for j in range(12)] + \
                 [(I["kv_ada_w"], j * 512) for j in range(4)]
        for ci, (w2d, col0) in enumerate(chunks):
            slot, wv = self.wload_k8(w2d, col0)
            for mt in range(4):
                ot = ci * 4 + mt
                for k in range(8):
                    self.mm(adaps.ap[:, 2 * ot:2 * ot + 2], wv[:, k, 128 * mt:128 * mt + 128], sT.ap[:, k, :],
                            k == 0, k == 7, [slot.t(), sT.t()], [adaps.t()])
            if ci < 4:
                for b in range(2):
                    rp = ps[6 + b]
                    for k in range(8):
                        self.mm(rp.ap[0:1, :], sT.ap[:, k, b:b + 1], wv[:, k, :], k == 0, k == 7, [slot.t(), sT.t()], [rp.t()])
                    self.copy(ACT, rowb[b].ap[0:1, 512 * ci:512 * ci + 512], rp.ap[0:1, :], [rp.t()], [rowb[b].t()])
        self.tt(DVE, self.adaT.ap, adaps.ap[:, 0:224].rearrange("p (o b) -> p o b", b=2),
                VT.ap[:, 40:152].unsqueeze(2).to_broadcast([128, 112, 2]),
                ALU.add, [adaps.t(), VT.t()], [self.adaT.t()])
        adaT = self.adaT
        norms = {1: (16, 24, 32), 2: (32, 96, 104), 3: (8, 48, 56), 4: (24, 72, 80)}
        for s in range(2):
            for n, (gc, sh, sc) in norms.items():
                self.stt(self.modsc.ap[:, s, n, 0, :], adaT.ap[:, sc:sc + 8, s], 1.0, VT.ap[:, gc:gc + 8], ALU.add, ALU.mult,
                         [adaT.t(), VT.t()], [self.modsc.t()])
                self.copy(DVE, self.modsc.ap[:, s, n, 1, :], adaT.ap[:, sh:sh + 8, s], [adaT.t()], [self.modsc.t()])
        for b in range(2):
            self.tt(DVE, rowb[b].ap, rowb[b].ap, biasrow.ap, ALU.add, [rowb[b].t(), biasrow.t()], [rowb[b].t()])
            self.stt(abrow.ap[0:1, 0, :], rowb[b].ap[0:1, 1024:2048], 1.0, grow.ap, ALU.add, ALU.mult,
                     [rowb[b].t(), grow.t()], [abrow.t()])
            self.copy(DVE, abrow.ap[0:1, 1, :], rowb[b].ap[0:1, 0:1024], [rowb[b].t()], [abrow.t()])
            P.dma(SP, self.modrow[b:b + 1], abrow.ap, reads=[abrow.t()], writes=[self.modrow_tok], tok=self.modrow_tok)
        self.dbg("adaT", self.adaT.ap, [128, 112, 2], [self.adaT.t()])
        self.dbg("modsc", self.modsc.ap, [128, NS, 5, 2, 8], [self.modsc.t()])
        self.dbg("VT", self.VT.ap, [128, 160], [self.VT.t()])
        if self.stop_after == "ada":
            return
        s5_rest()
        P.barrier()

    def s5_setup_early(self):
        P, I, c = self.P, self.I, self.c
        A = self.arena
        ps = self.ps
        off = self.ring_end
        lamrows, off = A.view(off, [32, 2, 128], F32, "lamrows")
        ldt2, off = A.view(off, [2, 32], F32, "ldt2")
        NSM = 24
        smb, off = A.view(off, [128, NSM, 32], F32, "sm")
        Bq, off = A.view(off, [128, 2, 32, 16], F32, "Bq")
        Bb, off = A.view(off, [128, 2, 32, 16], F32, "Bb")
        Cin, off = A.view(off, [128, 2, 128], F32, "Cin")
        CT, off = A.view(off, [128, 2, 32, 16], F32, "CT")
        X, off = A.view(off, [128, 2, 32, 8, 16], F32, "X")
        Wo, off = A.view(off, [128, 2, 32, 8, 16], F32, "Wo")
        We, off = A.view(off, [128, 2, 32, 128], F32, "We")
        tmp, off = A.view(off, [128, 2, 512], F32, "s5tmp")
        Drows, off = A.view(off, [64, 16], F32, "Drows")
        Drep, off = A.view(off, [64, 8, 16], F32, "Drep")
        dcol, off = A.view(off, [128, 64], F32, "dcol")
        tmpm, off = A.view(off, [128, 2, 2, 128], F32, "tmpm")
        stage, off = A.view(off, [128, 2, 4, 768], BF16, "stage")
        WoM, off = A.view(off, [128, 2, 2, 2, 128], F32, "WoM")
        assert off <= self.s5_limit, (off, self.s5_limit)
        identf = c["ident_f"]

        names = ["lre", "lim", "dt", "mag", "th", "c", "s", "t1", "t2", "t3", "abre", "abim", "nr", "den",
                 "fre", "fim", "rm2", "lire", "liim", "mure", "muim", "w2r", "w2i"]
        sm = {n: (smb.ap[:, i, :], smb.t(n)) for i, n in enumerate(names)}

        def S(n):
            return sm[n][0]

        def St(n):
            return sm[n][1]

        P.dma(SP, lamrows.ap[:, 0, :], I["s5_a_re"][0].rearrange("(a g) p -> a (g p)", g=2), writes=[lamrows.t()])
        P.dma(SP, lamrows.ap[:, 1, :], I["s5_a_im"][0].rearrange("(a g) p -> a (g p)", g=2), writes=[lamrows.t()])
        P.dma(SP, ldt2.ap, I["s5_log_dt"][0].rearrange("(a g) -> g a", g=2), writes=[ldt2.t()], allow_slow_non_contiguous=True)
        for ri, nm in enumerate(["s5_b_re", "s5_b_im"]):
            src = I[nm][0].rearrange("(a g) p h -> (g p) a h", g=2)
            for q4 in range(4):
                P.dma(SP, Bq.ap[:, ri, 8 * q4:8 * q4 + 8, :], src[:, 8 * q4:8 * q4 + 8, :], writes=[Bq.t()])
        P.dma(SP, Drows.ap, I["s5_d"][0].rearrange("(g h) -> g h", h=16), writes=[Drows.t()])

        self.tr(ps[0].ap[:, 0:32], lamrows.ap[:, 0, :], identf.ap[0:32, 0:32], [lamrows.t(), identf.t()], [ps[0].t()])
        self.tr(ps[0].ap[:, 32:64], lamrows.ap[:, 1, :], identf.ap[0:32, 0:32], [lamrows.t(), identf.t()], [ps[0].t()])
        self.mm(ps[0].ap[:, 64:96], c["sel2"].ap, ldt2.ap, True, True, [c["sel2"].t(), ldt2.t()], [ps[0].t()])
        self.copy(DVE, S("lre"), ps[0].ap[:, 0:32], [ps[0].t()], [St("lre")])
        self.copy(DVE, S("lim"), ps[0].ap[:, 32:64], [ps[0].t()], [St("lim")])
        self.act(S("dt"), ps[0].ap[:, 64:96], AF.Exp, [ps[0].t()], [St("dt")])

        def tt(out, a, b, op, eng=DVE):
            self.tt(eng, S(out), S(a), S(b), op, [St(a), St(b)], [St(out)])

        if self.stop_after == "s5a":
            return

        def horner(out, y, coefs, last):
            self.ts(DVE, S(out), S(y), coefs[0], None, ALU.mult, None, [St(y)], [St(out)])
            for ck in coefs[1:]:
                self.stt(S(out), S(out), ck, S(y), ALU.add, ALU.mult, [St(out), St(y)], [St(out)])
            self.ts(DVE, S(out), S(out), last, None, ALU.add, None, [St(out)], [St(out)])

        tt("t1", "lre", "dt", ALU.mult)
        self.ts(DVE, S("t2"), S("t1"), 0.25, None, ALU.mult, None, [St("t1")], [St("t2")])
        horner("mag", "t2", [1.0 / 5040, 1.0 / 720, 1.0 / 120, 1.0 / 24, 1.0 / 6, 0.5, 1.0], 1.0)
        tt("mag", "mag", "mag", ALU.mult)
        tt("mag", "mag", "mag", ALU.mult)
        tt("th", "lim", "dt", ALU.mult)
        self.ts(DVE, S("t3"), S("th"), 1.0 / 32.0, None, ALU.mult, None, [St("th")], [St("t3")])
        tt("t2", "t3", "t3", ALU.mult)
        horner("c", "t2", [-1.0 / 3628800, 1.0 / 40320, -1.0 / 720, 1.0 / 24, -0.5], 1.0)
        horner("s", "t2", [1.0 / 362880, -1.0 / 5040, 1.0 / 120, -1.0 / 6], 1.0)
        tt("s", "s", "t3", ALU.mult)

        def csquare(a, b):
            tt("t1", a, a, ALU.mult)
            tt("t2", b, b, ALU.mult)
            self.stt(S(b), S(a), 2.0, S(b), ALU.mult, ALU.mult, [St(a), St(b)], [St(b)])
            tt(a, "t1", "t2", ALU.subtract)

        for _ in range(5):
            csquare("c", "s")
        tt("t1", "c", "c", ALU.mult)
        tt("t2", "s", "s", ALU.mult)
        tt("t1", "t1", "t2", ALU.add)
        self.ts(DVE, S("t1"), S("t1"), -0.5, 1.5, ALU.mult, ALU.add, [St("t1")], [St("t1")])
        tt("c", "c", "t1", ALU.mult)
        tt("s", "s", "t1", ALU.mult)
        tt("abre", "mag", "c", ALU.mult)
        tt("abim", "mag", "s", ALU.mult)
        self.ts(DVE, S("nr"), S("abre"), -1.0, None, ALU.add, None, [St("abre")], [St("nr")])
        tt("t1", "lre", "lre", ALU.mult)
        tt("t2", "lim", "lim", ALU.mult)
        tt("den", "t1", "t2", ALU.add)
        P.op(DVE, lambda e: e.reciprocal(S("den"), S("den")), [St("den")], [St("den")])
        tt("t1", "nr", "lre", ALU.mult)
        tt("t2", "abim", "lim", ALU.mult)
        tt("t1", "t1", "t2", ALU.add)
        tt("fre", "t1", "den", ALU.mult)
        tt("t1", "abim", "lre", ALU.mult)
        tt("t2", "nr", "lim", ALU.mult)
        tt("t1", "t1", "t2", ALU.subtract)
        tt("fim", "t1", "den", ALU.mult)
        tt("rm2", "mag", "mag", ALU.mult)
        P.op(DVE, lambda e: e.reciprocal(S("rm2"), S("rm2")), [St("rm2")], [St("rm2")])
        tt("lire", "abre", "rm2", ALU.mult)
        self.stt(S("liim"), S("abim"), -1.0, S("rm2"), ALU.mult, ALU.mult, [St("abim"), St("rm2")], [St("liim")])
        self.copy(DVE, S("mure"), S("abre"), [St("abre")], [St("mure")])
        self.copy(DVE, S("muim"), S("abim"), [St("abim")], [St("muim")])
        for _ in range(3):
            csquare("mure", "muim")
        coef = self.coef
        self.copy(DVE, S("w2r"), S("mure"), [St("mure")], [St("w2r")])
        self.copy(DVE, S("w2i"), S("muim"), [St("muim")], [St("w2i")])
        for k in range(8):
            self.copy(DVE, coef.ap[:, :, k, 0], S("w2r"), [St("w2r")], [coef.t()])
            self.copy(DVE, coef.ap[:, :, k, 1], S("w2i"), [St("w2i")], [coef.t()])
            self.ts(DVE, coef.ap[:, :, k, 2], S("w2i"), -1.0, None, ALU.mult, None, [St("w2i")], [coef.t()])
            if k < 7:
                csquare("w2r", "w2i")

        if self.stop_after == "s5b":
            return
        tmpt = [tmp.t(i) for i in range(2)]

        def cmul(ore, oim, otoks, are, aim, atoks, zre, zim, ztoks, n):
            ab = lambda nm: S(nm).unsqueeze(2).to_broadcast([128, 32, n])
            t = [tmp.ap[:, i, 0:32 * n].rearrange("p (a h) -> p a h", h=n) for i in range(2)]
            rd = atoks + ztoks
            self.tt(DVE, t[0], zre, ab(are), ALU.mult, rd, [tmpt[0]])
            self.tt(DVE, t[1], zim, ab(aim), ALU.mult, rd, [tmpt[1]])
            self.tt(DVE, ore, t[0], t[1], ALU.subtract, [tmpt[0], tmpt[1]], otoks)
            self.tt(DVE, t[0], zim, ab(are), ALU.mult, rd, [tmpt[0]])
            self.tt(DVE, t[1], zre, ab(aim), ALU.mult, rd, [tmpt[1]])
            self.tt(DVE, oim, t[0], t[1], ALU.add, [tmpt[0], tmpt[1]], otoks)

        fa = [St("fre"), St("fim")]
        cmul(Bb.ap[:, 0], Bb.ap[:, 1], [Bb.t()], "fre", "fim", fa, Bq.ap[:, 0], Bq.ap[:, 1], [Bq.t()], 16)
        la = [St("lire"), St("liim")]
        for j in range(8):
            if j == 0:
                zr, zi, zt = Bb.ap[:, 0], Bb.ap[:, 1], [Bb.t()]
            else:
                zr, zi, zt = X.ap[:, 0, :, j - 1, :], X.ap[:, 1, :, j - 1, :], [X.t(j - 1)]
            cmul(X.ap[:, 0, :, j, :], X.ap[:, 1, :, j, :], [X.t(j)], "lire", "liim", la, zr, zi, zt, 16)
        if self.stop_after == "s5c":
            return
        for r in range(8):
            for ri, nm in enumerate(["s5_c_re", "s5_c_im"]):
                db = (2 * r + ri) % 2
                P.dma(SP, Cin.ap[:, db, 0:64], I[nm][0][8 * r:8 * r + 8].rearrange("g h p -> (g h) p"), writes=[Cin.t(db)])
                pb = ps[1 + db]
                self.tr(pb.ap[0:64, 0:128], Cin.ap[:, db, 0:64], identf.ap, [Cin.t(db), identf.t()], [pb.t()])
                src = pb.ap[0:64, 0:128].rearrange("p (a g h) -> p a g h", g=2, h=16)
                self.copy(ACT, CT.ap[0:64, ri, 4 * r:4 * r + 4, :], src[:, :, 0, :], [pb.t()], [CT.t()])
                self.copy(ACT, CT.ap[64:128, ri, 4 * r:4 * r + 4, :], src[:, :, 1, :], [pb.t()], [CT.t()])
        if self.stop_after == "s5d":
            return
        ab_ = [St("abre"), St("abim")]
        for i in range(8):
            if i == 0:
                zr, zi, zt = CT.ap[:, 0], CT.ap[:, 1], [CT.t()]
            else:
                zr, zi, zt = Wo.ap[:, 0, :, i - 1, :], Wo.ap[:, 1, :, i - 1, :], [Wo.t(i - 1)]
            cmul(Wo.ap[:, 0, :, i, :], Wo.ap[:, 1, :, i, :], [Wo.t(i)], "abre", "abim", ab_, zr, zi, zt, 16)
        ma = [St("mure"), St("muim")]
        for j in range(8):
            cmul(We.ap[:, 0, :, 16 * j:16 * j + 16], We.ap[:, 1, :, 16 * j:16 * j + 16], [We.t(j)], "mure", "muim", ma,
                 X.ap[:, 0, :, j, :], X.ap[:, 1, :, j, :], [X.t(j)], 16)
        allWo = [Wo.t(i) for i in range(8)]
        WoN = Wo.t("neg")
        self.ts(DVE, Wo.ap[:, 1], Wo.ap[:, 1], -1.0, None, ALU.mult, None, allWo, allWo + [WoN])
        def rest():
            self.copy(DVE, Drep.ap, Drows.ap.unsqueeze(1).to_broadcast([64, 8, 16]), [Drows.t()], [Drep.t()])
            self.tr(ps[3].ap[:, 0:64], Drep.ap.rearrange("p j h -> p (j h)"), identf.ap[0:64, 0:64], [Drep.t(), identf.t()], [ps[3].t()])
            self.copy(DVE, dcol.ap, ps[3].ap[:, 0:64], [ps[3].t()], [dcol.t()])
            self.s5_setup_pairs(locals_=dict(We=We, X=X, Wo=Wo, WoN=WoN, allWo=allWo, stage=stage, WoM=WoM, tmpm=tmpm, dcol=dcol, identf=identf))
        return rest

    def s5_setup_pairs(self, locals_):
        P, c, ps = self.P, self.c, self.ps
        We, X, Wo, WoN, allWo, stage, WoM, tmpm, dcol, identf = (locals_[k] for k in
                                                                ["We", "X", "Wo", "WoN", "allWo", "stage", "WoM", "tmpm", "dcol", "identf"])
        allWe = [We.t(j) for j in range(8)]
        allX = [X.t(j) for j in range(8)]

        def phase1(pr):
            slot = (pr // 4) % 2
            p4 = pr % 4
            st_ = stage.t(slot)
            pa = ps[4 + pr % 2]
            pb = ps[6 + pr % 2]
            self.tr(pa.ap[:, 0:128], We.ap[:, 0, pr, :], identf.ap, allWe + [identf.t()], [pa.t()])
            self.tr(pa.ap[:, 128:256], We.ap[:, 1, pr, :], identf.ap, allWe + [identf.t()], [pa.t()])
            self.copy(ACT, stage.ap[:, slot, p4, 0:256], pa.ap[:, 0:256], [pa.t()], [st_])
            self.copy(ACT, stage.ap[:, slot, p4, 256:512].rearrange("p (r n) -> p r n", r=2),
                      Wo.ap[:, :, pr].rearrange("p r i h -> p r (i h)"), [WoN], [st_])
            wm = WoM.ap[:, pr % 2]
            for gl in range(2):
                self.ts(POOL if gl == 0 else DVE, wm[:, gl], Wo.ap[:, :, pr].rearrange("p r i h -> p r (i h)"), c["hm"].ap[:, gl:gl + 1], None, ALU.mult, None,
                        [WoN, c["hm"].t()], [WoM.t(pr % 2)])
            for gl in range(2):
                for ri in range(2):
                    self.mm(pb.ap[:, 128 * gl:128 * gl + 128], X.ap[:, ri, pr].rearrange("p j h -> p (j h)"),
                            wm[:, gl, ri, :], ri == 0, ri == 1, allX + [WoM.t(pr % 2)], [pb.t()])

        def phase2(pr):
            slot = (pr // 4) % 2
            p4 = pr % 4
            st_ = stage.t(slot)
            pb = ps[6 + pr % 2]
            tm = tmpm.ap[:, pr % 2]
            self.tt(DVE, tm, pb.ap[:, 0:256].rearrange("p (g n) -> p g n", g=2),
                    c["bmask_f"].ap.unsqueeze(1).to_broadcast([128, 2, 128]), ALU.mult, [pb.t(), c["bmask_f"].t()], [tmpm.t(pr % 2)])
            for gl in range(2):
                g = 2 * pr + gl
                self.stt(stage.ap[:, slot, p4, 512 + 128 * gl:512 + 128 * gl + 128], identf.ap, dcol.ap[:, g:g + 1], tm[:, gl, :],
                         ALU.mult, ALU.add, [identf.t(), dcol.t(), tmpm.t(pr % 2)], [st_])
            if p4 == 3:
                ftc = pr // 4
                P.dma(SP, self.s5w[4 * ftc:4 * ftc + 4].rearrange("a p n -> p a n"), stage.ap[:, slot], reads=[st_],
                      writes=[self.s5w_tok[ftc]])
                if ftc == 0:
                    self.dbg("s5w0", stage.ap[:, slot], [128, 4, 768], [st_], dt=BF16)
        phase1(0)
        for pr in range(32):
            if pr + 1 < 32:
                phase1(pr + 1)
            phase2(pr)
        self.dbg("coef", self.coef.ap, [128, 32, 8, 3], [self.coef.t()])

    R_RSTD = 16384
    R_XT = 18432
    R_ACTT = R_XT + 65536
    R_TAIL = R_ACTT + 32768

    def seq_pipeline(self, s):
        P = self.P
        A = self.arena
        self.xT, _ = A.view(self.R_XT, [128, 8, 2048], F32, f"xT{s}")
        self.actT, _ = A.view(self.R_ACTT, [128, 8, 2048], BF16, f"actT{s}")
        self.rstd, _ = A.view(self.R_RSTD, [128, 512], F32, f"rstd{s}")
        self.s5_prep(s)
        P.barrier()
        self.s5_core(s)
        P.barrier()
        self.dbg(f"gT{s}", self.actT.ap, [128, 8, 2048], [], dt=BF16)
        if self.stop_after == "s5core":
            return
        self.x_reload(s)
        P.barrier()
        self.dbg(f"xT{s}", self.xT.ap, [128, 8, 2048], [])
        if self.stop_after == "xreload":
            return
        self.glu(s)
        P.barrier()
        self.mlp(s, 0, 1)
        P.barrier()
        self.dbg(f"x1T{s}", self.xT.ap, [128, 8, 2048], [])
        if self.stop_after == "layer0":
            return
        self.kv_phase(s)
        P.barrier()
        self.dbg(f"KT{s}", self.KT.ap, [128, 8, 2048], [], dt=BF16)
        self.dbg(f"V{s}", self.V.ap, [128, 16, 1024], [], dt=BF16)
        if self.stop_after == "kv":
            return
        self.attn_phase(s)
        P.barrier()
        self.dbg(f"x2T{s}", self.xT.ap, [128, 8, 2048], [])
        if self.stop_after == "attn":
            return
        self.mlp(s, 1, 4)
        P.barrier()
        self.output_phase(s)
        P.barrier()

    def load_xtok(self, s, half, Xtok):
        src = self.I["x"][s, half * 1024:(half + 1) * 1024, :].rearrange("(c j) f -> c j f", j=8)
        for q in range(4):
            self.P.dma(SP, Xtok.ap[32 * q:32 * q + 32], src[32 * q:32 * q + 32], writes=[Xtok.t()])

    def s5_prep(self, s):
        P, c, ps = self.P, self.c, self.ps
        A = self.arena
        off = self.R_TAIL
        Xtok, off = A.view(off, [128, 8, 1024], F32, "Xtok")
        htok, off = A.view(off, [128, 64, 8, 16], BF16, "htok")
        tmpf, off = A.view(off, [128, 2, 1024], F32, "tmpf")
        Arep, off = A.view(off, [128, 1024], F32, "Arep")
        Brep, off = A.view(off, [128, 1024], F32, "Brep")
        ss, off = A.view(off, [128, 8], F32, "ss")
        rs, off = A.view(off, [128, 8], F32, "rs")
        junk, off = A.view(off, [128, 1024], BF16, "junk")
        D, _ = A.view(self.R_XT, [128, 64, 256], BF16, "D")
        self.D = D
        P.dma(SP, Arep.ap, self.modrow[s, 0, :].partition_broadcast(128), reads=[self.modrow_tok], writes=[Arep.t()])
        P.dma(SP, Brep.ap, self.modrow[s, 1, :].partition_broadcast(128), reads=[self.modrow_tok], writes=[Brep.t()])
        for half in range(2):
            self.load_xtok(s, half, Xtok)
            for j in range(8):
                self.act(junk.ap, Xtok.ap[:, j, :], AF.Square, [Xtok.t()], [junk.t(), ss.t()], accum_out=ss.ap[:, j:j + 1])
            self.ts(DVE, rs.ap, ss.ap, 1.0 / 1024.0, EPS, ALU.mult, ALU.add, [ss.t()], [rs.t()])
            self.act(rs.ap, rs.ap, AF.Sqrt, [rs.t()], [rs.t()])
            P.op(DVE, lambda e: e.reciprocal(rs.ap, rs.ap), [rs.t()], [rs.t()])
            for j in range(8):
                tb = j % 2
                self.stt(tmpf.ap[:, tb, :], Xtok.ap[:, j, :], rs.ap[:, j:j + 1], Arep.ap, ALU.mult, ALU.mult,
                         [Xtok.t(), rs.t(), Arep.t()], [tmpf.t(tb)])
                self.tt(POOL, htok.ap[:, :, j, :], tmpf.ap[:, tb, :].rearrange("p (g h) -> p g h", h=16),
                        Brep.ap.rearrange("p (g h) -> p g h", h=16), ALU.add, [tmpf.t(tb), Brep.t()], [htok.t()])
            for g4 in range(16):
                pb = ps[g4 % 4]
                pbv = pb.ap.bitcast(BF16)
                for gi in range(4):
                    g = 4 * g4 + gi
                    self.tr(pbv[:, 128 * gi:128 * gi + 128], htok.ap[:, g].rearrange("p j h -> p (j h)"), c["ident_b"].ap,
                            [htok.t(), c["ident_b"].t()], [pb.t()], sig=(gi == 3))
                self.copy(ACT if g4 % 2 == 0 else DVE, D.ap[:, 4 * g4:4 * g4 + 4, 128 * half:128 * half + 128],
                          pbv[:, 0:512].rearrange("p (g c) -> p g c", c=128), [pb.t()], [D.t()])

    def s5_core(self, s):
        P, c, ps = self.P, self.c, self.ps
        A = self.arena
        D = self.D
        coef = self.coef
        off = self.R_TAIL
        wch, off = A.view(off, [128, 2, 4, 768], BF16, "wch")
        AB, off = A.view(off, [128, 4, 2, 2, 384], F32, "AB")
        Sbf, off = A.view(off, [128, 4, 2, 256], BF16, "Sbf")
        Ygb, off = A.view(off, [128, 2, 2, 256], BF16, "Ygb")
        Gtok, off = A.view(off, [128, 2, 2, 8, 128], BF16, "Gtok")
        gT = self.actT
        abt = [AB.t((a, b_, c_)) for a in range(4) for b_ in range(2) for c_ in range(2)]
        P.op(POOL, lambda e: e.memset(AB.ap, 0.0), [], abt)
        P.op(POOL, lambda e: e.memset(Sbf.ap, 0.0), [], [Sbf.t(i) for i in range(4)])
        identb = c["ident_b"]

        def wslot_of(pr):
            return (pr // 4) % 2

        def pair_front(pr):
            ft, p4 = pr // 4, pr % 4
            wslot = wslot_of(pr)
            sl = pr % 4
            bankE = ps[pr % 4]
            for ri in range(2):
                for gl in range(2):
                    self.mm(bankE.ap[64 * gl:64 * gl + 64, 256 * ri:256 * ri + 256],
                            wch.ap[:, wslot, p4, 128 * ri + 64 * gl:128 * ri + 64 * gl + 64], D.ap[:, 2 * pr + gl, :], True, True,
                            [wch.t(wslot), D.t()], [bankE.t()], sig=(ri == 1 and gl == 1))
            for ri in range(2):
                self.copy(ACT, AB.ap[:, sl, 0, ri, 128:384], bankE.ap[:, 256 * ri:256 * ri + 256], [bankE.t()], [AB.t((sl, 0, ri))])

        ptmp, _ = A.view(off, [128, 2, 256], F32, "ptmp")

        def level(pr, k):
            sl = pr % 4
            d = 1 << k
            cur = k % 2
            nxt = 1 - cur
            src_r, src_i = AB.ap[:, sl, cur, 0], AB.ap[:, sl, cur, 1]
            dst_r, dst_i = AB.ap[:, sl, nxt, 0], AB.ap[:, sl, nxt, 1]
            wr, wi, nwi = (coef.ap[:, pr, k, j:j + 1] for j in range(3))
            tsr, tsi = AB.t((sl, cur, 0)), AB.t((sl, cur, 1))
            tdr, tdi = AB.t((sl, nxt, 0)), AB.t((sl, nxt, 1))
            ck = coef.t()
            hi = 384 if k < 7 else 383
            n = hi - 128
            if k < 7:
                o_r, o_i, to_r, to_i = dst_r[:, 128:384], dst_i[:, 128:384], tdr, tdi
            else:
                o_r, o_i, to_r, to_i = Sbf.ap[:, sl, 0, 1:256], Sbf.ap[:, sl, 1, 1:256], Sbf.t(sl), Sbf.t(sl)
            if False:
                def fma(out, otok, a, atok, w, b, btok, j, m):
                    self.ts(POOL, ptmp.ap[:, j, 0:m], a, w, None, ALU.mult, None, [atok, ck], [ptmp.t(j)])
                    self.tt(POOL, out, ptmp.ap[:, j, 0:m], b, ALU.add, [ptmp.t(j), btok], [otok])
                fma(dst_r[:, 128:384], tdr, src_r[:, 128 - d:384 - d], tsr, wr, src_r[:, 128:384], tsr, 0, 256)
                fma(dst_i[:, 128:384], tdi, src_i[:, 128 - d:384 - d], tsi, wr, src_i[:, 128:384], tsi, 1, 256)
                fma(o_r, to_r, src_i[:, 128 - d:hi - d], tsi, nwi, dst_r[:, 128:hi], tdr, 0, n)
                fma(o_i, to_i, src_r[:, 128 - d:hi - d], tsr, wi, dst_i[:, 128:hi], tdi, 1, n)
                return
            self.stt(dst_r[:, 128:384], src_r[:, 128 - d:384 - d], wr, src_r[:, 128:384], ALU.mult, ALU.add, [tsr, ck], [tdr])
            self.stt(dst_i[:, 128:384], src_i[:, 128 - d:384 - d], wr, src_i[:, 128:384], ALU.mult, ALU.add, [tsi, ck], [tdi])
            self.stt(o_r, src_i[:, 128 - d:hi - d], nwi, dst_r[:, 128:hi], ALU.mult, ALU.add, [tsi, tdr, ck], [to_r])
            self.stt(o_i, src_r[:, 128 - d:hi - d], wi, dst_i[:, 128:hi], ALU.mult, ALU.add, [tsr, tdi, ck], [to_i])

        def pair_back(pr):
            ft, p4 = pr // 4, pr % 4
            wslot = wslot_of(pr)
            gslot = ft % 2
            sl = pr % 4
            ys = pr % 2
            bankY = ps[4 + pr % 2]
            for gl in range(2):
                o = bankY.ap[:, 256 * gl:256 * gl + 256]
                hs = slice(64 * gl, 64 * gl + 64)
                self.mm(o, wch.ap[:, wslot, p4, 512 + 128 * gl:512 + 128 * gl + 128], D.ap[:, 2 * pr + gl, :], True, False,
                        [wch.t(wslot), D.t()], [bankY.t()])
                self.mm(o, wch.ap[hs, wslot, p4, 256:384], Sbf.ap[hs, sl, 0, :], False, False, [wch.t(wslot), Sbf.t(sl)], [bankY.t()])
                self.mm(o, wch.ap[hs, wslot, p4, 384:512], Sbf.ap[hs, sl, 1, :], False, True, [wch.t(wslot), Sbf.t(sl)], [bankY.t()],
                        sig=(gl == 1))
            self.act(Ygb.ap[:, ys].rearrange("p g c -> p (g c)"), bankY.ap, AF.Gelu, [bankY.t()], [Ygb.t(ys)])
            pT = ps[6]
            pTv = pT.ap.bitcast(BF16)
            for gl in range(2):
                for half in range(2):
                    q = 2 * gl + half
                    self.tr(pTv[:, 128 * q:128 * q + 128], Ygb.ap[:, ys, gl, 128 * half:128 * half + 128], identb.ap,
                            [Ygb.t(ys), identb.t()], [pT.t()], sig=(q == 3))
            for gl in range(2):
                fo = 32 * p4 + 16 * gl
                self.copy(ACT, Gtok.ap[:, gslot, :, :, fo:fo + 16],
                          pTv[:, 256 * gl:256 * gl + 256].rearrange("p (a i h) -> p a i h", a=2, h=16), [pT.t()], [Gtok.t(gslot)])

        def ft_done(ft):
            gslot = ft % 2
            for half in range(2):
                pT2 = ps[7]
                pv = pT2.ap.bitcast(BF16)
                for i in range(8):
                    self.tr(pv[:, 128 * i:128 * i + 128], Gtok.ap[:, gslot, half, i, :], identb.ap, [Gtok.t(gslot), identb.t()], [pT2.t()],
                            sig=(i == 7))
                self.copy(ACT, gT.ap[:, ft, 1024 * half:1024 * half + 1024].rearrange("p (c i) -> p i c", i=8),
                          pv.rearrange("p (i c) -> p i c", c=128), [pT2.t()], [gT.t(ft)])

        def load_w(ft):
            P.dma(SP, wch.ap[:, ft % 2], self.s5w[4 * ft:4 * ft + 4].rearrange("a p n -> p a n"), reads=[self.s5w_tok[ft]],
                  writes=[wch.t(ft % 2)])

        load_w(0)
        load_w(1)
        for p4 in (0, 1):
            pair_front(p4)
        for pp in range(16):
            prs = [2 * pp, 2 * pp + 1]
            for k in range(8):
                for pr in prs:
                    level(pr, k)
            if pp + 1 < 16:
                for pr in (2 * pp + 2, 2 * pp + 3):
                    pair_front(pr)
            for pr in prs:
                pair_back(pr)
            if pp % 2 == 1:
                ft = pp // 2
                ft_done(ft)
                if ft + 2 < 8:
                    load_w(ft + 2)

    def x_reload(self, s):
        P, c, ps = self.P, self.c, self.ps
        A = self.arena
        Xtok, _ = A.view(self.R_TAIL, [128, 8, 1024], F32, "Xtok2")
        xT = self.xT
        n = 0
        for half in range(2):
            self.load_xtok(s, half, Xtok)
            for ft in range(8):
                for jq in range(2):
                    bank = ps[n % 4]
                    for jj in range(4):
                        j = 4 * jq + jj
                        self.tr(bank.ap[:, 128 * jj:128 * jj + 128], Xtok.ap[:, j, 128 * ft:128 * ft + 128], c["ident_f"].ap,
                                [Xtok.t(), c["ident_f"].t()], [bank.t()], sig=(jj == 3))
                    dst = xT.ap[:, ft, 1024 * half:1024 * half + 1024].rearrange("p (c j) -> p j c", j=8)[:, 4 * jq:4 * jq + 4, :]
                    self.copy(ACT if n % 2 == 0 else DVE, dst, bank.ap.rearrange("p (j c) -> p j c", c=128), [bank.t()], [xT.t(ft)])
                    n += 1


    def stream(self, loaders, computes):
        n = len(loaders)
        slots = {0: loaders[0]()}
        for i in range(n):
            if i + 1 < n:
                slots[i + 1] = loaders[i + 1]()
            computes[i](slots.pop(i))

    def fm_norm(self, s, n, tt, dst, sq, tmpf, bank):
        c, xT, rstd = self.c, self.xT, self.rstd
        ts_ = slice(512 * tt, 512 * tt + 512)
        for ft in range(8):
            self.act(sq.ap[:, ft, :], xT.ap[:, ft, ts_], AF.Square, [xT.t(ft)], [sq.t(ft)])
        for ft in range(8):
            self.mm(bank.ap, c["onesm_b"].ap, sq.ap[:, ft, :], ft == 0, ft == 7, [c["onesm_b"].t(), sq.t(ft)], [bank.t()])
        self.act(rstd.ap, bank.ap, AF.Ln, [bank.t()], [rstd.t()], bias=self.epsc.ap)
        self.act(rstd.ap, rstd.ap, AF.Exp, [rstd.t()], [rstd.t()], scale=-0.5)
        ms = self.modsc
        for ft in range(8):
            tb = ft % 2
            self.stt(tmpf.ap[:, tb, :], xT.ap[:, ft, ts_], ms.ap[:, s, n, 0, ft:ft + 1], rstd.ap, ALU.mult, ALU.mult,
                     [xT.t(ft), ms.t(), rstd.t()], [tmpf.t(tb)])
            self.act(dst[:, ft, :], tmpf.ap[:, tb, :], AF.Identity, [tmpf.t(tb), ms.t()], [self.cur_h_tok], bias=ms.ap[:, s, n, 1, ft:ft + 1])

    def glu(self, s):
        P, ps = self.P, self.ps
        A = self.arena
        off = self.R_TAIL
        sg, off = A.view(off, [128, 2, 512], F32, "sg")
        mt_, off = A.view(off, [128, 2, 512], F32, "mt")
        gT, xT, adaT = self.actT, self.xT, self.adaT
        w = self.I["s5_w_glu"][0].rearrange("(k p) n -> p k n", p=128)
        allg = [gT.t(ft) for ft in range(8)]
        cnt = [0]

        def loader(ft):
            def f():
                v = lambda a: a[:, 0:2048].rearrange("p (k g n) -> p k g n", g=2, n=128)
                return self.wload([(lambda a: v(a)[:, :, 0, :], w[:, :, 128 * ft:128 * ft + 128]),
                                   (lambda a: v(a)[:, :, 1, :], w[:, :, 1024 + 128 * ft:1024 + 128 * ft + 128])])
            return f

        def compute(ft):
            def f(slot):
                wv = slot.ap[:, 0:2048].rearrange("p (k g n) -> p k g n", g=2, n=128)
                for tt in range(4):
                    n = cnt[0]
                    cnt[0] += 1
                    bv, bg = ps[(2 * n) % 8], ps[(2 * n + 1) % 8]
                    ts_ = slice(512 * tt, 512 * tt + 512)
                    for gi, bank in enumerate([bv, bg]):
                        for k in range(8):
                            self.mm(bank.ap, wv[:, k, gi, :], gT.ap[:, k, ts_], k == 0, k == 7, [slot.t()] + allg, [bank.t()])
                    b2 = n % 2
                    self.act(sg.ap[:, b2, :], bg.ap, AF.Sigmoid, [bg.t()], [sg.t(b2)])
                    self.tt(DVE, mt_.ap[:, b2, :], bv.ap, sg.ap[:, b2, :], ALU.mult, [bv.t(), sg.t(b2)], [mt_.t(b2)])
                    self.stt(xT.ap[:, ft, ts_], mt_.ap[:, b2, :], adaT.ap[:, 16 + ft, s:s + 1], xT.ap[:, ft, ts_], ALU.mult, ALU.add,
                             [mt_.t(b2), adaT.t(), xT.t(ft)], [xT.t(ft)])
            return f

        self.stream([loader(ft) for ft in range(8)], [compute(ft) for ft in range(8)])

    def mlp(self, s, l, nidx):
        P, ps = self.P, self.ps
        A = self.arena
        uT, _ = A.view(self.R_TAIL, [128, 32, 1024], BF16, "uT")
        off = self.R_ACTT
        htile, off = A.view(off, [128, 8, 1024], BF16, "htile")
        sq, off = A.view(off, [128, 8, 512], BF16, "sq")
        tmpf, off = A.view(off, [128, 2, 512], F32, "tmpf2")
        r, off = A.view(off, [128, 2, 1024], BF16, "r")
        xT, adaT = self.xT, self.adaT
        w1 = self.I["mlp_w1"][l]
        w2 = self.I["mlp_w2"][l].rearrange("(k p) n -> p k n", p=128)
        gm0 = 48 * l + 40
        nb = [0]
        for t2 in range(2):
            for sub in range(2):
                self.cur_h_tok = htile.t(sub)
                self.fm_norm(s, nidx, 2 * t2 + sub, htile.ap[:, :, 512 * sub:512 * sub + 512], sq, tmpf, ps[7])
            loaders, computes = [], []
            for ch in range(8):
                loaders.append(lambda ch=ch: self.wload_k8(w1, 512 * ch))

                def c1(sv, ch=ch):
                    slot, wv = sv
                    for m4 in range(4):
                        e = 4 * ch + m4
                        for sub in range(2):
                            bank = ps[nb[0] % 4]
                            nb[0] += 1
                            for k in range(8):
                                self.mm(bank.ap, wv[:, k, 128 * m4:128 * m4 + 128], htile.ap[:, k, 512 * sub:512 * sub + 512], k == 0, k == 7,
                                        [slot.t(), htile.t(sub)], [bank.t()])
                            self.act(r.ap[:, e % 2, 512 * sub:512 * sub + 512], bank.ap, AF.Relu, [bank.t()], [r.t((e % 2, sub))])
                        self.tt(DVE, uT.ap[:, e, :], r.ap[:, e % 2, :], r.ap[:, e % 2, :], ALU.mult,
                                [r.t((e % 2, 0)), r.t((e % 2, 1))], [uT.t(e)])
                computes.append(c1)
            allu = [uT.t(e) for e in range(32)]
            for o in range(8):
                def l2(o=o):
                    slot = self.wload([(lambda a: a.rearrange("p (k n) -> p k n", n=128), w2[:, :, 128 * o:128 * o + 128])])
                    return slot, slot.ap.rearrange("p (k n) -> p k n", n=128)
                loaders.append(l2)

                def c2(sv, o=o):
                    slot, wv = sv
                    for sub in range(2):
                        bank = ps[4 + (2 * o + sub) % 3]
                        ts_ = slice(1024 * t2 + 512 * sub, 1024 * t2 + 512 * sub + 512)
                        for k in range(32):
                            self.mm(bank.ap, wv[:, k, :], uT.ap[:, k, 512 * sub:512 * sub + 512], k == 0, k == 31, [slot.t()] + allu, [bank.t()])
                        self.stt(xT.ap[:, o, ts_], bank.ap, adaT.ap[:, gm0 + o, s:s + 1], xT.ap[:, o, ts_], ALU.mult, ALU.add,
                                 [bank.t(), adaT.t(), xT.t(o)], [xT.t(o)])
                computes.append(c2)
            self.stream(loaders, computes)

    def headnorm_batch(self, banks, sbanks, gaincol, dsts, dtoks, sq, tk4):
        c = self.c
        n = len(banks)
        for j in range(n):
            self.act(sq.ap[:, j, :], banks[j].ap, AF.Square, [banks[j].t()], [sq.t(j)])
        for j in range(n):
            self.mm(sbanks[j].ap, c["blk_b"].ap, sq.ap[:, j, :], True, True, [c["blk_b"].t(), sq.t(j)], [sbanks[j].t()])
        for j in range(n):
            self.act(tk4.ap[:, j, :], sbanks[j].ap, AF.Ln, [sbanks[j].t()], [tk4.t(j)], bias=self.epsc.ap)
        for j in range(n):
            self.act(tk4.ap[:, j, :], tk4.ap[:, j, :], AF.Exp, [tk4.t(j)], [tk4.t(j)], scale=-0.5)
        for j in range(n):
            self.stt(dsts[j], banks[j].ap, gaincol, tk4.ap[:, j, :], ALU.mult, ALU.mult, [banks[j].t(), self.qkg.t(), tk4.t(j)], dtoks[j])

    def kv_phase(self, s):
        P, ps = self.P, self.ps
        A = self.arena
        self.KT, _ = A.view(self.R_ACTT, [128, 8, 2048], BF16, "KT")
        off = self.R_TAIL
        self.V, off = A.view(off, [128, 16, 1024], BF16, "V")
        self.attn_off = off
        htile, off = A.view(off, [128, 8, 2048], BF16, "htile_kv")
        sq, off = A.view(off, [128, 8, 512], BF16, "sq_kv")
        tmpf, off = A.view(off, [128, 2, 512], F32, "tmpf_kv")
        tk2, off = A.view(off, [128, 2, 512], F32, "tk2")
        KT, V = self.KT, self.V
        wkv = self.I["w_kv"]
        for tt in range(4):
            self.cur_h_tok = htile.t(tt)
            self.fm_norm(s, 2, tt, htile.ap[:, :, 512 * tt:512 * tt + 512], sq, tmpf, ps[7])
        cnt = [0]
        loaders = [lambda ch=ch: self.wload_k8(wkv, 512 * ch) for ch in range(4)]
        computes = []
        for ch in range(4):
            def cK(sv, ch=ch):
                slot, wv = sv
                for tt in range(4):
                    ts_ = slice(512 * tt, 512 * tt + 512)
                    for hf in range(2):
                        n = cnt[0]
                        cnt[0] += 1
                        banks = [ps[(2 * n) % 4], ps[(2 * n + 1) % 4]]
                        sbanks = [ps[4 + (2 * n) % 4], ps[4 + (2 * n + 1) % 4]]
                        for j in range(2):
                            m4 = 2 * hf + j
                            for k in range(8):
                                self.mm(banks[j].ap, wv[:, k, 128 * m4:128 * m4 + 128], htile.ap[:, k, ts_], k == 0, k == 7,
                                        [slot.t(), htile.t(tt)], [banks[j].t()])
                        prs = [4 * ch + 2 * hf + j for j in range(2)]
                        sqv = Buf(sq.ap[:, 2 * (n % 4):2 * (n % 4) + 2, :], "sqv")
                        sqv.toks = {0: sq.t(2 * (n % 4)), 1: sq.t(2 * (n % 4) + 1)}
                        self.headnorm_batch(banks, sbanks, self.qkg.ap[:, 1:2], [KT.ap[:, p_, ts_] for p_ in prs],
                                            [[KT.t((p_, tt))] for p_ in prs], sqv, tk2)

            def cV(sv, ch=ch):
                slot, wv = sv
                for tb in range(16):
                    n = cnt[0]
                    cnt[0] += 1
                    bank = ps[n % 4]
                    for k in range(8):
                        self.mm(bank.ap, htile.ap[:, k, 128 * tb:128 * tb + 128], wv[:, k, :], k == 0, k == 7,
                                [slot.t(), htile.t(tb // 4)], [bank.t()])
                    self.copy(ACT if n % 2 == 0 else DVE, V.ap[:, tb, 512 * (ch - 2):512 * (ch - 2) + 512], bank.ap,
                              [bank.t()], [V.t(tb)])
            computes.append(cK if ch < 2 else cV)
        self.stream(loaders, computes)

    def attn_phase(self, s):
        P, ps, c = self.P, self.ps, self.c
        A = self.arena
        KT, V, xT, adaT = self.KT, self.V, self.xT, self.adaT
        off = self.attn_off
        qT, off = A.view(off, [128, 8, 512], BF16, "qT")
        oT, off = A.view(off, [128, 8, 512], BF16, "oT")
        o1 = off
        htile, o1 = A.view(o1, [128, 8, 512], BF16, "htile_q")
        sq, o1 = A.view(o1, [128, 8, 512], BF16, "sq_q")
        tmpf, o1 = A.view(o1, [128, 2, 512], F32, "tmpf_q")
        tk4, o1 = A.view(o1, [128, 4, 512], F32, "tk4_q")
        o2 = off
        Eb, o2 = A.view(o2, [128, 2, 2, 512], F32, "Eb")
        Lb, o2 = A.view(o2, [128, 3, 2, 512], BF16, "Lb")
        Wb, o2 = A.view(o2, [128, 3, 2, 512], BF16, "Wb")
        R32, o2 = A.view(o2, [128, 2, 512], F32, "R32")
        Rbf, o2 = A.view(o2, [128, 3, 2, 512], BF16, "Rbf")
        wq = self.I["sb_w_q"][0]
        wo = self.I["sb_w_o"][0]
        identb, negmask, negtri, negones = c["ident_b"], c["negmask_b"], c["negtri_b"], c["negones_b"]
        zeros = c["zeros512_b"]
        zbank = Buf(self.psall[:, 0:1024].rearrange("p (h t) -> p h t", h=2), "zbank")
        zbank.toks[0] = ps[0].t()
        abank = []
        for j in range(2):
            b_ = Buf(self.psall[:, 1024 + 1024 * j:2048 + 1024 * j].rearrange("p (h t) -> p h t", h=2), f"abank{j}")
            abank.append(b_)
        obank = ps[6]

        def ztoks():
            return [ps[0].t(), ps[1].t()]

        def atoks(j):
            return [ps[2 + 2 * j].t(), ps[3 + 2 * j].t()]

        for qt in range(4):
            ts_ = slice(512 * qt, 512 * qt + 512)
            self.cur_h_tok = htile.t()
            self.fm_norm(s, 3, qt, htile.ap, sq, tmpf, ps[7])
            cnt = [0]

            def cQ(sv, ch):
                slot, wv = sv
                banks = [ps[m4] for m4 in range(4)]
                for m4 in range(4):
                    for k in range(8):
                        self.mm(banks[m4].ap, wv[:, k, 128 * m4:128 * m4 + 128], htile.ap[:, k, :], k == 0, k == 7,
                                [slot.t(), htile.t()], [banks[m4].t()])
                self.headnorm_batch(banks, [ps[4 + m4] for m4 in range(4)], self.qkg.ap[:, 0:1],
                                    [qT.ap[:, 4 * ch + m4, :] for m4 in range(4)], [[qT.t(4 * ch + m4)] for m4 in range(4)], sq, tk4)

            self.stream([lambda ch=ch: self.wload_k8(wq, 512 * ch) for ch in range(2)],
                        [lambda sv, ch=ch: cQ(sv, ch) for ch in range(2)])
            P.barrier()
            tiles = []
            nkb = 4 * qt + 4
            for pair in range(8):
                for ii, kb in enumerate(range(nkb - 1, -1, -1)):
                    r_ = kb - 4 * qt
                    c0 = 128 * r_ if r_ >= 0 else 0
                    tiles.append(dict(pair=pair, kb=kb, first=(ii == 0), last=(kb == 0), diag=(r_ >= 0), c0=c0))
            nt = len(tiles)

            def zmm(bank, btoks, t, close):
                c0 = t["c0"]
                kb = t["kb"]
                rd = [KT.t((t["pair"], kb // 4)), qT.t(t["pair"])]
                for hl in range(2):
                    hs = slice(64 * hl, 64 * hl + 64)
                    self.mm(bank.ap[:, hl, c0:512], KT.ap[hs, t["pair"], 128 * kb:128 * kb + 128], qT.ap[hs, t["pair"], c0:512], True,
                            close and not t["diag"], rd, btoks, sig=(close and not t["diag"] and hl == 1))
                if t["diag"]:
                    for hl in range(2):
                        self.mm(bank.ap[:, hl, c0:c0 + 128], identb.ap, negmask.ap, False, close, [identb.t(), negmask.t()], btoks,
                                sig=(close and hl == 1))

            def stageA1(i):
                t = tiles[i]
                c0 = t["c0"]
                zmm(zbank, ztoks(), t, True)
                self.act(Eb.ap[:, i % 2, :, c0:512], zbank.ap[:, :, c0:512], AF.Exp, ztoks(), [Eb.t(i % 2)])

            def stageA2(i):
                t = tiles[i]
                c0 = t["c0"]
                self.act(Lb.ap[:, i % 3, :, c0:512], Eb.ap[:, i % 2, :, c0:512], AF.Ln, [Eb.t(i % 2)], [Lb.t(i % 3)], bias=self.onec.ap)
                if not t["last"]:
                    if t["first"]:
                        P.op(POOL, lambda e: e.memset(R32.ap, 0.0), [], [R32.t()])
                    self.tt(POOL, R32.ap[:, :, c0:512], R32.ap[:, :, c0:512], Lb.ap[:, i % 3, :, c0:512], ALU.add,
                            [R32.t(), Lb.t(i % 3)], [R32.t()])
                    self.copy(DVE, Rbf.ap[:, i % 3], R32.ap, [R32.t()], [Rbf.t(i % 3)])

            def stageB(i):
                t = tiles[i]
                ab = abank[i % 2]
                at = atoks(i % 2)
                c0 = t["c0"]
                zmm(ab, at, t, False)
                for hl in range(2):
                    self.mm(ab.ap[:, hl, c0:512], negtri.ap, Lb.ap[:, i % 3, hl, c0:512], False, t["first"], [negtri.t(), Lb.t(i % 3)], at,
                            sig=(t["first"] and hl == 1))
                if not t["first"]:
                    for hl in range(2):
                        self.mm(ab.ap[:, hl, c0:512], negones.ap, Rbf.ap[:, (i - 1) % 3, hl, c0:512], False, True,
                                [negones.t(), Rbf.t((i - 1) % 3)], at, sig=(hl == 1))
                self.act(Wb.ap[:, i % 3, :, c0:512], ab.ap[:, :, c0:512], AF.Exp, at, [Wb.t(i % 3)])

            def stageC(i):
                t = tiles[i]
                c0 = t["c0"]
                if t["first"]:
                    self.mm(obank.ap, zeros.ap[:, 0:128], zeros.ap, True, False, [zeros.t()], [obank.t()])
                for hl in range(2):
                    h = 2 * t["pair"] + hl
                    hs = slice(64 * hl, 64 * hl + 64)
                    self.mm(obank.ap[hs, c0:512], V.ap[:, t["kb"], 64 * h:64 * h + 64], Wb.ap[:, i % 3, hl, c0:512], False, t["last"],
                            [V.t(t["kb"]), Wb.t(i % 3)], [obank.t()], sig=(t["last"] and hl == 1))
                if t["last"]:
                    self.copy(DVE, oT.ap[:, t["pair"], :], obank.ap, [obank.t()], [oT.t(t["pair"])])

            for step in range(nt + 3):
                if step < nt:
                    stageA1(step)
                if 0 <= step - 2 < nt:
                    stageB(step - 2)
                if step < nt:
                    stageA2(step)
                if 0 <= step - 3 < nt:
                    stageC(step - 3)
            P.barrier()
            allo = [oT.t(p_) for p_ in range(8)]

            def cO(sv, ch):
                slot, wv = sv
                for m4 in range(4):
                    ft = 4 * ch + m4
                    bank = ps[4 + ft % 2]
                    for k in range(8):
                        self.mm(bank.ap, wv[:, k, 128 * m4:128 * m4 + 128], oT.ap[:, k, :], k == 0, k == 7, [slot.t()] + allo, [bank.t()])
                    self.stt(xT.ap[:, ft, ts_], bank.ap, adaT.ap[:, 48 + 16 + ft, s:s + 1], xT.ap[:, ft, ts_], ALU.mult, ALU.add,
                             [bank.t(), adaT.t(), xT.t(ft)], [xT.t(ft)])

            self.stream([lambda ch=ch: self.wload_k8(wo, 512 * ch) for ch in range(2)],
                        [lambda sv, ch=ch: cO(sv, ch) for ch in range(2)])

    def output_phase(self, s):
        P, ps, c = self.P, self.ps, self.c
        A = self.arena
        ost, _ = A.view(self.R_TAIL, [128, 2, 1024], F32, "ostage")
        xT = self.xT
        allx = [xT.t(ft) for ft in range(8)]
        n = 0
        for tb in range(16):
            ob = tb % 2
            for hf in range(2):
                bank = ps[n % 4]
                for jj in range(4):
                    ft = 4 * hf + jj
                    self.tr(bank.ap[:, 128 * jj:128 * jj + 128], xT.ap[:, ft, 128 * tb:128 * tb + 128], c["ident_f"].ap,
                            allx + [c["ident_f"].t()], [bank.t()], sig=(jj == 3))
                self.copy(ACT if n % 2 == 0 else DVE, ost.ap[:, ob, 512 * hf:512 * hf + 512], bank.ap, [bank.t()], [ost.t(ob)])
                n += 1
            otok = Tok(f"out{s}_{tb}")
            P.dma(SP, self.out[s, 128 * tb:128 * tb + 128, :], ost.ap[:, ob, :], reads=[ost.t(ob)], writes=[otok], tok=ost.t(ob))


def build_program(debug=None, stop_after=None, nseq=NS):
    b = Builder(debug=debug, stop_after=stop_after, nseq=nseq)
    nc = b.build()
    if b.P.nfwd:
        print("forward-redirected PE deps:", b.P.nfwd)
    return nc, b


_PARAM_NAMES = ["ada_w", "ada_b", "mix_norm_g", "mlp_norm_g", "mlp_w1", "mlp_w2", "s5_a_re", "s5_a_im", "s5_log_dt",
                "s5_b_re", "s5_b_im", "s5_c_re", "s5_c_im", "s5_d", "s5_w_glu", "kv_ada_w", "kv_ada_b", "kv_norm_g",
                "w_kv", "k_norm_g", "sb_w_q", "q_norm_g", "sb_w_o"]


def kernel(**inputs):
    x = np.ascontiguousarray(np.asarray(inputs["x"], dtype=np.float32))
    c = np.ascontiguousarray(np.asarray(inputs["c"], dtype=np.float32))
    params = {k: np.ascontiguousarray(np.asarray(inputs[k], dtype=np.float32)) for k in _PARAM_NAMES}
    nc, _ = build_program()
    in_maps = []
    for i in range(NCORES):
        m = dict(params)
        m["x"] = np.ascontiguousarray(x[NS * i:NS * i + NS])
        m["c"] = np.ascontiguousarray(c[NS * i:NS * i + NS])
        in_maps.append(m)
    res = run_bass_kernel_spmd(nc, in_maps, core_ids=list(range(NCORES)))
    out = np.concatenate([np.asarray(r["out"]) for r in res.results], axis=0)
    return out.astype(np.float32, copy=False)
```

```python
import math
from contextlib import ExitStack

import numpy as np
import concourse.bass as bass
import concourse.mybir as mybir
from concourse.bass_utils import run_bass_kernel_spmd

F32 = mybir.dt.float32
BF16 = mybir.dt.bfloat16
U8 = mybir.dt.uint8
AF = mybir.ActivationFunctionType
ALU = mybir.AluOpType
PE, DVE, ACT, POOL, SP = "tensor", "vector", "scalar", "gpsimd", "sync"
ENGS = [PE, DVE, ACT, POOL, SP]

D = 1024
T = 2048
NS = 2
FT = 8
TT = 512
NTT = T // TT
DFF = 4096
EPS = 1e-6
NCORES = 8
EPOCH_MAX = 12000


class Tok:
    __slots__ = ("w", "r", "sem", "semcnt", "name")

    def __init__(self, name=""):
        self.w = None
        self.r = []
        self.sem = None
        self.semcnt = 0
        self.name = name


class Op:
    __slots__ = ("eng", "fn", "deps", "dma", "sig", "semref", "semval", "inc", "id", "sigok")


class Prog:
    def __init__(self, nc, stack):
        self.nc = nc
        self.stack = stack
        self.ops = []
        self.last = {e: None for e in ENGS}
        self.barrier_deps = {e: [] for e in ENGS}
        self.dmas = []
        self.nsem = 0

    def new_sem(self, name):
        self.nsem += 1
        return self.stack.enter_context(self.nc.semaphore(f"{name}_{self.nsem}"))

    def op(self, eng, fn, reads=(), writes=(), dma=None, sigok=True):
        o = Op()
        o.sigok = sigok
        o.eng = eng
        o.fn = fn
        o.dma = dma
        o.sig = False
        o.semref = None
        o.semval = 0
        o.inc = 0
        o.id = len(self.ops)
        deps = {}

        def add(d, war=False):
            if d is None:
                return
            if d.dma is None and d.eng == eng:
                if eng == PE:
                    return
            deps[d.id] = d

        for t in reads:
            add(t.w)
        for t in writes:
            add(t.w)
            for r in t.r:
                add(r, war=True)
        for d in self.barrier_deps[eng]:
            add(d)
        self.barrier_deps[eng] = []
        o.deps = list(deps.values())
        for t in reads:
            t.r.append(o)
        for t in writes:
            t.w = o
            t.r = []
        self.ops.append(o)
        self.last[eng] = o
        if dma is not None:
            self.dmas.append(o)
        return o

    def dma(self, eng, out, in_, reads=(), writes=(), tok=None, **kw):
        if tok is None:
            tok = writes[0]

        def fn(e):
            return e.dma_start(out=out, in_=in_, **kw)
        return self.op(eng, fn, reads=reads, writes=writes, dma=tok)

    def barrier(self):
        deps = [o for o in self.last.values() if o is not None] + list(self.dmas)
        self.dmas = []
        for e in ENGS:
            self.barrier_deps[e] = list(self.barrier_deps[e]) + deps

    def emit(self):
        nc = self.nc
        pe_ops = [o for o in self.ops if o.eng == PE and o.dma is None]
        if pe_ops:
            pe_ops[-1].sigok = True
        nxt = {}
        cur = None
        for o in reversed(pe_ops):
            if o.sigok:
                cur = o
            nxt[o.id] = cur
        nfwd = 0
        for o in self.ops:
            nd = {}
            for d in o.deps:
                if d.eng == PE and d.dma is None and not d.sigok:
                    d = nxt[d.id]
                    if d.id > o.id:
                        nfwd += 1
                nd[d.id] = d
            o.deps = list(nd.values())
        self.nfwd = nfwd
        for o in self.ops:
            for d in o.deps:
                d.sig = True
        cnt = {e: 0 for e in ENGS}
        cursem = {e: None for e in ENGS}
        for o in self.ops:
            if o.dma is not None:
                t = o.dma
                if t.sem is None or t.semcnt >= 16 * 3000:
                    t.sem = self.new_sem("d")
                    t.semcnt = 0
                t.semcnt += 16
                o.semref = t.sem
                o.semval = t.semcnt
                o.inc = 16
            elif o.sig:
                e = o.eng
                if cursem[e] is None or cnt[e] >= EPOCH_MAX:
                    cursem[e] = self.new_sem("e" + e[:2])
                    cnt[e] = 0
                cnt[e] += 1
                o.semref = cursem[e]
                o.semval = cnt[e]
                o.inc = 1
        with nc.Block() as block:
            for e in ENGS:
                oplist = [o for o in self.ops if o.eng == e]

                def body(eh, oplist=oplist):
                    waited = {}
                    for o in oplist:
                        need = {}
                        for d in o.deps:
                            k = id(d.semref)
                            if k not in need or need[k][1] < d.semval:
                                need[k] = (d.semref, d.semval)
                        for k, (s, v) in need.items():
                            if waited.get(k, 0) < v:
                                eh.wait_ge(s, v)
                                waited[k] = v
                        if o.fn is not None:
                            inst = o.fn(eh)
                            if o.semref is not None:
                                inst.then_inc(o.semref, o.inc)

                getattr(block, e)(body)


class Buf:
    def __init__(self, ap, name=""):
        self.ap = ap
        self.name = name
        self.toks = {}

    def t(self, key=0):
        if key not in self.toks:
            self.toks[key] = Tok(f"{self.name}{key}")
        return self.toks[key]

    def ts(self, keys):
        return [self.t(k) for k in keys]


class Arena:
    def __init__(self, ap_u8):
        self.ap = ap_u8
        self.size = ap_u8.shape[1]

    def view(self, off, shape, dt, name=""):
        esz = 4 if dt == F32 else 2
        n = 1
        for s in shape[1:]:
            n *= s
        nbytes = n * esz
        assert off % 4 == 0 and off + nbytes <= self.size, (name, off, nbytes, self.size)
        v = self.ap[0:shape[0], off:off + nbytes].bitcast(dt)
        if len(shape) == 3:
            v = v.rearrange("p (a b) -> p a b", b=shape[2])
        elif len(shape) == 4:
            v = v.rearrange("p (a b c) -> p a b c", b=shape[2], c=shape[3])
        elif len(shape) == 5:
            v = v.rearrange("p (a b c d) -> p a b c d", b=shape[2], c=shape[3], d=shape[4])
        return Buf(v, name), off + nbytes


class Builder:
    def __init__(self, debug=None, stop_after=None, nseq=NS):
        self.debug = debug or []
        self.stop_after = stop_after
        self.nseq = nseq
        self.stack = ExitStack()
        self.nc = bass.Bass("TRN2", target_bir_lowering=False)
        self.P = Prog(self.nc, self.stack)
        self.dbg_out = {}

    def dram_in(self, name, shape):
        return self.nc.dram_tensor(name, list(shape), F32, kind="ExternalInput").ap()

    def sb(self, name, shape, dt):
        return self.stack.enter_context(self.nc.sbuf_tensor(name, list(shape), dt))

    def act(self, out, in_, func, reads, writes, bias=None, scale=None, accum_out=None):
        kw = {}
        if bias is not None:
            kw["bias"] = bias
        if scale is not None:
            kw["scale"] = scale
        if accum_out is not None:
            kw["accum_out"] = accum_out
        return self.P.op(ACT, lambda e: e.activation(out=out, in_=in_, func=func, **kw), reads, writes)

    def tt(self, eng, out, in0, in1, op, reads, writes):
        return self.P.op(eng, lambda e: e.tensor_tensor(out, in0, in1, op), reads, writes)

    def stt(self, out, in0, scalar, in1, op0, op1, reads, writes):
        return self.P.op(DVE, lambda e: e.scalar_tensor_tensor(out, in0, scalar, in1, op0, op1), reads, writes)

    def ts(self, eng, out, in0, s1, s2, op0, op1, reads, writes):
        if op1 is None:
            return self.P.op(eng, lambda e: e.tensor_scalar(out, in0, s1, None, op0), reads, writes)
        return self.P.op(eng, lambda e: e.tensor_scalar(out, in0, s1, s2, op0, op1), reads, writes)

    def copy(self, eng, out, in_, reads, writes):
        if eng == ACT:
            return self.P.op(ACT, lambda e: e.activation(out=out, in_=in_, func=AF.Identity), reads, writes)
        return self.P.op(eng, lambda e: e.tensor_copy(out, in_), reads, writes)

    def mm(self, out, lhsT, rhs, start, stop, reads, writes, sig=None):
        return self.P.op(PE, lambda e: e.matmul(out, lhsT=lhsT, rhs=rhs, start=start, stop=stop), reads, writes,
                         sigok=(stop if sig is None else sig))

    def tr(self, out, in_, ident, reads, writes, sig=True):
        return self.P.op(PE, lambda e: e.transpose(out, in_, ident), reads, writes, sigok=sig)

    def dbg(self, name, buf_ap, shape, reads, dt=F32):
        if name not in self.debug:
            return
        o = self.nc.dram_tensor("dbg_" + name, list(shape), dt, kind="ExternalOutput").ap()
        tok = Tok("dbg" + name)
        self.P.dma(SP, o, buf_ap, reads=reads, writes=[tok], tok=tok)
        self.dbg_out[name] = tok

    def build(self):
        nc, P = self.nc, self.P
        I = {}
        I["x"] = self.dram_in("x", [NS, T, D])
        I["c"] = self.dram_in("c", [NS, D])
        I["ada_w"] = self.dram_in("ada_w", [2, D, 6 * D])
        I["ada_b"] = self.dram_in("ada_b", [2, 6 * D])
        I["mix_norm_g"] = self.dram_in("mix_norm_g", [2, D])
        I["mlp_norm_g"] = self.dram_in("mlp_norm_g", [2, D])
        I["mlp_w1"] = self.dram_in("mlp_w1", [2, D, DFF])
        I["mlp_w2"] = self.dram_in("mlp_w2", [2, DFF, D])
        I["s5_a_re"] = self.dram_in("s5_a_re", [1, 64, 64])
        I["s5_a_im"] = self.dram_in("s5_a_im", [1, 64, 64])
        I["s5_log_dt"] = self.dram_in("s5_log_dt", [1, 64])
        I["s5_b_re"] = self.dram_in("s5_b_re", [1, 64, 64, 16])
        I["s5_b_im"] = self.dram_in("s5_b_im", [1, 64, 64, 16])
        I["s5_c_re"] = self.dram_in("s5_c_re", [1, 64, 16, 64])
        I["s5_c_im"] = self.dram_in("s5_c_im", [1, 64, 16, 64])
        I["s5_d"] = self.dram_in("s5_d", [1, D])
        I["s5_w_glu"] = self.dram_in("s5_w_glu", [1, D, 2 * D])
        I["kv_ada_w"] = self.dram_in("kv_ada_w", [D, 2 * D])
        I["kv_ada_b"] = self.dram_in("kv_ada_b", [2 * D])
        I["kv_norm_g"] = self.dram_in("kv_norm_g", [D])
        I["w_kv"] = self.dram_in("w_kv", [D, 2 * D])
        I["k_norm_g"] = self.dram_in("k_norm_g", [64])
        I["sb_w_q"] = self.dram_in("sb_w_q", [1, D, D])
        I["q_norm_g"] = self.dram_in("q_norm_g", [1, 64])
        I["sb_w_o"] = self.dram_in("sb_w_o", [1, D, D])
        self.I = I
        self.out = nc.dram_tensor("out", [NS, T, D], F32, kind="ExternalOutput").ap()
        self.s5w = nc.dram_tensor("s5w_scr", [32, 128, 768], BF16, kind="Internal").ap()
        self.s5w_tok = [Tok(f"s5w{i}") for i in range(8)]
        self.modrow = nc.dram_tensor("modrow_scr", [NS, 2, D], F32, kind="Internal").ap()
        self.modrow_tok = Tok("modrow")

        self.consts()
        ARENA = 194 * 1024
        self.arena = Arena(self.sb("arena", [128, ARENA], U8)[:])
        self.psall = self.stack.enter_context(nc.psum_tensor("psall", [128, 4096], F32))
        self.ps = [Buf(self.psall[:, 512 * i:512 * i + 512], f"ps{i}") for i in range(8)]

        self.setup_phase()
        if self.stop_after is not None and (self.stop_after == "setup" or self.stop_after.startswith("s5") or self.stop_after == "ada"):
            return self.finish()
        for s in range(self.nseq):
            self.seq_pipeline(s)
        return self.finish()

    def finish(self):
        P = self.P
        outs = [o for o in P.ops if o.dma is not None]
        fin = P.op(SP, None)
        fin.deps = list({o.id: o for o in outs}.values())
        P.emit()
        return self.nc

    def consts(self):
        P = self.P
        c = {}

        def mk(name, shape, dt):
            c[name] = Buf(self.sb("c_" + name, shape, dt)[:], name)
            return c[name]

        ident_f = mk("ident_f", [128, 128], F32)
        ident_b = mk("ident_b", [128, 128], BF16)
        onesm_b = mk("onesm_b", [128, 128], BF16)
        blk_b = mk("blk_b", [128, 128], BF16)
        negtri_b = mk("negtri_b", [128, 128], BF16)
        negones_b = mk("negones_b", [128, 128], BF16)
        negmask_b = mk("negmask_b", [128, 128], BF16)
        zeros_b = mk("zeros_b", [128, 64], BF16)
        onesrow_f = mk("onesrow_f", [1, 128], F32)
        sel2 = mk("sel2", [2, 128], F32)
        bmask_f = mk("bmask_f", [128, 128], F32)
        tmp_f = mk("ctmp_f", [128, 128], F32)
        hm = mk("hm", [128, 2], F32)
        zeros512_b = mk("zeros512_b", [128, 512], BF16)

        def pool(fn, reads, writes):
            return P.op(POOL, fn, reads, writes)

        pool(lambda e: e.memset(ident_f.ap, 1.0), [], [ident_f.t()])
        pool(lambda e: e.affine_select(out=ident_f.ap, in_=ident_f.ap, pattern=[[-1, 128]], compare_op=ALU.is_equal,
                                       fill=0.0, base=0, channel_multiplier=1), [ident_f.t()], [ident_f.t()])
        pool(lambda e: e.tensor_copy(ident_b.ap, ident_f.ap), [ident_f.t()], [ident_b.t()])
        pool(lambda e: e.memset(onesm_b.ap, 1.0 / 1024.0), [], [onesm_b.t()])
        pool(lambda e: e.memset(blk_b.ap, 0.0), [], [blk_b.t()])
        pool(lambda e: e.memset(blk_b.ap[0:64, 0:64], 1.0 / 64.0), [blk_b.t()], [blk_b.t()])
        pool(lambda e: e.memset(blk_b.ap[64:128, 64:128], 1.0 / 64.0), [blk_b.t()], [blk_b.t()])
        pool(lambda e: e.memset(tmp_f.ap, -1.0), [], [tmp_f.t()])
        pool(lambda e: e.affine_select(out=tmp_f.ap, in_=tmp_f.ap, pattern=[[-1, 128]], compare_op=ALU.is_ge,
                                       fill=0.0, base=0, channel_multiplier=1), [tmp_f.t()], [tmp_f.t()])
        pool(lambda e: e.tensor_copy(negtri_b.ap, tmp_f.ap), [tmp_f.t()], [negtri_b.t()])
        pool(lambda e: e.memset(negones_b.ap, -1.0), [], [negones_b.t()])
        pool(lambda e: e.tensor_scalar(negmask_b.ap, negtri_b.ap, 30000.0, None, ALU.mult), [negtri_b.t()], [negmask_b.t()])
        pool(lambda e: e.memset(zeros_b.ap, 0.0), [], [zeros_b.t()])
        pool(lambda e: e.memset(onesrow_f.ap, 1.0), [], [onesrow_f.t()])
        pool(lambda e: e.memset(sel2.ap, 1.0), [], [sel2.t()])
        pool(lambda e: e.affine_select(out=sel2.ap, in_=sel2.ap, pattern=[[1, 128]], compare_op=ALU.is_ge,
                                       fill=0.0, base=0, channel_multiplier=-64), [sel2.t()], [sel2.t()])
        pool(lambda e: e.affine_select(out=sel2.ap, in_=sel2.ap, pattern=[[-1, 128]], compare_op=ALU.is_ge,
                                       fill=0.0, base=63, channel_multiplier=64), [sel2.t()], [sel2.t()])
        pool(lambda e: e.memset(bmask_f.ap, 1.0), [], [bmask_f.t()])
        pool(lambda e: e.affine_select(out=bmask_f.ap.rearrange("p (i h) -> p i h", h=16), in_=bmask_f.ap.rearrange("p (i h) -> p i h", h=16),
                                       pattern=[[16, 8], [0, 16]], compare_op=ALU.is_ge,
                                       fill=0.0, base=15, channel_multiplier=-1), [bmask_f.t()], [bmask_f.t()])
        pool(lambda e: e.memset(zeros512_b.ap, 0.0), [], [zeros512_b.t()])
        pool(lambda e: e.memset(hm.ap, 0.0), [], [hm.t()])
        pool(lambda e: e.memset(hm.ap[0:64, 0:1], 1.0), [hm.t()], [hm.t()])
        pool(lambda e: e.memset(hm.ap[64:128, 1:2], 1.0), [hm.t()], [hm.t()])
        self.c = c
        self.VT = Buf(self.sb("VT", [128, 160], F32)[:], "VT")
        self.adaT = Buf(self.sb("adaT", [128, 112, 2], F32)[:], "adaT")
        self.coef = Buf(self.sb("coef", [128, 32, 8, 3], F32)[:], "coef")
        self.modsc = Buf(self.sb("modsc", [128, NS, 5, 2, 8], F32)[:], "modsc")
        self.qkg = Buf(self.sb("qkg", [128, 2], F32)[:], "qkg")
        self.epsc = Buf(self.sb("epsc", [128, 1], F32)[:], "epsc")
        self.onec = Buf(self.sb("onec", [128, 1], F32)[:], "onec")
        pool(lambda e: e.memset(self.epsc.ap, EPS), [], [self.epsc.t()])
        pool(lambda e: e.memset(self.onec.ap, 1.0), [], [self.onec.t()])

    def ring_init(self, off, nslots=2):
        self.ring = []
        for i in range(nslots):
            b, off = self.arena.view(off, [128, 4096], BF16, f"ring{i}")
            self.ring.append(b)
        self.ring_i = 0
        return off

    def wload(self, srcs):
        slot = self.ring[self.ring_i % len(self.ring)]
        self.ring_i += 1
        for dstf, src in srcs:
            self.P.dma(POOL, dstf(slot.ap), src, reads=[], writes=[slot.t()], tok=slot.t())
        return slot

    def wload_k8(self, w2d, col0, ncols=512):
        src = w2d.rearrange("(k p) n -> p k n", p=128)[:, :, col0:col0 + ncols]
        slot = self.wload([(lambda a: a[:, 0:8 * ncols].rearrange("p (k n) -> p k n", n=ncols), src)])
        return slot, slot.ap[:, 0:8 * ncols].rearrange("p (k n) -> p k n", n=ncols)

    def setup_phase(self):
        P, I, c = self.P, self.I, self.c
        A = self.arena
        ps = self.ps
        off = 0
        off = self.ring_init(off, 2)
        self.ring_end = off
        off = A.size - 42 * 1024
        self.s5_limit = off
        vrA, off = A.view(off, [128, 128], F32, "vrA")
        vrB, off = A.view(off, [128, 128], F32, "vrB")
        cs, off = A.view(off, [2, 1024], F32, "cs")
        sT, off = A.view(off, [128, 8, 2], BF16, "sT")
        rowb = []
        for b in range(2):
            r, off = A.view(off, [1, 2048], F32, f"rowb{b}")
            rowb.append(r)
        biasrow, off = A.view(off, [1, 2048], F32, "biasrow")
        grow, off = A.view(off, [1, 1024], F32, "grow")
        abrow, off = A.view(off, [1, 2, 1024], F32, "abrow")
        ld = Tok("setup_ld")

        def ldma(out, in_, wtoks):
            P.dma(SP, out, in_, reads=[], writes=wtoks)

        ldma(vrA.ap[0:16, :], I["mix_norm_g"].rearrange("l (k p) -> (l k) p", p=128), [vrA.t()])
        ldma(vrA.ap[16:32, :], I["mlp_norm_g"].rearrange("l (k p) -> (l k) p", p=128), [vrA.t()])
        ldma(vrA.ap[32:40, :], I["kv_norm_g"].rearrange("(k p) -> k p", p=128), [vrA.t()])
        adab = I["ada_b"].rearrange("l (k p) -> (l k) p", p=128)
        ldma(vrA.ap[40:128, :], adab[0:88, :], [vrA.t()])
        ldma(vrB.ap[0:8, :], adab[88:96, :], [vrB.t()])
        ldma(vrB.ap[8:24, :], I["kv_ada_b"].rearrange("(k p) -> k p", p=128), [vrB.t()])
        for hh in range(2):
            ldma(vrB.ap[24:25, 64 * hh:64 * hh + 64], I["q_norm_g"], [vrB.t()])
            ldma(vrB.ap[25:26, 64 * hh:64 * hh + 64], I["k_norm_g"].rearrange("(o d) -> o d", o=1), [vrB.t()])
        ldma(cs.ap, I["c"], [cs.t()])
        ldma(biasrow.ap, I["ada_b"][0:1, 0:2048], [biasrow.t()])
        ldma(grow.ap, I["mix_norm_g"][0:1, :], [grow.t()])
        VT = self.VT
        self.tr(ps[0].ap[:, 0:128], vrA.ap, c["ident_f"].ap, [vrA.t(), c["ident_f"].t()], [ps[0].t()])
        self.tr(ps[0].ap[:, 128:154], vrB.ap[0:26, :], c["ident_f"].ap[0:26, 0:26], [vrB.t(), c["ident_f"].t()], [ps[0].t()])
        self.copy(DVE, VT.ap[:, 0:154], ps[0].ap[:, 0:154], [ps[0].t()], [VT.t()])
        self.ts(DVE, self.qkg.ap[:, 0:1], VT.ap[:, 152:153], 0.125, None, ALU.mult, None, [VT.t()], [self.qkg.t()])
        self.copy(DVE, self.qkg.ap[:, 1:2], VT.ap[:, 153:154], [VT.t()], [self.qkg.t()])
        self.act(cs.ap, cs.ap, AF.Silu, [cs.t()], [cs.t()])
        for k in range(8):
            self.tr(ps[1].ap[:, 2 * k:2 * k + 2], cs.ap[0:2, 128 * k:128 * k + 128], c["ident_f"].ap[0:2, 0:2],
                    [cs.t(), c["ident_f"].t()], [ps[1].t()])
        self.copy(DVE, sT.ap, ps[1].ap[:, 0:16].rearrange("p (k b) -> p k b", b=2), [ps[1].t()], [sT.t()])
        s5_rest = self.s5_setup_early()
        adaps = ps[5]
        chunks = [(I["ada_w"][0], j * 512) for j in range(12)] + [(I["ada_w"][1], j * 512) for j in range(12)] + \
                 [(I["kv_ada_w"], j * 512) for j in range(4)]
        for ci, (w2d, col0) in enumerate(chunks):
            slot, wv = self.wload_k8(w2d, col0)
            for mt in range(4):
                ot = ci * 4 + mt
                for k in range(8):
                    self.mm(adaps.ap[:, 2 * ot:2 * ot + 2], wv[:, k, 128 * mt:128 * mt + 128], sT.ap[:, k, :],
                            k == 0, k == 7, [slot.t(), sT.t()], [adaps.t()])
            if ci < 4:
                for b in range(2):
                    rp = ps[6 + b]
                    for k in range(8):
                        self.mm(rp.ap[0:1, :], sT.ap[:, k, b:b + 1], wv[:, k, :], k == 0, k == 7, [slot.t(), sT.t()], [rp.t()])
                    self.copy(ACT, rowb[b].ap[0:1, 512 * ci:512 * ci + 512], rp.ap[0:1, :], [rp.t()], [rowb[b].t()])
        self.tt(DVE, self.adaT.ap, adaps.ap[:, 0:224].rearrange("p (o b) -> p o b", b=2),
                VT.ap[:, 40:152].unsqueeze(2).to_broadcast([128, 112, 2]),
                ALU.add, [adaps.t(), VT.t()], [self.adaT.t()])
        adaT = self.adaT
        norms = {1: (16, 24, 32), 2: (32, 96, 104), 3: (8, 48, 56), 4: (24, 72, 80)}
        for s in range(2):
            for n, (gc, sh, sc) in norms.items():
                self.stt(self.modsc.ap[:, s, n, 0, :], adaT.ap[:, sc:sc + 8, s], 1.0, VT.ap[:, gc:gc + 8], ALU.add, ALU.mult,
                         [adaT.t(), VT.t()], [self.modsc.t()])
                self.copy(DVE, self.modsc.ap[:, s, n, 1, :], adaT.ap[:, sh:sh + 8, s], [adaT.t()], [self.modsc.t()])
        for b in range(2):
            self.tt(DVE, rowb[b].ap, rowb[b].ap, biasrow.ap, ALU.add, [rowb[b].t(), biasrow.t()], [rowb[b].t()])
            self.stt(abrow.ap[0:1, 0, :], rowb[b].ap[0:1, 1024:2048], 1.0, grow.ap, ALU.add, ALU.mult,
                     [rowb[b].t(), grow.t()], [abrow.t()])
            self.copy(DVE, abrow.ap[0:1, 1, :], rowb[b].ap[0:1, 0:1024], [rowb[b].t()], [abrow.t()])
            P.dma(SP, self.modrow[b:b + 1], abrow.ap, reads=[abrow.t()], writes=[self.modrow_tok], tok=self.modrow_tok)
        self.dbg("adaT", self.adaT.ap, [128, 112, 2], [self.adaT.t()])
        self.dbg("modsc", self.modsc.ap, [128, NS, 5, 2, 8], [self.modsc.t()])
        self.dbg("VT", self.VT.ap, [128, 160], [self.VT.t()])
        if self.stop_after == "ada":
            return
        s5_rest()
        P.barrier()

    def s5_setup_early(self):
        P, I, c = self.P, self.I, self.c
        A = self.arena
        ps = self.ps
        off = self.ring_end
        lamrows, off = A.view(off, [32, 2, 128], F32, "lamrows")
        ldt2, off = A.view(off, [2, 32], F32, "ldt2")
        NSM = 24
        smb, off = A.view(off, [128, NSM, 32], F32, "sm")
        Bq, off = A.view(off, [128, 2, 32, 16], F32, "Bq")
        Bb, off = A.view(off, [128, 2, 32, 16], F32, "Bb")
        Cin, off = A.view(off, [128, 2, 128], F32, "Cin")
        CT, off = A.view(off, [128, 2, 32, 16], F32, "CT")
        X, off = A.view(off, [128, 2, 32, 8, 16], F32, "X")
        Wo, off = A.view(off, [128, 2, 32, 8, 16], F32, "Wo")
        We, off = A.view(off, [128, 2, 32, 128], F32, "We")
        tmp, off = A.view(off, [128, 2, 512], F32, "s5tmp")
        Drows, off = A.view(off, [64, 16], F32, "Drows")
        Drep, off = A.view(off, [64, 8, 16], F32, "Drep")
        dcol, off = A.view(off, [128, 64], F32, "dcol")
        tmpm, off = A.view(off, [128, 2, 2, 128], F32, "tmpm")
        stage, off = A.view(off, [128, 2, 4, 768], BF16, "stage")
        WoM, off = A.view(off, [128, 2, 2, 2, 128], F32, "WoM")
        assert off <= self.s5_limit, (off, self.s5_limit)
        identf = c["ident_f"]

        names = ["lre", "lim", "dt", "mag", "th", "c", "s", "t1", "t2", "t3", "abre", "abim", "nr", "den",
                 "fre", "fim", "rm2", "lire", "liim", "mure", "muim", "w2r", "w2i"]
        sm = {n: (smb.ap[:, i, :], smb.t(n)) for i, n in enumerate(names)}

        def S(n):
            return sm[n][0]

        def St(n):
            return sm[n][1]

        P.dma(SP, lamrows.ap[:, 0, :], I["s5_a_re"][0].rearrange("(a g) p -> a (g p)", g=2), writes=[lamrows.t()])
        P.dma(SP, lamrows.ap[:, 1, :], I["s5_a_im"][0].rearrange("(a g) p -> a (g p)", g=2), writes=[lamrows.t()])
        P.dma(SP, ldt2.ap, I["s5_log_dt"][0].rearrange("(a g) -> g a", g=2), writes=[ldt2.t()], allow_slow_non_contiguous=True)
        for ri, nm in enumerate(["s5_b_re", "s5_b_im"]):
            src = I[nm][0].rearrange("(a g) p h -> (g p) a h", g=2)
            for q4 in range(4):
                P.dma(SP, Bq.ap[:, ri, 8 * q4:8 * q4 + 8, :], src[:, 8 * q4:8 * q4 + 8, :], writes=[Bq.t()])
        P.dma(SP, Drows.ap, I["s5_d"][0].rearrange("(g h) -> g h", h=16), writes=[Drows.t()])

        self.tr(ps[0].ap[:, 0:32], lamrows.ap[:, 0, :], identf.ap[0:32, 0:32], [lamrows.t(), identf.t()], [ps[0].t()])
        self.tr(ps[0].ap[:, 32:64], lamrows.ap[:, 1, :], identf.ap[0:32, 0:32], [lamrows.t(), identf.t()], [ps[0].t()])
        self.mm(ps[0].ap[:, 64:96], c["sel2"].ap, ldt2.ap, True, True, [c["sel2"].t(), ldt2.t()], [ps[0].t()])
        self.copy(DVE, S("lre"), ps[0].ap[:, 0:32], [ps[0].t()], [St("lre")])
        self.copy(DVE, S("lim"), ps[0].ap[:, 32:64], [ps[0].t()], [St("lim")])
        self.act(S("dt"), ps[0].ap[:, 64:96], AF.Exp, [ps[0].t()], [St("dt")])

        def tt(out, a, b, op, eng=DVE):
            self.tt(eng, S(out), S(a), S(b), op, [St(a), St(b)], [St(out)])

        if self.stop_after == "s5a":
            return

        def horner(out, y, coefs, last):
            self.ts(DVE, S(out), S(y), coefs[0], None, ALU.mult, None, [St(y)], [St(out)])
            for ck in coefs[1:]:
                self.stt(S(out), S(out), ck, S(y), ALU.add, ALU.mult, [St(out), St(y)], [St(out)])
            self.ts(DVE, S(out), S(out), last, None, ALU.add, None, [St(out)], [St(out)])

        tt("t1", "lre", "dt", ALU.mult)
        self.ts(DVE, S("t2"), S("t1"), 0.25, None, ALU.mult, None, [St("t1")], [St("t2")])
        horner("mag", "t2", [1.0 / 5040, 1.0 / 720, 1.0 / 120, 1.0 / 24, 1.0 / 6, 0.5, 1.0], 1.0)
        tt("mag", "mag", "mag", ALU.mult)
        tt("mag", "mag", "mag", ALU.mult)
        tt("th", "lim", "dt", ALU.mult)
        self.ts(DVE, S("t3"), S("th"), 1.0 / 32.0, None, ALU.mult, None, [St("th")], [St("t3")])
        tt("t2", "t3", "t3", ALU.mult)
        horner("c", "t2", [-1.0 / 3628800, 1.0 / 40320, -1.0 / 720, 1.0 / 24, -0.5], 1.0)
        horner("s", "t2", [1.0 / 362880, -1.0 / 5040, 1.0 / 120, -1.0 / 6], 1.0)
        tt("s", "s", "t3", ALU.mult)

        def csquare(a, b):
            tt("t1", a, a, ALU.mult)
            tt("t2", b, b, ALU.mult)
            self.stt(S(b), S(a), 2.0, S(b), ALU.mult, ALU.mult, [St(a), St(b)], [St(b)])
            tt(a, "t1", "t2", ALU.subtract)

        for _ in range(5):
            csquare("c", "s")
        tt("t1", "c", "c", ALU.mult)
        tt("t2", "s", "s", ALU.mult)
        tt("t1", "t1", "t2", ALU.add)
        self.ts(DVE, S("t1"), S("t1"), -0.5, 1.5, ALU.mult, ALU.add, [St("t1")], [St("t1")])
        tt("c", "c", "t1", ALU.mult)
        tt("s", "s", "t1", ALU.mult)
        tt("abre", "mag", "c", ALU.mult)
        tt("abim", "mag", "s", ALU.mult)
        self.ts(DVE, S("nr"), S("abre"), -1.0, None, ALU.add, None, [St("abre")], [St("nr")])
        tt("t1", "lre", "lre", ALU.mult)
        tt("t2", "lim", "lim", ALU.mult)
        tt("den", "t1", "t2", ALU.add)
        P.op(DVE, lambda e: e.reciprocal(S("den"), S("den")), [St("den")], [St("den")])
        tt("t1", "nr", "lre", ALU.mult)
        tt("t2", "abim", "lim", ALU.mult)
        tt("t1", "t1", "t2", ALU.add)
        tt("fre", "t1", "den", ALU.mult)
        tt("t1", "abim", "lre", ALU.mult)
        tt("t2", "nr", "lim", ALU.mult)
        tt("t1", "t1", "t2", ALU.subtract)
        tt("fim", "t1", "den", ALU.mult)
        tt("rm2", "mag", "mag", ALU.mult)
        P.op(DVE, lambda e: e.reciprocal(S("rm2"), S("rm2")), [St("rm2")], [St("rm2")])
        tt("lire", "abre", "rm2", ALU.mult)
        self.stt(S("liim"), S("abim"), -1.0, S("rm2"), ALU.mult, ALU.mult, [St("abim"), St("rm2")], [St("liim")])
        self.copy(DVE, S("mure"), S("abre"), [St("abre")], [St("mure")])
        self.copy(DVE, S("muim"), S("abim"), [St("abim")], [St("muim")])
        for _ in range(3):
            csquare("mure", "muim")
        coef = self.coef
        self.copy(DVE, S("w2r"), S("mure"), [St("mure")], [St("w2r")])
        self.copy(DVE, S("w2i"), S("muim"), [St("muim")], [St("w2i")])
        for k in range(8):
            self.copy(DVE, coef.ap[:, :, k, 0], S("w2r"), [St("w2r")], [coef.t()])
            self.copy(DVE, coef.ap[:, :, k, 1], S("w2i"), [St("w2i")], [coef.t()])
            self.ts(DVE, coef.ap[:, :, k, 2], S("w2i"), -1.0, None, ALU.mult, None, [St("w2i")], [coef.t()])
            if k < 7:
                csquare("w2r", "w2i")

        if self.stop_after == "s5b":
            return
        tmpt = [tmp.t(i) for i in range(2)]

        def cmul(ore, oim, otoks, are, aim, atoks, zre, zim, ztoks, n):
            ab = lambda nm: S(nm).unsqueeze(2).to_broadcast([128, 32, n])
            t = [tmp.ap[:, i, 0:32 * n].rearrange("p (a h) -> p a h", h=n) for i in range(2)]
            rd = atoks + ztoks
            self.tt(DVE, t[0], zre, ab(are), ALU.mult, rd, [tmpt[0]])
            self.tt(DVE, t[1], zim, ab(aim), ALU.mult, rd, [tmpt[1]])
            self.tt(DVE, ore, t[0], t[1], ALU.subtract, [tmpt[0], tmpt[1]], otoks)
            self.tt(DVE, t[0], zim, ab(are), ALU.mult, rd, [tmpt[0]])
            self.tt(DVE, t[1], zre, ab(aim), ALU.mult, rd, [tmpt[1]])
            self.tt(DVE, oim, t[0], t[1], ALU.add, [tmpt[0], tmpt[1]], otoks)

        fa = [St("fre"), St("fim")]
        cmul(Bb.ap[:, 0], Bb.ap[:, 1], [Bb.t()], "fre", "fim", fa, Bq.ap[:, 0], Bq.ap[:, 1], [Bq.t()], 16)
        la = [St("lire"), St("liim")]
        for j in range(8):
            if j == 0:
                zr, zi, zt = Bb.ap[:, 0], Bb.ap[:, 1], [Bb.t()]
            else:
                zr, zi, zt = X.ap[:, 0, :, j - 1, :], X.ap[:, 1, :, j - 1, :], [X.t(j - 1)]
            cmul(X.ap[:, 0, :, j, :], X.ap[:, 1, :, j, :], [X.t(j)], "lire", "liim", la, zr, zi, zt, 16)
        if self.stop_after == "s5c":
            return
        for r in range(8):
            for ri, nm in enumerate(["s5_c_re", "s5_c_im"]):
                db = (2 * r + ri) % 2
                P.dma(SP, Cin.ap[:, db, 0:64], I[nm][0][8 * r:8 * r + 8].rearrange("g h p -> (g h) p"), writes=[Cin.t(db)])
                pb = ps[1 + db]
                self.tr(pb.ap[0:64, 0:128], Cin.ap[:, db, 0:64], identf.ap, [Cin.t(db), identf.t()], [pb.t()])
                src = pb.ap[0:64, 0:128].rearrange("p (a g h) -> p a g h", g=2, h=16)
                self.copy(ACT, CT.ap[0:64, ri, 4 * r:4 * r + 4, :], src[:, :, 0, :], [pb.t()], [CT.t()])
                self.copy(ACT, CT.ap[64:128, ri, 4 * r:4 * r + 4, :], src[:, :, 1, :], [pb.t()], [CT.t()])
        if self.stop_after == "s5d":
            return
        ab_ = [St("abre"), St("abim")]
        for i in range(8):
            if i == 0:
                zr, zi, zt = CT.ap[:, 0], CT.ap[:, 1], [CT.t()]
            else:
                zr, zi, zt = Wo.ap[:, 0, :, i - 1, :], Wo.ap[:, 1, :, i - 1, :], [Wo.t(i - 1)]
            cmul(Wo.ap[:, 0, :, i, :], Wo.ap[:, 1, :, i, :], [Wo.t(i)], "abre", "abim", ab_, zr, zi, zt, 16)
        ma = [St("mure"), St("muim")]
        for j in range(8):
            cmul(We.ap[:, 0, :, 16 * j:16 * j + 16], We.ap[:, 1, :, 16 * j:16 * j + 16], [We.t(j)], "mure", "muim", ma,
                 X.ap[:, 0, :, j, :], X.ap[:, 1, :, j, :], [X.t(j)], 16)
        allWo = [Wo.t(i) for i in range(8)]
        WoN = Wo.t("neg")
        self.ts(DVE, Wo.ap[:, 1], Wo.ap[:, 1], -1.0, None, ALU.mult, None, allWo, allWo + [WoN])
        def rest():
            self.copy(DVE, Drep.ap, Drows.ap.unsqueeze(1).to_broadcast([64, 8, 16]), [Drows.t()], [Drep.t()])
            self.tr(ps[3].ap[:, 0:64], Drep.ap.rearrange("p j h -> p (j h)"), identf.ap[0:64, 0:64], [Drep.t(), identf.t()], [ps[3].t()])
            self.copy(DVE, dcol.ap, ps[3].ap[:, 0:64], [ps[3].t()], [dcol.t()])
            self.s5_setup_pairs(locals_=dict(We=We, X=X, Wo=Wo, WoN=WoN, allWo=allWo, stage=stage, WoM=WoM, tmpm=tmpm, dcol=dcol, identf=identf))
        return rest

    def s5_setup_pairs(self, locals_):
        P, c, ps = self.P, self.c, self.ps
        We, X, Wo, WoN, allWo, stage, WoM, tmpm, dcol, identf = (locals_[k] for k in
                                                                ["We", "X", "Wo", "WoN", "allWo", "stage", "WoM", "tmpm", "dcol", "identf"])
        allWe = [We.t(j) for j in range(8)]
        allX = [X.t(j) for j in range(8)]

        def phase1(pr):
            slot = (pr // 4) % 2
            p4 = pr % 4
            st_ = stage.t(slot)
            pa = ps[4 + pr % 2]
            pb = ps[6 + pr % 2]
            self.tr(pa.ap[:, 0:128], We.ap[:, 0, pr, :], identf.ap, allWe + [identf.t()], [pa.t()])
            self.tr(pa.ap[:, 128:256], We.ap[:, 1, pr, :], identf.ap, allWe + [identf.t()], [pa.t()])
            self.copy(ACT, stage.ap[:, slot, p4, 0:256], pa.ap[:, 0:256], [pa.t()], [st_])
            self.copy(ACT, stage.ap[:, slot, p4, 256:512].rearrange("p (r n) -> p r n", r=2),
                      Wo.ap[:, :, pr].rearrange("p r i h -> p r (i h)"), [WoN], [st_])
            wm = WoM.ap[:, pr % 2]
            for gl in range(2):
                self.ts(POOL if gl == 0 else DVE, wm[:, gl], Wo.ap[:, :, pr].rearrange("p r i h -> p r (i h)"), c["hm"].ap[:, gl:gl + 1], None, ALU.mult, None,
                        [WoN, c["hm"].t()], [WoM.t(pr % 2)])
            for gl in range(2):
                for ri in range(2):
                    self.mm(pb.ap[:, 128 * gl:128 * gl + 128], X.ap[:, ri, pr].rearrange("p j h -> p (j h)"),
                            wm[:, gl, ri, :], ri == 0, ri == 1, allX + [WoM.t(pr % 2)], [pb.t()])

        def phase2(pr):
            slot = (pr // 4) % 2
            p4 = pr % 4
            st_ = stage.t(slot)
            pb = ps[6 + pr % 2]
            tm = tmpm.ap[:, pr % 2]
            self.tt(DVE, tm, pb.ap[:, 0:256].rearrange("p (g n) -> p g n", g=2),
                    c["bmask_f"].ap.unsqueeze(1).to_broadcast([128, 2, 128]), ALU.mult, [pb.t(), c["bmask_f"].t()], [tmpm.t(pr % 2)])
            for gl in range(2):
                g = 2 * pr + gl
                self.stt(stage.ap[:, slot, p4, 512 + 128 * gl:512 + 128 * gl + 128], identf.ap, dcol.ap[:, g:g + 1], tm[:, gl, :],
                         ALU.mult, ALU.add, [identf.t(), dcol.t(), tmpm.t(pr % 2)], [st_])
            if p4 == 3:
                ftc = pr // 4
                P.dma(SP, self.s5w[4 * ftc:4 * ftc + 4].rearrange("a p n -> p a n"), stage.ap[:, slot], reads=[st_],
                      writes=[self.s5w_tok[ftc]])
                if ftc == 0:
                    self.dbg("s5w0", stage.ap[:, slot], [128, 4, 768], [st_], dt=BF16)
        phase1(0)
        for pr in range(32):
            if pr + 1 < 32:
                phase1(pr + 1)
            phase2(pr)
        self.dbg("coef", self.coef.ap, [128, 32, 8, 3], [self.coef.t()])

    R_RSTD = 16384
    R_XT = 18432
    R_ACTT = R_XT + 65536
    R_TAIL = R_ACTT + 32768

    def seq_pipeline(self, s):
        P = self.P
        A = self.arena
        self.xT, _ = A.view(self.R_XT, [128, 8, 2048], F32, f"xT{s}")
        self.actT, _ = A.view(self.R_ACTT, [128, 8, 2048], BF16, f"actT{s}")
        self.rstd, _ = A.view(self.R_RSTD, [128, 512], F32, f"rstd{s}")
        self.s5_prep(s)
        P.barrier()
        self.s5_core(s)
        P.barrier()
        self.dbg(f"gT{s}", self.actT.ap, [128, 8, 2048], [], dt=BF16)
        self.dbg(f"xT{s}", self.xT.ap, [128, 8, 2048], [])
        if self.stop_after == "xreload":
            return
        self.glu(s)
        P.barrier()
        self.mlp(s, 0, 1)
        P.barrier()
        self.dbg(f"x1T{s}", self.xT.ap, [128, 8, 2048], [])
        if self.stop_after == "layer0":
            return
        self.kv_phase(s)
        P.barrier()
        self.dbg(f"KT{s}", self.KT.ap, [128, 8, 2048], [], dt=BF16)
        self.dbg(f"V{s}", self.V.ap, [128, 16, 1024], [], dt=BF16)
        if self.stop_after == "kv":
            return
        self.attn_phase(s)
        P.barrier()
        self.dbg(f"x2T{s}", self.xT.ap, [128, 8, 2048], [])
        if self.stop_after == "attn":
            return
        self.mlp(s, 1, 4)
        P.barrier()
        self.output_phase(s)
        P.barrier()

    def load_xtok(self, s, half, Xtok):
        src = self.I["x"][s, half * 1024:(half + 1) * 1024, :].rearrange("(c j) f -> c j f", j=8)
        for q in range(4):
            self.P.dma(SP, Xtok.ap[32 * q:32 * q + 32], src[32 * q:32 * q + 32], writes=[Xtok.t()])

    def s5_prep(self, s):
        P, c, ps = self.P, self.c, self.ps
        A = self.arena
        off = self.R_TAIL
        D, off = A.view(off, [128, 64, 256], BF16, "D")
        self.D = D
        self.s5_tail_off = off
        Xtok, off = A.view(off, [128, 8, 1024], F32, "Xtok")
        Arep, off = A.view(off, [128, 1024], F32, "Arep")
        Brep, off = A.view(off, [128, 1024], F32, "Brep")
        ss, off = A.view(off, [128, 8], F32, "ss")
        rs, off = A.view(off, [128, 8], F32, "rs")
        junk, off = A.view(off, [128, 1024], BF16, "junk")
        o2 = self.R_ACTT
        htok, o2 = A.view(o2, [128, 64, 8, 16], BF16, "htok")
        tmpf, o2 = A.view(o2, [128, 2, 1024], F32, "tmpf")
        xT = self.xT
        P.dma(SP, Arep.ap, self.modrow[s, 0, :].partition_broadcast(128), reads=[self.modrow_tok], writes=[Arep.t()])
        P.dma(SP, Brep.ap, self.modrow[s, 1, :].partition_broadcast(128), reads=[self.modrow_tok], writes=[Brep.t()])
        n = 0
        for half in range(2):
            self.load_xtok(s, half, Xtok)
            for j in range(8):
                self.act(junk.ap, Xtok.ap[:, j, :], AF.Square, [Xtok.t()], [junk.t(), ss.t()], accum_out=ss.ap[:, j:j + 1])
            self.ts(DVE, rs.ap, ss.ap, 1.0 / 1024.0, EPS, ALU.mult, ALU.add, [ss.t()], [rs.t()])
            self.act(rs.ap, rs.ap, AF.Sqrt, [rs.t()], [rs.t()])
            P.op(DVE, lambda e: e.reciprocal(rs.ap, rs.ap), [rs.t()], [rs.t()])
            for j in range(8):
                tb = j % 2
                self.stt(tmpf.ap[:, tb, :], Xtok.ap[:, j, :], rs.ap[:, j:j + 1], Arep.ap, ALU.mult, ALU.mult,
                         [Xtok.t(), rs.t(), Arep.t()], [tmpf.t(tb)])
                self.tt(POOL, htok.ap[:, :, j, :], tmpf.ap[:, tb, :].rearrange("p (g h) -> p g h", h=16),
                        Brep.ap.rearrange("p (g h) -> p g h", h=16), ALU.add, [tmpf.t(tb), Brep.t()], [htok.t()])
            for ft in range(8):
                for jq in range(2):
                    bank = ps[4 + n % 4]
                    for jj in range(4):
                        j = 4 * jq + jj
                        self.tr(bank.ap[:, 128 * jj:128 * jj + 128], Xtok.ap[:, j, 128 * ft:128 * ft + 128], c["ident_f"].ap,
                                [Xtok.t(), c["ident_f"].t()], [bank.t()], sig=(jj == 3))
                    dst = xT.ap[:, ft, 1024 * half:1024 * half + 1024].rearrange("p (c j) -> p j c", j=8)[:, 4 * jq:4 * jq + 4, :]
                    self.copy(ACT if n % 2 == 0 else DVE, dst, bank.ap.rearrange("p (j c) -> p j c", c=128), [bank.t()], [xT.t(ft)])
                    n += 1
            for g4 in range(16):
                pb = ps[g4 % 4]
                pbv = pb.ap.bitcast(BF16)
                for gi in range(4):
                    g = 4 * g4 + gi
                    self.tr(pbv[:, 128 * gi:128 * gi + 128], htok.ap[:, g].rearrange("p j h -> p (j h)"), c["ident_b"].ap,
                            [htok.t(), c["ident_b"].t()], [pb.t()], sig=(gi == 3))
                self.copy(ACT if g4 % 2 == 0 else DVE, D.ap[:, 4 * g4:4 * g4 + 4, 128 * half:128 * half + 128],
                          pbv[:, 0:512].rearrange("p (g c) -> p g c", c=128), [pb.t()], [D.t()])

    def s5_core(self, s):
        P, c, ps = self.P, self.c, self.ps
        A = self.arena
        D = self.D
        coef = self.coef
        off = self.s5_tail_off
        wch, _ = A.view(0, [128, 2, 4, 768], BF16, "wch")
        AB, off = A.view(off, [128, 4, 2, 2, 384], F32, "AB")
        Sbf, off = A.view(off, [128, 4, 2, 256], BF16, "Sbf")
        Ygb, off = A.view(off, [128, 2, 2, 256], BF16, "Ygb")
        Gtok, off = A.view(off, [128, 2, 2, 8, 128], BF16, "Gtok")
        gT = self.actT
        abt = [AB.t((a, b_, c_)) for a in range(4) for b_ in range(2) for c_ in range(2)]
        P.op(POOL, lambda e: e.memset(AB.ap, 0.0), [], abt)
        P.op(POOL, lambda e: e.memset(Sbf.ap, 0.0), [], [Sbf.t(i) for i in range(4)])
        identb = c["ident_b"]

        def wslot_of(pr):
            return (pr // 4) % 2

        def pair_front(pr):
            ft, p4 = pr // 4, pr % 4
            wslot = wslot_of(pr)
            sl = pr % 4
            bankE = ps[pr % 4]
            for ri in range(2):
                for gl in range(2):
                    self.mm(bankE.ap[64 * gl:64 * gl + 64, 256 * ri:256 * ri + 256],
                            wch.ap[:, wslot, p4, 128 * ri + 64 * gl:128 * ri + 64 * gl + 64], D.ap[:, 2 * pr + gl, :], True, True,
                            [wch.t(wslot), D.t()], [bankE.t()], sig=(ri == 1 and gl == 1))
            for ri in range(2):
                self.copy(ACT, AB.ap[:, sl, 0, ri, 128:384], bankE.ap[:, 256 * ri:256 * ri + 256], [bankE.t()], [AB.t((sl, 0, ri))])

        ptmp = None

        def level(pr, k):
            sl = pr % 4
            d = 1 << k
            cur = k % 2
            nxt = 1 - cur
            src_r, src_i = AB.ap[:, sl, cur, 0], AB.ap[:, sl, cur, 1]
            dst_r, dst_i = AB.ap[:, sl, nxt, 0], AB.ap[:, sl, nxt, 1]
            wr, wi, nwi = (coef.ap[:, pr, k, j:j + 1] for j in range(3))
            tsr, tsi = AB.t((sl, cur, 0)), AB.t((sl, cur, 1))
            tdr, tdi = AB.t((sl, nxt, 0)), AB.t((sl, nxt, 1))
            ck = coef.t()
            hi = 384 if k < 7 else 383
            n = hi - 128
            if k < 7:
                o_r, o_i, to_r, to_i = dst_r[:, 128:384], dst_i[:, 128:384], tdr, tdi
            else:
                o_r, o_i, to_r, to_i = Sbf.ap[:, sl, 0, 1:256], Sbf.ap[:, sl, 1, 1:256], Sbf.t(sl), Sbf.t(sl)
            if False:
                def fma(out, otok, a, atok, w, b, btok, j, m):
                    self.ts(POOL, ptmp.ap[:, j, 0:m], a, w, None, ALU.mult, None, [atok, ck], [ptmp.t(j)])
                    self.tt(POOL, out, ptmp.ap[:, j, 0:m], b, ALU.add, [ptmp.t(j), btok], [otok])
                fma(dst_r[:, 128:384], tdr, src_r[:, 128 - d:384 - d], tsr, wr, src_r[:, 128:384], tsr, 0, 256)
                fma(dst_i[:, 128:384], tdi, src_i[:, 128 - d:384 - d], tsi, wr, src_i[:, 128:384], tsi, 1, 256)
                fma(o_r, to_r, src_i[:, 128 - d:hi - d], tsi, nwi, dst_r[:, 128:hi], tdr, 0, n)
                fma(o_i, to_i, src_r[:, 128 - d:hi - d], tsr, wi, dst_i[:, 128:hi], tdi, 1, n)
                return
            self.stt(dst_r[:, 128:384], src_r[:, 128 - d:384 - d], wr, src_r[:, 128:384], ALU.mult, ALU.add, [tsr, ck], [tdr])
            self.stt(dst_i[:, 128:384], src_i[:, 128 - d:384 - d], wr, src_i[:, 128:384], ALU.mult, ALU.add, [tsi, ck], [tdi])
            self.stt(o_r, src_i[:, 128 - d:hi - d], nwi, dst_r[:, 128:hi], ALU.mult, ALU.add, [tsi, tdr, ck], [to_r])
            self.stt(o_i, src_r[:, 128 - d:hi - d], wi, dst_i[:, 128:hi], ALU.mult, ALU.add, [tsr, tdi, ck], [to_i])

        def pair_back(pr):
            ft, p4 = pr // 4, pr % 4
            wslot = wslot_of(pr)
            gslot = ft % 2
            sl = pr % 4
            ys = pr % 2
            bankY = ps[4 + pr % 2]
            for gl in range(2):
                o = bankY.ap[:, 256 * gl:256 * gl + 256]
                hs = slice(64 * gl, 64 * gl + 64)
                self.mm(o, wch.ap[:, wslot, p4, 512 + 128 * gl:512 + 128 * gl + 128], D.ap[:, 2 * pr + gl, :], True, False,
                        [wch.t(wslot), D.t()], [bankY.t()])
                self.mm(o, wch.ap[hs, wslot, p4, 256:384], Sbf.ap[hs, sl, 0, :], False, False, [wch.t(wslot), Sbf.t(sl)], [bankY.t()])
                self.mm(o, wch.ap[hs, wslot, p4, 384:512], Sbf.ap[hs, sl, 1, :], False, True, [wch.t(wslot), Sbf.t(sl)], [bankY.t()],
                        sig=(gl == 1))
            self.act(Ygb.ap[:, ys].rearrange("p g c -> p (g c)"), bankY.ap, AF.Gelu, [bankY.t()], [Ygb.t(ys)])
            pT = ps[6]
            pTv = pT.ap.bitcast(BF16)
            for gl in range(2):
                for half in range(2):
                    q = 2 * gl + half
                    self.tr(pTv[:, 128 * q:128 * q + 128], Ygb.ap[:, ys, gl, 128 * half:128 * half + 128], identb.ap,
                            [Ygb.t(ys), identb.t()], [pT.t()], sig=(q == 3))
            for gl in range(2):
                fo = 32 * p4 + 16 * gl
                self.copy(ACT, Gtok.ap[:, gslot, :, :, fo:fo + 16],
                          pTv[:, 256 * gl:256 * gl + 256].rearrange("p (a i h) -> p a i h", a=2, h=16), [pT.t()], [Gtok.t(gslot)])

        def ft_done(ft):
            gslot = ft % 2
            for half in range(2):
                pT2 = ps[7]
                pv = pT2.ap.bitcast(BF16)
                for i in range(8):
                    self.tr(pv[:, 128 * i:128 * i + 128], Gtok.ap[:, gslot, half, i, :], identb.ap, [Gtok.t(gslot), identb.t()], [pT2.t()],
                            sig=(i == 7))
                self.copy(ACT, gT.ap[:, ft, 1024 * half:1024 * half + 1024].rearrange("p (c i) -> p i c", i=8),
                          pv.rearrange("p (i c) -> p i c", c=128), [pT2.t()], [gT.t(ft)])

        def load_w(ft):
            P.dma(SP, wch.ap[:, ft % 2], self.s5w[4 * ft:4 * ft + 4].rearrange("a p n -> p a n"), reads=[self.s5w_tok[ft]],
                  writes=[wch.t(ft % 2)])

        load_w(0)
        load_w(1)
        for p4 in (0, 1):
            pair_front(p4)
        for pp in range(16):
            prs = [2 * pp, 2 * pp + 1]
            for k in range(8):
                for pr in prs:
                    level(pr, k)
            if pp + 1 < 16:
                for pr in (2 * pp + 2, 2 * pp + 3):
                    pair_front(pr)
            for pr in prs:
                pair_back(pr)
            if pp % 2 == 1:
                ft = pp // 2
                ft_done(ft)
                if ft + 2 < 8:
                    load_w(ft + 2)

    def x_reload(self, s):
        P, c, ps = self.P, self.c, self.ps
        A = self.arena
        Xtok, _ = A.view(self.R_TAIL, [128, 8, 1024], F32, "Xtok2")
        xT = self.xT
        n = 0
        for half in range(2):
            self.load_xtok(s, half, Xtok)
            for ft in range(8):
                for jq in range(2):
                    bank = ps[n % 4]
                    for jj in range(4):
                        j = 4 * jq + jj
                        self.tr(bank.ap[:, 128 * jj:128 * jj + 128], Xtok.ap[:, j, 128 * ft:128 * ft + 128], c["ident_f"].ap,
                                [Xtok.t(), c["ident_f"].t()], [bank.t()], sig=(jj == 3))
                    dst = xT.ap[:, ft, 1024 * half:1024 * half + 1024].rearrange("p (c j) -> p j c", j=8)[:, 4 * jq:4 * jq + 4, :]
                    self.copy(ACT if n % 2 == 0 else DVE, dst, bank.ap.rearrange("p (j c) -> p j c", c=128), [bank.t()], [xT.t(ft)])
                    n += 1


    def stream(self, loaders, computes):
        n = len(loaders)
        slots = {0: loaders[0]()}
        for i in range(n):
            if i + 1 < n:
                slots[i + 1] = loaders[i + 1]()
            computes[i](slots.pop(i))

    def fm_norm(self, s, n, tt, dst, sq, tmpf, bank):
        c, xT, rstd = self.c, self.xT, self.rstd
        ts_ = slice(512 * tt, 512 * tt + 512)
        for ft in range(8):
            self.act(sq.ap[:, ft, :], xT.ap[:, ft, ts_], AF.Square, [xT.t(ft)], [sq.t(ft)])
        for ft in range(8):
            self.mm(bank.ap, c["onesm_b"].ap, sq.ap[:, ft, :], ft == 0, ft == 7, [c["onesm_b"].t(), sq.t(ft)], [bank.t()])
        self.act(rstd.ap, bank.ap, AF.Ln, [bank.t()], [rstd.t()], bias=self.epsc.ap)
        self.act(rstd.ap, rstd.ap, AF.Exp, [rstd.t()], [rstd.t()], scale=-0.5)
        ms = self.modsc
        for ft in range(8):
            tb = ft % 2
            self.stt(tmpf.ap[:, tb, :], xT.ap[:, ft, ts_], ms.ap[:, s, n, 0, ft:ft + 1], rstd.ap, ALU.mult, ALU.mult,
                     [xT.t(ft), ms.t(), rstd.t()], [tmpf.t(tb)])
            self.act(dst[:, ft, :], tmpf.ap[:, tb, :], AF.Identity, [tmpf.t(tb), ms.t()], [self.cur_h_tok], bias=ms.ap[:, s, n, 1, ft:ft + 1])

    def glu(self, s):
        P, ps = self.P, self.ps
        A = self.arena
        off = self.R_TAIL
        sg, off = A.view(off, [128, 2, 512], F32, "sg")
        mt_, off = A.view(off, [128, 2, 512], F32, "mt")
        gT, xT, adaT = self.actT, self.xT, self.adaT
        w = self.I["s5_w_glu"][0].rearrange("(k p) n -> p k n", p=128)
        allg = [gT.t(ft) for ft in range(8)]
        cnt = [0]

        def loader(ft):
            def f():
                v = lambda a: a[:, 0:2048].rearrange("p (k g n) -> p k g n", g=2, n=128)
                return self.wload([(lambda a: v(a)[:, :, 0, :], w[:, :, 128 * ft:128 * ft + 128]),
                                   (lambda a: v(a)[:, :, 1, :], w[:, :, 1024 + 128 * ft:1024 + 128 * ft + 128])])
            return f

        def compute(ft):
            def f(slot):
                wv = slot.ap[:, 0:2048].rearrange("p (k g n) -> p k g n", g=2, n=128)
                for tt in range(4):
                    n = cnt[0]
                    cnt[0] += 1
                    bv, bg = ps[(2 * n) % 8], ps[(2 * n + 1) % 8]
                    ts_ = slice(512 * tt, 512 * tt + 512)
                    for gi, bank in enumerate([bv, bg]):
                        for k in range(8):
                            self.mm(bank.ap, wv[:, k, gi, :], gT.ap[:, k, ts_], k == 0, k == 7, [slot.t()] + allg, [bank.t()])
                    b2 = n % 2
                    self.act(sg.ap[:, b2, :], bg.ap, AF.Sigmoid, [bg.t()], [sg.t(b2)])
                    self.tt(DVE, mt_.ap[:, b2, :], bv.ap, sg.ap[:, b2, :], ALU.mult, [bv.t(), sg.t(b2)], [mt_.t(b2)])
                    self.stt(xT.ap[:, ft, ts_], mt_.ap[:, b2, :], adaT.ap[:, 16 + ft, s:s + 1], xT.ap[:, ft, ts_], ALU.mult, ALU.add,
                             [mt_.t(b2), adaT.t(), xT.t(ft)], [xT.t(ft)])
            return f

        self.stream([loader(ft) for ft in range(8)], [compute(ft) for ft in range(8)])

    def mlp(self, s, l, nidx):
        P, ps = self.P, self.ps
        A = self.arena
        uT, _ = A.view(self.R_TAIL, [128, 32, 1024], BF16, "uT")
        off = self.R_ACTT
        htile, off = A.view(off, [128, 8, 1024], BF16, "htile")
        sq, off = A.view(off, [128, 8, 512], BF16, "sq")
        tmpf, off = A.view(off, [128, 2, 512], F32, "tmpf2")
        r, off = A.view(off, [128, 2, 1024], BF16, "r")
        xT, adaT = self.xT, self.adaT
        w1 = self.I["mlp_w1"][l]
        w2 = self.I["mlp_w2"][l].rearrange("(k p) n -> p k n", p=128)
        gm0 = 48 * l + 40
        nb = [0]
        for t2 in range(2):
            for sub in range(2):
                self.cur_h_tok = htile.t(sub)
                self.fm_norm(s, nidx, 2 * t2 + sub, htile.ap[:, :, 512 * sub:512 * sub + 512], sq, tmpf, ps[7])
            loaders, computes = [], []
            for ch in range(8):
                loaders.append(lambda ch=ch: self.wload_k8(w1, 512 * ch))

                def c1(sv, ch=ch):
                    slot, wv = sv
                    for m4 in range(4):
                        e = 4 * ch + m4
                        for sub in range(2):
                            bank = ps[nb[0] % 4]
                            nb[0] += 1
                            for k in range(8):
                                self.mm(bank.ap, wv[:, k, 128 * m4:128 * m4 + 128], htile.ap[:, k, 512 * sub:512 * sub + 512], k == 0, k == 7,
                                        [slot.t(), htile.t(sub)], [bank.t()])
                            self.act(r.ap[:, e % 2, 512 * sub:512 * sub + 512], bank.ap, AF.Relu, [bank.t()], [r.t((e % 2, sub))])
                        self.tt(DVE, uT.ap[:, e, :], r.ap[:, e % 2, :], r.ap[:, e % 2, :], ALU.mult,
                                [r.t((e % 2, 0)), r.t((e % 2, 1))], [uT.t(e)])
                computes.append(c1)
            allu = [uT.t(e) for e in range(32)]
            for o in range(8):
                def l2(o=o):
                    slot = self.wload([(lambda a: a.rearrange("p (k n) -> p k n", n=128), w2[:, :, 128 * o:128 * o + 128])])
                    return slot, slot.ap.rearrange("p (k n) -> p k n", n=128)
                loaders.append(l2)

                def c2(sv, o=o):
                    slot, wv = sv
                    for sub in range(2):
                        bank = ps[4 + (2 * o + sub) % 3]
                        ts_ = slice(1024 * t2 + 512 * sub, 1024 * t2 + 512 * sub + 512)
                        for k in range(32):
                            self.mm(bank.ap, wv[:, k, :], uT.ap[:, k, 512 * sub:512 * sub + 512], k == 0, k == 31, [slot.t()] + allu, [bank.t()])
                        self.stt(xT.ap[:, o, ts_], bank.ap, adaT.ap[:, gm0 + o, s:s + 1], xT.ap[:, o, ts_], ALU.mult, ALU.add,
                                 [bank.t(), adaT.t(), xT.t(o)], [xT.t(o)])
                computes.append(c2)
            self.stream(loaders, computes)

    def headnorm_batch(self, banks, sbanks, gaincol, dsts, dtoks, sq, tk4):
        c = self.c
        n = len(banks)
        for j in range(n):
            self.act(sq.ap[:, j, :], banks[j].ap, AF.Square, [banks[j].t()], [sq.t(j)])
        for j in range(n):
            self.mm(sbanks[j].ap, c["blk_b"].ap, sq.ap[:, j, :], True, True, [c["blk_b"].t(), sq.t(j)], [sbanks[j].t()])
        for j in range(n):
            self.act(tk4.ap[:, j, :], sbanks[j].ap, AF.Ln, [sbanks[j].t()], [tk4.t(j)], bias=self.epsc.ap)
        for j in range(n):
            self.act(tk4.ap[:, j, :], tk4.ap[:, j, :], AF.Exp, [tk4.t(j)], [tk4.t(j)], scale=-0.5)
        for j in range(n):
            self.stt(dsts[j], banks[j].ap, gaincol, tk4.ap[:, j, :], ALU.mult, ALU.mult, [banks[j].t(), self.qkg.t(), tk4.t(j)], dtoks[j])

    def kv_phase(self, s):
        P, ps = self.P, self.ps
        A = self.arena
        self.KT, _ = A.view(self.R_ACTT, [128, 8, 2048], BF16, "KT")
        off = self.R_TAIL
        self.V, off = A.view(off, [128, 16, 1024], BF16, "V")
        self.attn_off = off
        htile, off = A.view(off, [128, 8, 2048], BF16, "htile_kv")
        sq, off = A.view(off, [128, 8, 512], BF16, "sq_kv")
        tmpf, off = A.view(off, [128, 2, 512], F32, "tmpf_kv")
        tk2, off = A.view(off, [128, 2, 512], F32, "tk2")
        KT, V = self.KT, self.V
        wkv = self.I["w_kv"]
        for tt in range(4):
            self.cur_h_tok = htile.t(tt)
            self.fm_norm(s, 2, tt, htile.ap[:, :, 512 * tt:512 * tt + 512], sq, tmpf, ps[7])
        cnt = [0]
        loaders = [lambda ch=ch: self.wload_k8(wkv, 512 * ch) for ch in range(4)]
        computes = []
        for ch in range(4):
            def cK(sv, ch=ch):
                slot, wv = sv
                for tt in range(4):
                    ts_ = slice(512 * tt, 512 * tt + 512)
                    for hf in range(2):
                        n = cnt[0]
                        cnt[0] += 1
                        banks = [ps[(2 * n) % 4], ps[(2 * n + 1) % 4]]
                        sbanks = [ps[4 + (2 * n) % 4], ps[4 + (2 * n + 1) % 4]]
                        for j in range(2):
                            m4 = 2 * hf + j
                            for k in range(8):
                                self.mm(banks[j].ap, wv[:, k, 128 * m4:128 * m4 + 128], htile.ap[:, k, ts_], k == 0, k == 7,
                                        [slot.t(), htile.t(tt)], [banks[j].t()])
                        prs = [4 * ch + 2 * hf + j for j in range(2)]
                        sqv = Buf(sq.ap[:, 2 * (n % 4):2 * (n % 4) + 2, :], "sqv")
                        sqv.toks = {0: sq.t(2 * (n % 4)), 1: sq.t(2 * (n % 4) + 1)}
                        self.headnorm_batch(banks, sbanks, self.qkg.ap[:, 1:2], [KT.ap[:, p_, ts_] for p_ in prs],
                                            [[KT.t((p_, tt))] for p_ in prs], sqv, tk2)

            def cV(sv, ch=ch):
                slot, wv = sv
                for tb in range(16):
                    n = cnt[0]
                    cnt[0] += 1
                    bank = ps[n % 4]
                    for k in range(8):
                        self.mm(bank.ap, htile.ap[:, k, 128 * tb:128 * tb + 128], wv[:, k, :], k == 0, k == 7,
                                [slot.t(), htile.t(tb // 4)], [bank.t()])
                    self.copy(ACT if n % 2 == 0 else DVE, V.ap[:, tb, 512 * (ch - 2):512 * (ch - 2) + 512], bank.ap,
                              [bank.t()], [V.t(tb)])
            computes.append(cK if ch < 2 else cV)
        self.stream(loaders, computes)

    def attn_phase(self, s):
        P, ps, c = self.P, self.ps, self.c
        A = self.arena
        KT, V, xT, adaT = self.KT, self.V, self.xT, self.adaT
        off = self.attn_off
        qT, off = A.view(off, [128, 8, 512], BF16, "qT")
        oT, off = A.view(off, [128, 8, 512], BF16, "oT")
        o1 = off
        htile, o1 = A.view(o1, [128, 8, 512], BF16, "htile_q")
        sq, o1 = A.view(o1, [128, 8, 512], BF16, "sq_q")
        tmpf, o1 = A.view(o1, [128, 2, 512], F32, "tmpf_q")
        tk4, o1 = A.view(o1, [128, 4, 512], F32, "tk4_q")
        o2 = off
        Eb, o2 = A.view(o2, [128, 2, 2, 512], F32, "Eb")
        Lb, o2 = A.view(o2, [128, 3, 2, 512], BF16, "Lb")
        Wb, o2 = A.view(o2, [128, 3, 2, 512], BF16, "Wb")
        R32, o2 = A.view(o2, [128, 2, 512], F32, "R32")
        Rbf, o2 = A.view(o2, [128, 3, 2, 512], BF16, "Rbf")
        wq = self.I["sb_w_q"][0]
        wo = self.I["sb_w_o"][0]
        identb, negmask, negtri, negones = c["ident_b"], c["negmask_b"], c["negtri_b"], c["negones_b"]
        zeros = c["zeros512_b"]
        zbank = Buf(self.psall[:, 0:1024].rearrange("p (h t) -> p h t", h=2), "zbank")
        zbank.toks[0] = ps[0].t()
        abank = []
        for j in range(2):
            b_ = Buf(self.psall[:, 1024 + 1024 * j:2048 + 1024 * j].rearrange("p (h t) -> p h t", h=2), f"abank{j}")
            abank.append(b_)
        obank = ps[6]

        def ztoks():
            return [ps[0].t(), ps[1].t()]

        def atoks(j):
            return [ps[2 + 2 * j].t(), ps[3 + 2 * j].t()]

        wq_slots = [self.wload_k8(wq, 512 * ch) for ch in range(2)]
        for qt in range(4):
            ts_ = slice(512 * qt, 512 * qt + 512)
            self.cur_h_tok = htile.t()
            self.fm_norm(s, 3, qt, htile.ap, sq, tmpf, ps[7])
            cnt = [0]

            def cQ(sv, ch):
                slot, wv = sv
                banks = [ps[m4] for m4 in range(4)]
                for m4 in range(4):
                    for k in range(8):
                        self.mm(banks[m4].ap, wv[:, k, 128 * m4:128 * m4 + 128], htile.ap[:, k, :], k == 0, k == 7,
                                [slot.t(), htile.t()], [banks[m4].t()])
                self.headnorm_batch(banks, [ps[4 + m4] for m4 in range(4)], self.qkg.ap[:, 0:1],
                                    [qT.ap[:, 4 * ch + m4, :] for m4 in range(4)], [[qT.t(4 * ch + m4)] for m4 in range(4)], sq, tk4)

            for ch in range(2):
                cQ(wq_slots[ch], ch)
            P.barrier()
            wo_slots = [self.wload_k8(wo, 512 * ch) for ch in range(2)]
            tiles = []
            nkb = 4 * qt + 4
            for pair in range(8):
                for ii, kb in enumerate(range(nkb - 1, -1, -1)):
                    r_ = kb - 4 * qt
                    c0 = 128 * r_ if r_ >= 0 else 0
                    tiles.append(dict(pair=pair, kb=kb, first=(ii == 0), last=(kb == 0), diag=(r_ >= 0), c0=c0))
            nt = len(tiles)

            def zmm(bank, btoks, t, close):
                c0 = t["c0"]
                kb = t["kb"]
                rd = [KT.t((t["pair"], kb // 4)), qT.t(t["pair"])]
                for hl in range(2):
                    hs = slice(64 * hl, 64 * hl + 64)
                    self.mm(bank.ap[:, hl, c0:512], KT.ap[hs, t["pair"], 128 * kb:128 * kb + 128], qT.ap[hs, t["pair"], c0:512], True,
                            close and not t["diag"], rd, btoks, sig=(close and not t["diag"] and hl == 1))
                if t["diag"]:
                    for hl in range(2):
                        self.mm(bank.ap[:, hl, c0:c0 + 128], identb.ap, negmask.ap, False, close, [identb.t(), negmask.t()], btoks,
                                sig=(close and hl == 1))

            def stageA1(i):
                t = tiles[i]
                c0 = t["c0"]
                zmm(zbank, ztoks(), t, True)
                self.act(Eb.ap[:, i % 2, :, c0:512], zbank.ap[:, :, c0:512], AF.Exp, ztoks(), [Eb.t(i % 2)])

            def stageA2(i):
                t = tiles[i]
                c0 = t["c0"]
                self.act(Lb.ap[:, i % 3, :, c0:512], Eb.ap[:, i % 2, :, c0:512], AF.Ln, [Eb.t(i % 2)], [Lb.t(i % 3)], bias=self.onec.ap)
                if not t["last"]:
                    if t["first"]:
                        P.op(POOL, lambda e: e.memset(R32.ap, 0.0), [], [R32.t()])
                    self.tt(POOL, R32.ap[:, :, c0:512], R32.ap[:, :, c0:512], Lb.ap[:, i % 3, :, c0:512], ALU.add,
                            [R32.t(), Lb.t(i % 3)], [R32.t()])
                    self.copy(DVE, Rbf.ap[:, i % 3], R32.ap, [R32.t()], [Rbf.t(i % 3)])

            def stageB(i):
                t = tiles[i]
                ab = abank[i % 2]
                at = atoks(i % 2)
                c0 = t["c0"]
                zmm(ab, at, t, False)
                for hl in range(2):
                    self.mm(ab.ap[:, hl, c0:512], negtri.ap, Lb.ap[:, i % 3, hl, c0:512], False, t["first"], [negtri.t(), Lb.t(i % 3)], at,
                            sig=(t["first"] and hl == 1))
                if not t["first"]:
                    for hl in range(2):
                        self.mm(ab.ap[:, hl, c0:512], negones.ap, Rbf.ap[:, (i - 1) % 3, hl, c0:512], False, True,
                                [negones.t(), Rbf.t((i - 1) % 3)], at, sig=(hl == 1))
                self.act(Wb.ap[:, i % 3, :, c0:512], ab.ap[:, :, c0:512], AF.Exp, at, [Wb.t(i % 3)])

            def stageC(i):
                t = tiles[i]
                c0 = t["c0"]
                if t["first"]:
                    self.mm(obank.ap, zeros.ap[:, 0:128], zeros.ap, True, False, [zeros.t()], [obank.t()])
                for hl in range(2):
                    h = 2 * t["pair"] + hl
                    hs = slice(64 * hl, 64 * hl + 64)
                    self.mm(obank.ap[hs, c0:512], V.ap[:, t["kb"], 64 * h:64 * h + 64], Wb.ap[:, i % 3, hl, c0:512], False, t["last"],
                            [V.t(t["kb"]), Wb.t(i % 3)], [obank.t()], sig=(t["last"] and hl == 1))
                if t["last"]:
                    self.copy(DVE, oT.ap[:, t["pair"], :], obank.ap, [obank.t()], [oT.t(t["pair"])])

            for step in range(nt + 3):
                if step < nt:
                    stageA1(step)
                if 0 <= step - 2 < nt:
                    stageB(step - 2)
                if step < nt:
                    stageA2(step)
                if 0 <= step - 3 < nt:
                    stageC(step - 3)
            P.barrier()
            allo = [oT.t(p_) for p_ in range(8)]

            def cO(sv, ch):
                slot, wv = sv
                for m4 in range(4):
                    ft = 4 * ch + m4
                    bank = ps[4 + ft % 2]
                    for k in range(8):
                        self.mm(bank.ap, wv[:, k, 128 * m4:128 * m4 + 128], oT.ap[:, k, :], k == 0, k == 7, [slot.t()] + allo, [bank.t()])
                    self.stt(xT.ap[:, ft, ts_], bank.ap, adaT.ap[:, 48 + 16 + ft, s:s + 1], xT.ap[:, ft, ts_], ALU.mult, ALU.add,
                             [bank.t(), adaT.t(), xT.t(ft)], [xT.t(ft)])

            for ch in range(2):
                cO(wo_slots[ch], ch)
            if qt < 3:
                wq_slots = [self.wload_k8(wq, 512 * ch) for ch in range(2)]

    def output_phase(self, s):
        P, ps, c = self.P, self.ps, self.c
        A = self.arena
        ost, _ = A.view(self.R_TAIL, [128, 2, 1024], F32, "ostage")
        xT = self.xT
        allx = [xT.t(ft) for ft in range(8)]
        n = 0
        for tb in range(16):
            ob = tb % 2
            for hf in range(2):
                bank = ps[n % 4]
                for jj in range(4):
                    ft = 4 * hf + jj
                    self.tr(bank.ap[:, 128 * jj:128 * jj + 128], xT.ap[:, ft, 128 * tb:128 * tb + 128], c["ident_f"].ap,
                            allx + [c["ident_f"].t()], [bank.t()], sig=(jj == 3))
                self.copy(ACT if n % 2 == 0 else DVE, ost.ap[:, ob, 512 * hf:512 * hf + 512], bank.ap, [bank.t()], [ost.t(ob)])
                n += 1
            otok = Tok(f"out{s}_{tb}")
            P.dma(SP, self.out[s, 128 * tb:128 * tb + 128, :], ost.ap[:, ob, :], reads=[ost.t(ob)], writes=[otok], tok=ost.t(ob))


def build_program(debug=None, stop_after=None, nseq=NS):
    b = Builder(debug=debug, stop_after=stop_after, nseq=nseq)
    nc = b.build()
    if b.P.nfwd:
        print("forward-redirected PE deps:", b.P.nfwd)
    return nc, b


_PARAM_NAMES = ["ada_w", "ada_b", "mix_norm_g", "mlp_norm_g", "mlp_w1", "mlp_w2", "s5_a_re", "s5_a_im", "s5_log_dt",
                "s5_b_re", "s5_b_im", "s5_c_re", "s5_c_im", "s5_d", "s5_w_glu", "kv_ada_w", "kv_ada_b", "kv_norm_g",
                "w_kv", "k_norm_g", "sb_w_q", "q_norm_g", "sb_w_o"]


def kernel(**inputs):
    x = np.ascontiguousarray(np.asarray(inputs["x"], dtype=np.float32))
    c = np.ascontiguousarray(np.asarray(inputs["c"], dtype=np.float32))
    params = {k: np.ascontiguousarray(np.asarray(inputs[k], dtype=np.float32)) for k in _PARAM_NAMES}
    nc, _ = build_program()
    in_maps = []
    for i in range(NCORES):
        m = dict(params)
        m["x"] = np.ascontiguousarray(x[NS * i:NS * i + NS])
        m["c"] = np.ascontiguousarray(c[NS * i:NS * i + NS])
        in_maps.append(m)
    res = run_bass_kernel_spmd(nc, in_maps, core_ids=list(range(NCORES)))
    out = np.concatenate([np.asarray(r["out"]) for r in res.results], axis=0)
    return out.astype(np.float32, copy=False)
```

```python
import math
from contextlib import ExitStack

import numpy as np
import concourse.bass as bass
import concourse.mybir as mybir
from concourse.bass_utils import run_bass_kernel_spmd

F32 = mybir.dt.float32
BF16 = mybir.dt.bfloat16
U8 = mybir.dt.uint8
AF = mybir.ActivationFunctionType
ALU = mybir.AluOpType
PE, DVE, ACT, POOL, SP = "tensor", "vector", "scalar", "gpsimd", "sync"
ENGS = [PE, DVE, ACT, POOL, SP]

D = 1024
T = 2048
NS = 2
FT = 8
TT = 512
NTT = T // TT
DFF = 4096
EPS = 1e-6
NCORES = 8
EPOCH_MAX = 12000


class Tok:
    __slots__ = ("w", "r", "sem", "semcnt", "name")

    def __init__(self, name=""):
        self.w = None
        self.r = []
        self.sem = None
        self.semcnt = 0
        self.name = name


class Op:
    __slots__ = ("eng", "fn", "deps", "dma", "sig", "semref", "semval", "inc", "id", "sigok")


class Prog:
    def __init__(self, nc, stack):
        self.nc = nc
        self.stack = stack
        self.ops = []
        self.last = {e: None for e in ENGS}
        self.barrier_deps = {e: [] for e in ENGS}
        self.dmas = []
        self.nsem = 0

    def new_sem(self, name):
        self.nsem += 1
        return self.stack.enter_context(self.nc.semaphore(f"{name}_{self.nsem}"))

    def op(self, eng, fn, reads=(), writes=(), dma=None, sigok=True):
        o = Op()
        o.sigok = sigok
        o.eng = eng
        o.fn = fn
        o.dma = dma
        o.sig = False
        o.semref = None
        o.semval = 0
        o.inc = 0
        o.id = len(self.ops)
        deps = {}

        def add(d, war=False):
            if d is None:
                return
            if d.dma is None and d.eng == eng:
                if eng == PE:
                    return
            deps[d.id] = d

        for t in reads:
            add(t.w)
        for t in writes:
            add(t.w)
            for r in t.r:
                add(r, war=True)
        for d in self.barrier_deps[eng]:
            add(d)
        self.barrier_deps[eng] = []
        o.deps = list(deps.values())
        for t in reads:
            t.r.append(o)
        for t in writes:
            t.w = o
            t.r = []
        self.ops.append(o)
        self.last[eng] = o
        if dma is not None:
            self.dmas.append(o)
        return o

    def dma(self, eng, out, in_, reads=(), writes=(), tok=None, **kw):
        if tok is None:
            tok = writes[0]

        def fn(e):
            return e.dma_start(out=out, in_=in_, **kw)
        return self.op(eng, fn, reads=reads, writes=writes, dma=tok)

    def barrier(self):
        deps = [o for o in self.last.values() if o is not None] + list(self.dmas)
        self.dmas = []
        for e in ENGS:
            self.barrier_deps[e] = list(self.barrier_deps[e]) + deps

    def emit(self):
        nc = self.nc
        pe_ops = [o for o in self.ops if o.eng == PE and o.dma is None]
        if pe_ops:
            pe_ops[-1].sigok = True
        nxt = {}
        cur = None
        for o in reversed(pe_ops):
            if o.sigok:
                cur = o
            nxt[o.id] = cur
        nfwd = 0
        for o in self.ops:
            nd = {}
            for d in o.deps:
                if d.eng == PE and d.dma is None and not d.sigok:
                    d = nxt[d.id]
                    if d.id > o.id:
                        nfwd += 1
                nd[d.id] = d
            o.deps = list(nd.values())
        self.nfwd = nfwd
        for o in self.ops:
            for d in o.deps:
                d.sig = True
        cnt = {e: 0 for e in ENGS}
        cursem = {e: None for e in ENGS}
        for o in self.ops:
            if o.dma is not None:
                t = o.dma
                if t.sem is None or t.semcnt >= 16 * 3000:
                    t.sem = self.new_sem("d")
                    t.semcnt = 0
                t.semcnt += 16
                o.semref = t.sem
                o.semval = t.semcnt
                o.inc = 16
            elif o.sig:
                e = o.eng
                if cursem[e] is None or cnt[e] >= EPOCH_MAX:
                    cursem[e] = self.new_sem("e" + e[:2])
                    cnt[e] = 0
                cnt[e] += 1
                o.semref = cursem[e]
                o.semval = cnt[e]
                o.inc = 1
        with nc.Block() as block:
            for e in ENGS:
                oplist = [o for o in self.ops if o.eng == e]

                def body(eh, oplist=oplist):
                    waited = {}
                    for o in oplist:
                        need = {}
                        for d in o.deps:
                            k = id(d.semref)
                            if k not in need or need[k][1] < d.semval:
                                need[k] = (d.semref, d.semval)
                        for k, (s, v) in need.items():
                            if waited.get(k, 0) < v:
                                eh.wait_ge(s, v)
                                waited[k] = v
                        if o.fn is not None:
                            inst = o.fn(eh)
                            if o.semref is not None:
                                inst.then_inc(o.semref, o.inc)

                getattr(block, e)(body)


class Buf:
    def __init__(self, ap, name=""):
        self.ap = ap
        self.name = name
        self.toks = {}

    def t(self, key=0):
        if key not in self.toks:
            self.toks[key] = Tok(f"{self.name}{key}")
        return self.toks[key]

    def ts(self, keys):
        return [self.t(k) for k in keys]


class Arena:
    def __init__(self, ap_u8):
        self.ap = ap_u8
        self.size = ap_u8.shape[1]

    def view(self, off, shape, dt, name=""):
        esz = 4 if dt == F32 else 2
        n = 1
        for s in shape[1:]:
            n *= s
        nbytes = n * esz
        assert off % 4 == 0 and off + nbytes <= self.size, (name, off, nbytes, self.size)
        v = self.ap[0:shape[0], off:off + nbytes].bitcast(dt)
        if len(shape) == 3:
            v = v.rearrange("p (a b) -> p a b", b=shape[2])
        elif len(shape) == 4:
            v = v.rearrange("p (a b c) -> p a b c", b=shape[2], c=shape[3])
        elif len(shape) == 5:
            v = v.rearrange("p (a b c d) -> p a b c d", b=shape[2], c=shape[3], d=shape[4])
        return Buf(v, name), off + nbytes


class Builder:
    def __init__(self, debug=None, stop_after=None, nseq=NS):
        self.debug = debug or []
        self.stop_after = stop_after
        self.nseq = nseq
        self.stack = ExitStack()
        self.nc = bass.Bass("TRN2", target_bir_lowering=False)
        self.P = Prog(self.nc, self.stack)
        self.dbg_out = {}

    def dram_in(self, name, shape):
        return self.nc.dram_tensor(name, list(shape), F32, kind="ExternalInput").ap()

    def sb(self, name, shape, dt):
        return self.stack.enter_context(self.nc.sbuf_tensor(name, list(shape), dt))

    def act(self, out, in_, func, reads, writes, bias=None, scale=None, accum_out=None):
        kw = {}
        if bias is not None:
            kw["bias"] = bias
        if scale is not None:
            kw["scale"] = scale
        if accum_out is not None:
            kw["accum_out"] = accum_out
        return self.P.op(ACT, lambda e: e.activation(out=out, in_=in_, func=func, **kw), reads, writes)

    def tt(self, eng, out, in0, in1, op, reads, writes):
        return self.P.op(eng, lambda e: e.tensor_tensor(out, in0, in1, op), reads, writes)

    def stt(self, out, in0, scalar, in1, op0, op1, reads, writes):
        return self.P.op(DVE, lambda e: e.scalar_tensor_tensor(out, in0, scalar, in1, op0, op1), reads, writes)

    def ts(self, eng, out, in0, s1, s2, op0, op1, reads, writes):
        if op1 is None:
            return self.P.op(eng, lambda e: e.tensor_scalar(out, in0, s1, None, op0), reads, writes)
        return self.P.op(eng, lambda e: e.tensor_scalar(out, in0, s1, s2, op0, op1), reads, writes)

    def copy(self, eng, out, in_, reads, writes):
        if eng == ACT:
            return self.P.op(ACT, lambda e: e.activation(out=out, in_=in_, func=AF.Identity), reads, writes)
        return self.P.op(eng, lambda e: e.tensor_copy(out, in_), reads, writes)

    def mm(self, out, lhsT, rhs, start, stop, reads, writes, sig=None):
        return self.P.op(PE, lambda e: e.matmul(out, lhsT=lhsT, rhs=rhs, start=start, stop=stop), reads, writes,
                         sigok=(stop if sig is None else sig))

    def tr(self, out, in_, ident, reads, writes, sig=True):
        return self.P.op(PE, lambda e: e.transpose(out, in_, ident), reads, writes, sigok=sig)

    def dbg(self, name, buf_ap, shape, reads, dt=F32):
        if name not in self.debug:
            return
        o = self.nc.dram_tensor("dbg_" + name, list(shape), dt, kind="ExternalOutput").ap()
        tok = Tok("dbg" + name)
        self.P.dma(SP, o, buf_ap, reads=reads, writes=[tok], tok=tok)
        self.dbg_out[name] = tok

    def build(self):
        nc, P = self.nc, self.P
        I = {}
        I["x"] = self.dram_in("x", [NS, T, D])
        I["c"] = self.dram_in("c", [NS, D])
        I["ada_w"] = self.dram_in("ada_w", [2, D, 6 * D])
        I["ada_b"] = self.dram_in("ada_b", [2, 6 * D])
        I["mix_norm_g"] = self.dram_in("mix_norm_g", [2, D])
        I["mlp_norm_g"] = self.dram_in("mlp_norm_g", [2, D])
        I["mlp_w1"] = self.dram_in("mlp_w1", [2, D, DFF])
        I["mlp_w2"] = self.dram_in("mlp_w2", [2, DFF, D])
        I["s5_a_re"] = self.dram_in("s5_a_re", [1, 64, 64])
        I["s5_a_im"] = self.dram_in("s5_a_im", [1, 64, 64])
        I["s5_log_dt"] = self.dram_in("s5_log_dt", [1, 64])
        I["s5_b_re"] = self.dram_in("s5_b_re", [1, 64, 64, 16])
        I["s5_b_im"] = self.dram_in("s5_b_im", [1, 64, 64, 16])
        I["s5_c_re"] = self.dram_in("s5_c_re", [1, 64, 16, 64])
        I["s5_c_im"] = self.dram_in("s5_c_im", [1, 64, 16, 64])
        I["s5_d"] = self.dram_in("s5_d", [1, D])
        I["s5_w_glu"] = self.dram_in("s5_w_glu", [1, D, 2 * D])
        I["kv_ada_w"] = self.dram_in("kv_ada_w", [D, 2 * D])
        I["kv_ada_b"] = self.dram_in("kv_ada_b", [2 * D])
        I["kv_norm_g"] = self.dram_in("kv_norm_g", [D])
        I["w_kv"] = self.dram_in("w_kv", [D, 2 * D])
        I["k_norm_g"] = self.dram_in("k_norm_g", [64])
        I["sb_w_q"] = self.dram_in("sb_w_q", [1, D, D])
        I["q_norm_g"] = self.dram_in("q_norm_g", [1, 64])
        I["sb_w_o"] = self.dram_in("sb_w_o", [1, D, D])
        self.I = I
        self.out = nc.dram_tensor("out", [NS, T, D], F32, kind="ExternalOutput").ap()
        self.s5w = nc.dram_tensor("s5w_scr", [32, 128, 768], BF16, kind="Internal").ap()
        self.s5w_tok = [Tok(f"s5w{i}") for i in range(8)]
        self.s5tab = nc.dram_tensor("s5tab_scr", [32, 128, 2, 256], F32, kind="Internal").ap()
        self.s5tab_tok = Tok("s5tab")
        self.modrow = nc.dram_tensor("modrow_scr", [NS, 2, D], F32, kind="Internal").ap()
        self.modrow_tok = Tok("modrow")

        self.consts()
        ARENA = 194 * 1024
        self.arena = Arena(self.sb("arena", [128, ARENA], U8)[:])
        self.psall = self.stack.enter_context(nc.psum_tensor("psall", [128, 4096], F32))
        self.ps = [Buf(self.psall[:, 512 * i:512 * i + 512], f"ps{i}") for i in range(8)]

        self.setup_phase()
        if self.stop_after is not None and (self.stop_after == "setup" or self.stop_after.startswith("s5") or self.stop_after == "ada"):
            return self.finish()
        for s in range(self.nseq):
            self.seq_pipeline(s)
        return self.finish()

    def finish(self):
        P = self.P
        outs = [o for o in P.ops if o.dma is not None]
        fin = P.op(SP, None)
        fin.deps = list({o.id: o for o in outs}.values())
        P.emit()
        return self.nc

    def consts(self):
        P = self.P
        c = {}

        def mk(name, shape, dt):
            c[name] = Buf(self.sb("c_" + name, shape, dt)[:], name)
            return c[name]

        ident_f = mk("ident_f", [128, 128], F32)
        ident_b = mk("ident_b", [128, 128], BF16)
        onesm_b = mk("onesm_b", [128, 128], BF16)
        blk_b = mk("blk_b", [128, 128], BF16)
        negtri_b = mk("negtri_b", [128, 128], BF16)
        negones_b = mk("negones_b", [128, 128], BF16)
        negmask_b = mk("negmask_b", [128, 128], BF16)
        zeros_b = mk("zeros_b", [128, 64], BF16)
        onesrow_f = mk("onesrow_f", [1, 128], F32)
        sel2 = mk("sel2", [2, 128], F32)
        bmask_f = mk("bmask_f", [128, 128], F32)
        tmp_f = mk("ctmp_f", [128, 128], F32)
        hm = mk("hm", [128, 2], F32)
        zeros512_b = mk("zeros512_b", [128, 512], BF16)

        def pool(fn, reads, writes):
            return P.op(POOL, fn, reads, writes)

        pool(lambda e: e.memset(ident_f.ap, 1.0), [], [ident_f.t()])
        pool(lambda e: e.affine_select(out=ident_f.ap, in_=ident_f.ap, pattern=[[-1, 128]], compare_op=ALU.is_equal,
                                       fill=0.0, base=0, channel_multiplier=1), [ident_f.t()], [ident_f.t()])
        pool(lambda e: e.tensor_copy(ident_b.ap, ident_f.ap), [ident_f.t()], [ident_b.t()])
        pool(lambda e: e.memset(onesm_b.ap, 1.0 / 1024.0), [], [onesm_b.t()])
        pool(lambda e: e.memset(blk_b.ap, 0.0), [], [blk_b.t()])
        pool(lambda e: e.memset(blk_b.ap[0:64, 0:64], 1.0 / 64.0), [blk_b.t()], [blk_b.t()])
        pool(lambda e: e.memset(blk_b.ap[64:128, 64:128], 1.0 / 64.0), [blk_b.t()], [blk_b.t()])
        pool(lambda e: e.memset(tmp_f.ap, -1.0), [], [tmp_f.t()])
        pool(lambda e: e.affine_select(out=tmp_f.ap, in_=tmp_f.ap, pattern=[[-1, 128]], compare_op=ALU.is_ge,
                                       fill=0.0, base=0, channel_multiplier=1), [tmp_f.t()], [tmp_f.t()])
        pool(lambda e: e.tensor_copy(negtri_b.ap, tmp_f.ap), [tmp_f.t()], [negtri_b.t()])
        pool(lambda e: e.memset(negones_b.ap, -1.0), [], [negones_b.t()])
        pool(lambda e: e.tensor_scalar(negmask_b.ap, negtri_b.ap, 30000.0, None, ALU.mult), [negtri_b.t()], [negmask_b.t()])
        pool(lambda e: e.memset(zeros_b.ap, 0.0), [], [zeros_b.t()])
        pool(lambda e: e.memset(onesrow_f.ap, 1.0), [], [onesrow_f.t()])
        pool(lambda e: e.memset(sel2.ap, 1.0), [], [sel2.t()])
        pool(lambda e: e.affine_select(out=sel2.ap, in_=sel2.ap, pattern=[[1, 128]], compare_op=ALU.is_ge,
                                       fill=0.0, base=0, channel_multiplier=-64), [sel2.t()], [sel2.t()])
        pool(lambda e: e.affine_select(out=sel2.ap, in_=sel2.ap, pattern=[[-1, 128]], compare_op=ALU.is_ge,
                                       fill=0.0, base=63, channel_multiplier=64), [sel2.t()], [sel2.t()])
        pool(lambda e: e.memset(bmask_f.ap, 1.0), [], [bmask_f.t()])
        pool(lambda e: e.affine_select(out=bmask_f.ap.rearrange("p (i h) -> p i h", h=16), in_=bmask_f.ap.rearrange("p (i h) -> p i h", h=16),
                                       pattern=[[16, 8], [0, 16]], compare_op=ALU.is_ge,
                                       fill=0.0, base=15, channel_multiplier=-1), [bmask_f.t()], [bmask_f.t()])
        pool(lambda e: e.memset(zeros512_b.ap, 0.0), [], [zeros512_b.t()])
        pool(lambda e: e.memset(hm.ap, 0.0), [], [hm.t()])
        pool(lambda e: e.memset(hm.ap[0:64, 0:1], 1.0), [hm.t()], [hm.t()])
        pool(lambda e: e.memset(hm.ap[64:128, 1:2], 1.0), [hm.t()], [hm.t()])
        self.c = c
        self.VT = Buf(self.sb("VT", [128, 160], F32)[:], "VT")
        self.adaT = Buf(self.sb("adaT", [128, 112, 2], F32)[:], "adaT")
        self.coef = Buf(self.sb("coef", [128, 32, 8, 3], F32)[:], "coef")
        self.modsc = Buf(self.sb("modsc", [128, NS, 5, 2, 8], F32)[:], "modsc")
        self.rho = Buf(self.sb("rho", [128, 32], F32)[:], "rho")
        self.qkg = Buf(self.sb("qkg", [128, 2], F32)[:], "qkg")
        self.epsc = Buf(self.sb("epsc", [128, 1], F32)[:], "epsc")
        self.onec = Buf(self.sb("onec", [128, 1], F32)[:], "onec")
        pool(lambda e: e.memset(self.epsc.ap, EPS), [], [self.epsc.t()])
        pool(lambda e: e.memset(self.onec.ap, 1.0), [], [self.onec.t()])

    def ring_init(self, off, nslots=2):
        self.ring = []
        for i in range(nslots):
            b, off = self.arena.view(off, [128, 4096], BF16, f"ring{i}")
            self.ring.append(b)
        self.ring_i = 0
        return off

    def wload(self, srcs):
        slot = self.ring[self.ring_i % len(self.ring)]
        self.ring_i += 1
        for dstf, src in srcs:
            self.P.dma(POOL, dstf(slot.ap), src, reads=[], writes=[slot.t()], tok=slot.t())
        return slot

    def wload_k8(self, w2d, col0, ncols=512):
        src = w2d.rearrange("(k p) n -> p k n", p=128)[:, :, col0:col0 + ncols]
        slot = self.wload([(lambda a: a[:, 0:8 * ncols].rearrange("p (k n) -> p k n", n=ncols), src)])
        return slot, slot.ap[:, 0:8 * ncols].rearrange("p (k n) -> p k n", n=ncols)

    def setup_phase(self):
        P, I, c = self.P, self.I, self.c
        A = self.arena
        ps = self.ps
        off = 0
        off = self.ring_init(off, 2)
        self.ring_end = off
        off = A.size - 42 * 1024
        self.s5_limit = off
        vrA, off = A.view(off, [128, 128], F32, "vrA")
        vrB, off = A.view(off, [128, 128], F32, "vrB")
        cs, off = A.view(off, [2, 1024], F32, "cs")
        sT, off = A.view(off, [128, 8, 2], BF16, "sT")
        rowb = []
        for b in range(2):
            r, off = A.view(off, [1, 2048], F32, f"rowb{b}")
            rowb.append(r)
        biasrow, off = A.view(off, [1, 2048], F32, "biasrow")
        grow, off = A.view(off, [1, 1024], F32, "grow")
        abrow, off = A.view(off, [1, 2, 1024], F32, "abrow")
        ld = Tok("setup_ld")

        def ldma(out, in_, wtoks):
            P.dma(SP, out, in_, reads=[], writes=wtoks)

        ldma(vrA.ap[0:16, :], I["mix_norm_g"].rearrange("l (k p) -> (l k) p", p=128), [vrA.t()])
        ldma(vrA.ap[16:32, :], I["mlp_norm_g"].rearrange("l (k p) -> (l k) p", p=128), [vrA.t()])
        ldma(vrA.ap[32:40, :], I["kv_norm_g"].rearrange("(k p) -> k p", p=128), [vrA.t()])
        adab = I["ada_b"].rearrange("l (k p) -> (l k) p", p=128)
        ldma(vrA.ap[40:128, :], adab[0:88, :], [vrA.t()])
        ldma(vrB.ap[0:8, :], adab[88:96, :], [vrB.t()])
        ldma(vrB.ap[8:24, :], I["kv_ada_b"].rearrange("(k p) -> k p", p=128), [vrB.t()])
        for hh in range(2):
            ldma(vrB.ap[24:25, 64 * hh:64 * hh + 64], I["q_norm_g"], [vrB.t()])
            ldma(vrB.ap[25:26, 64 * hh:64 * hh + 64], I["k_norm_g"].rearrange("(o d) -> o d", o=1), [vrB.t()])
        ldma(cs.ap, I["c"], [cs.t()])
        ldma(biasrow.ap, I["ada_b"][0:1, 0:2048], [biasrow.t()])
        ldma(grow.ap, I["mix_norm_g"][0:1, :], [grow.t()])
        VT = self.VT
        self.tr(ps[0].ap[:, 0:128], vrA.ap, c["ident_f"].ap, [vrA.t(), c["ident_f"].t()], [ps[0].t()])
        self.tr(ps[0].ap[:, 128:154], vrB.ap[0:26, :], c["ident_f"].ap[0:26, 0:26], [vrB.t(), c["ident_f"].t()], [ps[0].t()])
        self.copy(DVE, VT.ap[:, 0:154], ps[0].ap[:, 0:154], [ps[0].t()], [VT.t()])
        self.ts(DVE, self.qkg.ap[:, 0:1], VT.ap[:, 152:153], 0.125, None, ALU.mult, None, [VT.t()], [self.qkg.t()])
        self.copy(DVE, self.qkg.ap[:, 1:2], VT.ap[:, 153:154], [VT.t()], [self.qkg.t()])
        self.act(cs.ap, cs.ap, AF.Silu, [cs.t()], [cs.t()])
        for k in range(8):
            self.tr(ps[1].ap[:, 2 * k:2 * k + 2], cs.ap[0:2, 128 * k:128 * k + 128], c["ident_f"].ap[0:2, 0:2],
                    [cs.t(), c["ident_f"].t()], [ps[1].t()])
        self.copy(DVE, sT.ap, ps[1].ap[:, 0:16].rearrange("p (k b) -> p k b", b=2), [ps[1].t()], [sT.t()])
        s5_rest = self.s5_setup_early()
        adaps = ps[5]
        chunks = [(I["ada_w"][0], j * 512) for j in range(12)] + [(I["ada_w"][1], j * 512) for j in range(12)] + \
                 [(I["kv_ada_w"], j * 512) for j in range(4)]
        for ci, (w2d, col0) in enumerate(chunks):
            slot, wv = self.wload_k8(w2d, col0)
            for mt in range(4):
                ot = ci * 4 + mt
                for k in range(8):
                    self.mm(adaps.ap[:, 2 * ot:2 * ot + 2], wv[:, k, 128 * mt:128 * mt + 128], sT.ap[:, k, :],
                            k == 0, k == 7, [slot.t(), sT.t()], [adaps.t()])
            if ci < 4:
                for b in range(2):
                    rp = ps[6 + b]
                    for k in range(8):
                        self.mm(rp.ap[0:1, :], sT.ap[:, k, b:b + 1], wv[:, k, :], k == 0, k == 7, [slot.t(), sT.t()], [rp.t()])
                    self.copy(ACT, rowb[b].ap[0:1, 512 * ci:512 * ci + 512], rp.ap[0:1, :], [rp.t()], [rowb[b].t()])
        self.tt(DVE, self.adaT.ap, adaps.ap[:, 0:224].rearrange("p (o b) -> p o b", b=2),
                VT.ap[:, 40:152].unsqueeze(2).to_broadcast([128, 112, 2]),
                ALU.add, [adaps.t(), VT.t()], [self.adaT.t()])
        adaT = self.adaT
        norms = {1: (16, 24, 32), 2: (32, 96, 104), 3: (8, 48, 56), 4: (24, 72, 80)}
        for s in range(2):
            for n, (gc, sh, sc) in norms.items():
                self.stt(self.modsc.ap[:, s, n, 0, :], adaT.ap[:, sc:sc + 8, s], 1.0, VT.ap[:, gc:gc + 8], ALU.add, ALU.mult,
                         [adaT.t(), VT.t()], [self.modsc.t()])
                self.copy(DVE, self.modsc.ap[:, s, n, 1, :], adaT.ap[:, sh:sh + 8, s], [adaT.t()], [self.modsc.t()])
        for b in range(2):
            self.tt(DVE, rowb[b].ap, rowb[b].ap, biasrow.ap, ALU.add, [rowb[b].t(), biasrow.t()], [rowb[b].t()])
            self.stt(abrow.ap[0:1, 0, :], rowb[b].ap[0:1, 1024:2048], 1.0, grow.ap, ALU.add, ALU.mult,
                     [rowb[b].t(), grow.t()], [abrow.t()])
            self.copy(DVE, abrow.ap[0:1, 1, :], rowb[b].ap[0:1, 0:1024], [rowb[b].t()], [abrow.t()])
            P.dma(SP, self.modrow[b:b + 1], abrow.ap, reads=[abrow.t()], writes=[self.modrow_tok], tok=self.modrow_tok)
        self.dbg("adaT", self.adaT.ap, [128, 112, 2], [self.adaT.t()])
        self.dbg("modsc", self.modsc.ap, [128, NS, 5, 2, 8], [self.modsc.t()])
        self.dbg("VT", self.VT.ap, [128, 160], [self.VT.t()])
        if self.stop_after == "ada":
            return
        s5_rest()
        P.barrier()

    def s5_setup_early(self):
        P, I, c = self.P, self.I, self.c
        A = self.arena
        ps = self.ps
        off = self.ring_end
        lamrows, off = A.view(off, [32, 2, 128], F32, "lamrows")
        ldt2, off = A.view(off, [2, 32], F32, "ldt2")
        NSM = 24
        smb, off = A.view(off, [128, NSM, 32], F32, "sm")
        Bq, off = A.view(off, [128, 2, 32, 16], F32, "Bq")
        Bb, off = A.view(off, [128, 2, 32, 16], F32, "Bb")
        Cin, off = A.view(off, [128, 2, 128], F32, "Cin")
        CT, off = A.view(off, [128, 2, 32, 16], F32, "CT")
        X, off = A.view(off, [128, 2, 32, 8, 16], F32, "X")
        Wo, off = A.view(off, [128, 2, 32, 8, 16], F32, "Wo")
        We, off = A.view(off, [128, 2, 32, 128], F32, "We")
        tmp, off = A.view(off, [128, 2, 512], F32, "s5tmp")
        Drows, off = A.view(off, [64, 16], F32, "Drows")
        Drep, off = A.view(off, [64, 8, 16], F32, "Drep")
        dcol, off = A.view(off, [128, 64], F32, "dcol")
        tmpm, off = A.view(off, [128, 2, 2, 128], F32, "tmpm")
        stage, off = A.view(off, [128, 2, 4, 768], BF16, "stage")
        WoM, off = A.view(off, [128, 2, 2, 2, 128], F32, "WoM")
        assert off <= self.s5_limit, (off, self.s5_limit)
        identf = c["ident_f"]

        names = ["lre", "lim", "dt", "mag", "th", "c", "s", "t1", "t2", "t3", "abre", "abim", "nr", "den",
                 "fre", "fim", "rm2", "lire", "liim", "mure", "muim", "w2r", "w2i"]
        sm = {n: (smb.ap[:, i, :], smb.t(n)) for i, n in enumerate(names)}

        def S(n):
            return sm[n][0]

        def St(n):
            return sm[n][1]

        P.dma(SP, lamrows.ap[:, 0, :], I["s5_a_re"][0].rearrange("(a g) p -> a (g p)", g=2), writes=[lamrows.t()])
        P.dma(SP, lamrows.ap[:, 1, :], I["s5_a_im"][0].rearrange("(a g) p -> a (g p)", g=2), writes=[lamrows.t()])
        P.dma(SP, ldt2.ap, I["s5_log_dt"][0].rearrange("(a g) -> g a", g=2), writes=[ldt2.t()], allow_slow_non_contiguous=True)
        for ri, nm in enumerate(["s5_b_re", "s5_b_im"]):
            src = I[nm][0].rearrange("(a g) p h -> (g p) a h", g=2)
            for q4 in range(4):
                P.dma(SP, Bq.ap[:, ri, 8 * q4:8 * q4 + 8, :], src[:, 8 * q4:8 * q4 + 8, :], writes=[Bq.t()])
        P.dma(SP, Drows.ap, I["s5_d"][0].rearrange("(g h) -> g h", h=16), writes=[Drows.t()])

        self.tr(ps[0].ap[:, 0:32], lamrows.ap[:, 0, :], identf.ap[0:32, 0:32], [lamrows.t(), identf.t()], [ps[0].t()])
        self.tr(ps[0].ap[:, 32:64], lamrows.ap[:, 1, :], identf.ap[0:32, 0:32], [lamrows.t(), identf.t()], [ps[0].t()])
        self.mm(ps[0].ap[:, 64:96], c["sel2"].ap, ldt2.ap, True, True, [c["sel2"].t(), ldt2.t()], [ps[0].t()])
        self.copy(DVE, S("lre"), ps[0].ap[:, 0:32], [ps[0].t()], [St("lre")])
        self.copy(DVE, S("lim"), ps[0].ap[:, 32:64], [ps[0].t()], [St("lim")])
        self.act(S("dt"), ps[0].ap[:, 64:96], AF.Exp, [ps[0].t()], [St("dt")])

        def tt(out, a, b, op, eng=DVE):
            self.tt(eng, S(out), S(a), S(b), op, [St(a), St(b)], [St(out)])

        if self.stop_after == "s5a":
            return

        def horner(out, y, coefs, last):
            self.ts(DVE, S(out), S(y), coefs[0], None, ALU.mult, None, [St(y)], [St(out)])
            for ck in coefs[1:]:
                self.stt(S(out), S(out), ck, S(y), ALU.add, ALU.mult, [St(out), St(y)], [St(out)])
            self.ts(DVE, S(out), S(out), last, None, ALU.add, None, [St(out)], [St(out)])

        tt("t1", "lre", "dt", ALU.mult)
        self.ts(DVE, S("t2"), S("t1"), 0.25, None, ALU.mult, None, [St("t1")], [St("t2")])
        horner("mag", "t2", [1.0 / 5040, 1.0 / 720, 1.0 / 120, 1.0 / 24, 1.0 / 6, 0.5, 1.0], 1.0)
        tt("mag", "mag", "mag", ALU.mult)
        tt("mag", "mag", "mag", ALU.mult)
        tt("th", "lim", "dt", ALU.mult)
        self.ts(DVE, S("t3"), S("th"), 1.0 / 32.0, None, ALU.mult, None, [St("th")], [St("t3")])
        tt("t2", "t3", "t3", ALU.mult)
        horner("c", "t2", [-1.0 / 3628800, 1.0 / 40320, -1.0 / 720, 1.0 / 24, -0.5], 1.0)
        horner("s", "t2", [1.0 / 362880, -1.0 / 5040, 1.0 / 120, -1.0 / 6], 1.0)
        tt("s", "s", "t3", ALU.mult)

        def csquare(a, b):
            tt("t1", a, a, ALU.mult)
            tt("t2", b, b, ALU.mult)
            self.stt(S(b), S(a), 2.0, S(b), ALU.mult, ALU.mult, [St(a), St(b)], [St(b)])
            tt(a, "t1", "t2", ALU.subtract)

        for _ in range(5):
            csquare("c", "s")
        tt("t1", "c", "c", ALU.mult)
        tt("t2", "s", "s", ALU.mult)
        tt("t1", "t1", "t2", ALU.add)
        self.ts(DVE, S("t1"), S("t1"), -0.5, 1.5, ALU.mult, ALU.add, [St("t1")], [St("t1")])
        tt("c", "c", "t1", ALU.mult)
        tt("s", "s", "t1", ALU.mult)
        tt("abre", "mag", "c", ALU.mult)
        tt("abim", "mag", "s", ALU.mult)
        self.ts(DVE, S("nr"), S("abre"), -1.0, None, ALU.add, None, [St("abre")], [St("nr")])
        tt("t1", "lre", "lre", ALU.mult)
        tt("t2", "lim", "lim", ALU.mult)
        tt("den", "t1", "t2", ALU.add)
        P.op(DVE, lambda e: e.reciprocal(S("den"), S("den")), [St("den")], [St("den")])
        tt("t1", "nr", "lre", ALU.mult)
        tt("t2", "abim", "lim", ALU.mult)
        tt("t1", "t1", "t2", ALU.add)
        tt("fre", "t1", "den", ALU.mult)
        tt("t1", "abim", "lre", ALU.mult)
        tt("t2", "nr", "lim", ALU.mult)
        tt("t1", "t1", "t2", ALU.subtract)
        tt("fim", "t1", "den", ALU.mult)
        tt("rm2", "mag", "mag", ALU.mult)
        P.op(DVE, lambda e: e.reciprocal(S("rm2"), S("rm2")), [St("rm2")], [St("rm2")])
        tt("lire", "abre", "rm2", ALU.mult)
        self.stt(S("liim"), S("abim"), -1.0, S("rm2"), ALU.mult, ALU.mult, [St("abim"), St("rm2")], [St("liim")])
        self.copy(DVE, S("mure"), S("abre"), [St("abre")], [St("mure")])
        self.copy(DVE, S("muim"), S("abim"), [St("abim")], [St("muim")])
        for _ in range(3):
            csquare("mure", "muim")
        coef = self.coef
        self.copy(DVE, S("w2r"), S("mure"), [St("mure")], [St("w2r")])
        self.copy(DVE, S("w2i"), S("muim"), [St("muim")], [St("w2i")])
        for k in range(8):
            self.copy(DVE, coef.ap[:, :, k, 0], S("w2r"), [St("w2r")], [coef.t()])
            self.copy(DVE, coef.ap[:, :, k, 1], S("w2i"), [St("w2i")], [coef.t()])
            self.ts(DVE, coef.ap[:, :, k, 2], S("w2i"), -1.0, None, ALU.mult, None, [St("w2i")], [coef.t()])
            if k < 7:
                csquare("w2r", "w2i")

        if self.stop_after == "s5b":
            return
        for _ in range(3):
            tt("mag", "mag", "mag", ALU.mult)
        self.copy(DVE, self.rho.ap, S("mag"), [St("mag")], [self.rho.t()])
        for _ in range(3):
            csquare("c", "s")
        tt("t1", "c", "c", ALU.mult)
        tt("t2", "s", "s", ALU.mult)
        tt("t1", "t1", "t2", ALU.add)
        self.ts(DVE, S("t1"), S("t1"), -0.5, 1.5, ALU.mult, ALU.add, [St("t1")], [St("t1")])
        tt("c", "c", "t1", ALU.mult)
        tt("s", "s", "t1", ALU.mult)
        tmpt = [tmp.t(i) for i in range(2)]

        def cmul(ore, oim, otoks, are, aim, atoks, zre, zim, ztoks, n):
            ab = lambda nm: S(nm).unsqueeze(2).to_broadcast([128, 32, n])
            t = [tmp.ap[:, i, 0:32 * n].rearrange("p (a h) -> p a h", h=n) for i in range(2)]
            rd = atoks + ztoks
            self.tt(DVE, t[0], zre, ab(are), ALU.mult, rd, [tmpt[0]])
            self.tt(DVE, t[1], zim, ab(aim), ALU.mult, rd, [tmpt[1]])
            self.tt(DVE, ore, t[0], t[1], ALU.subtract, [tmpt[0], tmpt[1]], otoks)
            self.tt(DVE, t[0], zim, ab(are), ALU.mult, rd, [tmpt[0]])
            self.tt(DVE, t[1], zre, ab(aim), ALU.mult, rd, [tmpt[1]])
            self.tt(DVE, oim, t[0], t[1], ALU.add, [tmpt[0], tmpt[1]], otoks)

        fa = [St("fre"), St("fim")]
        cmul(Bb.ap[:, 0], Bb.ap[:, 1], [Bb.t()], "fre", "fim", fa, Bq.ap[:, 0], Bq.ap[:, 1], [Bq.t()], 16)
        la = [St("lire"), St("liim")]
        for j in range(8):
            if j == 0:
                zr, zi, zt = Bb.ap[:, 0], Bb.ap[:, 1], [Bb.t()]
            else:
                zr, zi, zt = X.ap[:, 0, :, j - 1, :], X.ap[:, 1, :, j - 1, :], [X.t(j - 1)]
            cmul(X.ap[:, 0, :, j, :], X.ap[:, 1, :, j, :], [X.t(j)], "lire", "liim", la, zr, zi, zt, 16)
        if self.stop_after == "s5c":
            return
        for r in range(8):
            for ri, nm in enumerate(["s5_c_re", "s5_c_im"]):
                db = (2 * r + ri) % 2
                P.dma(SP, Cin.ap[:, db, 0:64], I[nm][0][8 * r:8 * r + 8].rearrange("g h p -> (g h) p"), writes=[Cin.t(db)])
                pb = ps[1 + db]
                self.tr(pb.ap[0:64, 0:128], Cin.ap[:, db, 0:64], identf.ap, [Cin.t(db), identf.t()], [pb.t()])
                src = pb.ap[0:64, 0:128].rearrange("p (a g h) -> p a g h", g=2, h=16)
                self.copy(ACT, CT.ap[0:64, ri, 4 * r:4 * r + 4, :], src[:, :, 0, :], [pb.t()], [CT.t()])
                self.copy(ACT, CT.ap[64:128, ri, 4 * r:4 * r + 4, :], src[:, :, 1, :], [pb.t()], [CT.t()])
        if self.stop_after == "s5d":
            return
        ab_ = [St("abre"), St("abim")]
        for i in range(8):
            if i == 0:
                zr, zi, zt = CT.ap[:, 0], CT.ap[:, 1], [CT.t()]
            else:
                zr, zi, zt = Wo.ap[:, 0, :, i - 1, :], Wo.ap[:, 1, :, i - 1, :], [Wo.t(i - 1)]
            cmul(Wo.ap[:, 0, :, i, :], Wo.ap[:, 1, :, i, :], [Wo.t(i)], "abre", "abim", ab_, zr, zi, zt, 16)
        ma = [St("mure"), St("muim")]
        for j in range(8):
            cmul(We.ap[:, 0, :, 16 * j:16 * j + 16], We.ap[:, 1, :, 16 * j:16 * j + 16], [We.t(j)], "mure", "muim", ma,
                 X.ap[:, 0, :, j, :], X.ap[:, 1, :, j, :], [X.t(j)], 16)
        allWo = [Wo.t(i) for i in range(8)]
        WoN = Wo.t("neg")
        self.ts(DVE, Wo.ap[:, 1], Wo.ap[:, 1], -1.0, None, ALU.mult, None, allWo, allWo + [WoN])
        def rest():
            self.copy(DVE, Drep.ap, Drows.ap.unsqueeze(1).to_broadcast([64, 8, 16]), [Drows.t()], [Drep.t()])
            self.tr(ps[3].ap[:, 0:64], Drep.ap.rearrange("p j h -> p (j h)"), identf.ap[0:64, 0:64], [Drep.t(), identf.t()], [ps[3].t()])
            self.copy(DVE, dcol.ap, ps[3].ap[:, 0:64], [ps[3].t()], [dcol.t()])
            self.s5_setup_pairs(locals_=dict(We=We, X=X, Wo=Wo, WoN=WoN, allWo=allWo, stage=stage, WoM=WoM, tmpm=tmpm, dcol=dcol, identf=identf))
            P.barrier()
            Tr = Buf(X.ap.rearrange("p r a j h -> p (r a j h)").rearrange("p (a c) -> p a c", c=256), "Tr")
            Ti = Buf(Wo.ap.rearrange("p r a j h -> p (r a j h)").rearrange("p (a c) -> p a c", c=256), "Ti")
            tb = [Buf(We.ap[:, i].rearrange("p a n -> p (a n)"), f"tbig{i}") for i in range(2)]
            P.op(DVE, lambda e: e.memset(Tr.ap[:, :, 0:1], 1.0), [], [Tr.t()])
            P.op(DVE, lambda e: e.memset(Ti.ap[:, :, 0:1], 0.0), [], [Ti.t()])
            self.copy(DVE, S("w2r"), S("c"), [St("c")], [St("w2r")])
            self.copy(DVE, S("w2i"), S("s"), [St("s")], [St("w2i")])
            for k in range(8):
                n = 1 << k
                wrb = S("w2r").unsqueeze(2).to_broadcast([128, 32, n])
                wib = S("w2i").unsqueeze(2).to_broadcast([128, 32, n])
                t0 = tb[0].ap[:, 0:32 * n].rearrange("p (a c) -> p a c", c=n)
                t1 = tb[1].ap[:, 0:32 * n].rearrange("p (a c) -> p a c", c=n)
                wt = [St("w2r"), St("w2i")]
                self.tt(DVE, t0, Tr.ap[:, :, 0:n], wrb, ALU.mult, [Tr.t()] + wt, [tb[0].t()])
                self.tt(DVE, t1, Ti.ap[:, :, 0:n], wib, ALU.mult, [Ti.t()] + wt, [tb[1].t()])
                self.tt(DVE, Tr.ap[:, :, n:2 * n], t0, t1, ALU.subtract, [tb[0].t(), tb[1].t()], [Tr.t()])
                self.tt(DVE, t0, Ti.ap[:, :, 0:n], wrb, ALU.mult, [Ti.t()] + wt, [tb[0].t()])
                self.tt(DVE, t1, Tr.ap[:, :, 0:n], wib, ALU.mult, [Tr.t()] + wt, [tb[1].t()])
                self.tt(DVE, Ti.ap[:, :, n:2 * n], t0, t1, ALU.add, [tb[0].t(), tb[1].t()], [Ti.t()])
                if k < 7:
                    csquare("w2r", "w2i")
            P.dma(SP, self.s5tab[:, :, 0, :].rearrange("a p c -> p a c"), Tr.ap, reads=[Tr.t()], writes=[self.s5tab_tok])
            P.dma(SP, self.s5tab[:, :, 1, :].rearrange("a p c -> p a c"), Ti.ap, reads=[Ti.t()], writes=[self.s5tab_tok])
            self.dbg("Tr", Tr.ap, [128, 32, 256], [Tr.t()])
            self.dbg("Ti", Ti.ap, [128, 32, 256], [Ti.t()])
            self.dbg("rho", self.rho.ap, [128, 32], [self.rho.t()])
        return rest

    def s5_setup_pairs(self, locals_):
        P, c, ps = self.P, self.c, self.ps
        We, X, Wo, WoN, allWo, stage, WoM, tmpm, dcol, identf = (locals_[k] for k in
                                                                ["We", "X", "Wo", "WoN", "allWo", "stage", "WoM", "tmpm", "dcol", "identf"])
        allWe = [We.t(j) for j in range(8)]
        allX = [X.t(j) for j in range(8)]

        def phase1(pr):
            slot = (pr // 4) % 2
            p4 = pr % 4
            st_ = stage.t(slot)
            pa = ps[4 + pr % 2]
            pb = ps[6 + pr % 2]
            self.tr(pa.ap[:, 0:128], We.ap[:, 0, pr, :], identf.ap, allWe + [identf.t()], [pa.t()])
            self.tr(pa.ap[:, 128:256], We.ap[:, 1, pr, :], identf.ap, allWe + [identf.t()], [pa.t()])
            self.copy(ACT, stage.ap[:, slot, p4, 0:256], pa.ap[:, 0:256], [pa.t()], [st_])
            self.copy(ACT, stage.ap[:, slot, p4, 256:512].rearrange("p (r n) -> p r n", r=2),
                      Wo.ap[:, :, pr].rearrange("p r i h -> p r (i h)"), [WoN], [st_])
            wm = WoM.ap[:, pr % 2]
            for gl in range(2):
                self.ts(POOL if gl == 0 else DVE, wm[:, gl], Wo.ap[:, :, pr].rearrange("p r i h -> p r (i h)"), c["hm"].ap[:, gl:gl + 1], None, ALU.mult, None,
                        [WoN, c["hm"].t()], [WoM.t(pr % 2)])
            for gl in range(2):
                for ri in range(2):
                    self.mm(pb.ap[:, 128 * gl:128 * gl + 128], X.ap[:, ri, pr].rearrange("p j h -> p (j h)"),
                            wm[:, gl, ri, :], ri == 0, ri == 1, allX + [WoM.t(pr % 2)], [pb.t()])

        def phase2(pr):
            slot = (pr // 4) % 2
            p4 = pr % 4
            st_ = stage.t(slot)
            pb = ps[6 + pr % 2]
            tm = tmpm.ap[:, pr % 2]
            self.tt(DVE, tm, pb.ap[:, 0:256].rearrange("p (g n) -> p g n", g=2),
                    c["bmask_f"].ap.unsqueeze(1).to_broadcast([128, 2, 128]), ALU.mult, [pb.t(), c["bmask_f"].t()], [tmpm.t(pr % 2)])
            for gl in range(2):
                g = 2 * pr + gl
                self.stt(stage.ap[:, slot, p4, 512 + 128 * gl:512 + 128 * gl + 128], identf.ap, dcol.ap[:, g:g + 1], tm[:, gl, :],
                         ALU.mult, ALU.add, [identf.t(), dcol.t(), tmpm.t(pr % 2)], [st_])
            if p4 == 3:
                ftc = pr // 4
                P.dma(SP, self.s5w[4 * ftc:4 * ftc + 4].rearrange("a p n -> p a n"), stage.ap[:, slot], reads=[st_],
                      writes=[self.s5w_tok[ftc]])
                if ftc == 0:
                    self.dbg("s5w0", stage.ap[:, slot], [128, 4, 768], [st_], dt=BF16)
        phase1(0)
        for pr in range(32):
            if pr + 1 < 32:
                phase1(pr + 1)
            phase2(pr)
        self.dbg("coef", self.coef.ap, [128, 32, 8, 3], [self.coef.t()])

    R_RSTD = 16384
    R_XT = 18432
    R_ACTT = R_XT + 65536
    R_TAIL = R_ACTT + 32768

    def seq_pipeline(self, s):
        P = self.P
        A = self.arena
        self.xT, _ = A.view(self.R_XT, [128, 8, 2048], F32, f"xT{s}")
        self.actT, _ = A.view(self.R_ACTT, [128, 8, 2048], BF16, f"actT{s}")
        self.rstd, _ = A.view(self.R_RSTD, [128, 512], F32, f"rstd{s}")
        self.s5_prep(s)
        P.barrier()
        self.s5_core(s)
        P.barrier()
        self.dbg(f"gT{s}", self.actT.ap, [128, 8, 2048], [], dt=BF16)
        self.dbg(f"xT{s}", self.xT.ap, [128, 8, 2048], [])
        if self.stop_after == "xreload":
            return
        self.glu(s)
        P.barrier()
        self.mlp(s, 0, 1)
        P.barrier()
        self.dbg(f"x1T{s}", self.xT.ap, [128, 8, 2048], [])
        if self.stop_after == "layer0":
            return
        self.kv_phase(s)
        P.barrier()
        self.dbg(f"KT{s}", self.KT.ap, [128, 8, 2048], [], dt=BF16)
        self.dbg(f"V{s}", self.V.ap, [128, 16, 1024], [], dt=BF16)
        if self.stop_after == "kv":
            return
        self.attn_phase(s)
        P.barrier()
        self.dbg(f"x2T{s}", self.xT.ap, [128, 8, 2048], [])
        if self.stop_after == "attn":
            return
        self.mlp(s, 1, 4)
        P.barrier()
        self.output_phase(s)
        P.barrier()

    def load_xtok(self, s, half, Xtok):
        src = self.I["x"][s, half * 1024:(half + 1) * 1024, :].rearrange("(c j) f -> c j f", j=8)
        for q in range(4):
            self.P.dma(SP, Xtok.ap[32 * q:32 * q + 32], src[32 * q:32 * q + 32], writes=[Xtok.t()])

    def s5_prep(self, s):
        P, c, ps = self.P, self.c, self.ps
        A = self.arena
        off = self.R_TAIL
        D, off = A.view(off, [128, 64, 256], BF16, "D")
        self.D = D
        self.s5_tail_off = off
        Xtok, off = A.view(off, [128, 8, 1024], F32, "Xtok")
        Arep, off = A.view(off, [128, 1024], F32, "Arep")
        Brep, off = A.view(off, [128, 1024], F32, "Brep")
        ss, off = A.view(off, [128, 8], F32, "ss")
        rs, off = A.view(off, [128, 8], F32, "rs")
        junk, off = A.view(off, [128, 1024], BF16, "junk")
        o2 = self.R_ACTT
        htok, o2 = A.view(o2, [128, 64, 8, 16], BF16, "htok")
        tmpf, o2 = A.view(o2, [128, 2, 1024], F32, "tmpf")
        xT = self.xT
        P.dma(SP, Arep.ap, self.modrow[s, 0, :].partition_broadcast(128), reads=[self.modrow_tok], writes=[Arep.t()])
        P.dma(SP, Brep.ap, self.modrow[s, 1, :].partition_broadcast(128), reads=[self.modrow_tok], writes=[Brep.t()])
        n = 0
        for half in range(2):
            self.load_xtok(s, half, Xtok)
            for j in range(8):
                self.act(junk.ap, Xtok.ap[:, j, :], AF.Square, [Xtok.t()], [junk.t(), ss.t()], accum_out=ss.ap[:, j:j + 1])
            self.ts(DVE, rs.ap, ss.ap, 1.0 / 1024.0, EPS, ALU.mult, ALU.add, [ss.t()], [rs.t()])
            self.act(rs.ap, rs.ap, AF.Sqrt, [rs.t()], [rs.t()])
            P.op(DVE, lambda e: e.reciprocal(rs.ap, rs.ap), [rs.t()], [rs.t()])
            for j in range(8):
                tb = j % 2
                self.stt(tmpf.ap[:, tb, :], Xtok.ap[:, j, :], rs.ap[:, j:j + 1], Arep.ap, ALU.mult, ALU.mult,
                         [Xtok.t(), rs.t(), Arep.t()], [tmpf.t(tb)])
                self.tt(POOL, htok.ap[:, :, j, :], tmpf.ap[:, tb, :].rearrange("p (g h) -> p g h", h=16),
                        Brep.ap.rearrange("p (g h) -> p g h", h=16), ALU.add, [tmpf.t(tb), Brep.t()], [htok.t()])
            for ft in range(8):
                for jq in range(2):
                    bank = ps[4 + n % 4]
                    for jj in range(4):
                        j = 4 * jq + jj
                        self.tr(bank.ap[:, 128 * jj:128 * jj + 128], Xtok.ap[:, j, 128 * ft:128 * ft + 128], c["ident_f"].ap,
                                [Xtok.t(), c["ident_f"].t()], [bank.t()], sig=(jj == 3))
                    dst = xT.ap[:, ft, 1024 * half:1024 * half + 1024].rearrange("p (c j) -> p j c", j=8)[:, 4 * jq:4 * jq + 4, :]
                    self.copy(ACT if n % 2 == 0 else DVE, dst, bank.ap.rearrange("p (j c) -> p j c", c=128), [bank.t()], [xT.t(ft)])
                    n += 1
            for g4 in range(16):
                pb = ps[g4 % 4]
                pbv = pb.ap.bitcast(BF16)
                for gi in range(4):
                    g = 4 * g4 + gi
                    self.tr(pbv[:, 128 * gi:128 * gi + 128], htok.ap[:, g].rearrange("p j h -> p (j h)"), c["ident_b"].ap,
                            [htok.t(), c["ident_b"].t()], [pb.t()], sig=(gi == 3))
                self.copy(ACT if g4 % 2 == 0 else DVE, D.ap[:, 4 * g4:4 * g4 + 4, 128 * half:128 * half + 128],
                          pbv[:, 0:512].rearrange("p (g c) -> p g c", c=128), [pb.t()], [D.t()])

    def s5_core(self, s):
        P, c, ps = self.P, self.c, self.ps
        A = self.arena
        D = self.D
        rho = self.rho
        off = self.s5_tail_off
        wch, _ = A.view(0, [128, 2, 4, 768], BF16, "wch")
        tabs, off = A.view(off, [128, 2, 4, 2, 256], F32, "tabs")
        Wk, off = A.view(off, [128, 2, 8, 256], F32, "Wk")
        Sbf, off = A.view(off, [128, 4, 2, 256], BF16, "Sbf")
        Ygb, off = A.view(off, [128, 2, 2, 256], BF16, "Ygb")
        Gtok, off = A.view(off, [128, 2, 2, 8, 128], BF16, "Gtok")
        gT = self.actT
        P.op(POOL, lambda e: e.memset(Sbf.ap, 0.0), [], [Sbf.t(i) for i in range(4)])
        identb = c["ident_b"]

        def wslot_of(pr):
            return (pr // 4) % 2

        def pair_front(pr):
            p4 = pr % 4
            wslot = wslot_of(pr)
            bankE = ps[pr % 4]
            for ri in range(2):
                for gl in range(2):
                    self.mm(bankE.ap[64 * gl:64 * gl + 64, 256 * ri:256 * ri + 256],
                            wch.ap[:, wslot, p4, 128 * ri + 64 * gl:128 * ri + 64 * gl + 64], D.ap[:, 2 * pr + gl, :], True, True,
                            [wch.t(wslot), D.t()], [bankE.t()], sig=(ri == 1 and gl == 1))

        def scan_ops(pr):
            p4 = pr % 4
            wslot = wslot_of(pr)
            sl, w2 = pr % 4, pr % 2
            bankE = ps[pr % 4]
            Er, Ei = bankE.ap[:, 0:256], bankE.ap[:, 256:512]
            cr, sr = tabs.ap[:, wslot, p4, 0, :], tabs.ap[:, wslot, p4, 1, :]
            W = lambda j: Wk.ap[:, w2, j, :]
            wt = lambda j: Wk.t((w2, j))
            tb_, eb = tabs.t(wslot), bankE.t()
            rb = rho.ap[:, pr:pr + 1].to_broadcast([128, 256])
            ops = []
            ops.append(lambda: self.tt(DVE, W(4), Er, cr, ALU.mult, [eb, tb_], [wt(4)]))
            ops.append(lambda: self.tt(DVE, W(5), Ei, sr, ALU.mult, [eb, tb_], [wt(5)]))
            ops.append(lambda: self.tt(DVE, W(6), Ei, cr, ALU.mult, [eb, tb_], [wt(6)]))
            ops.append(lambda: self.tt(DVE, W(7), Er, sr, ALU.mult, [eb, tb_], [wt(7)]))
            ops.append(lambda: self.tt(DVE, W(0), W(4), W(5), ALU.add, [wt(4), wt(5)], [wt(0)]))
            ops.append(lambda: self.tt(DVE, W(1), W(6), W(7), ALU.subtract, [wt(6), wt(7)], [wt(1)]))
            ops.append(lambda: self.P.op(DVE, lambda e: e.tensor_tensor_scan(W(2), rb, W(0), 0.0, ALU.mult, ALU.add),
                                         [wt(0), rho.t()], [wt(2)]))
            ops.append(lambda: self.P.op(DVE, lambda e: e.tensor_tensor_scan(W(3), rb, W(1), 0.0, ALU.mult, ALU.add),
                                         [wt(1), rho.t()], [wt(3)]))
            n = 255
            ops.append(lambda: self.tt(DVE, W(4)[:, 0:n], W(2)[:, 0:n], cr[:, 0:n], ALU.mult, [wt(2), tb_], [wt(4)]))
            ops.append(lambda: self.tt(DVE, W(5)[:, 0:n], W(3)[:, 0:n], sr[:, 0:n], ALU.mult, [wt(3), tb_], [wt(5)]))
            ops.append(lambda: self.tt(DVE, W(6)[:, 0:n], W(3)[:, 0:n], cr[:, 0:n], ALU.mult, [wt(3), tb_], [wt(6)]))
            ops.append(lambda: self.tt(DVE, W(7)[:, 0:n], W(2)[:, 0:n], sr[:, 0:n], ALU.mult, [wt(2), tb_], [wt(7)]))
            ops.append(lambda: self.tt(DVE, Sbf.ap[:, sl, 0, 1:256], W(4)[:, 0:n], W(5)[:, 0:n], ALU.subtract, [wt(4), wt(5)], [Sbf.t(sl)]))
            ops.append(lambda: self.tt(DVE, Sbf.ap[:, sl, 1, 1:256], W(6)[:, 0:n], W(7)[:, 0:n], ALU.add, [wt(6), wt(7)], [Sbf.t(sl)]))
            return ops

        def pair_back(pr):
            ft, p4 = pr // 4, pr % 4
            wslot = wslot_of(pr)
            gslot = ft % 2
            sl = pr % 4
            ys = pr % 2
            bankY = ps[4 + pr % 2]
            for gl in range(2):
                o = bankY.ap[:, 256 * gl:256 * gl + 256]
                hs = slice(64 * gl, 64 * gl + 64)
                self.mm(o, wch.ap[:, wslot, p4, 512 + 128 * gl:512 + 128 * gl + 128], D.ap[:, 2 * pr + gl, :], True, False,
                        [wch.t(wslot), D.t()], [bankY.t()])
                self.mm(o, wch.ap[hs, wslot, p4, 256:384], Sbf.ap[hs, sl, 0, :], False, False, [wch.t(wslot), Sbf.t(sl)], [bankY.t()])
                self.mm(o, wch.ap[hs, wslot, p4, 384:512], Sbf.ap[hs, sl, 1, :], False, True, [wch.t(wslot), Sbf.t(sl)], [bankY.t()],
                        sig=(gl == 1))
            self.act(Ygb.ap[:, ys].rearrange("p g c -> p (g c)"), bankY.ap, AF.Gelu, [bankY.t()], [Ygb.t(ys)])
            pT = ps[6]
            pTv = pT.ap.bitcast(BF16)
            for gl in range(2):
                for half in range(2):
                    q = 2 * gl + half
                    self.tr(pTv[:, 128 * q:128 * q + 128], Ygb.ap[:, ys, gl, 128 * half:128 * half + 128], identb.ap,
                            [Ygb.t(ys), identb.t()], [pT.t()], sig=(q == 3))
            for gl in range(2):
                fo = 32 * p4 + 16 * gl
                self.copy(ACT, Gtok.ap[:, gslot, :, :, fo:fo + 16],
                          pTv[:, 256 * gl:256 * gl + 256].rearrange("p (a i h) -> p a i h", a=2, h=16), [pT.t()], [Gtok.t(gslot)])

        def ft_done(ft):
            gslot = ft % 2
            for half in range(2):
                pT2 = ps[7]
                pv = pT2.ap.bitcast(BF16)
                for i in range(8):
                    self.tr(pv[:, 128 * i:128 * i + 128], Gtok.ap[:, gslot, half, i, :], identb.ap, [Gtok.t(gslot), identb.t()], [pT2.t()],
                            sig=(i == 7))
                self.copy(ACT, gT.ap[:, ft, 1024 * half:1024 * half + 1024].rearrange("p (c i) -> p i c", i=8),
                          pv.rearrange("p (i c) -> p i c", c=128), [pT2.t()], [gT.t(ft)])

        def load_w(ft):
            P.dma(SP, wch.ap[:, ft % 2], self.s5w[4 * ft:4 * ft + 4].rearrange("a p n -> p a n"), reads=[self.s5w_tok[ft]],
                  writes=[wch.t(ft % 2)])
            P.dma(SP, tabs.ap[:, ft % 2], self.s5tab[4 * ft:4 * ft + 4].rearrange("a p r c -> p a r c"), reads=[self.s5tab_tok],
                  writes=[tabs.t(ft % 2)])

        load_w(0)
        load_w(1)
        for p4 in (0, 1):
            pair_front(p4)
        for pp in range(16):
            prs = [2 * pp, 2 * pp + 1]
            oa, ob_ = scan_ops(prs[0]), scan_ops(prs[1])
            for fa, fb in zip(oa, ob_):
                fa()
                fb()
            if pp + 1 < 16:
                for pr in (2 * pp + 2, 2 * pp + 3):
                    pair_front(pr)
            for pr in prs:
                pair_back(pr)
            if pp % 2 == 1:
                ft = pp // 2
                ft_done(ft)
                if ft + 2 < 8:
                    load_w(ft + 2)

    def x_reload(self, s):
        P, c, ps = self.P, self.c, self.ps
        A = self.arena
        Xtok, _ = A.view(self.R_TAIL, [128, 8, 1024], F32, "Xtok2")
        xT = self.xT
        n = 0
        for half in range(2):
            self.load_xtok(s, half, Xtok)
            for ft in range(8):
                for jq in range(2):
                    bank = ps[n % 4]
                    for jj in range(4):
                        j = 4 * jq + jj
                        self.tr(bank.ap[:, 128 * jj:128 * jj + 128], Xtok.ap[:, j, 128 * ft:128 * ft + 128], c["ident_f"].ap,
                                [Xtok.t(), c["ident_f"].t()], [bank.t()], sig=(jj == 3))
                    dst = xT.ap[:, ft, 1024 * half:1024 * half + 1024].rearrange("p (c j) -> p j c", j=8)[:, 4 * jq:4 * jq + 4, :]
                    self.copy(ACT if n % 2 == 0 else DVE, dst, bank.ap.rearrange("p (j c) -> p j c", c=128), [bank.t()], [xT.t(ft)])
                    n += 1


    def stream(self, loaders, computes):
        n = len(loaders)
        slots = {0: loaders[0]()}
        for i in range(n):
            if i + 1 < n:
                slots[i + 1] = loaders[i + 1]()
            computes[i](slots.pop(i))

    def fm_norm(self, s, n, tt, dst, sq, tmpf, bank):
        c, xT, rstd = self.c, self.xT, self.rstd
        ts_ = slice(512 * tt, 512 * tt + 512)
        for ft in range(8):
            self.act(sq.ap[:, ft, :], xT.ap[:, ft, ts_], AF.Square, [xT.t(ft)], [sq.t(ft)])
        for ft in range(8):
            self.mm(bank.ap, c["onesm_b"].ap, sq.ap[:, ft, :], ft == 0, ft == 7, [c["onesm_b"].t(), sq.t(ft)], [bank.t()])
        self.act(rstd.ap, bank.ap, AF.Ln, [bank.t()], [rstd.t()], bias=self.epsc.ap)
        self.act(rstd.ap, rstd.ap, AF.Exp, [rstd.t()], [rstd.t()], scale=-0.5)
        ms = self.modsc
        for ft in range(8):
            tb = ft % 2
            self.stt(tmpf.ap[:, tb, :], xT.ap[:, ft, ts_], ms.ap[:, s, n, 0, ft:ft + 1], rstd.ap, ALU.mult, ALU.mult,
                     [xT.t(ft), ms.t(), rstd.t()], [tmpf.t(tb)])
            self.act(dst[:, ft, :], tmpf.ap[:, tb, :], AF.Identity, [tmpf.t(tb), ms.t()], [self.cur_h_tok], bias=ms.ap[:, s, n, 1, ft:ft + 1])

    def glu(self, s):
        P, ps = self.P, self.ps
        A = self.arena
        off = self.R_TAIL
        sg, off = A.view(off, [128, 2, 512], F32, "sg")
        mt_, off = A.view(off, [128, 2, 512], F32, "mt")
        gT, xT, adaT = self.actT, self.xT, self.adaT
        w = self.I["s5_w_glu"][0].rearrange("(k p) n -> p k n", p=128)
        allg = [gT.t(ft) for ft in range(8)]
        cnt = [0]

        def loader(ft):
            def f():
                v = lambda a: a[:, 0:2048].rearrange("p (k g n) -> p k g n", g=2, n=128)
                return self.wload([(lambda a: v(a)[:, :, 0, :], w[:, :, 128 * ft:128 * ft + 128]),
                                   (lambda a: v(a)[:, :, 1, :], w[:, :, 1024 + 128 * ft:1024 + 128 * ft + 128])])
            return f

        def compute(ft):
            def f(slot):
                wv = slot.ap[:, 0:2048].rearrange("p (k g n) -> p k g n", g=2, n=128)
                for tt in range(4):
                    n = cnt[0]
                    cnt[0] += 1
                    bv, bg = ps[(2 * n) % 8], ps[(2 * n + 1) % 8]
                    ts_ = slice(512 * tt, 512 * tt + 512)
                    for gi, bank in enumerate([bv, bg]):
                        for k in range(8):
                            self.mm(bank.ap, wv[:, k, gi, :], gT.ap[:, k, ts_], k == 0, k == 7, [slot.t()] + allg, [bank.t()])
                    b2 = n % 2
                    self.act(sg.ap[:, b2, :], bg.ap, AF.Sigmoid, [bg.t()], [sg.t(b2)])
                    self.tt(DVE, mt_.ap[:, b2, :], bv.ap, sg.ap[:, b2, :], ALU.mult, [bv.t(), sg.t(b2)], [mt_.t(b2)])
                    self.stt(xT.ap[:, ft, ts_], mt_.ap[:, b2, :], adaT.ap[:, 16 + ft, s:s + 1], xT.ap[:, ft, ts_], ALU.mult, ALU.add,
                             [mt_.t(b2), adaT.t(), xT.t(ft)], [xT.t(ft)])
            return f

        self.stream([loader(ft) for ft in range(8)], [compute(ft) for ft in range(8)])

    def mlp(self, s, l, nidx):
        P, ps = self.P, self.ps
        A = self.arena
        uT, o_t = A.view(self.R_TAIL, [128, 32, 1024], BF16, "uT")
        htile1, _ = A.view(o_t, [128, 8, 1024], BF16, "htile1")
        off = self.R_ACTT
        htile0, off = A.view(off, [128, 8, 1024], BF16, "htile0")
        sq, off = A.view(off, [128, 8, 512], BF16, "sq")
        tmpf, off = A.view(off, [128, 2, 512], F32, "tmpf2")
        r, off = A.view(off, [128, 2, 1024], BF16, "r")
        htiles = [htile0, htile1]
        xT, adaT = self.xT, self.adaT
        w1 = self.I["mlp_w1"][l]
        w2 = self.I["mlp_w2"][l].rearrange("(k p) n -> p k n", p=128)
        gm0 = 48 * l + 40
        nb = [0]

        def norm(t2):
            for sub in range(2):
                self.cur_h_tok = htiles[t2].t(sub)
                self.fm_norm(s, nidx, 2 * t2 + sub, htiles[t2].ap[:, :, 512 * sub:512 * sub + 512], sq, tmpf, ps[7])

        norm(0)
        for t2 in range(2):
            htile = htiles[t2]
            loaders, computes = [], []
            for ch in range(8):
                loaders.append(lambda ch=ch: self.wload_k8(w1, 512 * ch))

                def c1(sv, ch=ch):
                    slot, wv = sv
                    for m4 in range(4):
                        e = 4 * ch + m4
                        for sub in range(2):
                            bank = ps[nb[0] % 4]
                            nb[0] += 1
                            for k in range(8):
                                self.mm(bank.ap, wv[:, k, 128 * m4:128 * m4 + 128], htile.ap[:, k, 512 * sub:512 * sub + 512], k == 0, k == 7,
                                        [slot.t(), htile.t(sub)], [bank.t()])
                            self.act(r.ap[:, e % 2, 512 * sub:512 * sub + 512], bank.ap, AF.Relu, [bank.t()], [r.t((e % 2, sub))])
                        self.tt(DVE, uT.ap[:, e, :], r.ap[:, e % 2, :], r.ap[:, e % 2, :], ALU.mult,
                                [r.t((e % 2, 0)), r.t((e % 2, 1))], [uT.t(e)])
                    if ch == 7 and t2 == 0:
                        norm(1)
                computes.append(c1)
            allu = [uT.t(e) for e in range(32)]
            for o in range(8):
                def l2(o=o):
                    slot = self.wload([(lambda a: a.rearrange("p (k n) -> p k n", n=128), w2[:, :, 128 * o:128 * o + 128])])
                    return slot, slot.ap.rearrange("p (k n) -> p k n", n=128)
                loaders.append(l2)

                def c2(sv, o=o, t2=t2):
                    slot, wv = sv
                    for sub in range(2):
                        bank = ps[4 + (2 * o + sub) % 3]
                        ts_ = slice(1024 * t2 + 512 * sub, 1024 * t2 + 512 * sub + 512)
                        for k in range(32):
                            self.mm(bank.ap, wv[:, k, :], uT.ap[:, k, 512 * sub:512 * sub + 512], k == 0, k == 31, [slot.t()] + allu, [bank.t()])
                        self.stt(xT.ap[:, o, ts_], bank.ap, adaT.ap[:, gm0 + o, s:s + 1], xT.ap[:, o, ts_], ALU.mult, ALU.add,
                                 [bank.t(), adaT.t(), xT.t(o)], [xT.t(o)])
                computes.append(c2)
            self.stream(loaders, computes)

    def headnorm_batch(self, banks, sbanks, gaincol, dsts, dtoks, sq, tk4):
        c = self.c
        n = len(banks)
        for j in range(n):
            self.act(sq.ap[:, j, :], banks[j].ap, AF.Square, [banks[j].t()], [sq.t(j)])
        for j in range(n):
            self.mm(sbanks[j].ap, c["blk_b"].ap, sq.ap[:, j, :], True, True, [c["blk_b"].t(), sq.t(j)], [sbanks[j].t()])
        for j in range(n):
            self.act(tk4.ap[:, j, :], sbanks[j].ap, AF.Ln, [sbanks[j].t()], [tk4.t(j)], bias=self.epsc.ap)
        for j in range(n):
            self.act(tk4.ap[:, j, :], tk4.ap[:, j, :], AF.Exp, [tk4.t(j)], [tk4.t(j)], scale=-0.5)
        for j in range(n):
            self.stt(dsts[j], banks[j].ap, gaincol, tk4.ap[:, j, :], ALU.mult, ALU.mult, [banks[j].t(), self.qkg.t(), tk4.t(j)], dtoks[j])

    def kv_phase(self, s):
        P, ps = self.P, self.ps
        A = self.arena
        self.KT, _ = A.view(self.R_ACTT, [128, 8, 2048], BF16, "KT")
        off = self.R_TAIL
        self.V, off = A.view(off, [128, 16, 1024], BF16, "V")
        self.attn_off = off
        htile, off = A.view(off, [128, 8, 2048], BF16, "htile_kv")
        sq, off = A.view(off, [128, 8, 512], BF16, "sq_kv")
        tmpf, off = A.view(off, [128, 2, 512], F32, "tmpf_kv")
        tk2, off = A.view(off, [128, 2, 512], F32, "tk2")
        KT, V = self.KT, self.V
        wkv = self.I["w_kv"]
        for tt in range(4):
            self.cur_h_tok = htile.t(tt)
            self.fm_norm(s, 2, tt, htile.ap[:, :, 512 * tt:512 * tt + 512], sq, tmpf, ps[7])
        cnt = [0]
        loaders = [lambda ch=ch: self.wload_k8(wkv, 512 * ch) for ch in range(4)]
        computes = []
        for ch in range(4):
            def cK(sv, ch=ch):
                slot, wv = sv
                for tt in range(4):
                    ts_ = slice(512 * tt, 512 * tt + 512)
                    for hf in range(2):
                        n = cnt[0]
                        cnt[0] += 1
                        banks = [ps[(2 * n) % 4], ps[(2 * n + 1) % 4]]
                        sbanks = [ps[4 + (2 * n) % 4], ps[4 + (2 * n + 1) % 4]]
                        for j in range(2):
                            m4 = 2 * hf + j
                            for k in range(8):
                                self.mm(banks[j].ap, wv[:, k, 128 * m4:128 * m4 + 128], htile.ap[:, k, ts_], k == 0, k == 7,
                                        [slot.t(), htile.t(tt)], [banks[j].t()])
                        prs = [4 * ch + 2 * hf + j for j in range(2)]
                        sqv = Buf(sq.ap[:, 2 * (n % 4):2 * (n % 4) + 2, :], "sqv")
                        sqv.toks = {0: sq.t(2 * (n % 4)), 1: sq.t(2 * (n % 4) + 1)}
                        self.headnorm_batch(banks, sbanks, self.qkg.ap[:, 1:2], [KT.ap[:, p_, ts_] for p_ in prs],
                                            [[KT.t((p_, tt))] for p_ in prs], sqv, tk2)

            def cV(sv, ch=ch):
                slot, wv = sv
                for tb in range(16):
                    n = cnt[0]
                    cnt[0] += 1
                    bank = ps[n % 4]
                    for k in range(8):
                        self.mm(bank.ap, htile.ap[:, k, 128 * tb:128 * tb + 128], wv[:, k, :], k == 0, k == 7,
                                [slot.t(), htile.t(tb // 4)], [bank.t()])
                    self.copy(ACT if n % 2 == 0 else DVE, V.ap[:, tb, 512 * (ch - 2):512 * (ch - 2) + 512], bank.ap,
                              [bank.t()], [V.t(tb)])
            computes.append(cK if ch < 2 else cV)
        self.stream(loaders, computes)

    def attn_phase(self, s):
        P, ps, c = self.P, self.ps, self.c
        A = self.arena
        KT, V, xT, adaT = self.KT, self.V, self.xT, self.adaT
        off = self.attn_off
        qT, off = A.view(off, [128, 8, 512], BF16, "qT")
        oT, off = A.view(off, [128, 8, 512], BF16, "oT")
        o1 = off
        htile, o1 = A.view(o1, [128, 8, 512], BF16, "htile_q")
        sq, o1 = A.view(o1, [128, 8, 512], BF16, "sq_q")
        tmpf, o1 = A.view(o1, [128, 2, 512], F32, "tmpf_q")
        tk4, o1 = A.view(o1, [128, 4, 512], F32, "tk4_q")
        o2 = off
        Eb, o2 = A.view(o2, [128, 2, 2, 512], F32, "Eb")
        Lb, o2 = A.view(o2, [128, 3, 2, 512], BF16, "Lb")
        Wb, o2 = A.view(o2, [128, 3, 2, 512], BF16, "Wb")
        R32, o2 = A.view(o2, [128, 2, 512], F32, "R32")
        Rbf, o2 = A.view(o2, [128, 3, 2, 512], BF16, "Rbf")
        wq = self.I["sb_w_q"][0]
        wo = self.I["sb_w_o"][0]
        identb, negmask, negtri, negones = c["ident_b"], c["negmask_b"], c["negtri_b"], c["negones_b"]
        zeros = c["zeros512_b"]
        zbank = Buf(self.psall[:, 0:1024].rearrange("p (h t) -> p h t", h=2), "zbank")
        zbank.toks[0] = ps[0].t()
        abank = []
        for j in range(2):
            b_ = Buf(self.psall[:, 1024 + 1024 * j:2048 + 1024 * j].rearrange("p (h t) -> p h t", h=2), f"abank{j}")
            abank.append(b_)
        obank = ps[6]

        def ztoks():
            return [ps[0].t(), ps[1].t()]

        def atoks(j):
            return [ps[2 + 2 * j].t(), ps[3 + 2 * j].t()]

        wq_slots = [self.wload_k8(wq, 512 * ch) for ch in range(2)]
        self.cur_h_tok = htile.t()
        self.fm_norm(s, 3, 0, htile.ap, sq, tmpf, ps[7])
        for qt in range(4):
            ts_ = slice(512 * qt, 512 * qt + 512)
            cnt = [0]

            def cQ(sv, ch):
                slot, wv = sv
                banks = [ps[m4] for m4 in range(4)]
                for m4 in range(4):
                    for k in range(8):
                        self.mm(banks[m4].ap, wv[:, k, 128 * m4:128 * m4 + 128], htile.ap[:, k, :], k == 0, k == 7,
                                [slot.t(), htile.t()], [banks[m4].t()])
                self.headnorm_batch(banks, [ps[4 + m4] for m4 in range(4)], self.qkg.ap[:, 0:1],
                                    [qT.ap[:, 4 * ch + m4, :] for m4 in range(4)], [[qT.t(4 * ch + m4)] for m4 in range(4)], sq, tk4)

            for ch in range(2):
                cQ(wq_slots[ch], ch)
            P.barrier()
            wo_slots = [self.wload_k8(wo, 512 * ch) for ch in range(2)]
            tiles = []
            nkb = 4 * qt + 4
            for pair in range(8):
                for ii, kb in enumerate(range(nkb - 1, -1, -1)):
                    r_ = kb - 4 * qt
                    c0 = 128 * r_ if r_ >= 0 else 0
                    tiles.append(dict(pair=pair, kb=kb, first=(ii == 0), last=(kb == 0), diag=(r_ >= 0), c0=c0))
            nt = len(tiles)

            def zmm(bank, btoks, t, close):
                c0 = t["c0"]
                kb = t["kb"]
                rd = [KT.t((t["pair"], kb // 4)), qT.t(t["pair"])]
                for hl in range(2):
                    hs = slice(64 * hl, 64 * hl + 64)
                    self.mm(bank.ap[:, hl, c0:512], KT.ap[hs, t["pair"], 128 * kb:128 * kb + 128], qT.ap[hs, t["pair"], c0:512], True,
                            close and not t["diag"], rd, btoks, sig=(close and not t["diag"] and hl == 1))
                if t["diag"]:
                    for hl in range(2):
                        self.mm(bank.ap[:, hl, c0:c0 + 128], identb.ap, negmask.ap, False, close, [identb.t(), negmask.t()], btoks,
                                sig=(close and hl == 1))

            def stageA1(i):
                t = tiles[i]
                c0 = t["c0"]
                zmm(zbank, ztoks(), t, True)
                self.act(Eb.ap[:, i % 2, :, c0:512], zbank.ap[:, :, c0:512], AF.Exp, ztoks(), [Eb.t(i % 2)])

            def stageA2(i):
                t = tiles[i]
                c0 = t["c0"]
                self.act(Lb.ap[:, i % 3, :, c0:512], Eb.ap[:, i % 2, :, c0:512], AF.Ln, [Eb.t(i % 2)], [Lb.t(i % 3)], bias=self.onec.ap)
                if not t["last"]:
                    if t["first"]:
                        P.op(POOL, lambda e: e.memset(R32.ap, 0.0), [], [R32.t()])
                    self.tt(POOL, R32.ap[:, :, c0:512], R32.ap[:, :, c0:512], Lb.ap[:, i % 3, :, c0:512], ALU.add,
                            [R32.t(), Lb.t(i % 3)], [R32.t()])
                    self.copy(DVE, Rbf.ap[:, i % 3], R32.ap, [R32.t()], [Rbf.t(i % 3)])

            def stageB(i):
                t = tiles[i]
                ab = abank[i % 2]
                at = atoks(i % 2)
                c0 = t["c0"]
                zmm(ab, at, t, False)
                for hl in range(2):
                    self.mm(ab.ap[:, hl, c0:512], negtri.ap, Lb.ap[:, i % 3, hl, c0:512], False, t["first"], [negtri.t(), Lb.t(i % 3)], at,
                            sig=(t["first"] and hl == 1))
                if not t["first"]:
                    for hl in range(2):
                        self.mm(ab.ap[:, hl, c0:512], negones.ap, Rbf.ap[:, (i - 1) % 3, hl, c0:512], False, True,
                                [negones.t(), Rbf.t((i - 1) % 3)], at, sig=(hl == 1))
                self.act(Wb.ap[:, i % 3, :, c0:512], ab.ap[:, :, c0:512], AF.Exp, at, [Wb.t(i % 3)])

            def stageC(i):
                t = tiles[i]
                c0 = t["c0"]
                if t["first"]:
                    self.mm(obank.ap, zeros.ap[:, 0:128], zeros.ap, True, False, [zeros.t()], [obank.t()])
                for hl in range(2):
                    h = 2 * t["pair"] + hl
                    hs = slice(64 * hl, 64 * hl + 64)
                    self.mm(obank.ap[hs, c0:512], V.ap[:, t["kb"], 64 * h:64 * h + 64], Wb.ap[:, i % 3, hl, c0:512], False, t["last"],
                            [V.t(t["kb"]), Wb.t(i % 3)], [obank.t()], sig=(t["last"] and hl == 1))
                if t["last"]:
                    self.copy(DVE, oT.ap[:, t["pair"], :], obank.ap, [obank.t()], [oT.t(t["pair"])])

            for step in range(nt + 3):
                if step < nt:
                    stageA1(step)
                if 0 <= step - 2 < nt:
                    stageB(step - 2)
                if step < nt:
                    stageA2(step)
                if 0 <= step - 3 < nt:
                    stageC(step - 3)
            P.barrier()
            allo = [oT.t(p_) for p_ in range(8)]

            def cO(sv, ch):
                slot, wv = sv
                for m4 in range(4):
                    ft = 4 * ch + m4
                    bank = ps[4 + ft % 2]
                    for k in range(8):
                        self.mm(bank.ap, wv[:, k, 128 * m4:128 * m4 + 128], oT.ap[:, k, :], k == 0, k == 7, [slot.t()] + allo, [bank.t()])
                    self.stt(xT.ap[:, ft, ts_], bank.ap, adaT.ap[:, 48 + 16 + ft, s:s + 1], xT.ap[:, ft, ts_], ALU.mult, ALU.add,
                             [bank.t(), adaT.t(), xT.t(ft)], [xT.t(ft)])

            if qt < 3:
                self.cur_h_tok = htile.t()
                self.fm_norm(s, 3, qt + 1, htile.ap, sq, tmpf, ps[7])
            cO(wo_slots[0], 0)
            if qt < 3:
                wq0 = self.wload_k8(wq, 0)
            cO(wo_slots[1], 1)
            if qt < 3:
                wq_slots = [wq0, self.wload_k8(wq, 512)]

    def output_phase(self, s):
        P, ps, c = self.P, self.ps, self.c
        A = self.arena
        ost, _ = A.view(self.R_TAIL, [128, 2, 1024], F32, "ostage")
        xT = self.xT
        allx = [xT.t(ft) for ft in range(8)]
        n = 0
        for tb in range(16):
            ob = tb % 2
            for hf in range(2):
                bank = ps[n % 4]
                for jj in range(4):
                    ft = 4 * hf + jj
                    self.tr(bank.ap[:, 128 * jj:128 * jj + 128], xT.ap[:, ft, 128 * tb:128 * tb + 128], c["ident_f"].ap,
                            allx + [c["ident_f"].t()], [bank.t()], sig=(jj == 3))
                self.copy(ACT if n % 2 == 0 else DVE, ost.ap[:, ob, 512 * hf:512 * hf + 512], bank.ap, [bank.t()], [ost.t(ob)])
                n += 1
            otok = Tok(f"out{s}_{tb}")
            P.dma(SP, self.out[s, 128 * tb:128 * tb + 128, :], ost.ap[:, ob, :], reads=[ost.t(ob)], writes=[otok], tok=ost.t(ob))


def build_program(debug=None, stop_after=None, nseq=NS):
    b = Builder(debug=debug, stop_after=stop_after, nseq=nseq)
    nc = b.build()
    if b.P.nfwd:
        print("forward-redirected PE deps:", b.P.nfwd)
    return nc, b


_PARAM_NAMES = ["ada_w", "ada_b", "mix_norm_g", "mlp_norm_g", "mlp_w1", "mlp_w2", "s5_a_re", "s5_a_im", "s5_log_dt",
                "s5_b_re", "s5_b_im", "s5_c_re", "s5_c_im", "s5_d", "s5_w_glu", "kv_ada_w", "kv_ada_b", "kv_norm_g",
                "w_kv", "k_norm_g", "sb_w_q", "q_norm_g", "sb_w_o"]


def kernel(**inputs):
    x = np.ascontiguousarray(np.asarray(inputs["x"], dtype=np.float32))
    c = np.ascontiguousarray(np.asarray(inputs["c"], dtype=np.float32))
    params = {k: np.ascontiguousarray(np.asarray(inputs[k], dtype=np.float32)) for k in _PARAM_NAMES}
    nc, _ = build_program()
    in_maps = []
    for i in range(NCORES):
        m = dict(params)
        m["x"] = np.ascontiguousarray(x[NS * i:NS * i + NS])
        m["c"] = np.ascontiguousarray(c[NS * i:NS * i + NS])
        in_maps.append(m)
    res = run_bass_kernel_spmd(nc, in_maps, core_ids=list(range(NCORES)))
    out = np.concatenate([np.asarray(r["out"]) for r in res.results], axis=0)
    return out.astype(np.float32, copy=False)
```

```python
import math
from contextlib import ExitStack

import numpy as np
import concourse.bass as bass
import concourse.mybir as mybir
from concourse.bass_utils import run_bass_kernel_spmd

F32 = mybir.dt.float32
BF16 = mybir.dt.bfloat16
U8 = mybir.dt.uint8
AF = mybir.ActivationFunctionType
ALU = mybir.AluOpType
PE, DVE, ACT, POOL, SP = "tensor", "vector", "scalar", "gpsimd", "sync"
ENGS = [PE, DVE, ACT, POOL, SP]

D = 1024
T = 2048
NS = 2
FT = 8
TT = 512
NTT = T // TT
DFF = 4096
EPS = 1e-6
NCORES = 8
EPOCH_MAX = 12000


class Tok:
    __slots__ = ("w", "r", "sem", "semcnt", "name")

    def __init__(self, name=""):
        self.w = None
        self.r = []
        self.sem = None
        self.semcnt = 0
        self.name = name


class Op:
    __slots__ = ("eng", "fn", "deps", "dma", "sig", "semref", "semval", "inc", "id", "sigok")


class Prog:
    def __init__(self, nc, stack):
        self.nc = nc
        self.stack = stack
        self.ops = []
        self.last = {e: None for e in ENGS}
        self.barrier_deps = {e: [] for e in ENGS}
        self.dmas = []
        self.nsem = 0

    def new_sem(self, name):
        self.nsem += 1
        return self.stack.enter_context(self.nc.semaphore(f"{name}_{self.nsem}"))

    def op(self, eng, fn, reads=(), writes=(), dma=None, sigok=True):
        o = Op()
        o.sigok = sigok
        o.eng = eng
        o.fn = fn
        o.dma = dma
        o.sig = False
        o.semref = None
        o.semval = 0
        o.inc = 0
        o.id = len(self.ops)
        deps = {}

        def add(d, war=False):
            if d is None:
                return
            if d.dma is None and d.eng == eng:
                if eng == PE:
                    return
            deps[d.id] = d

        for t in reads:
            add(t.w)
        for t in writes:
            add(t.w)
            for r in t.r:
                add(r, war=True)
        for d in self.barrier_deps[eng]:
            add(d)
        self.barrier_deps[eng] = []
        o.deps = list(deps.values())
        for t in reads:
            t.r.append(o)
        for t in writes:
            t.w = o
            t.r = []
        self.ops.append(o)
        self.last[eng] = o
        if dma is not None:
            self.dmas.append(o)
        return o

    def dma(self, eng, out, in_, reads=(), writes=(), tok=None, **kw):
        if tok is None:
            tok = writes[0]

        def fn(e):
            return e.dma_start(out=out, in_=in_, **kw)
        return self.op(eng, fn, reads=reads, writes=writes, dma=tok)

    def barrier(self):
        deps = [o for o in self.last.values() if o is not None] + list(self.dmas)
        self.dmas = []
        for e in ENGS:
            self.barrier_deps[e] = list(self.barrier_deps[e]) + deps

    def emit(self):
        nc = self.nc
        pe_ops = [o for o in self.ops if o.eng == PE and o.dma is None]
        if pe_ops:
            pe_ops[-1].sigok = True
        nxt = {}
        cur = None
        for o in reversed(pe_ops):
            if o.sigok:
                cur = o
            nxt[o.id] = cur
        nfwd = 0
        for o in self.ops:
            nd = {}
            for d in o.deps:
                if d.eng == PE and d.dma is None and not d.sigok:
                    d = nxt[d.id]
                    if d.id > o.id:
                        nfwd += 1
                nd[d.id] = d
            o.deps = list(nd.values())
        self.nfwd = nfwd
        for o in self.ops:
            for d in o.deps:
                d.sig = True
        cnt = {e: 0 for e in ENGS}
        cursem = {e: None for e in ENGS}
        for o in self.ops:
            if o.dma is not None:
                t = o.dma
                if t.sem is None or t.semcnt >= 16 * 3000:
                    t.sem = self.new_sem("d")
                    t.semcnt = 0
                t.semcnt += 16
                o.semref = t.sem
                o.semval = t.semcnt
                o.inc = 16
            elif o.sig:
                e = o.eng
                if cursem[e] is None or cnt[e] >= EPOCH_MAX:
                    cursem[e] = self.new_sem("e" + e[:2])
                    cnt[e] = 0
                cnt[e] += 1
                o.semref = cursem[e]
                o.semval = cnt[e]
                o.inc = 1
        with nc.Block() as block:
            for e in ENGS:
                oplist = [o for o in self.ops if o.eng == e]

                def body(eh, oplist=oplist):
                    waited = {}
                    for o in oplist:
                        need = {}
                        for d in o.deps:
                            k = id(d.semref)
                            if k not in need or need[k][1] < d.semval:
                                need[k] = (d.semref, d.semval)
                        for k, (s, v) in need.items():
                            if waited.get(k, 0) < v:
                                eh.wait_ge(s, v)
                                waited[k] = v
                        if o.fn is not None:
                            inst = o.fn(eh)
                            if o.semref is not None:
                                inst.then_inc(o.semref, o.inc)

                getattr(block, e)(body)


class Buf:
    def __init__(self, ap, name=""):
        self.ap = ap
        self.name = name
        self.toks = {}

    def t(self, key=0):
        if key not in self.toks:
            self.toks[key] = Tok(f"{self.name}{key}")
        return self.toks[key]

    def ts(self, keys):
        return [self.t(k) for k in keys]


class Arena:
    def __init__(self, ap_u8):
        self.ap = ap_u8
        self.size = ap_u8.shape[1]

    def view(self, off, shape, dt, name=""):
        esz = 4 if dt == F32 else 2
        n = 1
        for s in shape[1:]:
            n *= s
        nbytes = n * esz
        assert off % 4 == 0 and off + nbytes <= self.size, (name, off, nbytes, self.size)
        v = self.ap[0:shape[0], off:off + nbytes].bitcast(dt)
        if len(shape) == 3:
            v = v.rearrange("p (a b) -> p a b", b=shape[2])
        elif len(shape) == 4:
            v = v.rearrange("p (a b c) -> p a b c", b=shape[2], c=shape[3])
        elif len(shape) == 5:
            v = v.rearrange("p (a b c d) -> p a b c d", b=shape[2], c=shape[3], d=shape[4])
        return Buf(v, name), off + nbytes


class Builder:
    def __init__(self, debug=None, stop_after=None, nseq=NS):
        self.debug = debug or []
        self.stop_after = stop_after
        self.nseq = nseq
        self.stack = ExitStack()
        self.nc = bass.Bass("TRN2", target_bir_lowering=False)
        self.P = Prog(self.nc, self.stack)
        self.dbg_out = {}

    def dram_in(self, name, shape):
        return self.nc.dram_tensor(name, list(shape), F32, kind="ExternalInput").ap()

    def sb(self, name, shape, dt):
        return self.stack.enter_context(self.nc.sbuf_tensor(name, list(shape), dt))

    def act(self, out, in_, func, reads, writes, bias=None, scale=None, accum_out=None):
        kw = {}
        if bias is not None:
            kw["bias"] = bias
        if scale is not None:
            kw["scale"] = scale
        if accum_out is not None:
            kw["accum_out"] = accum_out
        return self.P.op(ACT, lambda e: e.activation(out=out, in_=in_, func=func, **kw), reads, writes)

    def tt(self, eng, out, in0, in1, op, reads, writes):
        return self.P.op(eng, lambda e: e.tensor_tensor(out, in0, in1, op), reads, writes)

    def stt(self, out, in0, scalar, in1, op0, op1, reads, writes):
        return self.P.op(DVE, lambda e: e.scalar_tensor_tensor(out, in0, scalar, in1, op0, op1), reads, writes)

    def ts(self, eng, out, in0, s1, s2, op0, op1, reads, writes):
        if op1 is None:
            return self.P.op(eng, lambda e: e.tensor_scalar(out, in0, s1, None, op0), reads, writes)
        return self.P.op(eng, lambda e: e.tensor_scalar(out, in0, s1, s2, op0, op1), reads, writes)

    def copy(self, eng, out, in_, reads, writes):
        if eng == ACT:
            return self.P.op(ACT, lambda e: e.activation(out=out, in_=in_, func=AF.Identity), reads, writes)
        return self.P.op(eng, lambda e: e.tensor_copy(out, in_), reads, writes)

    def mm(self, out, lhsT, rhs, start, stop, reads, writes, sig=None):
        return self.P.op(PE, lambda e: e.matmul(out, lhsT=lhsT, rhs=rhs, start=start, stop=stop), reads, writes,
                         sigok=(stop if sig is None else sig))

    def tr(self, out, in_, ident, reads, writes, sig=True):
        return self.P.op(PE, lambda e: e.transpose(out, in_, ident), reads, writes, sigok=sig)

    def dbg(self, name, buf_ap, shape, reads, dt=F32):
        if name not in self.debug:
            return
        o = self.nc.dram_tensor("dbg_" + name, list(shape), dt, kind="ExternalOutput").ap()
        tok = Tok("dbg" + name)
        self.P.dma(SP, o, buf_ap, reads=reads, writes=[tok], tok=tok)
        self.dbg_out[name] = tok

    def build(self):
        nc, P = self.nc, self.P
        I = {}
        I["x"] = self.dram_in("x", [NS, T, D])
        I["c"] = self.dram_in("c", [NS, D])
        I["ada_w"] = self.dram_in("ada_w", [2, D, 6 * D])
        I["ada_b"] = self.dram_in("ada_b", [2, 6 * D])
        I["mix_norm_g"] = self.dram_in("mix_norm_g", [2, D])
        I["mlp_norm_g"] = self.dram_in("mlp_norm_g", [2, D])
        I["mlp_w1"] = self.dram_in("mlp_w1", [2, D, DFF])
        I["mlp_w2"] = self.dram_in("mlp_w2", [2, DFF, D])
        I["s5_a_re"] = self.dram_in("s5_a_re", [1, 64, 64])
        I["s5_a_im"] = self.dram_in("s5_a_im", [1, 64, 64])
        I["s5_log_dt"] = self.dram_in("s5_log_dt", [1, 64])
        I["s5_b_re"] = self.dram_in("s5_b_re", [1, 64, 64, 16])
        I["s5_b_im"] = self.dram_in("s5_b_im", [1, 64, 64, 16])
        I["s5_c_re"] = self.dram_in("s5_c_re", [1, 64, 16, 64])
        I["s5_c_im"] = self.dram_in("s5_c_im", [1, 64, 16, 64])
        I["s5_d"] = self.dram_in("s5_d", [1, D])
        I["s5_w_glu"] = self.dram_in("s5_w_glu", [1, D, 2 * D])
        I["kv_ada_w"] = self.dram_in("kv_ada_w", [D, 2 * D])
        I["kv_ada_b"] = self.dram_in("kv_ada_b", [2 * D])
        I["kv_norm_g"] = self.dram_in("kv_norm_g", [D])
        I["w_kv"] = self.dram_in("w_kv", [D, 2 * D])
        I["k_norm_g"] = self.dram_in("k_norm_g", [64])
        I["sb_w_q"] = self.dram_in("sb_w_q", [1, D, D])
        I["q_norm_g"] = self.dram_in("q_norm_g", [1, 64])
        I["sb_w_o"] = self.dram_in("sb_w_o", [1, D, D])
        self.I = I
        self.out = nc.dram_tensor("out", [NS, T, D], F32, kind="ExternalOutput").ap()
        self.s5w = nc.dram_tensor("s5w_scr", [32, 128, 768], BF16, kind="Internal").ap()
        self.s5w_tok = [Tok(f"s5w{i}") for i in range(8)]
        self.s5tab = nc.dram_tensor("s5tab_scr", [32, 128, 2, 256], F32, kind="Internal").ap()
        self.s5tab_tok = Tok("s5tab")
        self.modrow = nc.dram_tensor("modrow_scr", [NS, 2, D], F32, kind="Internal").ap()
        self.modrow_tok = Tok("modrow")

        self.consts()
        ARENA = 194 * 1024
        self.arena = Arena(self.sb("arena", [128, ARENA], U8)[:])
        self.psall = self.stack.enter_context(nc.psum_tensor("psall", [128, 4096], F32))
        self.ps = [Buf(self.psall[:, 512 * i:512 * i + 512], f"ps{i}") for i in range(8)]

        self.setup_phase()
        if self.stop_after is not None and (self.stop_after == "setup" or self.stop_after.startswith("s5") or self.stop_after == "ada"):
            return self.finish()
        for s in range(self.nseq):
            self.seq_pipeline(s)
        return self.finish()

    def finish(self):
        P = self.P
        outs = [o for o in P.ops if o.dma is not None]
        fin = P.op(SP, None)
        fin.deps = list({o.id: o for o in outs}.values())
        P.emit()
        return self.nc

    def consts(self):
        P = self.P
        c = {}

        def mk(name, shape, dt):
            c[name] = Buf(self.sb("c_" + name, shape, dt)[:], name)
            return c[name]

        ident_f = mk("ident_f", [128, 128], F32)
        ident_b = mk("ident_b", [128, 128], BF16)
        onesm_b = mk("onesm_b", [128, 128], BF16)
        blk_b = mk("blk_b", [128, 128], BF16)
        negtri_b = mk("negtri_b", [128, 128], BF16)
        negones_b = mk("negones_b", [128, 128], BF16)
        negmask_b = mk("negmask_b", [128, 128], BF16)
        zeros_b = mk("zeros_b", [128, 64], BF16)
        onesrow_f = mk("onesrow_f", [1, 128], F32)
        sel2 = mk("sel2", [2, 128], F32)
        bmask_f = mk("bmask_f", [128, 128], F32)
        tmp_f = mk("ctmp_f", [128, 128], F32)
        hm = mk("hm", [128, 2], F32)
        zeros512_b = mk("zeros512_b", [128, 512], BF16)

        def pool(fn, reads, writes):
            return P.op(POOL, fn, reads, writes)

        pool(lambda e: e.memset(ident_f.ap, 1.0), [], [ident_f.t()])
        pool(lambda e: e.affine_select(out=ident_f.ap, in_=ident_f.ap, pattern=[[-1, 128]], compare_op=ALU.is_equal,
                                       fill=0.0, base=0, channel_multiplier=1), [ident_f.t()], [ident_f.t()])
        pool(lambda e: e.tensor_copy(ident_b.ap, ident_f.ap), [ident_f.t()], [ident_b.t()])
        pool(lambda e: e.memset(onesm_b.ap, 1.0 / 1024.0), [], [onesm_b.t()])
        pool(lambda e: e.memset(blk_b.ap, 0.0), [], [blk_b.t()])
        pool(lambda e: e.memset(blk_b.ap[0:64, 0:64], 1.0 / 64.0), [blk_b.t()], [blk_b.t()])
        pool(lambda e: e.memset(blk_b.ap[64:128, 64:128], 1.0 / 64.0), [blk_b.t()], [blk_b.t()])
        pool(lambda e: e.memset(tmp_f.ap, -1.0), [], [tmp_f.t()])
        pool(lambda e: e.affine_select(out=tmp_f.ap, in_=tmp_f.ap, pattern=[[-1, 128]], compare_op=ALU.is_ge,
                                       fill=0.0, base=0, channel_multiplier=1), [tmp_f.t()], [tmp_f.t()])
        pool(lambda e: e.tensor_copy(negtri_b.ap, tmp_f.ap), [tmp_f.t()], [negtri_b.t()])
        pool(lambda e: e.memset(negones_b.ap, -1.0), [], [negones_b.t()])
        pool(lambda e: e.tensor_scalar(negmask_b.ap, negtri_b.ap, 30000.0, None, ALU.mult), [negtri_b.t()], [negmask_b.t()])
        pool(lambda e: e.memset(zeros_b.ap, 0.0), [], [zeros_b.t()])
        pool(lambda e: e.memset(onesrow_f.ap, 1.0), [], [onesrow_f.t()])
        pool(lambda e: e.memset(sel2.ap, 1.0), [], [sel2.t()])
        pool(lambda e: e.affine_select(out=sel2.ap, in_=sel2.ap, pattern=[[1, 128]], compare_op=ALU.is_ge,
                                       fill=0.0, base=0, channel_multiplier=-64), [sel2.t()], [sel2.t()])
        pool(lambda e: e.affine_select(out=sel2.ap, in_=sel2.ap, pattern=[[-1, 128]], compare_op=ALU.is_ge,
                                       fill=0.0, base=63, channel_multiplier=64), [sel2.t()], [sel2.t()])
        pool(lambda e: e.memset(bmask_f.ap, 1.0), [], [bmask_f.t()])
        pool(lambda e: e.affine_select(out=bmask_f.ap.rearrange("p (i h) -> p i h", h=16), in_=bmask_f.ap.rearrange("p (i h) -> p i h", h=16),
                                       pattern=[[16, 8], [0, 16]], compare_op=ALU.is_ge,
                                       fill=0.0, base=15, channel_multiplier=-1), [bmask_f.t()], [bmask_f.t()])
        pool(lambda e: e.memset(zeros512_b.ap, 0.0), [], [zeros512_b.t()])
        pool(lambda e: e.memset(hm.ap, 0.0), [], [hm.t()])
        pool(lambda e: e.memset(hm.ap[0:64, 0:1], 1.0), [hm.t()], [hm.t()])
        pool(lambda e: e.memset(hm.ap[64:128, 1:2], 1.0), [hm.t()], [hm.t()])
        self.c = c
        self.VT = Buf(self.sb("VT", [128, 160], F32)[:], "VT")
        self.adaT = Buf(self.sb("adaT", [128, 112, 2], F32)[:], "adaT")
        self.coef = Buf(self.sb("coef", [128, 32, 8, 3], F32)[:], "coef")
        self.modsc = Buf(self.sb("modsc", [128, NS, 5, 2, 8], F32)[:], "modsc")
        self.rho = Buf(self.sb("rho", [128, 32], F32)[:], "rho")
        self.qkg = Buf(self.sb("qkg", [128, 2], F32)[:], "qkg")
        self.epsc = Buf(self.sb("epsc", [128, 1], F32)[:], "epsc")
        self.onec = Buf(self.sb("onec", [128, 1], F32)[:], "onec")
        pool(lambda e: e.memset(self.epsc.ap, EPS), [], [self.epsc.t()])
        pool(lambda e: e.memset(self.onec.ap, 1.0), [], [self.onec.t()])

    def ring_init(self, off, nslots=2):
        self.ring = []
        for i in range(nslots):
            b, off = self.arena.view(off, [128, 4096], BF16, f"ring{i}")
            self.ring.append(b)
        self.ring_i = 0
        return off

    def wload(self, srcs):
        slot = self.ring[self.ring_i % len(self.ring)]
        self.ring_i += 1
        for dstf, src in srcs:
            self.P.dma(POOL, dstf(slot.ap), src, reads=[], writes=[slot.t()], tok=slot.t())
        return slot

    def wload_k8(self, w2d, col0, ncols=512):
        src = w2d.rearrange("(k p) n -> p k n", p=128)[:, :, col0:col0 + ncols]
        slot = self.wload([(lambda a: a[:, 0:8 * ncols].rearrange("p (k n) -> p k n", n=ncols), src)])
        return slot, slot.ap[:, 0:8 * ncols].rearrange("p (k n) -> p k n", n=ncols)

    def setup_phase(self):
        P, I, c = self.P, self.I, self.c
        A = self.arena
        ps = self.ps
        off = 0
        off = self.ring_init(off, 2)
        self.ring_end = off
        off = A.size - 42 * 1024
        self.s5_limit = off
        vrA, off = A.view(off, [128, 128], F32, "vrA")
        vrB, off = A.view(off, [128, 128], F32, "vrB")
        cs, off = A.view(off, [2, 1024], F32, "cs")
        sT, off = A.view(off, [128, 8, 2], BF16, "sT")
        rowb = []
        for b in range(2):
            r, off = A.view(off, [1, 2048], F32, f"rowb{b}")
            rowb.append(r)
        biasrow, off = A.view(off, [1, 2048], F32, "biasrow")
        grow, off = A.view(off, [1, 1024], F32, "grow")
        abrow, off = A.view(off, [1, 2, 1024], F32, "abrow")
        ld = Tok("setup_ld")

        def ldma(out, in_, wtoks):
            P.dma(SP, out, in_, reads=[], writes=wtoks)

        ldma(vrA.ap[0:16, :], I["mix_norm_g"].rearrange("l (k p) -> (l k) p", p=128), [vrA.t()])
        ldma(vrA.ap[16:32, :], I["mlp_norm_g"].rearrange("l (k p) -> (l k) p", p=128), [vrA.t()])
        ldma(vrA.ap[32:40, :], I["kv_norm_g"].rearrange("(k p) -> k p", p=128), [vrA.t()])
        adab = I["ada_b"].rearrange("l (k p) -> (l k) p", p=128)
        ldma(vrA.ap[40:128, :], adab[0:88, :], [vrA.t()])
        ldma(vrB.ap[0:8, :], adab[88:96, :], [vrB.t()])
        ldma(vrB.ap[8:24, :], I["kv_ada_b"].rearrange("(k p) -> k p", p=128), [vrB.t()])
        for hh in range(2):
            ldma(vrB.ap[24:25, 64 * hh:64 * hh + 64], I["q_norm_g"], [vrB.t()])
            ldma(vrB.ap[25:26, 64 * hh:64 * hh + 64], I["k_norm_g"].rearrange("(o d) -> o d", o=1), [vrB.t()])
        ldma(cs.ap, I["c"], [cs.t()])
        ldma(biasrow.ap, I["ada_b"][0:1, 0:2048], [biasrow.t()])
        ldma(grow.ap, I["mix_norm_g"][0:1, :], [grow.t()])
        VT = self.VT
        self.tr(ps[0].ap[:, 0:128], vrA.ap, c["ident_f"].ap, [vrA.t(), c["ident_f"].t()], [ps[0].t()])
        self.tr(ps[0].ap[:, 128:154], vrB.ap[0:26, :], c["ident_f"].ap[0:26, 0:26], [vrB.t(), c["ident_f"].t()], [ps[0].t()])
        self.copy(DVE, VT.ap[:, 0:154], ps[0].ap[:, 0:154], [ps[0].t()], [VT.t()])
        self.ts(DVE, self.qkg.ap[:, 0:1], VT.ap[:, 152:153], 0.125, None, ALU.mult, None, [VT.t()], [self.qkg.t()])
        self.copy(DVE, self.qkg.ap[:, 1:2], VT.ap[:, 153:154], [VT.t()], [self.qkg.t()])
        self.act(cs.ap, cs.ap, AF.Silu, [cs.t()], [cs.t()])
        for k in range(8):
            self.tr(ps[1].ap[:, 2 * k:2 * k + 2], cs.ap[0:2, 128 * k:128 * k + 128], c["ident_f"].ap[0:2, 0:2],
                    [cs.t(), c["ident_f"].t()], [ps[1].t()])
        self.copy(DVE, sT.ap, ps[1].ap[:, 0:16].rearrange("p (k b) -> p k b", b=2), [ps[1].t()], [sT.t()])
        s5_rest = self.s5_setup_early()
        adaps = ps[5]
        chunks = [(I["ada_w"][0], j * 512) for j in range(12)] + [(I["ada_w"][1], j * 512) for j in range(12)] + \
                 [(I["kv_ada_w"], j * 512) for j in range(4)]
        for ci, (w2d, col0) in enumerate(chunks):
            slot, wv = self.wload_k8(w2d, col0)
            for mt in range(4):
                ot = ci * 4 + mt
                for k in range(8):
                    self.mm(adaps.ap[:, 2 * ot:2 * ot + 2], wv[:, k, 128 * mt:128 * mt + 128], sT.ap[:, k, :],
                            k == 0, k == 7, [slot.t(), sT.t()], [adaps.t()])
            if ci < 4:
                for b in range(2):
                    rp = ps[6 + b]
                    for k in range(8):
                        self.mm(rp.ap[0:1, :], sT.ap[:, k, b:b + 1], wv[:, k, :], k == 0, k == 7, [slot.t(), sT.t()], [rp.t()])
                    self.copy(ACT, rowb[b].ap[0:1, 512 * ci:512 * ci + 512], rp.ap[0:1, :], [rp.t()], [rowb[b].t()])
        self.tt(DVE, self.adaT.ap, adaps.ap[:, 0:224].rearrange("p (o b) -> p o b", b=2),
                VT.ap[:, 40:152].unsqueeze(2).to_broadcast([128, 112, 2]),
                ALU.add, [adaps.t(), VT.t()], [self.adaT.t()])
        adaT = self.adaT
        norms = {1: (16, 24, 32), 2: (32, 96, 104), 3: (8, 48, 56), 4: (24, 72, 80)}
        for s in range(2):
            for n, (gc, sh, sc) in norms.items():
                self.stt(self.modsc.ap[:, s, n, 0, :], adaT.ap[:, sc:sc + 8, s], 1.0, VT.ap[:, gc:gc + 8], ALU.add, ALU.mult,
                         [adaT.t(), VT.t()], [self.modsc.t()])
                self.copy(DVE, self.modsc.ap[:, s, n, 1, :], adaT.ap[:, sh:sh + 8, s], [adaT.t()], [self.modsc.t()])
        for b in range(2):
            self.tt(DVE, rowb[b].ap, rowb[b].ap, biasrow.ap, ALU.add, [rowb[b].t(), biasrow.t()], [rowb[b].t()])
            self.stt(abrow.ap[0:1, 0, :], rowb[b].ap[0:1, 1024:2048], 1.0, grow.ap, ALU.add, ALU.mult,
                     [rowb[b].t(), grow.t()], [abrow.t()])
            self.copy(DVE, abrow.ap[0:1, 1, :], rowb[b].ap[0:1, 0:1024], [rowb[b].t()], [abrow.t()])
            P.dma(SP, self.modrow[b:b + 1], abrow.ap, reads=[abrow.t()], writes=[self.modrow_tok], tok=self.modrow_tok)
        self.dbg("adaT", self.adaT.ap, [128, 112, 2], [self.adaT.t()])
        self.dbg("modsc", self.modsc.ap, [128, NS, 5, 2, 8], [self.modsc.t()])
        self.dbg("VT", self.VT.ap, [128, 160], [self.VT.t()])
        if self.stop_after == "ada":
            return
        s5_rest()
        P.barrier()

    def s5_setup_early(self):
        P, I, c = self.P, self.I, self.c
        A = self.arena
        ps = self.ps
        off = self.ring_end
        lamrows, off = A.view(off, [32, 2, 128], F32, "lamrows")
        ldt2, off = A.view(off, [2, 32], F32, "ldt2")
        NSM = 24
        smb, off = A.view(off, [128, NSM, 32], F32, "sm")
        Bq, off = A.view(off, [128, 2, 32, 16], F32, "Bq")
        Bb, off = A.view(off, [128, 2, 32, 16], F32, "Bb")
        Cin, off = A.view(off, [128, 2, 128], F32, "Cin")
        CT, off = A.view(off, [128, 2, 32, 16], F32, "CT")
        X, off = A.view(off, [128, 2, 32, 8, 16], F32, "X")
        Wo, off = A.view(off, [128, 2, 32, 8, 16], F32, "Wo")
        We, off = A.view(off, [128, 2, 32, 128], F32, "We")
        tmp, off = A.view(off, [128, 2, 512], F32, "s5tmp")
        Drows, off = A.view(off, [64, 16], F32, "Drows")
        Drep, off = A.view(off, [64, 8, 16], F32, "Drep")
        dcol, off = A.view(off, [128, 64], F32, "dcol")
        tmpm, off = A.view(off, [128, 2, 2, 128], F32, "tmpm")
        stage, off = A.view(off, [128, 2, 4, 768], BF16, "stage")
        WoM, off = A.view(off, [128, 2, 2, 2, 128], F32, "WoM")
        assert off <= self.s5_limit, (off, self.s5_limit)
        identf = c["ident_f"]

        names = ["lre", "lim", "dt", "mag", "th", "c", "s", "t1", "t2", "t3", "abre", "abim", "nr", "den",
                 "fre", "fim", "rm2", "lire", "liim", "mure", "muim", "w2r", "w2i"]
        sm = {n: (smb.ap[:, i, :], smb.t(n)) for i, n in enumerate(names)}

        def S(n):
            return sm[n][0]

        def St(n):
            return sm[n][1]

        P.dma(SP, lamrows.ap[:, 0, :], I["s5_a_re"][0].rearrange("(a g) p -> a (g p)", g=2), writes=[lamrows.t()])
        P.dma(SP, lamrows.ap[:, 1, :], I["s5_a_im"][0].rearrange("(a g) p -> a (g p)", g=2), writes=[lamrows.t()])
        P.dma(SP, ldt2.ap, I["s5_log_dt"][0].rearrange("(a g) -> g a", g=2), writes=[ldt2.t()], allow_slow_non_contiguous=True)
        for ri, nm in enumerate(["s5_b_re", "s5_b_im"]):
            src = I[nm][0].rearrange("(a g) p h -> (g p) a h", g=2)
            for q4 in range(4):
                P.dma(SP, Bq.ap[:, ri, 8 * q4:8 * q4 + 8, :], src[:, 8 * q4:8 * q4 + 8, :], writes=[Bq.t()])
        P.dma(SP, Drows.ap, I["s5_d"][0].rearrange("(g h) -> g h", h=16), writes=[Drows.t()])

        self.tr(ps[0].ap[:, 0:32], lamrows.ap[:, 0, :], identf.ap[0:32, 0:32], [lamrows.t(), identf.t()], [ps[0].t()])
        self.tr(ps[0].ap[:, 32:64], lamrows.ap[:, 1, :], identf.ap[0:32, 0:32], [lamrows.t(), identf.t()], [ps[0].t()])
        self.mm(ps[0].ap[:, 64:96], c["sel2"].ap, ldt2.ap, True, True, [c["sel2"].t(), ldt2.t()], [ps[0].t()])
        self.copy(DVE, S("lre"), ps[0].ap[:, 0:32], [ps[0].t()], [St("lre")])
        self.copy(DVE, S("lim"), ps[0].ap[:, 32:64], [ps[0].t()], [St("lim")])
        self.act(S("dt"), ps[0].ap[:, 64:96], AF.Exp, [ps[0].t()], [St("dt")])

        def tt(out, a, b, op, eng=DVE):
            self.tt(eng, S(out), S(a), S(b), op, [St(a), St(b)], [St(out)])

        if self.stop_after == "s5a":
            return

        def horner(out, y, coefs, last):
            self.ts(DVE, S(out), S(y), coefs[0], None, ALU.mult, None, [St(y)], [St(out)])
            for ck in coefs[1:]:
                self.stt(S(out), S(out), ck, S(y), ALU.add, ALU.mult, [St(out), St(y)], [St(out)])
            self.ts(DVE, S(out), S(out), last, None, ALU.add, None, [St(out)], [St(out)])

        tt("t1", "lre", "dt", ALU.mult)
        self.ts(DVE, S("t2"), S("t1"), 0.25, None, ALU.mult, None, [St("t1")], [St("t2")])
        horner("mag", "t2", [1.0 / 5040, 1.0 / 720, 1.0 / 120, 1.0 / 24, 1.0 / 6, 0.5, 1.0], 1.0)
        tt("mag", "mag", "mag", ALU.mult)
        tt("mag", "mag", "mag", ALU.mult)
        tt("th", "lim", "dt", ALU.mult)
        self.ts(DVE, S("t3"), S("th"), 1.0 / 32.0, None, ALU.mult, None, [St("th")], [St("t3")])
        tt("t2", "t3", "t3", ALU.mult)
        horner("c", "t2", [-1.0 / 3628800, 1.0 / 40320, -1.0 / 720, 1.0 / 24, -0.5], 1.0)
        horner("s", "t2", [1.0 / 362880, -1.0 / 5040, 1.0 / 120, -1.0 / 6], 1.0)
        tt("s", "s", "t3", ALU.mult)

        def csquare(a, b):
            tt("t1", a, a, ALU.mult)
            tt("t2", b, b, ALU.mult)
            self.stt(S(b), S(a), 2.0, S(b), ALU.mult, ALU.mult, [St(a), St(b)], [St(b)])
            tt(a, "t1", "t2", ALU.subtract)

        for _ in range(5):
            csquare("c", "s")
        tt("t1", "c", "c", ALU.mult)
        tt("t2", "s", "s", ALU.mult)
        tt("t1", "t1", "t2", ALU.add)
        self.ts(DVE, S("t1"), S("t1"), -0.5, 1.5, ALU.mult, ALU.add, [St("t1")], [St("t1")])
        tt("c", "c", "t1", ALU.mult)
        tt("s", "s", "t1", ALU.mult)
        tt("abre", "mag", "c", ALU.mult)
        tt("abim", "mag", "s", ALU.mult)
        self.ts(DVE, S("nr"), S("abre"), -1.0, None, ALU.add, None, [St("abre")], [St("nr")])
        tt("t1", "lre", "lre", ALU.mult)
        tt("t2", "lim", "lim", ALU.mult)
        tt("den", "t1", "t2", ALU.add)
        P.op(DVE, lambda e: e.reciprocal(S("den"), S("den")), [St("den")], [St("den")])
        tt("t1", "nr", "lre", ALU.mult)
        tt("t2", "abim", "lim", ALU.mult)
        tt("t1", "t1", "t2", ALU.add)
        tt("fre", "t1", "den", ALU.mult)
        tt("t1", "abim", "lre", ALU.mult)
        tt("t2", "nr", "lim", ALU.mult)
        tt("t1", "t1", "t2", ALU.subtract)
        tt("fim", "t1", "den", ALU.mult)
        tt("rm2", "mag", "mag", ALU.mult)
        P.op(DVE, lambda e: e.reciprocal(S("rm2"), S("rm2")), [St("rm2")], [St("rm2")])
        tt("lire", "abre", "rm2", ALU.mult)
        self.stt(S("liim"), S("abim"), -1.0, S("rm2"), ALU.mult, ALU.mult, [St("abim"), St("rm2")], [St("liim")])
        self.copy(DVE, S("mure"), S("abre"), [St("abre")], [St("mure")])
        self.copy(DVE, S("muim"), S("abim"), [St("abim")], [St("muim")])
        for _ in range(3):
            csquare("mure", "muim")
        coef = self.coef
        self.copy(DVE, S("w2r"), S("mure"), [St("mure")], [St("w2r")])
        self.copy(DVE, S("w2i"), S("muim"), [St("muim")], [St("w2i")])
        for k in range(8):
            self.copy(DVE, coef.ap[:, :, k, 0], S("w2r"), [St("w2r")], [coef.t()])
            self.copy(DVE, coef.ap[:, :, k, 1], S("w2i"), [St("w2i")], [coef.t()])
            self.ts(DVE, coef.ap[:, :, k, 2], S("w2i"), -1.0, None, ALU.mult, None, [St("w2i")], [coef.t()])
            if k < 7:
                csquare("w2r", "w2i")

        if self.stop_after == "s5b":
            return
        for _ in range(3):
            tt("mag", "mag", "mag", ALU.mult)
        self.copy(DVE, self.rho.ap, S("mag"), [St("mag")], [self.rho.t()])
        for _ in range(3):
            csquare("c", "s")
        tt("t1", "c", "c", ALU.mult)
        tt("t2", "s", "s", ALU.mult)
        tt("t1", "t1", "t2", ALU.add)
        self.ts(DVE, S("t1"), S("t1"), -0.5, 1.5, ALU.mult, ALU.add, [St("t1")], [St("t1")])
        tt("c", "c", "t1", ALU.mult)
        tt("s", "s", "t1", ALU.mult)
        tmpt = [tmp.t(i) for i in range(2)]

        def cmul(ore, oim, otoks, are, aim, atoks, zre, zim, ztoks, n):
            ab = lambda nm: S(nm).unsqueeze(2).to_broadcast([128, 32, n])
            t = [tmp.ap[:, i, 0:32 * n].rearrange("p (a h) -> p a h", h=n) for i in range(2)]
            rd = atoks + ztoks
            self.tt(DVE, t[0], zre, ab(are), ALU.mult, rd, [tmpt[0]])
            self.tt(DVE, t[1], zim, ab(aim), ALU.mult, rd, [tmpt[1]])
            self.tt(DVE, ore, t[0], t[1], ALU.subtract, [tmpt[0], tmpt[1]], otoks)
            self.tt(DVE, t[0], zim, ab(are), ALU.mult, rd, [tmpt[0]])
            self.tt(DVE, t[1], zre, ab(aim), ALU.mult, rd, [tmpt[1]])
            self.tt(DVE, oim, t[0], t[1], ALU.add, [tmpt[0], tmpt[1]], otoks)

        fa = [St("fre"), St("fim")]
        cmul(Bb.ap[:, 0], Bb.ap[:, 1], [Bb.t()], "fre", "fim", fa, Bq.ap[:, 0], Bq.ap[:, 1], [Bq.t()], 16)
        la = [St("lire"), St("liim")]
        for j in range(8):
            if j == 0:
                zr, zi, zt = Bb.ap[:, 0], Bb.ap[:, 1], [Bb.t()]
            else:
                zr, zi, zt = X.ap[:, 0, :, j - 1, :], X.ap[:, 1, :, j - 1, :], [X.t(j - 1)]
            cmul(X.ap[:, 0, :, j, :], X.ap[:, 1, :, j, :], [X.t(j)], "lire", "liim", la, zr, zi, zt, 16)
        if self.stop_after == "s5c":
            return
        for r in range(8):
            for ri, nm in enumerate(["s5_c_re", "s5_c_im"]):
                db = (2 * r + ri) % 2
                P.dma(SP, Cin.ap[:, db, 0:64], I[nm][0][8 * r:8 * r + 8].rearrange("g h p -> (g h) p"), writes=[Cin.t(db)])
                pb = ps[1 + db]
                self.tr(pb.ap[0:64, 0:128], Cin.ap[:, db, 0:64], identf.ap, [Cin.t(db), identf.t()], [pb.t()])
                src = pb.ap[0:64, 0:128].rearrange("p (a g h) -> p a g h", g=2, h=16)
                self.copy(ACT, CT.ap[0:64, ri, 4 * r:4 * r + 4, :], src[:, :, 0, :], [pb.t()], [CT.t()])
                self.copy(ACT, CT.ap[64:128, ri, 4 * r:4 * r + 4, :], src[:, :, 1, :], [pb.t()], [CT.t()])
        if self.stop_after == "s5d":
            return
        ab_ = [St("abre"), St("abim")]
        for i in range(8):
            if i == 0:
                zr, zi, zt = CT.ap[:, 0], CT.ap[:, 1], [CT.t()]
            else:
                zr, zi, zt = Wo.ap[:, 0, :, i - 1, :], Wo.ap[:, 1, :, i - 1, :], [Wo.t(i - 1)]
            cmul(Wo.ap[:, 0, :, i, :], Wo.ap[:, 1, :, i, :], [Wo.t(i)], "abre", "abim", ab_, zr, zi, zt, 16)
        ma = [St("mure"), St("muim")]
        for j in range(8):
            cmul(We.ap[:, 0, :, 16 * j:16 * j + 16], We.ap[:, 1, :, 16 * j:16 * j + 16], [We.t(j)], "mure", "muim", ma,
                 X.ap[:, 0, :, j, :], X.ap[:, 1, :, j, :], [X.t(j)], 16)
        allWo = [Wo.t(i) for i in range(8)]
        WoN = Wo.t("neg")
        self.ts(DVE, Wo.ap[:, 1], Wo.ap[:, 1], -1.0, None, ALU.mult, None, allWo, allWo + [WoN])
        def rest():
            self.copy(DVE, Drep.ap, Drows.ap.unsqueeze(1).to_broadcast([64, 8, 16]), [Drows.t()], [Drep.t()])
            self.tr(ps[3].ap[:, 0:64], Drep.ap.rearrange("p j h -> p (j h)"), identf.ap[0:64, 0:64], [Drep.t(), identf.t()], [ps[3].t()])
            self.copy(DVE, dcol.ap, ps[3].ap[:, 0:64], [ps[3].t()], [dcol.t()])
            self.s5_setup_pairs(locals_=dict(We=We, X=X, Wo=Wo, WoN=WoN, allWo=allWo, stage=stage, WoM=WoM, tmpm=tmpm, dcol=dcol, identf=identf))
            P.barrier()
            Tr = Buf(X.ap.rearrange("p r a j h -> p (r a j h)").rearrange("p (a c) -> p a c", c=256), "Tr")
            Ti = Buf(Wo.ap.rearrange("p r a j h -> p (r a j h)").rearrange("p (a c) -> p a c", c=256), "Ti")
            tb = [Buf(We.ap[:, i].rearrange("p a n -> p (a n)"), f"tbig{i}") for i in range(2)]
            P.op(DVE, lambda e: e.memset(Tr.ap[:, :, 0:1], 1.0), [], [Tr.t()])
            P.op(DVE, lambda e: e.memset(Ti.ap[:, :, 0:1], 0.0), [], [Ti.t()])
            self.copy(DVE, S("w2r"), S("c"), [St("c")], [St("w2r")])
            self.copy(DVE, S("w2i"), S("s"), [St("s")], [St("w2i")])
            for k in range(8):
                n = 1 << k
                wrb = S("w2r").unsqueeze(2).to_broadcast([128, 32, n])
                wib = S("w2i").unsqueeze(2).to_broadcast([128, 32, n])
                t0 = tb[0].ap[:, 0:32 * n].rearrange("p (a c) -> p a c", c=n)
                t1 = tb[1].ap[:, 0:32 * n].rearrange("p (a c) -> p a c", c=n)
                wt = [St("w2r"), St("w2i")]
                self.tt(DVE, t0, Tr.ap[:, :, 0:n], wrb, ALU.mult, [Tr.t()] + wt, [tb[0].t()])
                self.tt(DVE, t1, Ti.ap[:, :, 0:n], wib, ALU.mult, [Ti.t()] + wt, [tb[1].t()])
                self.tt(DVE, Tr.ap[:, :, n:2 * n], t0, t1, ALU.subtract, [tb[0].t(), tb[1].t()], [Tr.t()])
                self.tt(DVE, t0, Ti.ap[:, :, 0:n], wrb, ALU.mult, [Ti.t()] + wt, [tb[0].t()])
                self.tt(DVE, t1, Tr.ap[:, :, 0:n], wib, ALU.mult, [Tr.t()] + wt, [tb[1].t()])
                self.tt(DVE, Ti.ap[:, :, n:2 * n], t0, t1, ALU.add, [tb[0].t(), tb[1].t()], [Ti.t()])
                if k < 7:
                    csquare("w2r", "w2i")
            P.dma(SP, self.s5tab[:, :, 0, :].rearrange("a p c -> p a c"), Tr.ap, reads=[Tr.t()], writes=[self.s5tab_tok])
            P.dma(SP, self.s5tab[:, :, 1, :].rearrange("a p c -> p a c"), Ti.ap, reads=[Ti.t()], writes=[self.s5tab_tok])
            self.dbg("Tr", Tr.ap, [128, 32, 256], [Tr.t()])
            self.dbg("Ti", Ti.ap, [128, 32, 256], [Ti.t()])
            self.dbg("rho", self.rho.ap, [128, 32], [self.rho.t()])
        return rest

    def s5_setup_pairs(self, locals_):
        P, c, ps = self.P, self.c, self.ps
        We, X, Wo, WoN, allWo, stage, WoM, tmpm, dcol, identf = (locals_[k] for k in
                                                                ["We", "X", "Wo", "WoN", "allWo", "stage", "WoM", "tmpm", "dcol", "identf"])
        allWe = [We.t(j) for j in range(8)]
        allX = [X.t(j) for j in range(8)]

        def phase1(pr):
            slot = (pr // 4) % 2
            p4 = pr % 4
            st_ = stage.t(slot)
            pa = ps[4 + pr % 2]
            pb = ps[6 + pr % 2]
            self.tr(pa.ap[:, 0:128], We.ap[:, 0, pr, :], identf.ap, allWe + [identf.t()], [pa.t()])
            self.tr(pa.ap[:, 128:256], We.ap[:, 1, pr, :], identf.ap, allWe + [identf.t()], [pa.t()])
            self.copy(ACT, stage.ap[:, slot, p4, 0:256], pa.ap[:, 0:256], [pa.t()], [st_])
            self.copy(ACT, stage.ap[:, slot, p4, 256:512].rearrange("p (r n) -> p r n", r=2),
                      Wo.ap[:, :, pr].rearrange("p r i h -> p r (i h)"), [WoN], [st_])
            wm = WoM.ap[:, pr % 2]
            for gl in range(2):
                self.ts(POOL if gl == 0 else DVE, wm[:, gl], Wo.ap[:, :, pr].rearrange("p r i h -> p r (i h)"), c["hm"].ap[:, gl:gl + 1], None, ALU.mult, None,
                        [WoN, c["hm"].t()], [WoM.t(pr % 2)])
            for gl in range(2):
                for ri in range(2):
                    self.mm(pb.ap[:, 128 * gl:128 * gl + 128], X.ap[:, ri, pr].rearrange("p j h -> p (j h)"),
                            wm[:, gl, ri, :], ri == 0, ri == 1, allX + [WoM.t(pr % 2)], [pb.t()])

        def phase2(pr):
            slot = (pr // 4) % 2
            p4 = pr % 4
            st_ = stage.t(slot)
            pb = ps[6 + pr % 2]
            tm = tmpm.ap[:, pr % 2]
            self.tt(DVE, tm, pb.ap[:, 0:256].rearrange("p (g n) -> p g n", g=2),
                    c["bmask_f"].ap.unsqueeze(1).to_broadcast([128, 2, 128]), ALU.mult, [pb.t(), c["bmask_f"].t()], [tmpm.t(pr % 2)])
            for gl in range(2):
                g = 2 * pr + gl
                self.stt(stage.ap[:, slot, p4, 512 + 128 * gl:512 + 128 * gl + 128], identf.ap, dcol.ap[:, g:g + 1], tm[:, gl, :],
                         ALU.mult, ALU.add, [identf.t(), dcol.t(), tmpm.t(pr % 2)], [st_])
            if p4 == 3:
                ftc = pr // 4
                P.dma(SP, self.s5w[4 * ftc:4 * ftc + 4].rearrange("a p n -> p a n"), stage.ap[:, slot], reads=[st_],
                      writes=[self.s5w_tok[ftc]])
                if ftc == 0:
                    self.dbg("s5w0", stage.ap[:, slot], [128, 4, 768], [st_], dt=BF16)
        phase1(0)
        for pr in range(32):
            if pr + 1 < 32:
                phase1(pr + 1)
            phase2(pr)
        self.dbg("coef", self.coef.ap, [128, 32, 8, 3], [self.coef.t()])

    R_RSTD = 16384
    R_XT = 18432
    R_ACTT = R_XT + 65536
    R_TAIL = R_ACTT + 32768

    def seq_pipeline(self, s):
        P = self.P
        A = self.arena
        self.xT, _ = A.view(self.R_XT, [128, 8, 2048], F32, f"xT{s}")
        self.actT, _ = A.view(self.R_ACTT, [128, 8, 2048], BF16, f"actT{s}")
        self.rstd, _ = A.view(self.R_RSTD, [128, 512], F32, f"rstd{s}")
        self.s5_prep(s)
        P.barrier()
        self.s5_core(s)
        P.barrier()
        self.dbg(f"gT{s}", self.actT.ap, [128, 8, 2048], [], dt=BF16)
        self.dbg(f"xT{s}", self.xT.ap, [128, 8, 2048], [])
        if self.stop_after == "xreload":
            return
        self.glu(s)
        P.barrier()
        self.mlp(s, 0, 1)
        P.barrier()
        self.dbg(f"x1T{s}", self.xT.ap, [128, 8, 2048], [])
        if self.stop_after == "layer0":
            return
        self.kv_phase(s)
        P.barrier()
        self.dbg(f"KT{s}", self.KT.ap, [128, 8, 2048], [], dt=BF16)
        self.dbg(f"V{s}", self.V.ap, [128, 16, 1024], [], dt=BF16)
        if self.stop_after == "kv":
            return
        self.attn_phase(s)
        P.barrier()
        self.dbg(f"x2T{s}", self.xT.ap, [128, 8, 2048], [])
        if self.stop_after == "attn":
            return
        self.mlp(s, 1, 4)
        P.barrier()
        self.output_phase(s)
        P.barrier()

    def load_xtok(self, s, half, Xtok):
        src = self.I["x"][s, half * 1024:(half + 1) * 1024, :].rearrange("(c j) f -> c j f", j=8)
        for q in range(4):
            self.P.dma(SP, Xtok.ap[32 * q:32 * q + 32], src[32 * q:32 * q + 32], writes=[Xtok.t()])

    def s5_prep(self, s):
        P, c, ps = self.P, self.c, self.ps
        A = self.arena
        off = self.R_TAIL
        D, off = A.view(off, [128, 64, 256], BF16, "D")
        self.D = D
        self.s5_tail_off = off
        Xtok, off = A.view(off, [128, 8, 1024], F32, "Xtok")
        Arep, off = A.view(off, [128, 1024], F32, "Arep")
        Brep, off = A.view(off, [128, 1024], F32, "Brep")
        ss, off = A.view(off, [128, 8], F32, "ss")
        rs, off = A.view(off, [128, 8], F32, "rs")
        junk, off = A.view(off, [128, 1024], BF16, "junk")
        o2 = self.R_ACTT
        htok, o2 = A.view(o2, [128, 64, 8, 16], BF16, "htok")
        tmpf, o2 = A.view(o2, [128, 2, 1024], F32, "tmpf")
        xT = self.xT
        P.dma(SP, Arep.ap, self.modrow[s, 0, :].partition_broadcast(128), reads=[self.modrow_tok], writes=[Arep.t()])
        P.dma(SP, Brep.ap, self.modrow[s, 1, :].partition_broadcast(128), reads=[self.modrow_tok], writes=[Brep.t()])
        n = 0
        for half in range(2):
            self.load_xtok(s, half, Xtok)
            for j in range(8):
                self.act(junk.ap, Xtok.ap[:, j, :], AF.Square, [Xtok.t()], [junk.t(), ss.t()], accum_out=ss.ap[:, j:j + 1])
            self.ts(DVE, rs.ap, ss.ap, 1.0 / 1024.0, EPS, ALU.mult, ALU.add, [ss.t()], [rs.t()])
            self.act(rs.ap, rs.ap, AF.Sqrt, [rs.t()], [rs.t()])
            P.op(DVE, lambda e: e.reciprocal(rs.ap, rs.ap), [rs.t()], [rs.t()])
            for j in range(8):
                tb = j % 2
                self.stt(tmpf.ap[:, tb, :], Xtok.ap[:, j, :], rs.ap[:, j:j + 1], Arep.ap, ALU.mult, ALU.mult,
                         [Xtok.t(), rs.t(), Arep.t()], [tmpf.t(tb)])
                self.tt(POOL, htok.ap[:, :, j, :], tmpf.ap[:, tb, :].rearrange("p (g h) -> p g h", h=16),
                        Brep.ap.rearrange("p (g h) -> p g h", h=16), ALU.add, [tmpf.t(tb), Brep.t()], [htok.t()])
            for ft in range(8):
                for jq in range(2):
                    bank = ps[4 + n % 4]
                    for jj in range(4):
                        j = 4 * jq + jj
                        self.tr(bank.ap[:, 128 * jj:128 * jj + 128], Xtok.ap[:, j, 128 * ft:128 * ft + 128], c["ident_f"].ap,
                                [Xtok.t(), c["ident_f"].t()], [bank.t()], sig=(jj == 3))
                    dst = xT.ap[:, ft, 1024 * half:1024 * half + 1024].rearrange("p (c j) -> p j c", j=8)[:, 4 * jq:4 * jq + 4, :]
                    self.copy(ACT if n % 2 == 0 else DVE, dst, bank.ap.rearrange("p (j c) -> p j c", c=128), [bank.t()], [xT.t(ft)])
                    n += 1
            for g4 in range(16):
                pb = ps[g4 % 4]
                pbv = pb.ap.bitcast(BF16)
                for gi in range(4):
                    g = 4 * g4 + gi
                    self.tr(pbv[:, 128 * gi:128 * gi + 128], htok.ap[:, g].rearrange("p j h -> p (j h)"), c["ident_b"].ap,
                            [htok.t(), c["ident_b"].t()], [pb.t()], sig=(gi == 3))
                self.copy(ACT if g4 % 2 == 0 else DVE, D.ap[:, 4 * g4:4 * g4 + 4, 128 * half:128 * half + 128],
                          pbv[:, 0:512].rearrange("p (g c) -> p g c", c=128), [pb.t()], [D.t()])

    def s5_core(self, s):
        P, c, ps = self.P, self.c, self.ps
        A = self.arena
        D = self.D
        rho = self.rho
        off = self.s5_tail_off
        wch, _ = A.view(0, [128, 2, 4, 768], BF16, "wch")
        tabs, off = A.view(off, [128, 2, 4, 2, 256], F32, "tabs")
        Wk, off = A.view(off, [128, 2, 8, 256], F32, "Wk")
        Sbf, off = A.view(off, [128, 4, 2, 256], BF16, "Sbf")
        Ygb, off = A.view(off, [128, 2, 2, 256], BF16, "Ygb")
        Gtok, off = A.view(off, [128, 2, 2, 8, 128], BF16, "Gtok")
        gT = self.actT
        P.op(POOL, lambda e: e.memset(Sbf.ap, 0.0), [], [Sbf.t(i) for i in range(4)])
        identb = c["ident_b"]

        def wslot_of(pr):
            return (pr // 4) % 2

        def pair_front(pr):
            p4 = pr % 4
            wslot = wslot_of(pr)
            bankE = ps[pr % 4]
            for ri in range(2):
                for gl in range(2):
                    self.mm(bankE.ap[64 * gl:64 * gl + 64, 256 * ri:256 * ri + 256],
                            wch.ap[:, wslot, p4, 128 * ri + 64 * gl:128 * ri + 64 * gl + 64], D.ap[:, 2 * pr + gl, :], True, True,
                            [wch.t(wslot), D.t()], [bankE.t()], sig=(ri == 1 and gl == 1))

        def scan_ops(pr):
            p4 = pr % 4
            wslot = wslot_of(pr)
            sl, w2 = pr % 4, pr % 2
            bankE = ps[pr % 4]
            Er, Ei = bankE.ap[:, 0:256], bankE.ap[:, 256:512]
            cr, sr = tabs.ap[:, wslot, p4, 0, :], tabs.ap[:, wslot, p4, 1, :]
            W = lambda j: Wk.ap[:, w2, j, :]
            wt = lambda j: Wk.t((w2, j))
            tb_, eb = tabs.t(wslot), bankE.t()
            rb = rho.ap[:, pr:pr + 1].to_broadcast([128, 256])
            ops = []
            ops.append(lambda: self.tt(DVE, W(4), Er, cr, ALU.mult, [eb, tb_], [wt(4)]))
            ops.append(lambda: self.tt(DVE, W(5), Ei, sr, ALU.mult, [eb, tb_], [wt(5)]))
            ops.append(lambda: self.tt(DVE, W(6), Ei, cr, ALU.mult, [eb, tb_], [wt(6)]))
            ops.append(lambda: self.tt(DVE, W(7), Er, sr, ALU.mult, [eb, tb_], [wt(7)]))
            ops.append(lambda: self.tt(DVE, W(0), W(4), W(5), ALU.add, [wt(4), wt(5)], [wt(0)]))
            ops.append(lambda: self.tt(DVE, W(1), W(6), W(7), ALU.subtract, [wt(6), wt(7)], [wt(1)]))
            ops.append(lambda: self.P.op(DVE, lambda e: e.tensor_tensor_scan(W(2), rb, W(0), 0.0, ALU.mult, ALU.add),
                                         [wt(0), rho.t()], [wt(2)]))
            ops.append(lambda: self.P.op(DVE, lambda e: e.tensor_tensor_scan(W(3), rb, W(1), 0.0, ALU.mult, ALU.add),
                                         [wt(1), rho.t()], [wt(3)]))
            n = 255
            ops.append(lambda: self.tt(DVE, W(4)[:, 0:n], W(2)[:, 0:n], cr[:, 0:n], ALU.mult, [wt(2), tb_], [wt(4)]))
            ops.append(lambda: self.tt(DVE, W(5)[:, 0:n], W(3)[:, 0:n], sr[:, 0:n], ALU.mult, [wt(3), tb_], [wt(5)]))
            ops.append(lambda: self.tt(DVE, W(6)[:, 0:n], W(3)[:, 0:n], cr[:, 0:n], ALU.mult, [wt(3), tb_], [wt(6)]))
            ops.append(lambda: self.tt(DVE, W(7)[:, 0:n], W(2)[:, 0:n], sr[:, 0:n], ALU.mult, [wt(2), tb_], [wt(7)]))
            ops.append(lambda: self.tt(DVE, Sbf.ap[:, sl, 0, 1:256], W(4)[:, 0:n], W(5)[:, 0:n], ALU.subtract, [wt(4), wt(5)], [Sbf.t(sl)]))
            ops.append(lambda: self.tt(DVE, Sbf.ap[:, sl, 1, 1:256], W(6)[:, 0:n], W(7)[:, 0:n], ALU.add, [wt(6), wt(7)], [Sbf.t(sl)]))
            return ops

        def pair_back(pr):
            ft, p4 = pr // 4, pr % 4
            wslot = wslot_of(pr)
            gslot = ft % 2
            sl = pr % 4
            ys = pr % 2
            bankY = ps[4 + pr % 2]
            for gl in range(2):
                o = bankY.ap[:, 256 * gl:256 * gl + 256]
                hs = slice(64 * gl, 64 * gl + 64)
                self.mm(o, wch.ap[:, wslot, p4, 512 + 128 * gl:512 + 128 * gl + 128], D.ap[:, 2 * pr + gl, :], True, False,
                        [wch.t(wslot), D.t()], [bankY.t()])
                self.mm(o, wch.ap[hs, wslot, p4, 256:384], Sbf.ap[hs, sl, 0, :], False, False, [wch.t(wslot), Sbf.t(sl)], [bankY.t()])
                self.mm(o, wch.ap[hs, wslot, p4, 384:512], Sbf.ap[hs, sl, 1, :], False, True, [wch.t(wslot), Sbf.t(sl)], [bankY.t()],
                        sig=(gl == 1))
            self.act(Ygb.ap[:, ys].rearrange("p g c -> p (g c)"), bankY.ap, AF.Gelu, [bankY.t()], [Ygb.t(ys)])
            pT = ps[6]
            pTv = pT.ap.bitcast(BF16)
            for gl in range(2):
                for half in range(2):
                    q = 2 * gl + half
                    self.tr(pTv[:, 128 * q:128 * q + 128], Ygb.ap[:, ys, gl, 128 * half:128 * half + 128], identb.ap,
                            [Ygb.t(ys), identb.t()], [pT.t()], sig=(q == 3))
            for gl in range(2):
                fo = 32 * p4 + 16 * gl
                self.copy(ACT, Gtok.ap[:, gslot, :, :, fo:fo + 16],
                          pTv[:, 256 * gl:256 * gl + 256].rearrange("p (a i h) -> p a i h", a=2, h=16), [pT.t()], [Gtok.t(gslot)])

        def ft_done(ft):
            gslot = ft % 2
            for half in range(2):
                pT2 = ps[7]
                pv = pT2.ap.bitcast(BF16)
                for i in range(8):
                    self.tr(pv[:, 128 * i:128 * i + 128], Gtok.ap[:, gslot, half, i, :], identb.ap, [Gtok.t(gslot), identb.t()], [pT2.t()],
                            sig=(i == 7))
                self.copy(ACT, gT.ap[:, ft, 1024 * half:1024 * half + 1024].rearrange("p (c i) -> p i c", i=8),
                          pv.rearrange("p (i c) -> p i c", c=128), [pT2.t()], [gT.t(ft)])

        def load_w(ft):
            P.dma(SP, wch.ap[:, ft % 2], self.s5w[4 * ft:4 * ft + 4].rearrange("a p n -> p a n"), reads=[self.s5w_tok[ft]],
                  writes=[wch.t(ft % 2)])
            P.dma(SP, tabs.ap[:, ft % 2], self.s5tab[4 * ft:4 * ft + 4].rearrange("a p r c -> p a r c"), reads=[self.s5tab_tok],
                  writes=[tabs.t(ft % 2)])

        load_w(0)
        load_w(1)
        for p4 in (0, 1):
            pair_front(p4)
        for pp in range(16):
            prs = [2 * pp, 2 * pp + 1]
            oa, ob_ = scan_ops(prs[0]), scan_ops(prs[1])
            for fa, fb in zip(oa, ob_):
                fa()
                fb()
            if pp + 1 < 16:
                for pr in (2 * pp + 2, 2 * pp + 3):
                    pair_front(pr)
            for pr in prs:
                pair_back(pr)
            if pp % 2 == 1:
                ft = pp // 2
                ft_done(ft)
                if ft + 2 < 8:
                    load_w(ft + 2)

    def x_reload(self, s):
        P, c, ps = self.P, self.c, self.ps
        A = self.arena
        Xtok, _ = A.view(self.R_TAIL, [128, 8, 1024], F32, "Xtok2")
        xT = self.xT
        n = 0
        for half in range(2):
            self.load_xtok(s, half, Xtok)
            for ft in range(8):
                for jq in range(2):
                    bank = ps[n % 4]
                    for jj in range(4):
                        j = 4 * jq + jj
                        self.tr(bank.ap[:, 128 * jj:128 * jj + 128], Xtok.ap[:, j, 128 * ft:128 * ft + 128], c["ident_f"].ap,
                                [Xtok.t(), c["ident_f"].t()], [bank.t()], sig=(jj == 3))
                    dst = xT.ap[:, ft, 1024 * half:1024 * half + 1024].rearrange("p (c j) -> p j c", j=8)[:, 4 * jq:4 * jq + 4, :]
                    self.copy(ACT if n % 2 == 0 else DVE, dst, bank.ap.rearrange("p (j c) -> p j c", c=128), [bank.t()], [xT.t(ft)])
                    n += 1


    def stream(self, loaders, computes):
        n = len(loaders)
        slots = {0: loaders[0]()}
        for i in range(n):
            if i + 1 < n:
                slots[i + 1] = loaders[i + 1]()
            computes[i](slots.pop(i))

    def fm_norm(self, s, n, tt, dst, sq, tmpf, bank):
        c, xT, rstd = self.c, self.xT, self.rstd
        ts_ = slice(512 * tt, 512 * tt + 512)
        for ft in range(8):
            self.act(sq.ap[:, ft, :], xT.ap[:, ft, ts_], AF.Square, [xT.t(ft)], [sq.t(ft)])
        for ft in range(8):
            self.mm(bank.ap, c["onesm_b"].ap, sq.ap[:, ft, :], ft == 0, ft == 7, [c["onesm_b"].t(), sq.t(ft)], [bank.t()])
        self.act(rstd.ap, bank.ap, AF.Ln, [bank.t()], [rstd.t()], bias=self.epsc.ap)
        self.act(rstd.ap, rstd.ap, AF.Exp, [rstd.t()], [rstd.t()], scale=-0.5)
        ms = self.modsc
        for ft in range(8):
            tb = ft % 2
            self.stt(tmpf.ap[:, tb, :], xT.ap[:, ft, ts_], ms.ap[:, s, n, 0, ft:ft + 1], rstd.ap, ALU.mult, ALU.mult,
                     [xT.t(ft), ms.t(), rstd.t()], [tmpf.t(tb)])
            self.act(dst[:, ft, :], tmpf.ap[:, tb, :], AF.Identity, [tmpf.t(tb), ms.t()], [self.cur_h_tok], bias=ms.ap[:, s, n, 1, ft:ft + 1])

    def glu(self, s):
        P, ps = self.P, self.ps
        A = self.arena
        off = self.R_TAIL
        sg, off = A.view(off, [128, 2, 512], F32, "sg")
        mt_, off = A.view(off, [128, 2, 512], F32, "mt")
        gT, xT, adaT = self.actT, self.xT, self.adaT
        w = self.I["s5_w_glu"][0].rearrange("(k p) n -> p k n", p=128)
        allg = [gT.t(ft) for ft in range(8)]
        cnt = [0]

        def loader(ft):
            def f():
                v = lambda a: a[:, 0:2048].rearrange("p (k g n) -> p k g n", g=2, n=128)
                return self.wload([(lambda a: v(a)[:, :, 0, :], w[:, :, 128 * ft:128 * ft + 128]),
                                   (lambda a: v(a)[:, :, 1, :], w[:, :, 1024 + 128 * ft:1024 + 128 * ft + 128])])
            return f

        def compute(ft):
            def f(slot):
                wv = slot.ap[:, 0:2048].rearrange("p (k g n) -> p k g n", g=2, n=128)
                for tt in range(4):
                    n = cnt[0]
                    cnt[0] += 1
                    bv, bg = ps[(2 * n) % 8], ps[(2 * n + 1) % 8]
                    ts_ = slice(512 * tt, 512 * tt + 512)
                    for gi, bank in enumerate([bv, bg]):
                        for k in range(8):
                            self.mm(bank.ap, wv[:, k, gi, :], gT.ap[:, k, ts_], k == 0, k == 7, [slot.t()] + allg, [bank.t()])
                    b2 = n % 2
                    self.act(sg.ap[:, b2, :], bg.ap, AF.Sigmoid, [bg.t()], [sg.t(b2)])
                    self.tt(DVE, mt_.ap[:, b2, :], bv.ap, sg.ap[:, b2, :], ALU.mult, [bv.t(), sg.t(b2)], [mt_.t(b2)])
                    self.stt(xT.ap[:, ft, ts_], mt_.ap[:, b2, :], adaT.ap[:, 16 + ft, s:s + 1], xT.ap[:, ft, ts_], ALU.mult, ALU.add,
                             [mt_.t(b2), adaT.t(), xT.t(ft)], [xT.t(ft)])
            return f

        self.stream([loader(ft) for ft in range(8)], [compute(ft) for ft in range(8)])

    def mlp(self, s, l, nidx):
        P, ps = self.P, self.ps
        A = self.arena
        uT, o_t = A.view(self.R_TAIL, [128, 32, 1024], BF16, "uT")
        htile1, _ = A.view(o_t, [128, 8, 1024], BF16, "htile1")
        off = self.R_ACTT
        htile0, off = A.view(off, [128, 8, 1024], BF16, "htile0")
        sq, off = A.view(off, [128, 8, 512], BF16, "sq")
        tmpf, off = A.view(off, [128, 2, 512], F32, "tmpf2")
        r, off = A.view(off, [128, 2, 1024], BF16, "r")
        htiles = [htile0, htile1]
        xT, adaT = self.xT, self.adaT
        w1 = self.I["mlp_w1"][l]
        w2 = self.I["mlp_w2"][l].rearrange("(k p) n -> p k n", p=128)
        gm0 = 48 * l + 40
        nb = [0]

        def norm(t2):
            for sub in range(2):
                self.cur_h_tok = htiles[t2].t(sub)
                self.fm_norm(s, nidx, 2 * t2 + sub, htiles[t2].ap[:, :, 512 * sub:512 * sub + 512], sq, tmpf, ps[7])

        norm(0)
        for t2 in range(2):
            htile = htiles[t2]
            loaders, computes = [], []
            for ch in range(8):
                loaders.append(lambda ch=ch: self.wload_k8(w1, 512 * ch))

                def c1(sv, ch=ch):
                    slot, wv = sv
                    for m4 in range(4):
                        e = 4 * ch + m4
                        for sub in range(2):
                            bank = ps[nb[0] % 4]
                            nb[0] += 1
                            for k in range(8):
                                self.mm(bank.ap, wv[:, k, 128 * m4:128 * m4 + 128], htile.ap[:, k, 512 * sub:512 * sub + 512], k == 0, k == 7,
                                        [slot.t(), htile.t(sub)], [bank.t()])
                            self.act(r.ap[:, e % 2, 512 * sub:512 * sub + 512], bank.ap, AF.Relu, [bank.t()], [r.t((e % 2, sub))])
                        self.tt(DVE, uT.ap[:, e, :], r.ap[:, e % 2, :], r.ap[:, e % 2, :], ALU.mult,
                                [r.t((e % 2, 0)), r.t((e % 2, 1))], [uT.t(e)])
                    if ch == 7 and t2 == 0:
                        norm(1)
                computes.append(c1)
            allu = [uT.t(e) for e in range(32)]
            for o in range(8):
                def l2(o=o):
                    slot = self.wload([(lambda a: a.rearrange("p (k n) -> p k n", n=128), w2[:, :, 128 * o:128 * o + 128])])
                    return slot, slot.ap.rearrange("p (k n) -> p k n", n=128)
                loaders.append(l2)

                def c2(sv, o=o, t2=t2):
                    slot, wv = sv
                    for sub in range(2):
                        bank = ps[4 + (2 * o + sub) % 3]
                        ts_ = slice(1024 * t2 + 512 * sub, 1024 * t2 + 512 * sub + 512)
                        for k in range(32):
                            self.mm(bank.ap, wv[:, k, :], uT.ap[:, k, 512 * sub:512 * sub + 512], k == 0, k == 31, [slot.t()] + allu, [bank.t()])
                        self.stt(xT.ap[:, o, ts_], bank.ap, adaT.ap[:, gm0 + o, s:s + 1], xT.ap[:, o, ts_], ALU.mult, ALU.add,
                                 [bank.t(), adaT.t(), xT.t(o)], [xT.t(o)])
                computes.append(c2)
            self.stream(loaders, computes)

    def headnorm_batch(self, banks, sbanks, gaincol, dsts, dtoks, sq, tk4):
        c = self.c
        n = len(banks)
        for j in range(n):
            self.act(sq.ap[:, j, :], banks[j].ap, AF.Square, [banks[j].t()], [sq.t(j)])
        for j in range(n):
            self.mm(sbanks[j].ap, c["blk_b"].ap, sq.ap[:, j, :], True, True, [c["blk_b"].t(), sq.t(j)], [sbanks[j].t()])
        for j in range(n):
            self.act(tk4.ap[:, j, :], sbanks[j].ap, AF.Ln, [sbanks[j].t()], [tk4.t(j)], bias=self.epsc.ap)
        for j in range(n):
            self.act(tk4.ap[:, j, :], tk4.ap[:, j, :], AF.Exp, [tk4.t(j)], [tk4.t(j)], scale=-0.5)
        for j in range(n):
            self.stt(dsts[j], banks[j].ap, gaincol, tk4.ap[:, j, :], ALU.mult, ALU.mult, [banks[j].t(), self.qkg.t(), tk4.t(j)], dtoks[j])

    def kv_phase(self, s):
        P, ps = self.P, self.ps
        A = self.arena
        self.KT, _ = A.view(self.R_ACTT, [128, 8, 2048], BF16, "KT")
        off = self.R_TAIL
        self.V, off = A.view(off, [128, 16, 1024], BF16, "V")
        self.attn_off = off
        htile, off = A.view(off, [128, 8, 2048], BF16, "htile_kv")
        sq, off = A.view(off, [128, 8, 512], BF16, "sq_kv")
        tmpf, off = A.view(off, [128, 2, 512], F32, "tmpf_kv")
        tk2, off = A.view(off, [128, 2, 512], F32, "tk2")
        KT, V = self.KT, self.V
        wkv = self.I["w_kv"]
        for tt in range(4):
            self.cur_h_tok = htile.t(tt)
            self.fm_norm(s, 2, tt, htile.ap[:, :, 512 * tt:512 * tt + 512], sq, tmpf, ps[7])
        cnt = [0]
        loaders = [lambda ch=ch: self.wload_k8(wkv, 512 * ch) for ch in range(4)]
        computes = []
        for ch in range(4):
            def cK(sv, ch=ch):
                slot, wv = sv
                for tt in range(4):
                    ts_ = slice(512 * tt, 512 * tt + 512)
                    for hf in range(2):
                        n = cnt[0]
                        cnt[0] += 1
                        banks = [ps[(2 * n) % 4], ps[(2 * n + 1) % 4]]
                        sbanks = [ps[4 + (2 * n) % 4], ps[4 + (2 * n + 1) % 4]]
                        for j in range(2):
                            m4 = 2 * hf + j
                            for k in range(8):
                                self.mm(banks[j].ap, wv[:, k, 128 * m4:128 * m4 + 128], htile.ap[:, k, ts_], k == 0, k == 7,
                                        [slot.t(), htile.t(tt)], [banks[j].t()])
                        prs = [4 * ch + 2 * hf + j for j in range(2)]
                        sqv = Buf(sq.ap[:, 2 * (n % 4):2 * (n % 4) + 2, :], "sqv")
                        sqv.toks = {0: sq.t(2 * (n % 4)), 1: sq.t(2 * (n % 4) + 1)}
                        self.headnorm_batch(banks, sbanks, self.qkg.ap[:, 1:2], [KT.ap[:, p_, ts_] for p_ in prs],
                                            [[KT.t((p_, tt))] for p_ in prs], sqv, tk2)

            def cV(sv, ch=ch):
                slot, wv = sv
                for tb in range(16):
                    n = cnt[0]
                    cnt[0] += 1
                    bank = ps[n % 4]
                    for k in range(8):
                        self.mm(bank.ap, htile.ap[:, k, 128 * tb:128 * tb + 128], wv[:, k, :], k == 0, k == 7,
                                [slot.t(), htile.t(tb // 4)], [bank.t()])
                    self.copy(ACT if n % 2 == 0 else DVE, V.ap[:, tb, 512 * (ch - 2):512 * (ch - 2) + 512], bank.ap,
                              [bank.t()], [V.t(tb)])
            computes.append(cK if ch < 2 else cV)
        self.stream(loaders, computes)

    def attn_phase(self, s):
        P, ps, c = self.P, self.ps, self.c
        A = self.arena
        KT, V, xT, adaT = self.KT, self.V, self.xT, self.adaT
        off = self.attn_off
        qT, off = A.view(off, [128, 8, 512], BF16, "qT")
        oT, off = A.view(off, [128, 8, 512], BF16, "oT")
        o1 = off
        htile, o1 = A.view(o1, [128, 8, 512], BF16, "htile_q")
        sq, o1 = A.view(o1, [128, 8, 512], BF16, "sq_q")
        tmpf, o1 = A.view(o1, [128, 2, 512], F32, "tmpf_q")
        tk4, o1 = A.view(o1, [128, 4, 512], F32, "tk4_q")
        o2 = off
        Eb, o2 = A.view(o2, [128, 2, 2, 512], F32, "Eb")
        Lb, o2 = A.view(o2, [128, 3, 2, 512], BF16, "Lb")
        Wb, o2 = A.view(o2, [128, 3, 2, 512], BF16, "Wb")
        R32, o2 = A.view(o2, [128, 2, 512], F32, "R32")
        Rbf, o2 = A.view(o2, [128, 3, 2, 512], BF16, "Rbf")
        wq = self.I["sb_w_q"][0]
        wo = self.I["sb_w_o"][0]
        identb, negmask, negtri, negones = c["ident_b"], c["negmask_b"], c["negtri_b"], c["negones_b"]
        zeros = c["zeros512_b"]
        zbank = Buf(self.psall[:, 0:1024].rearrange("p (h t) -> p h t", h=2), "zbank")
        zbank.toks[0] = ps[0].t()
        abank = []
        for j in range(2):
            b_ = Buf(self.psall[:, 1024 + 1024 * j:2048 + 1024 * j].rearrange("p (h t) -> p h t", h=2), f"abank{j}")
            abank.append(b_)
        obank = ps[6]

        def ztoks():
            return [ps[0].t(), ps[1].t()]

        def atoks(j):
            return [ps[2 + 2 * j].t(), ps[3 + 2 * j].t()]

        wq_slots = [self.wload_k8(wq, 512 * ch) for ch in range(2)]
        self.cur_h_tok = htile.t()
        self.fm_norm(s, 3, 0, htile.ap, sq, tmpf, ps[7])
        for qt in range(4):
            ts_ = slice(512 * qt, 512 * qt + 512)
            cnt = [0]

            def cQ(sv, ch):
                slot, wv = sv
                banks = [ps[m4] for m4 in range(4)]
                for m4 in range(4):
                    for k in range(8):
                        self.mm(banks[m4].ap, wv[:, k, 128 * m4:128 * m4 + 128], htile.ap[:, k, :], k == 0, k == 7,
                                [slot.t(), htile.t()], [banks[m4].t()])
                self.headnorm_batch(banks, [ps[4 + m4] for m4 in range(4)], self.qkg.ap[:, 0:1],
                                    [qT.ap[:, 4 * ch + m4, :] for m4 in range(4)], [[qT.t(4 * ch + m4)] for m4 in range(4)], sq, tk4)

            for ch in range(2):
                cQ(wq_slots[ch], ch)
            P.barrier()
            wo_slots = [self.wload_k8(wo, 512 * ch) for ch in range(2)]
            tiles = []
            nkb = 4 * qt + 4
            for pair in range(8):
                for ii, kb in enumerate(range(nkb - 1, -1, -1)):
                    r_ = kb - 4 * qt
                    c0 = 128 * r_ if r_ >= 0 else 0
                    tiles.append(dict(pair=pair, kb=kb, first=(ii == 0), last=(kb == 0), diag=(r_ >= 0), c0=c0))
            nt = len(tiles)

            def zmm(bank, btoks, t, close):
                c0 = t["c0"]
                kb = t["kb"]
                rd = [KT.t((t["pair"], kb // 4)), qT.t(t["pair"])]
                for hl in range(2):
                    hs = slice(64 * hl, 64 * hl + 64)
                    self.mm(bank.ap[:, hl, c0:512], KT.ap[hs, t["pair"], 128 * kb:128 * kb + 128], qT.ap[hs, t["pair"], c0:512], True,
                            close and not t["diag"], rd, btoks, sig=(close and not t["diag"] and hl == 1))
                if t["diag"]:
                    for hl in range(2):
                        self.mm(bank.ap[:, hl, c0:c0 + 128], identb.ap, negmask.ap, False, close, [identb.t(), negmask.t()], btoks,
                                sig=(close and hl == 1))

            def stageA1(i):
                t = tiles[i]
                c0 = t["c0"]
                zmm(zbank, ztoks(), t, True)
                self.act(Eb.ap[:, i % 2, :, c0:512], zbank.ap[:, :, c0:512], AF.Exp, ztoks(), [Eb.t(i % 2)])

            def stageA2(i):
                t = tiles[i]
                c0 = t["c0"]
                self.act(Lb.ap[:, i % 3, :, c0:512], Eb.ap[:, i % 2, :, c0:512], AF.Ln, [Eb.t(i % 2)], [Lb.t(i % 3)], bias=self.onec.ap)
                if not t["last"]:
                    if t["first"]:
                        P.op(POOL, lambda e: e.memset(R32.ap, 0.0), [], [R32.t()])
                    self.tt(POOL, R32.ap[:, :, c0:512], R32.ap[:, :, c0:512], Lb.ap[:, i % 3, :, c0:512], ALU.add,
                            [R32.t(), Lb.t(i % 3)], [R32.t()])
                    self.copy(DVE, Rbf.ap[:, i % 3], R32.ap, [R32.t()], [Rbf.t(i % 3)])

            def stageB(i):
                t = tiles[i]
                ab = abank[i % 2]
                at = atoks(i % 2)
                c0 = t["c0"]
                zmm(ab, at, t, False)
                for hl in range(2):
                    self.mm(ab.ap[:, hl, c0:512], negtri.ap, Lb.ap[:, i % 3, hl, c0:512], False, t["first"], [negtri.t(), Lb.t(i % 3)], at,
                            sig=(t["first"] and hl == 1))
                if not t["first"]:
                    for hl in range(2):
                        self.mm(ab.ap[:, hl, c0:512], negones.ap, Rbf.ap[:, (i - 1) % 3, hl, c0:512], False, True,
                                [negones.t(), Rbf.t((i - 1) % 3)], at, sig=(hl == 1))
                self.act(Wb.ap[:, i % 3, :, c0:512], ab.ap[:, :, c0:512], AF.Exp, at, [Wb.t(i % 3)])

            def stageC(i):
                t = tiles[i]
                c0 = t["c0"]
                obank = ps[6 + t["pair"] % 2]
                if t["first"]:
                    self.mm(obank.ap, zeros.ap[:, 0:128], zeros.ap, True, False, [zeros.t()], [obank.t()])
                for hl in range(2):
                    h = 2 * t["pair"] + hl
                    hs = slice(64 * hl, 64 * hl + 64)
                    self.mm(obank.ap[hs, c0:512], V.ap[:, t["kb"], 64 * h:64 * h + 64], Wb.ap[:, i % 3, hl, c0:512], False, t["last"],
                            [V.t(t["kb"]), Wb.t(i % 3)], [obank.t()], sig=(t["last"] and hl == 1))
                if t["last"]:
                    self.copy(DVE, oT.ap[:, t["pair"], :], obank.ap, [obank.t()], [oT.t(t["pair"])])

            for step in range(nt + 3):
                if step < nt:
                    stageA1(step)
                if 0 <= step - 2 < nt:
                    stageB(step - 2)
                if step < nt:
                    stageA2(step)
                if 0 <= step - 3 < nt:
                    stageC(step - 3)
            P.barrier()
            allo = [oT.t(p_) for p_ in range(8)]

            def cO(sv, ch):
                slot, wv = sv
                for m4 in range(4):
                    ft = 4 * ch + m4
                    bank = ps[4 + ft % 2]
                    for k in range(8):
                        self.mm(bank.ap, wv[:, k, 128 * m4:128 * m4 + 128], oT.ap[:, k, :], k == 0, k == 7, [slot.t()] + allo, [bank.t()])
                    self.stt(xT.ap[:, ft, ts_], bank.ap, adaT.ap[:, 48 + 16 + ft, s:s + 1], xT.ap[:, ft, ts_], ALU.mult, ALU.add,
                             [bank.t(), adaT.t(), xT.t(ft)], [xT.t(ft)])

            if qt < 3:
                self.cur_h_tok = htile.t()
                self.fm_norm(s, 3, qt + 1, htile.ap, sq, tmpf, ps[7])
            cO(wo_slots[0], 0)
            if qt < 3:
                wq0 = self.wload_k8(wq, 0)
            cO(wo_slots[1], 1)
            if qt < 3:
                wq_slots = [wq0, self.wload_k8(wq, 512)]

    def output_phase(self, s):
        P, ps, c = self.P, self.ps, self.c
        A = self.arena
        ost, _ = A.view(self.R_TAIL, [128, 2, 1024], F32, "ostage")
        xT = self.xT
        allx = [xT.t(ft) for ft in range(8)]
        n = 0
        for tb in range(16):
            ob = tb % 2
            for hf in range(2):
                bank = ps[n % 4]
                for jj in range(4):
                    ft = 4 * hf + jj
                    self.tr(bank.ap[:, 128 * jj:128 * jj + 128], xT.ap[:, ft, 128 * tb:128 * tb + 128], c["ident_f"].ap,
                            allx + [c["ident_f"].t()], [bank.t()], sig=(jj == 3))
                self.copy(ACT if n % 2 == 0 else DVE, ost.ap[:, ob, 512 * hf:512 * hf + 512], bank.ap, [bank.t()], [ost.t(ob)])
                n += 1
            otok = Tok(f"out{s}_{tb}")
            P.dma(SP, self.out[s, 128 * tb:128 * tb + 128, :], ost.ap[:, ob, :], reads=[ost.t(ob)], writes=[otok], tok=ost.t(ob))


def build_program(debug=None, stop_after=None, nseq=NS):
    b = Builder(debug=debug, stop_after=stop_after, nseq=nseq)
    nc = b.build()
    if b.P.nfwd:
        print("forward-redirected PE deps:", b.P.nfwd)
    return nc, b


_PARAM_NAMES = ["ada_w", "ada_b", "mix_norm_g", "mlp_norm_g", "mlp_w1", "mlp_w2", "s5_a_re", "s5_a_im", "s5_log_dt",
                "s5_b_re", "s5_b_im", "s5_c_re", "s5_c_im", "s5_d", "s5_w_glu", "kv_ada_w", "kv_ada_b", "kv_norm_g",
                "w_kv", "k_norm_g", "sb_w_q", "q_norm_g", "sb_w_o"]


def kernel(**inputs):
    x = np.ascontiguousarray(np.asarray(inputs["x"], dtype=np.float32))
    c = np.ascontiguousarray(np.asarray(inputs["c"], dtype=np.float32))
    params = {k: np.ascontiguousarray(np.asarray(inputs[k], dtype=np.float32)) for k in _PARAM_NAMES}
    nc, _ = build_program()
    in_maps = []
    for i in range(NCORES):
        m = dict(params)
        m["x"] = np.ascontiguousarray(x[NS * i:NS * i + NS])
        m["c"] = np.ascontiguousarray(c[NS * i:NS * i + NS])
        in_maps.append(m)
    res = run_bass_kernel_spmd(nc, in_maps, core_ids=list(range(NCORES)))
    out = np.concatenate([np.asarray(r["out"]) for r in res.results], axis=0)
    return out.astype(np.float32, copy=False)
```

```python
import math
from contextlib import ExitStack

import numpy as np
import concourse.bass as bass
import concourse.mybir as mybir
from concourse.bass_utils import run_bass_kernel_spmd

F32 = mybir.dt.float32
BF16 = mybir.dt.bfloat16
U8 = mybir.dt.uint8
AF = mybir.ActivationFunctionType
ALU = mybir.AluOpType
PE, DVE, ACT, POOL, SP = "tensor", "vector", "scalar", "gpsimd", "sync"
ENGS = [PE, DVE, ACT, POOL, SP]

D = 1024
T = 2048
NS = 2
FT = 8
TT = 512
NTT = T // TT
DFF = 4096
EPS = 1e-6
NCORES = 8
EPOCH_MAX = 12000


class Tok:
    __slots__ = ("w", "r", "sem", "semcnt", "name")

    def __init__(self, name=""):
        self.w = None
        self.r = []
        self.sem = None
        self.semcnt = 0
        self.name = name


class Op:
    __slots__ = ("eng", "fn", "deps", "dma", "sig", "semref", "semval", "inc", "id", "sigok")


class Prog:
    def __init__(self, nc, stack):
        self.nc = nc
        self.stack = stack
        self.ops = []
        self.last = {e: None for e in ENGS}
        self.barrier_deps = {e: [] for e in ENGS}
        self.dmas = []
        self.nsem = 0

    def new_sem(self, name):
        self.nsem += 1
        return self.stack.enter_context(self.nc.semaphore(f"{name}_{self.nsem}"))

    def op(self, eng, fn, reads=(), writes=(), dma=None, sigok=True):
        o = Op()
        o.sigok = sigok
        o.eng = eng
        o.fn = fn
        o.dma = dma
        o.sig = False
        o.semref = None
        o.semval = 0
        o.inc = 0
        o.id = len(self.ops)
        deps = {}

        def add(d, war=False):
            if d is None:
                return
            if d.dma is None and d.eng == eng:
                if eng == PE:
                    return
            deps[d.id] = d

        for t in reads:
            add(t.w)
        for t in writes:
            add(t.w)
            for r in t.r:
                add(r, war=True)
        for d in self.barrier_deps[eng]:
            add(d)
        self.barrier_deps[eng] = []
        o.deps = list(deps.values())
        for t in reads:
            t.r.append(o)
        for t in writes:
            t.w = o
            t.r = []
        self.ops.append(o)
        self.last[eng] = o
        if dma is not None:
            self.dmas.append(o)
        return o

    def dma(self, eng, out, in_, reads=(), writes=(), tok=None, **kw):
        if tok is None:
            tok = writes[0]

        def fn(e):
            return e.dma_start(out=out, in_=in_, **kw)
        return self.op(eng, fn, reads=reads, writes=writes, dma=tok)

    def barrier(self):
        deps = [o for o in self.last.values() if o is not None] + list(self.dmas)
        self.dmas = []
        for e in ENGS:
            self.barrier_deps[e] = list(self.barrier_deps[e]) + deps

    def emit(self):
        nc = self.nc
        pe_ops = [o for o in self.ops if o.eng == PE and o.dma is None]
        if pe_ops:
            pe_ops[-1].sigok = True
        nxt = {}
        cur = None
        for o in reversed(pe_ops):
            if o.sigok:
                cur = o
            nxt[o.id] = cur
        nfwd = 0
        for o in self.ops:
            nd = {}
            for d in o.deps:
                if d.eng == PE and d.dma is None and not d.sigok:
                    d = nxt[d.id]
                    if d.id > o.id:
                        nfwd += 1
                nd[d.id] = d
            o.deps = list(nd.values())
        self.nfwd = nfwd
        for o in self.ops:
            for d in o.deps:
                d.sig = True
        cnt = {e: 0 for e in ENGS}
        cursem = {e: None for e in ENGS}
        for o in self.ops:
            if o.dma is not None:
                t = o.dma
                if t.sem is None or t.semcnt >= 16 * 3000:
                    t.sem = self.new_sem("d")
                    t.semcnt = 0
                t.semcnt += 16
                o.semref = t.sem
                o.semval = t.semcnt
                o.inc = 16
            elif o.sig:
                e = o.eng
                if cursem[e] is None or cnt[e] >= EPOCH_MAX:
                    cursem[e] = self.new_sem("e" + e[:2])
                    cnt[e] = 0
                cnt[e] += 1
                o.semref = cursem[e]
                o.semval = cnt[e]
                o.inc = 1
        with nc.Block() as block:
            for e in ENGS:
                oplist = [o for o in self.ops if o.eng == e]

                def body(eh, oplist=oplist):
                    waited = {}
                    for o in oplist:
                        need = {}
                        for d in o.deps:
                            k = id(d.semref)
                            if k not in need or need[k][1] < d.semval:
                                need[k] = (d.semref, d.semval)
                        for k, (s, v) in need.items():
                            if waited.get(k, 0) < v:
                                eh.wait_ge(s, v)
                                waited[k] = v
                        if o.fn is not None:
                            inst = o.fn(eh)
                            if o.semref is not None:
                                inst.then_inc(o.semref, o.inc)

                getattr(block, e)(body)


class Buf:
    def __init__(self, ap, name=""):
        self.ap = ap
        self.name = name
        self.toks = {}

    def t(self, key=0):
        if key not in self.toks:
            self.toks[key] = Tok(f"{self.name}{key}")
        return self.toks[key]

    def ts(self, keys):
        return [self.t(k) for k in keys]


class Arena:
    def __init__(self, ap_u8):
        self.ap = ap_u8
        self.size = ap_u8.shape[1]

    def view(self, off, shape, dt, name=""):
        esz = 4 if dt == F32 else 2
        n = 1
        for s in shape[1:]:
            n *= s
        nbytes = n * esz
        assert off % 4 == 0 and off + nbytes <= self.size, (name, off, nbytes, self.size)
        v = self.ap[0:shape[0], off:off + nbytes].bitcast(dt)
        if len(shape) == 3:
            v = v.rearrange("p (a b) -> p a b", b=shape[2])
        elif len(shape) == 4:
            v = v.rearrange("p (a b c) -> p a b c", b=shape[2], c=shape[3])
        elif len(shape) == 5:
            v = v.rearrange("p (a b c d) -> p a b c d", b=shape[2], c=shape[3], d=shape[4])
        return Buf(v, name), off + nbytes


class Builder:
    def __init__(self, debug=None, stop_after=None, nseq=NS):
        self.debug = debug or []
        self.stop_after = stop_after
        self.nseq = nseq
        self.stack = ExitStack()
        self.nc = bass.Bass("TRN2", target_bir_lowering=False)
        self.P = Prog(self.nc, self.stack)
        self.dbg_out = {}

    def dram_in(self, name, shape):
        return self.nc.dram_tensor(name, list(shape), F32, kind="ExternalInput").ap()

    def sb(self, name, shape, dt):
        return self.stack.enter_context(self.nc.sbuf_tensor(name, list(shape), dt))

    def act(self, out, in_, func, reads, writes, bias=None, scale=None, accum_out=None):
        kw = {}
        if bias is not None:
            kw["bias"] = bias
        if scale is not None:
            kw["scale"] = scale
        if accum_out is not None:
            kw["accum_out"] = accum_out
        return self.P.op(ACT, lambda e: e.activation(out=out, in_=in_, func=func, **kw), reads, writes)

    def tt(self, eng, out, in0, in1, op, reads, writes):
        return self.P.op(eng, lambda e: e.tensor_tensor(out, in0, in1, op), reads, writes)

    def stt(self, out, in0, scalar, in1, op0, op1, reads, writes):
        return self.P.op(DVE, lambda e: e.scalar_tensor_tensor(out, in0, scalar, in1, op0, op1), reads, writes)

    def ts(self, eng, out, in0, s1, s2, op0, op1, reads, writes):
        if op1 is None:
            return self.P.op(eng, lambda e: e.tensor_scalar(out, in0, s1, None, op0), reads, writes)
        return self.P.op(eng, lambda e: e.tensor_scalar(out, in0, s1, s2, op0, op1), reads, writes)

    def copy(self, eng, out, in_, reads, writes):
        if eng == ACT:
            return self.P.op(ACT, lambda e: e.activation(out=out, in_=in_, func=AF.Identity), reads, writes)
        return self.P.op(eng, lambda e: e.tensor_copy(out, in_), reads, writes)

    def mm(self, out, lhsT, rhs, start, stop, reads, writes, sig=None):
        return self.P.op(PE, lambda e: e.matmul(out, lhsT=lhsT, rhs=rhs, start=start, stop=stop), reads, writes,
                         sigok=(stop if sig is None else sig))

    def tr(self, out, in_, ident, reads, writes, sig=True):
        return self.P.op(PE, lambda e: e.transpose(out, in_, ident), reads, writes, sigok=sig)

    def dbg(self, name, buf_ap, shape, reads, dt=F32):
        if name not in self.debug:
            return
        o = self.nc.dram_tensor("dbg_" + name, list(shape), dt, kind="ExternalOutput").ap()
        tok = Tok("dbg" + name)
        self.P.dma(SP, o, buf_ap, reads=reads, writes=[tok], tok=tok)
        self.dbg_out[name] = tok

    def build(self):
        nc, P = self.nc, self.P
        I = {}
        I["x"] = self.dram_in("x", [NS, T, D])
        I["c"] = self.dram_in("c", [NS, D])
        I["ada_w"] = self.dram_in("ada_w", [2, D, 6 * D])
        I["ada_b"] = self.dram_in("ada_b", [2, 6 * D])
        I["mix_norm_g"] = self.dram_in("mix_norm_g", [2, D])
        I["mlp_norm_g"] = self.dram_in("mlp_norm_g", [2, D])
        I["mlp_w1"] = self.dram_in("mlp_w1", [2, D, DFF])
        I["mlp_w2"] = self.dram_in("mlp_w2", [2, DFF, D])
        I["s5_a_re"] = self.dram_in("s5_a_re", [1, 64, 64])
        I["s5_a_im"] = self.dram_in("s5_a_im", [1, 64, 64])
        I["s5_log_dt"] = self.dram_in("s5_log_dt", [1, 64])
        I["s5_b_re"] = self.dram_in("s5_b_re", [1, 64, 64, 16])
        I["s5_b_im"] = self.dram_in("s5_b_im", [1, 64, 64, 16])
        I["s5_c_re"] = self.dram_in("s5_c_re", [1, 64, 16, 64])
        I["s5_c_im"] = self.dram_in("s5_c_im", [1, 64, 16, 64])
        I["s5_d"] = self.dram_in("s5_d", [1, D])
        I["s5_w_glu"] = self.dram_in("s5_w_glu", [1, D, 2 * D])
        I["kv_ada_w"] = self.dram_in("kv_ada_w", [D, 2 * D])
        I["kv_ada_b"] = self.dram_in("kv_ada_b", [2 * D])
        I["kv_norm_g"] = self.dram_in("kv_norm_g", [D])
        I["w_kv"] = self.dram_in("w_kv", [D, 2 * D])
        I["k_norm_g"] = self.dram_in("k_norm_g", [64])
        I["sb_w_q"] = self.dram_in("sb_w_q", [1, D, D])
        I["q_norm_g"] = self.dram_in("q_norm_g", [1, 64])
        I["sb_w_o"] = self.dram_in("sb_w_o", [1, D, D])
        self.I = I
        self.out = nc.dram_tensor("out", [NS, T, D], F32, kind="ExternalOutput").ap()
        self.s5w = nc.dram_tensor("s5w_scr", [32, 128, 768], BF16, kind="Internal").ap()
        self.s5w_tok = [Tok(f"s5w{i}") for i in range(8)]
        self.s5tab = nc.dram_tensor("s5tab_scr", [32, 128, 2, 256], F32, kind="Internal").ap()
        self.s5tab_tok = Tok("s5tab")
        self.modrow = nc.dram_tensor("modrow_scr", [NS, 2, D], F32, kind="Internal").ap()
        self.modrow_tok = Tok("modrow")

        self.consts()
        ARENA = 194 * 1024
        self.arena = Arena(self.sb("arena", [128, ARENA], U8)[:])
        self.psall = self.stack.enter_context(nc.psum_tensor("psall", [128, 4096], F32))
        self.ps = [Buf(self.psall[:, 512 * i:512 * i + 512], f"ps{i}") for i in range(8)]

        self.setup_phase()
        if self.stop_after is not None and (self.stop_after == "setup" or self.stop_after.startswith("s5") or self.stop_after == "ada"):
            return self.finish()
        for s in range(self.nseq):
            self.seq_pipeline(s)
        return self.finish()

    def finish(self):
        P = self.P
        outs = [o for o in P.ops if o.dma is not None]
        fin = P.op(SP, None)
        fin.deps = list({o.id: o for o in outs}.values())
        P.emit()
        return self.nc

    def consts(self):
        P = self.P
        c = {}

        def mk(name, shape, dt):
            c[name] = Buf(self.sb("c_" + name, shape, dt)[:], name)
            return c[name]

        ident_f = mk("ident_f", [128, 128], F32)
        ident_b = mk("ident_b", [128, 128], BF16)
        onesm_b = mk("onesm_b", [128, 128], BF16)
        blk_b = mk("blk_b", [128, 128], BF16)
        negtri_b = mk("negtri_b", [128, 128], BF16)
        negones_b = mk("negones_b", [128, 128], BF16)
        negmask_b = mk("negmask_b", [128, 128], BF16)
        zeros_b = mk("zeros_b", [128, 64], BF16)
        onesrow_f = mk("onesrow_f", [1, 128], F32)
        sel2 = mk("sel2", [2, 128], F32)
        bmask_f = mk("bmask_f", [128, 128], F32)
        tmp_f = mk("ctmp_f", [128, 128], F32)
        hm = mk("hm", [128, 2], F32)
        zeros512_b = mk("zeros512_b", [128, 512], BF16)

        def pool(fn, reads, writes):
            return P.op(POOL, fn, reads, writes)

        pool(lambda e: e.memset(ident_f.ap, 1.0), [], [ident_f.t()])
        pool(lambda e: e.affine_select(out=ident_f.ap, in_=ident_f.ap, pattern=[[-1, 128]], compare_op=ALU.is_equal,
                                       fill=0.0, base=0, channel_multiplier=1), [ident_f.t()], [ident_f.t()])
        pool(lambda e: e.tensor_copy(ident_b.ap, ident_f.ap), [ident_f.t()], [ident_b.t()])
        pool(lambda e: e.memset(onesm_b.ap, 1.0 / 1024.0), [], [onesm_b.t()])
        pool(lambda e: e.memset(blk_b.ap, 0.0), [], [blk_b.t()])
        pool(lambda e: e.memset(blk_b.ap[0:64, 0:64], 1.0 / 64.0), [blk_b.t()], [blk_b.t()])
        pool(lambda e: e.memset(blk_b.ap[64:128, 64:128], 1.0 / 64.0), [blk_b.t()], [blk_b.t()])
        pool(lambda e: e.memset(tmp_f.ap, -1.0), [], [tmp_f.t()])
        pool(lambda e: e.affine_select(out=tmp_f.ap, in_=tmp_f.ap, pattern=[[-1, 128]], compare_op=ALU.is_ge,
                                       fill=0.0, base=0, channel_multiplier=1), [tmp_f.t()], [tmp_f.t()])
        pool(lambda e: e.tensor_copy(negtri_b.ap, tmp_f.ap), [tmp_f.t()], [negtri_b.t()])
        pool(lambda e: e.memset(negones_b.ap, -1.0), [], [negones_b.t()])
        pool(lambda e: e.tensor_scalar(negmask_b.ap, negtri_b.ap, 30000.0, None, ALU.mult), [negtri_b.t()], [negmask_b.t()])
        pool(lambda e: e.memset(zeros_b.ap, 0.0), [], [zeros_b.t()])
        pool(lambda e: e.memset(onesrow_f.ap, 1.0), [], [onesrow_f.t()])
        pool(lambda e: e.memset(sel2.ap, 1.0), [], [sel2.t()])
        pool(lambda e: e.affine_select(out=sel2.ap, in_=sel2.ap, pattern=[[1, 128]], compare_op=ALU.is_ge,
                                       fill=0.0, base=0, channel_multiplier=-64), [sel2.t()], [sel2.t()])
        pool(lambda e: e.affine_select(out=sel2.ap, in_=sel2.ap, pattern=[[-1, 128]], compare_op=ALU.is_ge,
                                       fill=0.0, base=63, channel_multiplier=64), [sel2.t()], [sel2.t()])
        pool(lambda e: e.memset(bmask_f.ap, 1.0), [], [bmask_f.t()])
        pool(lambda e: e.affine_select(out=bmask_f.ap.rearrange("p (i h) -> p i h", h=16), in_=bmask_f.ap.rearrange("p (i h) -> p i h", h=16),
                                       pattern=[[16, 8], [0, 16]], compare_op=ALU.is_ge,
                                       fill=0.0, base=15, channel_multiplier=-1), [bmask_f.t()], [bmask_f.t()])
        pool(lambda e: e.memset(zeros512_b.ap, 0.0), [], [zeros512_b.t()])
        pool(lambda e: e.memset(hm.ap, 0.0), [], [hm.t()])
        pool(lambda e: e.memset(hm.ap[0:64, 0:1], 1.0), [hm.t()], [hm.t()])
        pool(lambda e: e.memset(hm.ap[64:128, 1:2], 1.0), [hm.t()], [hm.t()])
        self.c = c
        self.VT = Buf(self.sb("VT", [128, 160], F32)[:], "VT")
        self.adaT = Buf(self.sb("adaT", [128, 112, 2], F32)[:], "adaT")
        self.coef = Buf(self.sb("coef", [128, 32, 8, 3], F32)[:], "coef")
        self.modsc = Buf(self.sb("modsc", [128, NS, 5, 2, 8], F32)[:], "modsc")
        self.rho = Buf(self.sb("rho", [128, 32], F32)[:], "rho")
        self.qkg = Buf(self.sb("qkg", [128, 2], F32)[:], "qkg")
        self.epsc = Buf(self.sb("epsc", [128, 1], F32)[:], "epsc")
        self.onec = Buf(self.sb("onec", [128, 1], F32)[:], "onec")
        pool(lambda e: e.memset(self.epsc.ap, EPS), [], [self.epsc.t()])
        pool(lambda e: e.memset(self.onec.ap, 1.0), [], [self.onec.t()])

    def ring_init(self, off, nslots=2):
        self.ring = []
        for i in range(nslots):
            b, off = self.arena.view(off, [128, 4096], BF16, f"ring{i}")
            self.ring.append(b)
        self.ring_i = 0
        return off

    def wload(self, srcs):
        slot = self.ring[self.ring_i % len(self.ring)]
        self.ring_i += 1
        for dstf, src in srcs:
            self.P.dma(POOL, dstf(slot.ap), src, reads=[], writes=[slot.t()], tok=slot.t())
        return slot

    def wload_k8(self, w2d, col0, ncols=512):
        src = w2d.rearrange("(k p) n -> p k n", p=128)[:, :, col0:col0 + ncols]
        slot = self.wload([(lambda a: a[:, 0:8 * ncols].rearrange("p (k n) -> p k n", n=ncols), src)])
        return slot, slot.ap[:, 0:8 * ncols].rearrange("p (k n) -> p k n", n=ncols)

    def setup_phase(self):
        P, I, c = self.P, self.I, self.c
        A = self.arena
        ps = self.ps
        off = 0
        off = self.ring_init(off, 2)
        self.ring_end = off
        off = A.size - 42 * 1024
        self.s5_limit = off
        vrA, off = A.view(off, [128, 128], F32, "vrA")
        vrB, off = A.view(off, [128, 128], F32, "vrB")
        cs, off = A.view(off, [2, 1024], F32, "cs")
        sT, off = A.view(off, [128, 8, 2], BF16, "sT")
        rowb = []
        for b in range(2):
            r, off = A.view(off, [1, 2048], F32, f"rowb{b}")
            rowb.append(r)
        biasrow, off = A.view(off, [1, 2048], F32, "biasrow")
        grow, off = A.view(off, [1, 1024], F32, "grow")
        abrow, off = A.view(off, [1, 2, 1024], F32, "abrow")
        ld = Tok("setup_ld")

        def ldma(out, in_, wtoks):
            P.dma(SP, out, in_, reads=[], writes=wtoks)

        ldma(vrA.ap[0:16, :], I["mix_norm_g"].rearrange("l (k p) -> (l k) p", p=128), [vrA.t()])
        ldma(vrA.ap[16:32, :], I["mlp_norm_g"].rearrange("l (k p) -> (l k) p", p=128), [vrA.t()])
        ldma(vrA.ap[32:40, :], I["kv_norm_g"].rearrange("(k p) -> k p", p=128), [vrA.t()])
        adab = I["ada_b"].rearrange("l (k p) -> (l k) p", p=128)
        ldma(vrA.ap[40:128, :], adab[0:88, :], [vrA.t()])
        ldma(vrB.ap[0:8, :], adab[88:96, :], [vrB.t()])
        ldma(vrB.ap[8:24, :], I["kv_ada_b"].rearrange("(k p) -> k p", p=128), [vrB.t()])
        for hh in range(2):
            ldma(vrB.ap[24:25, 64 * hh:64 * hh + 64], I["q_norm_g"], [vrB.t()])
            ldma(vrB.ap[25:26, 64 * hh:64 * hh + 64], I["k_norm_g"].rearrange("(o d) -> o d", o=1), [vrB.t()])
        ldma(cs.ap, I["c"], [cs.t()])
        ldma(biasrow.ap, I["ada_b"][0:1, 0:2048], [biasrow.t()])
        ldma(grow.ap, I["mix_norm_g"][0:1, :], [grow.t()])
        VT = self.VT
        self.tr(ps[0].ap[:, 0:128], vrA.ap, c["ident_f"].ap, [vrA.t(), c["ident_f"].t()], [ps[0].t()])
        self.tr(ps[0].ap[:, 128:154], vrB.ap[0:26, :], c["ident_f"].ap[0:26, 0:26], [vrB.t(), c["ident_f"].t()], [ps[0].t()])
        self.copy(DVE, VT.ap[:, 0:154], ps[0].ap[:, 0:154], [ps[0].t()], [VT.t()])
        self.ts(DVE, self.qkg.ap[:, 0:1], VT.ap[:, 152:153], 0.125, None, ALU.mult, None, [VT.t()], [self.qkg.t()])
        self.copy(DVE, self.qkg.ap[:, 1:2], VT.ap[:, 153:154], [VT.t()], [self.qkg.t()])
        self.act(cs.ap, cs.ap, AF.Silu, [cs.t()], [cs.t()])
        for k in range(8):
            self.tr(ps[1].ap[:, 2 * k:2 * k + 2], cs.ap[0:2, 128 * k:128 * k + 128], c["ident_f"].ap[0:2, 0:2],
                    [cs.t(), c["ident_f"].t()], [ps[1].t()])
        self.copy(DVE, sT.ap, ps[1].ap[:, 0:16].rearrange("p (k b) -> p k b", b=2), [ps[1].t()], [sT.t()])
        s5_rest = self.s5_setup_early()
        adaps = ps[5]
        chunks = [(I["ada_w"][0], j * 512) for j in range(12)] + [(I["ada_w"][1], j * 512) for j in range(12)] + \
                 [(I["kv_ada_w"], j * 512) for j in range(4)]
        for ci, (w2d, col0) in enumerate(chunks):
            slot, wv = self.wload_k8(w2d, col0)
            for mt in range(4):
                ot = ci * 4 + mt
                for k in range(8):
                    self.mm(adaps.ap[:, 2 * ot:2 * ot + 2], wv[:, k, 128 * mt:128 * mt + 128], sT.ap[:, k, :],
                            k == 0, k == 7, [slot.t(), sT.t()], [adaps.t()])
            if ci < 4:
                for b in range(2):
                    rp = ps[6 + b]
                    for k in range(8):
                        self.mm(rp.ap[0:1, :], sT.ap[:, k, b:b + 1], wv[:, k, :], k == 0, k == 7, [slot.t(), sT.t()], [rp.t()])
                    self.copy(ACT, rowb[b].ap[0:1, 512 * ci:512 * ci + 512], rp.ap[0:1, :], [rp.t()], [rowb[b].t()])
        self.tt(DVE, self.adaT.ap, adaps.ap[:, 0:224].rearrange("p (o b) -> p o b", b=2),
                VT.ap[:, 40:152].unsqueeze(2).to_broadcast([128, 112, 2]),
                ALU.add, [adaps.t(), VT.t()], [self.adaT.t()])
        adaT = self.adaT
        norms = {1: (16, 24, 32), 2: (32, 96, 104), 3: (8, 48, 56), 4: (24, 72, 80)}
        for s in range(2):
            for n, (gc, sh, sc) in norms.items():
                self.stt(self.modsc.ap[:, s, n, 0, :], adaT.ap[:, sc:sc + 8, s], 1.0, VT.ap[:, gc:gc + 8], ALU.add, ALU.mult,
                         [adaT.t(), VT.t()], [self.modsc.t()])
                self.copy(DVE, self.modsc.ap[:, s, n, 1, :], adaT.ap[:, sh:sh + 8, s], [adaT.t()], [self.modsc.t()])
        for b in range(2):
            self.tt(DVE, rowb[b].ap, rowb[b].ap, biasrow.ap, ALU.add, [rowb[b].t(), biasrow.t()], [rowb[b].t()])
            self.stt(abrow.ap[0:1, 0, :], rowb[b].ap[0:1, 1024:2048], 1.0, grow.ap, ALU.add, ALU.mult,
                     [rowb[b].t(), grow.t()], [abrow.t()])
            self.copy(DVE, abrow.ap[0:1, 1, :], rowb[b].ap[0:1, 0:1024], [rowb[b].t()], [abrow.t()])
            P.dma(SP, self.modrow[b:b + 1], abrow.ap, reads=[abrow.t()], writes=[self.modrow_tok], tok=self.modrow_tok)
        self.dbg("adaT", self.adaT.ap, [128, 112, 2], [self.adaT.t()])
        self.dbg("modsc", self.modsc.ap, [128, NS, 5, 2, 8], [self.modsc.t()])
        self.dbg("VT", self.VT.ap, [128, 160], [self.VT.t()])
        if self.stop_after == "ada":
            return
        s5_rest()
        P.barrier()

    def s5_setup_early(self):
        P, I, c = self.P, self.I, self.c
        A = self.arena
        ps = self.ps
        off = self.ring_end
        lamrows, off = A.view(off, [32, 2, 128], F32, "lamrows")
        ldt2, off = A.view(off, [2, 32], F32, "ldt2")
        NSM = 24
        smb, off = A.view(off, [128, NSM, 32], F32, "sm")
        Bq, off = A.view(off, [128, 2, 32, 16], F32, "Bq")
        Bb, off = A.view(off, [128, 2, 32, 16], F32, "Bb")
        Cin, off = A.view(off, [128, 2, 128], F32, "Cin")
        CT, off = A.view(off, [128, 2, 32, 16], F32, "CT")
        X, off = A.view(off, [128, 2, 32, 8, 16], F32, "X")
        Wo, off = A.view(off, [128, 2, 32, 8, 16], F32, "Wo")
        We, off = A.view(off, [128, 2, 32, 128], F32, "We")
        tmp, off = A.view(off, [128, 2, 512], F32, "s5tmp")
        Drows, off = A.view(off, [64, 16], F32, "Drows")
        Drep, off = A.view(off, [64, 8, 16], F32, "Drep")
        dcol, off = A.view(off, [128, 64], F32, "dcol")
        tmpm, off = A.view(off, [128, 2, 2, 128], F32, "tmpm")
        stage, off = A.view(off, [128, 2, 4, 768], BF16, "stage")
        WoM, off = A.view(off, [128, 2, 2, 2, 128], F32, "WoM")
        assert off <= self.s5_limit, (off, self.s5_limit)
        identf = c["ident_f"]

        names = ["lre", "lim", "dt", "mag", "th", "c", "s", "t1", "t2", "t3", "abre", "abim", "nr", "den",
                 "fre", "fim", "rm2", "lire", "liim", "mure", "muim", "w2r", "w2i"]
        sm = {n: (smb.ap[:, i, :], smb.t(n)) for i, n in enumerate(names)}

        def S(n):
            return sm[n][0]

        def St(n):
            return sm[n][1]

        P.dma(SP, lamrows.ap[:, 0, :], I["s5_a_re"][0].rearrange("(a g) p -> a (g p)", g=2), writes=[lamrows.t()])
        P.dma(SP, lamrows.ap[:, 1, :], I["s5_a_im"][0].rearrange("(a g) p -> a (g p)", g=2), writes=[lamrows.t()])
        P.dma(SP, ldt2.ap, I["s5_log_dt"][0].rearrange("(a g) -> g a", g=2), writes=[ldt2.t()], allow_slow_non_contiguous=True)
        for ri, nm in enumerate(["s5_b_re", "s5_b_im"]):
            src = I[nm][0].rearrange("(a g) p h -> (g p) a h", g=2)
            for q4 in range(4):
                P.dma(SP, Bq.ap[:, ri, 8 * q4:8 * q4 + 8, :], src[:, 8 * q4:8 * q4 + 8, :], writes=[Bq.t()])
        P.dma(SP, Drows.ap, I["s5_d"][0].rearrange("(g h) -> g h", h=16), writes=[Drows.t()])

        self.tr(ps[0].ap[:, 0:32], lamrows.ap[:, 0, :], identf.ap[0:32, 0:32], [lamrows.t(), identf.t()], [ps[0].t()])
        self.tr(ps[0].ap[:, 32:64], lamrows.ap[:, 1, :], identf.ap[0:32, 0:32], [lamrows.t(), identf.t()], [ps[0].t()])
        self.mm(ps[0].ap[:, 64:96], c["sel2"].ap, ldt2.ap, True, True, [c["sel2"].t(), ldt2.t()], [ps[0].t()])
        self.copy(DVE, S("lre"), ps[0].ap[:, 0:32], [ps[0].t()], [St("lre")])
        self.copy(DVE, S("lim"), ps[0].ap[:, 32:64], [ps[0].t()], [St("lim")])
        self.act(S("dt"), ps[0].ap[:, 64:96], AF.Exp, [ps[0].t()], [St("dt")])

        def tt(out, a, b, op, eng=DVE):
            self.tt(eng, S(out), S(a), S(b), op, [St(a), St(b)], [St(out)])

        if self.stop_after == "s5a":
            return

        def horner(out, y, coefs, last):
            self.ts(DVE, S(out), S(y), coefs[0], None, ALU.mult, None, [St(y)], [St(out)])
            for ck in coefs[1:]:
                self.stt(S(out), S(out), ck, S(y), ALU.add, ALU.mult, [St(out), St(y)], [St(out)])
            self.ts(DVE, S(out), S(out), last, None, ALU.add, None, [St(out)], [St(out)])

        tt("t1", "lre", "dt", ALU.mult)
        self.ts(DVE, S("t2"), S("t1"), 0.25, None, ALU.mult, None, [St("t1")], [St("t2")])
        horner("mag", "t2", [1.0 / 5040, 1.0 / 720, 1.0 / 120, 1.0 / 24, 1.0 / 6, 0.5, 1.0], 1.0)
        tt("mag", "mag", "mag", ALU.mult)
        tt("mag", "mag", "mag", ALU.mult)
        tt("th", "lim", "dt", ALU.mult)
        self.ts(DVE, S("t3"), S("th"), 1.0 / 32.0, None, ALU.mult, None, [St("th")], [St("t3")])
        tt("t2", "t3", "t3", ALU.mult)
        horner("c", "t2", [-1.0 / 3628800, 1.0 / 40320, -1.0 / 720, 1.0 / 24, -0.5], 1.0)
        horner("s", "t2", [1.0 / 362880, -1.0 / 5040, 1.0 / 120, -1.0 / 6], 1.0)
        tt("s", "s", "t3", ALU.mult)

        def csquare(a, b):
            tt("t1", a, a, ALU.mult)
            tt("t2", b, b, ALU.mult)
            self.stt(S(b), S(a), 2.0, S(b), ALU.mult, ALU.mult, [St(a), St(b)], [St(b)])
            tt(a, "t1", "t2", ALU.subtract)

        for _ in range(5):
            csquare("c", "s")
        tt("t1", "c", "c", ALU.mult)
        tt("t2", "s", "s", ALU.mult)
        tt("t1", "t1", "t2", ALU.add)
        self.ts(DVE, S("t1"), S("t1"), -0.5, 1.5, ALU.mult, ALU.add, [St("t1")], [St("t1")])
        tt("c", "c", "t1", ALU.mult)
        tt("s", "s", "t1", ALU.mult)
        tt("abre", "mag", "c", ALU.mult)
        tt("abim", "mag", "s", ALU.mult)
        self.ts(DVE, S("nr"), S("abre"), -1.0, None, ALU.add, None, [St("abre")], [St("nr")])
        tt("t1", "lre", "lre", ALU.mult)
        tt("t2", "lim", "lim", ALU.mult)
        tt("den", "t1", "t2", ALU.add)
        P.op(DVE, lambda e: e.reciprocal(S("den"), S("den")), [St("den")], [St("den")])
        tt("t1", "nr", "lre", ALU.mult)
        tt("t2", "abim", "lim", ALU.mult)
        tt("t1", "t1", "t2", ALU.add)
        tt("fre", "t1", "den", ALU.mult)
        tt("t1", "abim", "lre", ALU.mult)
        tt("t2", "nr", "lim", ALU.mult)
        tt("t1", "t1", "t2", ALU.subtract)
        tt("fim", "t1", "den", ALU.mult)
        tt("rm2", "mag", "mag", ALU.mult)
        P.op(DVE, lambda e: e.reciprocal(S("rm2"), S("rm2")), [St("rm2")], [St("rm2")])
        tt("lire", "abre", "rm2", ALU.mult)
        self.stt(S("liim"), S("abim"), -1.0, S("rm2"), ALU.mult, ALU.mult, [St("abim"), St("rm2")], [St("liim")])
        self.copy(DVE, S("mure"), S("abre"), [St("abre")], [St("mure")])
        self.copy(DVE, S("muim"), S("abim"), [St("abim")], [St("muim")])
        for _ in range(3):
            csquare("mure", "muim")
        coef = self.coef
        self.copy(DVE, S("w2r"), S("mure"), [St("mure")], [St("w2r")])
        self.copy(DVE, S("w2i"), S("muim"), [St("muim")], [St("w2i")])
        for k in range(8):
            self.copy(DVE, coef.ap[:, :, k, 0], S("w2r"), [St("w2r")], [coef.t()])
            self.copy(DVE, coef.ap[:, :, k, 1], S("w2i"), [St("w2i")], [coef.t()])
            self.ts(DVE, coef.ap[:, :, k, 2], S("w2i"), -1.0, None, ALU.mult, None, [St("w2i")], [coef.t()])
            if k < 7:
                csquare("w2r", "w2i")

        if self.stop_after == "s5b":
            return
        for _ in range(3):
            tt("mag", "mag", "mag", ALU.mult)
        self.copy(DVE, self.rho.ap, S("mag"), [St("mag")], [self.rho.t()])
        for _ in range(3):
            csquare("c", "s")
        tt("t1", "c", "c", ALU.mult)
        tt("t2", "s", "s", ALU.mult)
        tt("t1", "t1", "t2", ALU.add)
        self.ts(DVE, S("t1"), S("t1"), -0.5, 1.5, ALU.mult, ALU.add, [St("t1")], [St("t1")])
        tt("c", "c", "t1", ALU.mult)
        tt("s", "s", "t1", ALU.mult)
        tmpt = [tmp.t(i) for i in range(2)]

        def cmul(ore, oim, otoks, are, aim, atoks, zre, zim, ztoks, n):
            ab = lambda nm: S(nm).unsqueeze(2).to_broadcast([128, 32, n])
            t = [tmp.ap[:, i, 0:32 * n].rearrange("p (a h) -> p a h", h=n) for i in range(2)]
            rd = atoks + ztoks
            self.tt(DVE, t[0], zre, ab(are), ALU.mult, rd, [tmpt[0]])
            self.tt(DVE, t[1], zim, ab(aim), ALU.mult, rd, [tmpt[1]])
            self.tt(DVE, ore, t[0], t[1], ALU.subtract, [tmpt[0], tmpt[1]], otoks)
            self.tt(DVE, t[0], zim, ab(are), ALU.mult, rd, [tmpt[0]])
            self.tt(DVE, t[1], zre, ab(aim), ALU.mult, rd, [tmpt[1]])
            self.tt(DVE, oim, t[0], t[1], ALU.add, [tmpt[0], tmpt[1]], otoks)

        fa = [St("fre"), St("fim")]
        cmul(Bb.ap[:, 0], Bb.ap[:, 1], [Bb.t()], "fre", "fim", fa, Bq.ap[:, 0], Bq.ap[:, 1], [Bq.t()], 16)
        la = [St("lire"), St("liim")]
        for j in range(8):
            if j == 0:
                zr, zi, zt = Bb.ap[:, 0], Bb.ap[:, 1], [Bb.t()]
            else:
                zr, zi, zt = X.ap[:, 0, :, j - 1, :], X.ap[:, 1, :, j - 1, :], [X.t(j - 1)]
            cmul(X.ap[:, 0, :, j, :], X.ap[:, 1, :, j, :], [X.t(j)], "lire", "liim", la, zr, zi, zt, 16)
        if self.stop_after == "s5c":
            return
        for r in range(8):
            for ri, nm in enumerate(["s5_c_re", "s5_c_im"]):
                db = (2 * r + ri) % 2
                P.dma(SP, Cin.ap[:, db, 0:64], I[nm][0][8 * r:8 * r + 8].rearrange("g h p -> (g h) p"), writes=[Cin.t(db)])
                pb = ps[1 + db]
                self.tr(pb.ap[0:64, 0:128], Cin.ap[:, db, 0:64], identf.ap, [Cin.t(db), identf.t()], [pb.t()])
                src = pb.ap[0:64, 0:128].rearrange("p (a g h) -> p a g h", g=2, h=16)
                self.copy(ACT, CT.ap[0:64, ri, 4 * r:4 * r + 4, :], src[:, :, 0, :], [pb.t()], [CT.t()])
                self.copy(ACT, CT.ap[64:128, ri, 4 * r:4 * r + 4, :], src[:, :, 1, :], [pb.t()], [CT.t()])
        if self.stop_after == "s5d":
            return
        ab_ = [St("abre"), St("abim")]
        for i in range(8):
            if i == 0:
                zr, zi, zt = CT.ap[:, 0], CT.ap[:, 1], [CT.t()]
            else:
                zr, zi, zt = Wo.ap[:, 0, :, i - 1, :], Wo.ap[:, 1, :, i - 1, :], [Wo.t(i - 1)]
            cmul(Wo.ap[:, 0, :, i, :], Wo.ap[:, 1, :, i, :], [Wo.t(i)], "abre", "abim", ab_, zr, zi, zt, 16)
        ma = [St("mure"), St("muim")]
        for j in range(8):
            cmul(We.ap[:, 0, :, 16 * j:16 * j + 16], We.ap[:, 1, :, 16 * j:16 * j + 16], [We.t(j)], "mure", "muim", ma,
                 X.ap[:, 0, :, j, :], X.ap[:, 1, :, j, :], [X.t(j)], 16)
        allWo = [Wo.t(i) for i in range(8)]
        WoN = Wo.t("neg")
        self.ts(DVE, Wo.ap[:, 1], Wo.ap[:, 1], -1.0, None, ALU.mult, None, allWo, allWo + [WoN])
        def rest():
            self.copy(DVE, Drep.ap, Drows.ap.unsqueeze(1).to_broadcast([64, 8, 16]), [Drows.t()], [Drep.t()])
            self.tr(ps[3].ap[:, 0:64], Drep.ap.rearrange("p j h -> p (j h)"), identf.ap[0:64, 0:64], [Drep.t(), identf.t()], [ps[3].t()])
            self.copy(DVE, dcol.ap, ps[3].ap[:, 0:64], [ps[3].t()], [dcol.t()])
            self.s5_setup_pairs(locals_=dict(We=We, X=X, Wo=Wo, WoN=WoN, allWo=allWo, stage=stage, WoM=WoM, tmpm=tmpm, dcol=dcol, identf=identf))
            P.barrier()
            Tr = Buf(X.ap.rearrange("p r a j h -> p (r a j h)").rearrange("p (a c) -> p a c", c=256), "Tr")
            Ti = Buf(Wo.ap.rearrange("p r a j h -> p (r a j h)").rearrange("p (a c) -> p a c", c=256), "Ti")
            tb = [Buf(We.ap[:, i].rearrange("p a n -> p (a n)"), f"tbig{i}") for i in range(2)]
            P.op(DVE, lambda e: e.memset(Tr.ap[:, :, 0:1], 1.0), [], [Tr.t()])
            P.op(DVE, lambda e: e.memset(Ti.ap[:, :, 0:1], 0.0), [], [Ti.t()])
            self.copy(DVE, S("w2r"), S("c"), [St("c")], [St("w2r")])
            self.copy(DVE, S("w2i"), S("s"), [St("s")], [St("w2i")])
            for k in range(8):
                n = 1 << k
                wrb = S("w2r").unsqueeze(2).to_broadcast([128, 32, n])
                wib = S("w2i").unsqueeze(2).to_broadcast([128, 32, n])
                t0 = tb[0].ap[:, 0:32 * n].rearrange("p (a c) -> p a c", c=n)
                t1 = tb[1].ap[:, 0:32 * n].rearrange("p (a c) -> p a c", c=n)
                wt = [St("w2r"), St("w2i")]
                self.tt(DVE, t0, Tr.ap[:, :, 0:n], wrb, ALU.mult, [Tr.t()] + wt, [tb[0].t()])
                self.tt(DVE, t1, Ti.ap[:, :, 0:n], wib, ALU.mult, [Ti.t()] + wt, [tb[1].t()])
                self.tt(DVE, Tr.ap[:, :, n:2 * n], t0, t1, ALU.subtract, [tb[0].t(), tb[1].t()], [Tr.t()])
                self.tt(DVE, t0, Ti.ap[:, :, 0:n], wrb, ALU.mult, [Ti.t()] + wt, [tb[0].t()])
                self.tt(DVE, t1, Tr.ap[:, :, 0:n], wib, ALU.mult, [Tr.t()] + wt, [tb[1].t()])
                self.tt(DVE, Ti.ap[:, :, n:2 * n], t0, t1, ALU.add, [tb[0].t(), tb[1].t()], [Ti.t()])
                if k < 7:
                    csquare("w2r", "w2i")
            P.dma(SP, self.s5tab[:, :, 0, :].rearrange("a p c -> p a c"), Tr.ap, reads=[Tr.t()], writes=[self.s5tab_tok])
            P.dma(SP, self.s5tab[:, :, 1, :].rearrange("a p c -> p a c"), Ti.ap, reads=[Ti.t()], writes=[self.s5tab_tok])
            self.dbg("Tr", Tr.ap, [128, 32, 256], [Tr.t()])
            self.dbg("Ti", Ti.ap, [128, 32, 256], [Ti.t()])
            self.dbg("rho", self.rho.ap, [128, 32], [self.rho.t()])
        return rest

    def s5_setup_pairs(self, locals_):
        P, c, ps = self.P, self.c, self.ps
        We, X, Wo, WoN, allWo, stage, WoM, tmpm, dcol, identf = (locals_[k] for k in
                                                                ["We", "X", "Wo", "WoN", "allWo", "stage", "WoM", "tmpm", "dcol", "identf"])
        allWe = [We.t(j) for j in range(8)]
        allX = [X.t(j) for j in range(8)]

        def phase1(pr):
            slot = (pr // 4) % 2
            p4 = pr % 4
            st_ = stage.t(slot)
            pa = ps[4 + pr % 2]
            pb = ps[6 + pr % 2]
            self.tr(pa.ap[:, 0:128], We.ap[:, 0, pr, :], identf.ap, allWe + [identf.t()], [pa.t()])
            self.tr(pa.ap[:, 128:256], We.ap[:, 1, pr, :], identf.ap, allWe + [identf.t()], [pa.t()])
            self.copy(ACT, stage.ap[:, slot, p4, 0:256], pa.ap[:, 0:256], [pa.t()], [st_])
            self.copy(ACT, stage.ap[:, slot, p4, 256:512].rearrange("p (r n) -> p r n", r=2),
                      Wo.ap[:, :, pr].rearrange("p r i h -> p r (i h)"), [WoN], [st_])
            wm = WoM.ap[:, pr % 2]
            for gl in range(2):
                self.ts(POOL if gl == 0 else DVE, wm[:, gl], Wo.ap[:, :, pr].rearrange("p r i h -> p r (i h)"), c["hm"].ap[:, gl:gl + 1], None, ALU.mult, None,
                        [WoN, c["hm"].t()], [WoM.t(pr % 2)])
            for gl in range(2):
                for ri in range(2):
                    self.mm(pb.ap[:, 128 * gl:128 * gl + 128], X.ap[:, ri, pr].rearrange("p j h -> p (j h)"),
                            wm[:, gl, ri, :], ri == 0, ri == 1, allX + [WoM.t(pr % 2)], [pb.t()])

        def phase2(pr):
            slot = (pr // 4) % 2
            p4 = pr % 4
            st_ = stage.t(slot)
            pb = ps[6 + pr % 2]
            tm = tmpm.ap[:, pr % 2]
            self.tt(DVE, tm, pb.ap[:, 0:256].rearrange("p (g n) -> p g n", g=2),
                    c["bmask_f"].ap.unsqueeze(1).to_broadcast([128, 2, 128]), ALU.mult, [pb.t(), c["bmask_f"].t()], [tmpm.t(pr % 2)])
            for gl in range(2):
                g = 2 * pr + gl
                self.stt(stage.ap[:, slot, p4, 512 + 128 * gl:512 + 128 * gl + 128], identf.ap, dcol.ap[:, g:g + 1], tm[:, gl, :],
                         ALU.mult, ALU.add, [identf.t(), dcol.t(), tmpm.t(pr % 2)], [st_])
            if p4 == 3:
                ftc = pr // 4
                P.dma(SP, self.s5w[4 * ftc:4 * ftc + 4].rearrange("a p n -> p a n"), stage.ap[:, slot], reads=[st_],
                      writes=[self.s5w_tok[ftc]])
                if ftc == 0:
                    self.dbg("s5w0", stage.ap[:, slot], [128, 4, 768], [st_], dt=BF16)
        phase1(0)
        for pr in range(32):
            if pr + 1 < 32:
                phase1(pr + 1)
            phase2(pr)
        self.dbg("coef", self.coef.ap, [128, 32, 8, 3], [self.coef.t()])

    R_RSTD = 16384
    R_XT = 18432
    R_ACTT = R_XT + 65536
    R_TAIL = R_ACTT + 32768

    def seq_pipeline(self, s):
        P = self.P
        A = self.arena
        self.xT, _ = A.view(self.R_XT, [128, 8, 2048], F32, f"xT{s}")
        self.actT, _ = A.view(self.R_ACTT, [128, 8, 2048], BF16, f"actT{s}")
        self.rstd, _ = A.view(self.R_RSTD, [128, 512], F32, f"rstd{s}")
        self.s5_prep(s)
        P.barrier()
        self.s5_core(s)
        P.barrier()
        self.dbg(f"gT{s}", self.actT.ap, [128, 8, 2048], [], dt=BF16)
        self.dbg(f"xT{s}", self.xT.ap, [128, 8, 2048], [])
        if self.stop_after == "xreload":
            return
        self.glu(s)
        P.barrier()
        self.mlp(s, 0, 1)
        P.barrier()
        self.dbg(f"x1T{s}", self.xT.ap, [128, 8, 2048], [])
        if self.stop_after == "layer0":
            return
        self.kv_phase(s)
        P.barrier()
        self.dbg(f"KT{s}", self.KT.ap, [128, 8, 2048], [], dt=BF16)
        self.dbg(f"V{s}", self.V.ap, [128, 16, 1024], [], dt=BF16)
        if self.stop_after == "kv":
            return
        self.attn_phase(s)
        P.barrier()
        self.dbg(f"x2T{s}", self.xT.ap, [128, 8, 2048], [])
        if self.stop_after == "attn":
            return
        self.mlp(s, 1, 4)
        P.barrier()
        self.output_phase(s)
        P.barrier()

    def load_xtok(self, s, half, Xtok):
        src = self.I["x"][s, half * 1024:(half + 1) * 1024, :].rearrange("(c j) f -> c j f", j=8)
        for q in range(4):
            self.P.dma(SP, Xtok.ap[32 * q:32 * q + 32], src[32 * q:32 * q + 32], writes=[Xtok.t()])

    def s5_prep(self, s):
        P, c, ps = self.P, self.c, self.ps
        A = self.arena
        off = self.R_TAIL
        D, off = A.view(off, [128, 64, 256], BF16, "D")
        self.D = D
        self.s5_tail_off = off
        Xtok, off = A.view(off, [128, 8, 1024], F32, "Xtok")
        Arep, off = A.view(off, [128, 1024], F32, "Arep")
        Brep, off = A.view(off, [128, 1024], F32, "Brep")
        ss, off = A.view(off, [128, 8], F32, "ss")
        rs, off = A.view(off, [128, 8], F32, "rs")
        junk, off = A.view(off, [128, 1024], BF16, "junk")
        o2 = self.R_ACTT
        htok, o2 = A.view(o2, [128, 64, 8, 16], BF16, "htok")
        tmpf, o2 = A.view(o2, [128, 2, 1024], F32, "tmpf")
        xT = self.xT
        P.dma(SP, Arep.ap, self.modrow[s, 0, :].partition_broadcast(128), reads=[self.modrow_tok], writes=[Arep.t()])
        P.dma(SP, Brep.ap, self.modrow[s, 1, :].partition_broadcast(128), reads=[self.modrow_tok], writes=[Brep.t()])
        n = 0
        for half in range(2):
            self.load_xtok(s, half, Xtok)
            for j in range(8):
                self.act(junk.ap, Xtok.ap[:, j, :], AF.Square, [Xtok.t()], [junk.t(), ss.t()], accum_out=ss.ap[:, j:j + 1])
            self.ts(DVE, rs.ap, ss.ap, 1.0 / 1024.0, EPS, ALU.mult, ALU.add, [ss.t()], [rs.t()])
            self.act(rs.ap, rs.ap, AF.Sqrt, [rs.t()], [rs.t()])
            P.op(DVE, lambda e: e.reciprocal(rs.ap, rs.ap), [rs.t()], [rs.t()])
            for j in range(8):
                tb = j % 2
                self.stt(tmpf.ap[:, tb, :], Xtok.ap[:, j, :], rs.ap[:, j:j + 1], Arep.ap, ALU.mult, ALU.mult,
                         [Xtok.t(), rs.t(), Arep.t()], [tmpf.t(tb)])
                self.tt(POOL, htok.ap[:, :, j, :], tmpf.ap[:, tb, :].rearrange("p (g h) -> p g h", h=16),
                        Brep.ap.rearrange("p (g h) -> p g h", h=16), ALU.add, [tmpf.t(tb), Brep.t()], [htok.t()])
            for ft in range(8):
                for jq in range(2):
                    bank = ps[4 + n % 4]
                    for jj in range(4):
                        j = 4 * jq + jj
                        self.tr(bank.ap[:, 128 * jj:128 * jj + 128], Xtok.ap[:, j, 128 * ft:128 * ft + 128], c["ident_f"].ap,
                                [Xtok.t(), c["ident_f"].t()], [bank.t()], sig=(jj == 3))
                    dst = xT.ap[:, ft, 1024 * half:1024 * half + 1024].rearrange("p (c j) -> p j c", j=8)[:, 4 * jq:4 * jq + 4, :]
                    self.copy(ACT if n % 2 == 0 else DVE, dst, bank.ap.rearrange("p (j c) -> p j c", c=128), [bank.t()], [xT.t(ft)])
                    n += 1
            for g4 in range(16):
                pb = ps[g4 % 4]
                pbv = pb.ap.bitcast(BF16)
                for gi in range(4):
                    g = 4 * g4 + gi
                    self.tr(pbv[:, 128 * gi:128 * gi + 128], htok.ap[:, g].rearrange("p j h -> p (j h)"), c["ident_b"].ap,
                            [htok.t(), c["ident_b"].t()], [pb.t()], sig=(gi == 3))
                self.copy(ACT if g4 % 2 == 0 else DVE, D.ap[:, 4 * g4:4 * g4 + 4, 128 * half:128 * half + 128],
                          pbv[:, 0:512].rearrange("p (g c) -> p g c", c=128), [pb.t()], [D.t()])

    def s5_core(self, s):
        P, c, ps = self.P, self.c, self.ps
        A = self.arena
        D = self.D
        rho = self.rho
        off = self.s5_tail_off
        wch, _ = A.view(0, [128, 2, 4, 768], BF16, "wch")
        tabs, off = A.view(off, [128, 2, 4, 2, 256], F32, "tabs")
        Wk, off = A.view(off, [128, 2, 8, 256], F32, "Wk")
        Sbf, off = A.view(off, [128, 4, 2, 256], BF16, "Sbf")
        Ygb, off = A.view(off, [128, 2, 2, 256], BF16, "Ygb")
        Gtok, off = A.view(off, [128, 2, 2, 8, 128], BF16, "Gtok")
        gT = self.actT
        P.op(POOL, lambda e: e.memset(Sbf.ap, 0.0), [], [Sbf.t(i) for i in range(4)])
        identb = c["ident_b"]

        def wslot_of(pr):
            return (pr // 4) % 2

        def pair_front(pr):
            p4 = pr % 4
            wslot = wslot_of(pr)
            bankE = ps[pr % 4]
            for ri in range(2):
                for gl in range(2):
                    self.mm(bankE.ap[64 * gl:64 * gl + 64, 256 * ri:256 * ri + 256],
                            wch.ap[:, wslot, p4, 128 * ri + 64 * gl:128 * ri + 64 * gl + 64], D.ap[:, 2 * pr + gl, :], True, True,
                            [wch.t(wslot), D.t()], [bankE.t()], sig=(ri == 1 and gl == 1))

        def scan_ops(pr):
            p4 = pr % 4
            wslot = wslot_of(pr)
            sl, w2 = pr % 4, pr % 2
            bankE = ps[pr % 4]
            Er, Ei = bankE.ap[:, 0:256], bankE.ap[:, 256:512]
            cr, sr = tabs.ap[:, wslot, p4, 0, :], tabs.ap[:, wslot, p4, 1, :]
            W = lambda j: Wk.ap[:, w2, j, :]
            wt = lambda j: Wk.t((w2, j))
            tb_, eb = tabs.t(wslot), bankE.t()
            rb = rho.ap[:, pr:pr + 1].to_broadcast([128, 256])
            ops = []
            ops.append(lambda: self.tt(DVE, W(4), Er, cr, ALU.mult, [eb, tb_], [wt(4)]))
            ops.append(lambda: self.tt(DVE, W(5), Ei, sr, ALU.mult, [eb, tb_], [wt(5)]))
            ops.append(lambda: self.tt(DVE, W(6), Ei, cr, ALU.mult, [eb, tb_], [wt(6)]))
            ops.append(lambda: self.tt(DVE, W(7), Er, sr, ALU.mult, [eb, tb_], [wt(7)]))
            ops.append(lambda: self.tt(DVE, W(0), W(4), W(5), ALU.add, [wt(4), wt(5)], [wt(0)]))
            ops.append(lambda: self.tt(DVE, W(1), W(6), W(7), ALU.subtract, [wt(6), wt(7)], [wt(1)]))
            ops.append(lambda: self.P.op(DVE, lambda e: e.tensor_tensor_scan(W(2), rb, W(0), 0.0, ALU.mult, ALU.add),
                                         [wt(0), rho.t()], [wt(2)]))
            ops.append(lambda: self.P.op(DVE, lambda e: e.tensor_tensor_scan(W(3), rb, W(1), 0.0, ALU.mult, ALU.add),
                                         [wt(1), rho.t()], [wt(3)]))
            n = 255
            ops.append(lambda: self.tt(DVE, W(4)[:, 0:n], W(2)[:, 0:n], cr[:, 0:n], ALU.mult, [wt(2), tb_], [wt(4)]))
            ops.append(lambda: self.tt(DVE, W(5)[:, 0:n], W(3)[:, 0:n], sr[:, 0:n], ALU.mult, [wt(3), tb_], [wt(5)]))
            ops.append(lambda: self.tt(DVE, W(6)[:, 0:n], W(3)[:, 0:n], cr[:, 0:n], ALU.mult, [wt(3), tb_], [wt(6)]))
            ops.append(lambda: self.tt(DVE, W(7)[:, 0:n], W(2)[:, 0:n], sr[:, 0:n], ALU.mult, [wt(2), tb_], [wt(7)]))
            ops.append(lambda: self.tt(DVE, Sbf.ap[:, sl, 0, 1:256], W(4)[:, 0:n], W(5)[:, 0:n], ALU.subtract, [wt(4), wt(5)], [Sbf.t(sl)]))
            ops.append(lambda: self.tt(DVE, Sbf.ap[:, sl, 1, 1:256], W(6)[:, 0:n], W(7)[:, 0:n], ALU.add, [wt(6), wt(7)], [Sbf.t(sl)]))
            return ops

        def pair_back(pr):
            ft, p4 = pr // 4, pr % 4
            wslot = wslot_of(pr)
            gslot = ft % 2
            sl = pr % 4
            ys = pr % 2
            bankY = ps[4 + pr % 2]
            for gl in range(2):
                o = bankY.ap[:, 256 * gl:256 * gl + 256]
                hs = slice(64 * gl, 64 * gl + 64)
                self.mm(o, wch.ap[:, wslot, p4, 512 + 128 * gl:512 + 128 * gl + 128], D.ap[:, 2 * pr + gl, :], True, False,
                        [wch.t(wslot), D.t()], [bankY.t()])
                self.mm(o, wch.ap[hs, wslot, p4, 256:384], Sbf.ap[hs, sl, 0, :], False, False, [wch.t(wslot), Sbf.t(sl)], [bankY.t()])
                self.mm(o, wch.ap[hs, wslot, p4, 384:512], Sbf.ap[hs, sl, 1, :], False, True, [wch.t(wslot), Sbf.t(sl)], [bankY.t()],
                        sig=(gl == 1))
            self.act(Ygb.ap[:, ys].rearrange("p g c -> p (g c)"), bankY.ap, AF.Gelu, [bankY.t()], [Ygb.t(ys)])
            pT = ps[6]
            pTv = pT.ap.bitcast(BF16)
            for gl in range(2):
                for half in range(2):
                    q = 2 * gl + half
                    self.tr(pTv[:, 128 * q:128 * q + 128], Ygb.ap[:, ys, gl, 128 * half:128 * half + 128], identb.ap,
                            [Ygb.t(ys), identb.t()], [pT.t()], sig=(q == 3))
            for gl in range(2):
                fo = 32 * p4 + 16 * gl
                self.copy(ACT, Gtok.ap[:, gslot, :, :, fo:fo + 16],
                          pTv[:, 256 * gl:256 * gl + 256].rearrange("p (a i h) -> p a i h", a=2, h=16), [pT.t()], [Gtok.t(gslot)])

        def ft_done(ft):
            gslot = ft % 2
            for half in range(2):
                pT2 = ps[7]
                pv = pT2.ap.bitcast(BF16)
                for i in range(8):
                    self.tr(pv[:, 128 * i:128 * i + 128], Gtok.ap[:, gslot, half, i, :], identb.ap, [Gtok.t(gslot), identb.t()], [pT2.t()],
                            sig=(i == 7))
                self.copy(ACT, gT.ap[:, ft, 1024 * half:1024 * half + 1024].rearrange("p (c i) -> p i c", i=8),
                          pv.rearrange("p (i c) -> p i c", c=128), [pT2.t()], [gT.t(ft)])

        def load_w(ft):
            P.dma(SP, wch.ap[:, ft % 2], self.s5w[4 * ft:4 * ft + 4].rearrange("a p n -> p a n"), reads=[self.s5w_tok[ft]],
                  writes=[wch.t(ft % 2)])
            P.dma(SP, tabs.ap[:, ft % 2], self.s5tab[4 * ft:4 * ft + 4].rearrange("a p r c -> p a r c"), reads=[self.s5tab_tok],
                  writes=[tabs.t(ft % 2)])

        load_w(0)
        load_w(1)
        for p4 in (0, 1):
            pair_front(p4)
        for pp in range(16):
            prs = [2 * pp, 2 * pp + 1]
            oa, ob_ = scan_ops(prs[0]), scan_ops(prs[1])
            for fa, fb in zip(oa, ob_):
                fa()
                fb()
            if pp + 1 < 16:
                for pr in (2 * pp + 2, 2 * pp + 3):
                    pair_front(pr)
            for pr in prs:
                pair_back(pr)
            if pp % 2 == 1:
                ft = pp // 2
                ft_done(ft)
                if ft + 2 < 8:
                    load_w(ft + 2)

    def x_reload(self, s):
        P, c, ps = self.P, self.c, self.ps
        A = self.arena
        Xtok, _ = A.view(self.R_TAIL, [128, 8, 1024], F32, "Xtok2")
        xT = self.xT
        n = 0
        for half in range(2):
            self.load_xtok(s, half, Xtok)
            for ft in range(8):
                for jq in range(2):
                    bank = ps[n % 4]
                    for jj in range(4):
                        j = 4 * jq + jj
                        self.tr(bank.ap[:, 128 * jj:128 * jj + 128], Xtok.ap[:, j, 128 * ft:128 * ft + 128], c["ident_f"].ap,
                                [Xtok.t(), c["ident_f"].t()], [bank.t()], sig=(jj == 3))
                    dst = xT.ap[:, ft, 1024 * half:1024 * half + 1024].rearrange("p (c j) -> p j c", j=8)[:, 4 * jq:4 * jq + 4, :]
                    self.copy(ACT if n % 2 == 0 else DVE, dst, bank.ap.rearrange("p (j c) -> p j c", c=128), [bank.t()], [xT.t(ft)])
                    n += 1


    def take_prefetched(self, key):
        pf = getattr(self, "_prefetched", None)
        if pf is not None and pf[0] == key:
            self._prefetched = None
            return pf[1]
        return None

    def prefetch(self, key, loader):
        self._prefetched = (key, loader())

    def stream(self, loaders, computes, key=None):
        n = len(loaders)
        first = self.take_prefetched(key) if key is not None else None
        slots = {0: first if first is not None else loaders[0]()}
        for i in range(n):
            if i + 1 < n:
                slots[i + 1] = loaders[i + 1]()
            computes[i](slots.pop(i))

    def fm_norm(self, s, n, tt, dst, sq, tmpf, bank):
        c, xT, rstd = self.c, self.xT, self.rstd
        ts_ = slice(512 * tt, 512 * tt + 512)
        for ft in range(8):
            self.act(sq.ap[:, ft, :], xT.ap[:, ft, ts_], AF.Square, [xT.t(ft)], [sq.t(ft)])
        for ft in range(8):
            self.mm(bank.ap, c["onesm_b"].ap, sq.ap[:, ft, :], ft == 0, ft == 7, [c["onesm_b"].t(), sq.t(ft)], [bank.t()])
        self.act(rstd.ap, bank.ap, AF.Ln, [bank.t()], [rstd.t()], bias=self.epsc.ap)
        self.act(rstd.ap, rstd.ap, AF.Exp, [rstd.t()], [rstd.t()], scale=-0.5)
        ms = self.modsc
        for ft in range(8):
            tb = ft % 2
            self.stt(tmpf.ap[:, tb, :], xT.ap[:, ft, ts_], ms.ap[:, s, n, 0, ft:ft + 1], rstd.ap, ALU.mult, ALU.mult,
                     [xT.t(ft), ms.t(), rstd.t()], [tmpf.t(tb)])
            self.act(dst[:, ft, :], tmpf.ap[:, tb, :], AF.Identity, [tmpf.t(tb), ms.t()], [self.cur_h_tok], bias=ms.ap[:, s, n, 1, ft:ft + 1])

    def glu(self, s):
        P, ps = self.P, self.ps
        A = self.arena
        off = self.R_TAIL
        sg, off = A.view(off, [128, 2, 512], F32, "sg")
        mt_, off = A.view(off, [128, 2, 512], F32, "mt")
        gT, xT, adaT = self.actT, self.xT, self.adaT
        w = self.I["s5_w_glu"][0].rearrange("(k p) n -> p k n", p=128)
        allg = [gT.t(ft) for ft in range(8)]
        cnt = [0]

        def loader(ft):
            def f():
                v = lambda a: a[:, 0:2048].rearrange("p (k g n) -> p k g n", g=2, n=128)
                return self.wload([(lambda a: v(a)[:, :, 0, :], w[:, :, 128 * ft:128 * ft + 128]),
                                   (lambda a: v(a)[:, :, 1, :], w[:, :, 1024 + 128 * ft:1024 + 128 * ft + 128])])
            return f

        def compute(ft):
            def f(slot):
                wv = slot.ap[:, 0:2048].rearrange("p (k g n) -> p k g n", g=2, n=128)
                for tt in range(4):
                    n = cnt[0]
                    cnt[0] += 1
                    bv, bg = ps[(2 * n) % 8], ps[(2 * n + 1) % 8]
                    ts_ = slice(512 * tt, 512 * tt + 512)
                    for gi, bank in enumerate([bv, bg]):
                        for k in range(8):
                            self.mm(bank.ap, wv[:, k, gi, :], gT.ap[:, k, ts_], k == 0, k == 7, [slot.t()] + allg, [bank.t()])
                    b2 = n % 2
                    self.act(sg.ap[:, b2, :], bg.ap, AF.Sigmoid, [bg.t()], [sg.t(b2)])
                    self.tt(DVE, mt_.ap[:, b2, :], bv.ap, sg.ap[:, b2, :], ALU.mult, [bv.t(), sg.t(b2)], [mt_.t(b2)])
                    self.stt(xT.ap[:, ft, ts_], mt_.ap[:, b2, :], adaT.ap[:, 16 + ft, s:s + 1], xT.ap[:, ft, ts_], ALU.mult, ALU.add,
                             [mt_.t(b2), adaT.t(), xT.t(ft)], [xT.t(ft)])
            return f

        self.stream([loader(ft) for ft in range(8)], [compute(ft) for ft in range(8)])
        self.prefetch(("w1", 0), lambda: self.wload_k8(self.I["mlp_w1"][0], 0))

    def mlp(self, s, l, nidx):
        P, ps = self.P, self.ps
        A = self.arena
        uT, o_t = A.view(self.R_TAIL, [128, 32, 1024], BF16, "uT")
        htile1, _ = A.view(o_t, [128, 8, 1024], BF16, "htile1")
        off = self.R_ACTT
        htile0, off = A.view(off, [128, 8, 1024], BF16, "htile0")
        sq, off = A.view(off, [128, 8, 512], BF16, "sq")
        tmpf, off = A.view(off, [128, 2, 512], F32, "tmpf2")
        r, off = A.view(off, [128, 2, 1024], BF16, "r")
        htiles = [htile0, htile1]
        xT, adaT = self.xT, self.adaT
        w1 = self.I["mlp_w1"][l]
        w2 = self.I["mlp_w2"][l].rearrange("(k p) n -> p k n", p=128)
        gm0 = 48 * l + 40
        nb = [0]

        def norm(t2):
            for sub in range(2):
                self.cur_h_tok = htiles[t2].t(sub)
                self.fm_norm(s, nidx, 2 * t2 + sub, htiles[t2].ap[:, :, 512 * sub:512 * sub + 512], sq, tmpf, ps[7])

        norm(0)
        for t2 in range(2):
            htile = htiles[t2]
            loaders, computes = [], []
            for ch in range(8):
                loaders.append(lambda ch=ch: self.wload_k8(w1, 512 * ch))

                def c1(sv, ch=ch):
                    slot, wv = sv
                    for m4 in range(4):
                        e = 4 * ch + m4
                        for sub in range(2):
                            bank = ps[nb[0] % 4]
                            nb[0] += 1
                            for k in range(8):
                                self.mm(bank.ap, wv[:, k, 128 * m4:128 * m4 + 128], htile.ap[:, k, 512 * sub:512 * sub + 512], k == 0, k == 7,
                                        [slot.t(), htile.t(sub)], [bank.t()])
                            self.act(r.ap[:, e % 2, 512 * sub:512 * sub + 512], bank.ap, AF.Relu, [bank.t()], [r.t((e % 2, sub))])
                        self.tt(DVE, uT.ap[:, e, :], r.ap[:, e % 2, :], r.ap[:, e % 2, :], ALU.mult,
                                [r.t((e % 2, 0)), r.t((e % 2, 1))], [uT.t(e)])
                    if ch == 7 and t2 == 0:
                        norm(1)
                computes.append(c1)
            allu = [uT.t(e) for e in range(32)]
            for o in range(8):
                def l2(o=o):
                    slot = self.wload([(lambda a: a.rearrange("p (k n) -> p k n", n=128), w2[:, :, 128 * o:128 * o + 128])])
                    return slot, slot.ap.rearrange("p (k n) -> p k n", n=128)
                loaders.append(l2)

                def c2(sv, o=o, t2=t2):
                    slot, wv = sv
                    for sub in range(2):
                        bank = ps[4 + (2 * o + sub) % 3]
                        ts_ = slice(1024 * t2 + 512 * sub, 1024 * t2 + 512 * sub + 512)
                        for k in range(32):
                            self.mm(bank.ap, wv[:, k, :], uT.ap[:, k, 512 * sub:512 * sub + 512], k == 0, k == 31, [slot.t()] + allu, [bank.t()])
                        self.stt(xT.ap[:, o, ts_], bank.ap, adaT.ap[:, gm0 + o, s:s + 1], xT.ap[:, o, ts_], ALU.mult, ALU.add,
                                 [bank.t(), adaT.t(), xT.t(o)], [xT.t(o)])
                computes.append(c2)
            self.stream(loaders, computes, key=("w1", l) if t2 == 0 else None)
        if l == 0:
            self.prefetch(("wkv", 0), lambda: self.wload_k8(self.I["w_kv"], 0))

    def headnorm_batch(self, banks, sbanks, gaincol, dsts, dtoks, sq, tk4):
        c = self.c
        n = len(banks)
        for j in range(n):
            self.act(sq.ap[:, j, :], banks[j].ap, AF.Square, [banks[j].t()], [sq.t(j)])
        for j in range(n):
            self.mm(sbanks[j].ap, c["blk_b"].ap, sq.ap[:, j, :], True, True, [c["blk_b"].t(), sq.t(j)], [sbanks[j].t()])
        for j in range(n):
            self.act(tk4.ap[:, j, :], sbanks[j].ap, AF.Ln, [sbanks[j].t()], [tk4.t(j)], bias=self.epsc.ap)
        for j in range(n):
            self.act(tk4.ap[:, j, :], tk4.ap[:, j, :], AF.Exp, [tk4.t(j)], [tk4.t(j)], scale=-0.5)
        for j in range(n):
            self.stt(dsts[j], banks[j].ap, gaincol, tk4.ap[:, j, :], ALU.mult, ALU.mult, [banks[j].t(), self.qkg.t(), tk4.t(j)], dtoks[j])

    def kv_phase(self, s):
        P, ps = self.P, self.ps
        A = self.arena
        self.KT, _ = A.view(self.R_ACTT, [128, 8, 2048], BF16, "KT")
        off = self.R_TAIL
        self.V, off = A.view(off, [128, 16, 1024], BF16, "V")
        self.attn_off = off
        htile, off = A.view(off, [128, 8, 2048], BF16, "htile_kv")
        sq, off = A.view(off, [128, 8, 512], BF16, "sq_kv")
        tmpf, off = A.view(off, [128, 2, 512], F32, "tmpf_kv")
        tk2, off = A.view(off, [128, 2, 512], F32, "tk2")
        KT, V = self.KT, self.V
        wkv = self.I["w_kv"]
        for tt in range(4):
            self.cur_h_tok = htile.t(tt)
            self.fm_norm(s, 2, tt, htile.ap[:, :, 512 * tt:512 * tt + 512], sq, tmpf, ps[7])
        cnt = [0]
        loaders = [lambda ch=ch: self.wload_k8(wkv, 512 * ch) for ch in range(4)]
        computes = []
        for ch in range(4):
            def cK(sv, ch=ch):
                slot, wv = sv
                for tt in range(4):
                    ts_ = slice(512 * tt, 512 * tt + 512)
                    for hf in range(2):
                        n = cnt[0]
                        cnt[0] += 1
                        banks = [ps[(2 * n) % 4], ps[(2 * n + 1) % 4]]
                        sbanks = [ps[4 + (2 * n) % 4], ps[4 + (2 * n + 1) % 4]]
                        for j in range(2):
                            m4 = 2 * hf + j
                            for k in range(8):
                                self.mm(banks[j].ap, wv[:, k, 128 * m4:128 * m4 + 128], htile.ap[:, k, ts_], k == 0, k == 7,
                                        [slot.t(), htile.t(tt)], [banks[j].t()])
                        prs = [4 * ch + 2 * hf + j for j in range(2)]
                        sqv = Buf(sq.ap[:, 2 * (n % 4):2 * (n % 4) + 2, :], "sqv")
                        sqv.toks = {0: sq.t(2 * (n % 4)), 1: sq.t(2 * (n % 4) + 1)}
                        self.headnorm_batch(banks, sbanks, self.qkg.ap[:, 1:2], [KT.ap[:, p_, ts_] for p_ in prs],
                                            [[KT.t((p_, tt))] for p_ in prs], sqv, tk2)

            def cV(sv, ch=ch):
                slot, wv = sv
                for tb in range(16):
                    n = cnt[0]
                    cnt[0] += 1
                    bank = ps[n % 4]
                    for k in range(8):
                        self.mm(bank.ap, htile.ap[:, k, 128 * tb:128 * tb + 128], wv[:, k, :], k == 0, k == 7,
                                [slot.t(), htile.t(tb // 4)], [bank.t()])
                    self.copy(ACT if n % 2 == 0 else DVE, V.ap[:, tb, 512 * (ch - 2):512 * (ch - 2) + 512], bank.ap,
                              [bank.t()], [V.t(tb)])
            computes.append(cK if ch < 2 else cV)
        self.stream(loaders, computes, key=("wkv", 0))
        self.prefetch(("wq", 0), lambda: self.wload_k8(self.I["sb_w_q"][0], 0))

    def attn_phase(self, s):
        P, ps, c = self.P, self.ps, self.c
        A = self.arena
        KT, V, xT, adaT = self.KT, self.V, self.xT, self.adaT
        off = self.attn_off
        qT, off = A.view(off, [128, 8, 512], BF16, "qT")
        oT, off = A.view(off, [128, 8, 512], BF16, "oT")
        o1 = off
        htile, o1 = A.view(o1, [128, 8, 512], BF16, "htile_q")
        sq, o1 = A.view(o1, [128, 8, 512], BF16, "sq_q")
        tmpf, o1 = A.view(o1, [128, 2, 512], F32, "tmpf_q")
        tk4, o1 = A.view(o1, [128, 4, 512], F32, "tk4_q")
        o2 = off
        Eb, o2 = A.view(o2, [128, 2, 2, 512], F32, "Eb")
        Lb, o2 = A.view(o2, [128, 3, 2, 512], BF16, "Lb")
        Wb, o2 = A.view(o2, [128, 3, 2, 512], BF16, "Wb")
        R32, o2 = A.view(o2, [128, 2, 512], F32, "R32")
        Rbf, o2 = A.view(o2, [128, 3, 2, 512], BF16, "Rbf")
        wq = self.I["sb_w_q"][0]
        wo = self.I["sb_w_o"][0]
        identb, negmask, negtri, negones = c["ident_b"], c["negmask_b"], c["negtri_b"], c["negones_b"]
        zeros = c["zeros512_b"]
        zbank = Buf(self.psall[:, 0:1024].rearrange("p (h t) -> p h t", h=2), "zbank")
        zbank.toks[0] = ps[0].t()
        abank = []
        for j in range(2):
            b_ = Buf(self.psall[:, 1024 + 1024 * j:2048 + 1024 * j].rearrange("p (h t) -> p h t", h=2), f"abank{j}")
            abank.append(b_)
        obank = ps[6]

        def ztoks():
            return [ps[0].t(), ps[1].t()]

        def atoks(j):
            return [ps[2 + 2 * j].t(), ps[3 + 2 * j].t()]

        wq_first = self.take_prefetched(("wq", 0))
        wq_slots = [wq_first if wq_first is not None else self.wload_k8(wq, 0), self.wload_k8(wq, 512)]
        self.cur_h_tok = htile.t()
        self.fm_norm(s, 3, 0, htile.ap, sq, tmpf, ps[7])
        for qt in range(4):
            ts_ = slice(512 * qt, 512 * qt + 512)
            cnt = [0]

            def cQ(sv, ch):
                slot, wv = sv
                banks = [ps[m4] for m4 in range(4)]
                for m4 in range(4):
                    for k in range(8):
                        self.mm(banks[m4].ap, wv[:, k, 128 * m4:128 * m4 + 128], htile.ap[:, k, :], k == 0, k == 7,
                                [slot.t(), htile.t()], [banks[m4].t()])
                self.headnorm_batch(banks, [ps[4 + m4] for m4 in range(4)], self.qkg.ap[:, 0:1],
                                    [qT.ap[:, 4 * ch + m4, :] for m4 in range(4)], [[qT.t(4 * ch + m4)] for m4 in range(4)], sq, tk4)

            for ch in range(2):
                cQ(wq_slots[ch], ch)
            P.barrier()
            wo_slots = [self.wload_k8(wo, 512 * ch) for ch in range(2)]
            tiles = []
            nkb = 4 * qt + 4
            for pair in range(8):
                for ii, kb in enumerate(range(nkb - 1, -1, -1)):
                    r_ = kb - 4 * qt
                    c0 = 128 * r_ if r_ >= 0 else 0
                    tiles.append(dict(pair=pair, kb=kb, first=(ii == 0), last=(kb == 0), diag=(r_ >= 0), c0=c0))
            nt = len(tiles)

            def zmm(bank, btoks, t, close):
                c0 = t["c0"]
                kb = t["kb"]
                rd = [KT.t((t["pair"], kb // 4)), qT.t(t["pair"])]
                for hl in range(2):
                    hs = slice(64 * hl, 64 * hl + 64)
                    self.mm(bank.ap[:, hl, c0:512], KT.ap[hs, t["pair"], 128 * kb:128 * kb + 128], qT.ap[hs, t["pair"], c0:512], True,
                            close and not t["diag"], rd, btoks, sig=(close and not t["diag"] and hl == 1))
                if t["diag"]:
                    for hl in range(2):
                        self.mm(bank.ap[:, hl, c0:c0 + 128], identb.ap, negmask.ap, False, close, [identb.t(), negmask.t()], btoks,
                                sig=(close and hl == 1))

            def stageA1(i):
                t = tiles[i]
                c0 = t["c0"]
                zmm(zbank, ztoks(), t, True)
                self.act(Eb.ap[:, i % 2, :, c0:512], zbank.ap[:, :, c0:512], AF.Exp, ztoks(), [Eb.t(i % 2)])

            def stageA2(i):
                t = tiles[i]
                c0 = t["c0"]
                self.act(Lb.ap[:, i % 3, :, c0:512], Eb.ap[:, i % 2, :, c0:512], AF.Ln, [Eb.t(i % 2)], [Lb.t(i % 3)], bias=self.onec.ap)
                if not t["last"]:
                    if t["first"]:
                        P.op(POOL, lambda e: e.memset(R32.ap, 0.0), [], [R32.t()])
                    self.tt(POOL, R32.ap[:, :, c0:512], R32.ap[:, :, c0:512], Lb.ap[:, i % 3, :, c0:512], ALU.add,
                            [R32.t(), Lb.t(i % 3)], [R32.t()])
                    self.copy(DVE, Rbf.ap[:, i % 3], R32.ap, [R32.t()], [Rbf.t(i % 3)])

            def stageB(i):
                t = tiles[i]
                ab = abank[i % 2]
                at = atoks(i % 2)
                c0 = t["c0"]
                zmm(ab, at, t, False)
                for hl in range(2):
                    self.mm(ab.ap[:, hl, c0:512], negtri.ap, Lb.ap[:, i % 3, hl, c0:512], False, t["first"], [negtri.t(), Lb.t(i % 3)], at,
                            sig=(t["first"] and hl == 1))
                if not t["first"]:
                    for hl in range(2):
                        self.mm(ab.ap[:, hl, c0:512], negones.ap, Rbf.ap[:, (i - 1) % 3, hl, c0:512], False, True,
                                [negones.t(), Rbf.t((i - 1) % 3)], at, sig=(hl == 1))
                self.act(Wb.ap[:, i % 3, :, c0:512], ab.ap[:, :, c0:512], AF.Exp, at, [Wb.t(i % 3)])

            def stageC(i):
                t = tiles[i]
                c0 = t["c0"]
                obank = ps[6 + t["pair"] % 2]
                if t["first"]:
                    self.mm(obank.ap, zeros.ap[:, 0:128], zeros.ap, True, False, [zeros.t()], [obank.t()])
                for hl in range(2):
                    h = 2 * t["pair"] + hl
                    hs = slice(64 * hl, 64 * hl + 64)
                    self.mm(obank.ap[hs, c0:512], V.ap[:, t["kb"], 64 * h:64 * h + 64], Wb.ap[:, i % 3, hl, c0:512], False, t["last"],
                            [V.t(t["kb"]), Wb.t(i % 3)], [obank.t()], sig=(t["last"] and hl == 1))
                if t["last"]:
                    self.copy(DVE, oT.ap[:, t["pair"], :], obank.ap, [obank.t()], [oT.t(t["pair"])])

            for step in range(nt + 3):
                if step < nt:
                    stageA1(step)
                if 0 <= step - 2 < nt:
                    stageB(step - 2)
                if step < nt:
                    stageA2(step)
                if 0 <= step - 3 < nt:
                    stageC(step - 3)
            P.barrier()
            allo = [oT.t(p_) for p_ in range(8)]

            def cO(sv, ch):
                slot, wv = sv
                for m4 in range(4):
                    ft = 4 * ch + m4
                    bank = ps[4 + ft % 2]
                    for k in range(8):
                        self.mm(bank.ap, wv[:, k, 128 * m4:128 * m4 + 128], oT.ap[:, k, :], k == 0, k == 7, [slot.t()] + allo, [bank.t()])
                    self.stt(xT.ap[:, ft, ts_], bank.ap, adaT.ap[:, 48 + 16 + ft, s:s + 1], xT.ap[:, ft, ts_], ALU.mult, ALU.add,
                             [bank.t(), adaT.t(), xT.t(ft)], [xT.t(ft)])

            if qt < 3:
                self.cur_h_tok = htile.t()
                self.fm_norm(s, 3, qt + 1, htile.ap, sq, tmpf, ps[7])
            cO(wo_slots[0], 0)
            if qt < 3:
                wq0 = self.wload_k8(wq, 0)
            cO(wo_slots[1], 1)
            if qt < 3:
                wq_slots = [wq0, self.wload_k8(wq, 512)]
            else:
                self.prefetch(("w1", 1), lambda: self.wload_k8(self.I["mlp_w1"][1], 0))

    def output_phase(self, s):
        P, ps, c = self.P, self.ps, self.c
        A = self.arena
        ost, _ = A.view(self.R_TAIL, [128, 2, 1024], F32, "ostage")
        xT = self.xT
        allx = [xT.t(ft) for ft in range(8)]
        n = 0
        for tb in range(16):
            ob = tb % 2
            for hf in range(2):
                bank = ps[n % 4]
                for jj in range(4):
                    ft = 4 * hf + jj
                    self.tr(bank.ap[:, 128 * jj:128 * jj + 128], xT.ap[:, ft, 128 * tb:128 * tb + 128], c["ident_f"].ap,
                            allx + [c["ident_f"].t()], [bank.t()], sig=(jj == 3))
                self.copy(ACT if n % 2 == 0 else DVE, ost.ap[:, ob, 512 * hf:512 * hf + 512], bank.ap, [bank.t()], [ost.t(ob)])
                n += 1
            otok = Tok(f"out{s}_{tb}")
            P.dma(SP, self.out[s, 128 * tb:128 * tb + 128, :], ost.ap[:, ob, :], reads=[ost.t(ob)], writes=[otok], tok=ost.t(ob))


def build_program(debug=None, stop_after=None, nseq=NS):
    b = Builder(debug=debug, stop_after=stop_after, nseq=nseq)
    nc = b.build()
    if b.P.nfwd:
        print("forward-redirected PE deps:", b.P.nfwd)
    return nc, b


_PARAM_NAMES = ["ada_w", "ada_b", "mix_norm_g", "mlp_norm_g", "mlp_w1", "mlp_w2", "s5_a_re", "s5_a_im", "s5_log_dt",
                "s5_b_re", "s5_b_im", "s5_c_re", "s5_c_im", "s5_d", "s5_w_glu", "kv_ada_w", "kv_ada_b", "kv_norm_g",
                "w_kv", "k_norm_g", "sb_w_q", "q_norm_g", "sb_w_o"]


def kernel(**inputs):
    x = np.ascontiguousarray(np.asarray(inputs["x"], dtype=np.float32))
    c = np.ascontiguousarray(np.asarray(inputs["c"], dtype=np.float32))
    params = {k: np.ascontiguousarray(np.asarray(inputs[k], dtype=np.float32)) for k in _PARAM_NAMES}
    nc, _ = build_program()
    in_maps = []
    for i in range(NCORES):
        m = dict(params)
        m["x"] = np.ascontiguousarray(x[NS * i:NS * i + NS])
        m["c"] = np.ascontiguousarray(c[NS * i:NS * i + NS])
        in_maps.append(m)
    res = run_bass_kernel_spmd(nc, in_maps, core_ids=list(range(NCORES)))
    out = np.concatenate([np.asarray(r["out"]) for r in res.results], axis=0)
    return out.astype(np.float32, copy=False)
```

```python
import math
from contextlib import ExitStack

import numpy as np
import concourse.bass as bass
import concourse.mybir as mybir
from concourse.bass_utils import run_bass_kernel_spmd

F32 = mybir.dt.float32
BF16 = mybir.dt.bfloat16
U8 = mybir.dt.uint8
AF = mybir.ActivationFunctionType
ALU = mybir.AluOpType
PE, DVE, ACT, POOL, SP = "tensor", "vector", "scalar", "gpsimd", "sync"
ENGS = [PE, DVE, ACT, POOL, SP]

D = 1024
T = 2048
NS = 2
FT = 8
TT = 512
NTT = T // TT
DFF = 4096
EPS = 1e-6
NCORES = 8
EPOCH_MAX = 12000


class Tok:
    __slots__ = ("w", "r", "sem", "semcnt", "name")

    def __init__(self, name=""):
        self.w = None
        self.r = []
        self.sem = None
        self.semcnt = 0
        self.name = name


class Op:
    __slots__ = ("eng", "fn", "deps", "dma", "sig", "semref", "semval", "inc", "id", "sigok")


class Prog:
    def __init__(self, nc, stack):
        self.nc = nc
        self.stack = stack
        self.ops = []
        self.last = {e: None for e in ENGS}
        self.barrier_deps = {e: [] for e in ENGS}
        self.dmas = []
        self.nsem = 0

    def new_sem(self, name):
        self.nsem += 1
        return self.stack.enter_context(self.nc.semaphore(f"{name}_{self.nsem}"))

    def op(self, eng, fn, reads=(), writes=(), dma=None, sigok=True):
        o = Op()
        o.sigok = sigok
        o.eng = eng
        o.fn = fn
        o.dma = dma
        o.sig = False
        o.semref = None
        o.semval = 0
        o.inc = 0
        o.id = len(self.ops)
        deps = {}

        def add(d, war=False):
            if d is None:
                return
            if d.dma is None and d.eng == eng:
                if eng == PE:
                    return
            deps[d.id] = d

        for t in reads:
            add(t.w)
        for t in writes:
            add(t.w)
            for r in t.r:
                add(r, war=True)
        for d in self.barrier_deps[eng]:
            add(d)
        self.barrier_deps[eng] = []
        o.deps = list(deps.values())
        for t in reads:
            t.r.append(o)
        for t in writes:
            t.w = o
            t.r = []
        self.ops.append(o)
        self.last[eng] = o
        if dma is not None:
            self.dmas.append(o)
        return o

    def dma(self, eng, out, in_, reads=(), writes=(), tok=None, **kw):
        if tok is None:
            tok = writes[0]

        def fn(e):
            return e.dma_start(out=out, in_=in_, **kw)
        return self.op(eng, fn, reads=reads, writes=writes, dma=tok)

    def barrier(self):
        deps = [o for o in self.last.values() if o is not None] + list(self.dmas)
        self.dmas = []
        for e in ENGS:
            self.barrier_deps[e] = list(self.barrier_deps[e]) + deps

    def emit(self):
        nc = self.nc
        pe_ops = [o for o in self.ops if o.eng == PE and o.dma is None]
        if pe_ops:
            pe_ops[-1].sigok = True
        nxt = {}
        cur = None
        for o in reversed(pe_ops):
            if o.sigok:
                cur = o
            nxt[o.id] = cur
        nfwd = 0
        for o in self.ops:
            nd = {}
            for d in o.deps:
                if d.eng == PE and d.dma is None and not d.sigok:
                    d = nxt[d.id]
                    if d.id > o.id:
                        nfwd += 1
                nd[d.id] = d
            o.deps = list(nd.values())
        self.nfwd = nfwd
        for o in self.ops:
            for d in o.deps:
                d.sig = True
        cnt = {e: 0 for e in ENGS}
        cursem = {e: None for e in ENGS}
        for o in self.ops:
            if o.dma is not None:
                t = o.dma
                if t.sem is None or t.semcnt >= 16 * 3000:
                    t.sem = self.new_sem("d")
                    t.semcnt = 0
                t.semcnt += 16
                o.semref = t.sem
                o.semval = t.semcnt
                o.inc = 16
            elif o.sig:
                e = o.eng
                if cursem[e] is None or cnt[e] >= EPOCH_MAX:
                    cursem[e] = self.new_sem("e" + e[:2])
                    cnt[e] = 0
                cnt[e] += 1
                o.semref = cursem[e]
                o.semval = cnt[e]
                o.inc = 1
        with nc.Block() as block:
            for e in ENGS:
                oplist = [o for o in self.ops if o.eng == e]

                def body(eh, oplist=oplist):
                    waited = {}
                    for o in oplist:
                        need = {}
                        for d in o.deps:
                            k = id(d.semref)
                            if k not in need or need[k][1] < d.semval:
                                need[k] = (d.semref, d.semval)
                        for k, (s, v) in need.items():
                            if waited.get(k, 0) < v:
                                eh.wait_ge(s, v)
                                waited[k] = v
                        if o.fn is not None:
                            inst = o.fn(eh)
                            if o.semref is not None:
                                inst.then_inc(o.semref, o.inc)

                getattr(block, e)(body)


class Buf:
    def __init__(self, ap, name=""):
        self.ap = ap
        self.name = name
        self.toks = {}

    def t(self, key=0):
        if key not in self.toks:
            self.toks[key] = Tok(f"{self.name}{key}")
        return self.toks[key]

    def ts(self, keys):
        return [self.t(k) for k in keys]


class Arena:
    def __init__(self, ap_u8):
        self.ap = ap_u8
        self.size = ap_u8.shape[1]

    def view(self, off, shape, dt, name=""):
        esz = 4 if dt == F32 else 2
        n = 1
        for s in shape[1:]:
            n *= s
        nbytes = n * esz
        assert off % 4 == 0 and off + nbytes <= self.size, (name, off, nbytes, self.size)
        v = self.ap[0:shape[0], off:off + nbytes].bitcast(dt)
        if len(shape) == 3:
            v = v.rearrange("p (a b) -> p a b", b=shape[2])
        elif len(shape) == 4:
            v = v.rearrange("p (a b c) -> p a b c", b=shape[2], c=shape[3])
        elif len(shape) == 5:
            v = v.rearrange("p (a b c d) -> p a b c d", b=shape[2], c=shape[3], d=shape[4])
        return Buf(v, name), off + nbytes


class Builder:
    def __init__(self, debug=None, stop_after=None, nseq=NS):
        self.debug = debug or []
        self.stop_after = stop_after
        self.nseq = nseq
        self.stack = ExitStack()
        self.nc = bass.Bass("TRN2", target_bir_lowering=False)
        self.P = Prog(self.nc, self.stack)
        self.dbg_out = {}

    def dram_in(self, name, shape):
        return self.nc.dram_tensor(name, list(shape), F32, kind="ExternalInput").ap()

    def sb(self, name, shape, dt):
        return self.stack.enter_context(self.nc.sbuf_tensor(name, list(shape), dt))

    def act(self, out, in_, func, reads, writes, bias=None, scale=None, accum_out=None):
        kw = {}
        if bias is not None:
            kw["bias"] = bias
        if scale is not None:
            kw["scale"] = scale
        if accum_out is not None:
            kw["accum_out"] = accum_out
        return self.P.op(ACT, lambda e: e.activation(out=out, in_=in_, func=func, **kw), reads, writes)

    def tt(self, eng, out, in0, in1, op, reads, writes):
        return self.P.op(eng, lambda e: e.tensor_tensor(out, in0, in1, op), reads, writes)

    def stt(self, out, in0, scalar, in1, op0, op1, reads, writes):
        return self.P.op(DVE, lambda e: e.scalar_tensor_tensor(out, in0, scalar, in1, op0, op1), reads, writes)

    def ts(self, eng, out, in0, s1, s2, op0, op1, reads, writes):
        if op1 is None:
            return self.P.op(eng, lambda e: e.tensor_scalar(out, in0, s1, None, op0), reads, writes)
        return self.P.op(eng, lambda e: e.tensor_scalar(out, in0, s1, s2, op0, op1), reads, writes)

    def copy(self, eng, out, in_, reads, writes):
        if eng == ACT:
            return self.P.op(ACT, lambda e: e.activation(out=out, in_=in_, func=AF.Identity), reads, writes)
        return self.P.op(eng, lambda e: e.tensor_copy(out, in_), reads, writes)

    def mm(self, out, lhsT, rhs, start, stop, reads, writes, sig=None):
        return self.P.op(PE, lambda e: e.matmul(out, lhsT=lhsT, rhs=rhs, start=start, stop=stop), reads, writes,
                         sigok=(stop if sig is None else sig))

    def tr(self, out, in_, ident, reads, writes, sig=True):
        return self.P.op(PE, lambda e: e.transpose(out, in_, ident), reads, writes, sigok=sig)

    def dbg(self, name, buf_ap, shape, reads, dt=F32):
        if name not in self.debug:
            return
        o = self.nc.dram_tensor("dbg_" + name, list(shape), dt, kind="ExternalOutput").ap()
        tok = Tok("dbg" + name)
        self.P.dma(SP, o, buf_ap, reads=reads, writes=[tok], tok=tok)
        self.dbg_out[name] = tok

    def build(self):
        nc, P = self.nc, self.P
        I = {}
        I["x"] = self.dram_in("x", [NS, T, D])
        I["c"] = self.dram_in("c", [NS, D])
        I["ada_w"] = self.dram_in("ada_w", [2, D, 6 * D])
        I["ada_b"] = self.dram_in("ada_b", [2, 6 * D])
        I["mix_norm_g"] = self.dram_in("mix_norm_g", [2, D])
        I["mlp_norm_g"] = self.dram_in("mlp_norm_g", [2, D])
        I["mlp_w1"] = self.dram_in("mlp_w1", [2, D, DFF])
        I["mlp_w2"] = self.dram_in("mlp_w2", [2, DFF, D])
        I["s5_a_re"] = self.dram_in("s5_a_re", [1, 64, 64])
        I["s5_a_im"] = self.dram_in("s5_a_im", [1, 64, 64])
        I["s5_log_dt"] = self.dram_in("s5_log_dt", [1, 64])
        I["s5_b_re"] = self.dram_in("s5_b_re", [1, 64, 64, 16])
        I["s5_b_im"] = self.dram_in("s5_b_im", [1, 64, 64, 16])
        I["s5_c_re"] = self.dram_in("s5_c_re", [1, 64, 16, 64])
        I["s5_c_im"] = self.dram_in("s5_c_im", [1, 64, 16, 64])
        I["s5_d"] = self.dram_in("s5_d", [1, D])
        I["s5_w_glu"] = self.dram_in("s5_w_glu", [1, D, 2 * D])
        I["kv_ada_w"] = self.dram_in("kv_ada_w", [D, 2 * D])
        I["kv_ada_b"] = self.dram_in("kv_ada_b", [2 * D])
        I["kv_norm_g"] = self.dram_in("kv_norm_g", [D])
        I["w_kv"] = self.dram_in("w_kv", [D, 2 * D])
        I["k_norm_g"] = self.dram_in("k_norm_g", [64])
        I["sb_w_q"] = self.dram_in("sb_w_q", [1, D, D])
        I["q_norm_g"] = self.dram_in("q_norm_g", [1, 64])
        I["sb_w_o"] = self.dram_in("sb_w_o", [1, D, D])
        self.I = I
        self.out = nc.dram_tensor("out", [NS, T, D], F32, kind="ExternalOutput").ap()
        self.s5w = nc.dram_tensor("s5w_scr", [32, 128, 768], BF16, kind="Internal").ap()
        self.s5w_tok = [Tok(f"s5w{i}") for i in range(8)]
        self.s5tab = nc.dram_tensor("s5tab_scr", [32, 128, 2, 256], F32, kind="Internal").ap()
        self.s5tab_tok = Tok("s5tab")
        self.modrow = nc.dram_tensor("modrow_scr", [NS, 2, D], F32, kind="Internal").ap()
        self.modrow_tok = Tok("modrow")

        self.consts()
        ARENA = 194 * 1024
        self.arena = Arena(self.sb("arena", [128, ARENA], U8)[:])
        self.psall = self.stack.enter_context(nc.psum_tensor("psall", [128, 4096], F32))
        self.ps = [Buf(self.psall[:, 512 * i:512 * i + 512], f"ps{i}") for i in range(8)]

        self.setup_phase()
        if self.stop_after is not None and (self.stop_after == "setup" or self.stop_after.startswith("s5") or self.stop_after == "ada"):
            return self.finish()
        for s in range(self.nseq):
            self.seq_pipeline(s)
        return self.finish()

    def finish(self):
        P = self.P
        outs = [o for o in P.ops if o.dma is not None]
        fin = P.op(SP, None)
        fin.deps = list({o.id: o for o in outs}.values())
        P.emit()
        return self.nc

    def consts(self):
        P = self.P
        c = {}

        def mk(name, shape, dt):
            c[name] = Buf(self.sb("c_" + name, shape, dt)[:], name)
            return c[name]

        ident_f = mk("ident_f", [128, 128], F32)
        ident_b = mk("ident_b", [128, 128], BF16)
        onesm_b = mk("onesm_b", [128, 128], BF16)
        blk_b = mk("blk_b", [128, 128], BF16)
        negtri_b = mk("negtri_b", [128, 128], BF16)
        negones_b = mk("negones_b", [128, 128], BF16)
        negmask_b = mk("negmask_b", [128, 128], BF16)
        zeros_b = mk("zeros_b", [128, 64], BF16)
        onesrow_f = mk("onesrow_f", [1, 128], F32)
        sel2 = mk("sel2", [2, 128], F32)
        bmask_f = mk("bmask_f", [128, 128], F32)
        tmp_f = mk("ctmp_f", [128, 128], F32)
        hm = mk("hm", [128, 2], F32)
        zeros512_b = mk("zeros512_b", [128, 512], BF16)

        def pool(fn, reads, writes):
            return P.op(POOL, fn, reads, writes)

        pool(lambda e: e.memset(ident_f.ap, 1.0), [], [ident_f.t()])
        pool(lambda e: e.affine_select(out=ident_f.ap, in_=ident_f.ap, pattern=[[-1, 128]], compare_op=ALU.is_equal,
                                       fill=0.0, base=0, channel_multiplier=1), [ident_f.t()], [ident_f.t()])
        pool(lambda e: e.tensor_copy(ident_b.ap, ident_f.ap), [ident_f.t()], [ident_b.t()])
        pool(lambda e: e.memset(onesm_b.ap, 1.0 / 1024.0), [], [onesm_b.t()])
        pool(lambda e: e.memset(blk_b.ap, 0.0), [], [blk_b.t()])
        pool(lambda e: e.memset(blk_b.ap[0:64, 0:64], 1.0 / 64.0), [blk_b.t()], [blk_b.t()])
        pool(lambda e: e.memset(blk_b.ap[64:128, 64:128], 1.0 / 64.0), [blk_b.t()], [blk_b.t()])
        pool(lambda e: e.memset(tmp_f.ap, -1.0), [], [tmp_f.t()])
        pool(lambda e: e.affine_select(out=tmp_f.ap, in_=tmp_f.ap, pattern=[[-1, 128]], compare_op=ALU.is_ge,
                                       fill=0.0, base=0, channel_multiplier=1), [tmp_f.t()], [tmp_f.t()])
        pool(lambda e: e.tensor_copy(negtri_b.ap, tmp_f.ap), [tmp_f.t()], [negtri_b.t()])
        pool(lambda e: e.memset(negones_b.ap, -1.0), [], [negones_b.t()])
        pool(lambda e: e.tensor_scalar(negmask_b.ap, negtri_b.ap, 30000.0, None, ALU.mult), [negtri_b.t()], [negmask_b.t()])
        pool(lambda e: e.memset(zeros_b.ap, 0.0), [], [zeros_b.t()])
        pool(lambda e: e.memset(onesrow_f.ap, 1.0), [], [onesrow_f.t()])
        pool(lambda e: e.memset(sel2.ap, 1.0), [], [sel2.t()])
        pool(lambda e: e.affine_select(out=sel2.ap, in_=sel2.ap, pattern=[[1, 128]], compare_op=ALU.is_ge,
                                       fill=0.0, base=0, channel_multiplier=-64), [sel2.t()], [sel2.t()])
        pool(lambda e: e.affine_select(out=sel2.ap, in_=sel2.ap, pattern=[[-1, 128]], compare_op=ALU.is_ge,
                                       fill=0.0, base=63, channel_multiplier=64), [sel2.t()], [sel2.t()])
        pool(lambda e: e.memset(bmask_f.ap, 1.0), [], [bmask_f.t()])
        pool(lambda e: e.affine_select(out=bmask_f.ap.rearrange("p (i h) -> p i h", h=16), in_=bmask_f.ap.rearrange("p (i h) -> p i h", h=16),
                                       pattern=[[16, 8], [0, 16]], compare_op=ALU.is_ge,
                                       fill=0.0, base=15, channel_multiplier=-1), [bmask_f.t()], [bmask_f.t()])
        pool(lambda e: e.memset(zeros512_b.ap, 0.0), [], [zeros512_b.t()])
        pool(lambda e: e.memset(hm.ap, 0.0), [], [hm.t()])
        pool(lambda e: e.memset(hm.ap[0:64, 0:1], 1.0), [hm.t()], [hm.t()])
        pool(lambda e: e.memset(hm.ap[64:128, 1:2], 1.0), [hm.t()], [hm.t()])
        self.c = c
        self.VT = Buf(self.sb("VT", [128, 160], F32)[:], "VT")
        self.adaT = Buf(self.sb("adaT", [128, 112, 2], F32)[:], "adaT")
        self.coef = Buf(self.sb("coef", [128, 32, 8, 3], F32)[:], "coef")
        self.modsc = Buf(self.sb("modsc", [128, NS, 5, 2, 8], F32)[:], "modsc")
        self.rho = Buf(self.sb("rho", [128, 32], F32)[:], "rho")
        self.qkg = Buf(self.sb("qkg", [128, 2], F32)[:], "qkg")
        self.epsc = Buf(self.sb("epsc", [128, 1], F32)[:], "epsc")
        self.onec = Buf(self.sb("onec", [128, 1], F32)[:], "onec")
        pool(lambda e: e.memset(self.epsc.ap, EPS), [], [self.epsc.t()])
        pool(lambda e: e.memset(self.onec.ap, 1.0), [], [self.onec.t()])

    def ring_init(self, off, nslots=2):
        self.ring = []
        for i in range(nslots):
            b, off = self.arena.view(off, [128, 4096], BF16, f"ring{i}")
            self.ring.append(b)
        self.ring_i = 0
        return off

    def wload(self, srcs):
        slot = self.ring[self.ring_i % len(self.ring)]
        self.ring_i += 1
        for dstf, src in srcs:
            self.P.dma(POOL, dstf(slot.ap), src, reads=[], writes=[slot.t()], tok=slot.t())
        return slot

    def wload_k8(self, w2d, col0, ncols=512):
        src = w2d.rearrange("(k p) n -> p k n", p=128)[:, :, col0:col0 + ncols]
        slot = self.wload([(lambda a: a[:, 0:8 * ncols].rearrange("p (k n) -> p k n", n=ncols), src)])
        return slot, slot.ap[:, 0:8 * ncols].rearrange("p (k n) -> p k n", n=ncols)

    def setup_phase(self):
        P, I, c = self.P, self.I, self.c
        A = self.arena
        ps = self.ps
        off = 0
        off = self.ring_init(off, 2)
        self.ring_end = off
        off = A.size - 42 * 1024
        self.s5_limit = off
        vrA, off = A.view(off, [128, 128], F32, "vrA")
        vrB, off = A.view(off, [128, 128], F32, "vrB")
        cs, off = A.view(off, [2, 1024], F32, "cs")
        sT, off = A.view(off, [128, 8, 2], BF16, "sT")
        rowb = []
        for b in range(2):
            r, off = A.view(off, [1, 2048], F32, f"rowb{b}")
            rowb.append(r)
        biasrow, off = A.view(off, [1, 2048], F32, "biasrow")
        grow, off = A.view(off, [1, 1024], F32, "grow")
        abrow, off = A.view(off, [1, 2, 1024], F32, "abrow")
        ld = Tok("setup_ld")

        def ldma(out, in_, wtoks):
            P.dma(SP, out, in_, reads=[], writes=wtoks)

        ldma(vrA.ap[0:16, :], I["mix_norm_g"].rearrange("l (k p) -> (l k) p", p=128), [vrA.t()])
        ldma(vrA.ap[16:32, :], I["mlp_norm_g"].rearrange("l (k p) -> (l k) p", p=128), [vrA.t()])
        ldma(vrA.ap[32:40, :], I["kv_norm_g"].rearrange("(k p) -> k p", p=128), [vrA.t()])
        adab = I["ada_b"].rearrange("l (k p) -> (l k) p", p=128)
        ldma(vrA.ap[40:128, :], adab[0:88, :], [vrA.t()])
        ldma(vrB.ap[0:8, :], adab[88:96, :], [vrB.t()])
        ldma(vrB.ap[8:24, :], I["kv_ada_b"].rearrange("(k p) -> k p", p=128), [vrB.t()])
        for hh in range(2):
            ldma(vrB.ap[24:25, 64 * hh:64 * hh + 64], I["q_norm_g"], [vrB.t()])
            ldma(vrB.ap[25:26, 64 * hh:64 * hh + 64], I["k_norm_g"].rearrange("(o d) -> o d", o=1), [vrB.t()])
        ldma(cs.ap, I["c"], [cs.t()])
        ldma(biasrow.ap, I["ada_b"][0:1, 0:2048], [biasrow.t()])
        ldma(grow.ap, I["mix_norm_g"][0:1, :], [grow.t()])
        VT = self.VT
        self.tr(ps[0].ap[:, 0:128], vrA.ap, c["ident_f"].ap, [vrA.t(), c["ident_f"].t()], [ps[0].t()])
        self.tr(ps[0].ap[:, 128:154], vrB.ap[0:26, :], c["ident_f"].ap[0:26, 0:26], [vrB.t(), c["ident_f"].t()], [ps[0].t()])
        self.copy(DVE, VT.ap[:, 0:154], ps[0].ap[:, 0:154], [ps[0].t()], [VT.t()])
        self.ts(DVE, self.qkg.ap[:, 0:1], VT.ap[:, 152:153], 0.125, None, ALU.mult, None, [VT.t()], [self.qkg.t()])
        self.copy(DVE, self.qkg.ap[:, 1:2], VT.ap[:, 153:154], [VT.t()], [self.qkg.t()])
        self.act(cs.ap, cs.ap, AF.Silu, [cs.t()], [cs.t()])
        for k in range(8):
            self.tr(ps[1].ap[:, 2 * k:2 * k + 2], cs.ap[0:2, 128 * k:128 * k + 128], c["ident_f"].ap[0:2, 0:2],
                    [cs.t(), c["ident_f"].t()], [ps[1].t()])
        self.copy(DVE, sT.ap, ps[1].ap[:, 0:16].rearrange("p (k b) -> p k b", b=2), [ps[1].t()], [sT.t()])
        s5_rest = self.s5_setup_early()
        adaps = ps[5]
        chunks = [(I["ada_w"][0], j * 512) for j in range(12)] + [(I["ada_w"][1], j * 512) for j in range(12)] + \
                 [(I["kv_ada_w"], j * 512) for j in range(4)]
        for ci, (w2d, col0) in enumerate(chunks):
            slot, wv = self.wload_k8(w2d, col0)
            for mt in range(4):
                ot = ci * 4 + mt
                for k in range(8):
                    self.mm(adaps.ap[:, 2 * ot:2 * ot + 2], wv[:, k, 128 * mt:128 * mt + 128], sT.ap[:, k, :],
                            k == 0, k == 7, [slot.t(), sT.t()], [adaps.t()])
            if ci < 4:
                for b in range(2):
                    rp = ps[6 + b]
                    for k in range(8):
                        self.mm(rp.ap[0:1, :], sT.ap[:, k, b:b + 1], wv[:, k, :], k == 0, k == 7, [slot.t(), sT.t()], [rp.t()])
                    self.copy(ACT, rowb[b].ap[0:1, 512 * ci:512 * ci + 512], rp.ap[0:1, :], [rp.t()], [rowb[b].t()])
        self.tt(DVE, self.adaT.ap, adaps.ap[:, 0:224].rearrange("p (o b) -> p o b", b=2),
                VT.ap[:, 40:152].unsqueeze(2).to_broadcast([128, 112, 2]),
                ALU.add, [adaps.t(), VT.t()], [self.adaT.t()])
        adaT = self.adaT
        norms = {1: (16, 24, 32), 2: (32, 96, 104), 3: (8, 48, 56), 4: (24, 72, 80)}
        for s in range(2):
            for n, (gc, sh, sc) in norms.items():
                self.stt(self.modsc.ap[:, s, n, 0, :], adaT.ap[:, sc:sc + 8, s], 1.0, VT.ap[:, gc:gc + 8], ALU.add, ALU.mult,
                         [adaT.t(), VT.t()], [self.modsc.t()])
                self.copy(DVE, self.modsc.ap[:, s, n, 1, :], adaT.ap[:, sh:sh + 8, s], [adaT.t()], [self.modsc.t()])
        for b in range(2):
            self.tt(DVE, rowb[b].ap, rowb[b].ap, biasrow.ap, ALU.add, [rowb[b].t(), biasrow.t()], [rowb[b].t()])
            self.stt(abrow.ap[0:1, 0, :], rowb[b].ap[0:1, 1024:2048], 1.0, grow.ap, ALU.add, ALU.mult,
                     [rowb[b].t(), grow.t()], [abrow.t()])
            self.copy(DVE, abrow.ap[0:1, 1, :], rowb[b].ap[0:1, 0:1024], [rowb[b].t()], [abrow.t()])
            P.dma(SP, self.modrow[b:b + 1], abrow.ap, reads=[abrow.t()], writes=[self.modrow_tok], tok=self.modrow_tok)
        self.dbg("adaT", self.adaT.ap, [128, 112, 2], [self.adaT.t()])
        self.dbg("modsc", self.modsc.ap, [128, NS, 5, 2, 8], [self.modsc.t()])
        self.dbg("VT", self.VT.ap, [128, 160], [self.VT.t()])
        if self.stop_after == "ada":
            return
        s5_rest()
        P.barrier()

    def s5_setup_early(self):
        P, I, c = self.P, self.I, self.c
        A = self.arena
        ps = self.ps
        off = self.ring_end
        lamrows, off = A.view(off, [32, 2, 128], F32, "lamrows")
        ldt2, off = A.view(off, [2, 32], F32, "ldt2")
        NSM = 24
        smb, off = A.view(off, [128, NSM, 32], F32, "sm")
        Bq, off = A.view(off, [128, 2, 32, 16], F32, "Bq")
        Bb, off = A.view(off, [128, 2, 32, 16], F32, "Bb")
        Cin, off = A.view(off, [128, 2, 128], F32, "Cin")
        CT, off = A.view(off, [128, 2, 32, 16], F32, "CT")
        X, off = A.view(off, [128, 2, 32, 8, 16], F32, "X")
        Wo, off = A.view(off, [128, 2, 32, 8, 16], F32, "Wo")
        We, off = A.view(off, [128, 2, 32, 128], F32, "We")
        tmp, off = A.view(off, [128, 2, 512], F32, "s5tmp")
        Drows, off = A.view(off, [64, 16], F32, "Drows")
        Drep, off = A.view(off, [64, 8, 16], F32, "Drep")
        dcol, off = A.view(off, [128, 64], F32, "dcol")
        tmpm, off = A.view(off, [128, 2, 2, 128], F32, "tmpm")
        stage, off = A.view(off, [128, 2, 4, 768], BF16, "stage")
        WoM, off = A.view(off, [128, 2, 2, 2, 128], F32, "WoM")
        assert off <= self.s5_limit, (off, self.s5_limit)
        identf = c["ident_f"]

        names = ["lre", "lim", "dt", "mag", "th", "c", "s", "t1", "t2", "t3", "abre", "abim", "nr", "den",
                 "fre", "fim", "rm2", "lire", "liim", "mure", "muim", "w2r", "w2i"]
        sm = {n: (smb.ap[:, i, :], smb.t(n)) for i, n in enumerate(names)}

        def S(n):
            return sm[n][0]

        def St(n):
            return sm[n][1]

        P.dma(SP, lamrows.ap[:, 0, :], I["s5_a_re"][0].rearrange("(a g) p -> a (g p)", g=2), writes=[lamrows.t()])
        P.dma(SP, lamrows.ap[:, 1, :], I["s5_a_im"][0].rearrange("(a g) p -> a (g p)", g=2), writes=[lamrows.t()])
        P.dma(SP, ldt2.ap, I["s5_log_dt"][0].rearrange("(a g) -> g a", g=2), writes=[ldt2.t()], allow_slow_non_contiguous=True)
        for ri, nm in enumerate(["s5_b_re", "s5_b_im"]):
            src = I[nm][0].rearrange("(a g) p h -> (g p) a h", g=2)
            for q4 in range(4):
                P.dma(SP, Bq.ap[:, ri, 8 * q4:8 * q4 + 8, :], src[:, 8 * q4:8 * q4 + 8, :], writes=[Bq.t()])
        P.dma(SP, Drows.ap, I["s5_d"][0].rearrange("(g h) -> g h", h=16), writes=[Drows.t()])

        self.tr(ps[0].ap[:, 0:32], lamrows.ap[:, 0, :], identf.ap[0:32, 0:32], [lamrows.t(), identf.t()], [ps[0].t()])
        self.tr(ps[0].ap[:, 32:64], lamrows.ap[:, 1, :], identf.ap[0:32, 0:32], [lamrows.t(), identf.t()], [ps[0].t()])
        self.mm(ps[0].ap[:, 64:96], c["sel2"].ap, ldt2.ap, True, True, [c["sel2"].t(), ldt2.t()], [ps[0].t()])
        self.copy(DVE, S("lre"), ps[0].ap[:, 0:32], [ps[0].t()], [St("lre")])
        self.copy(DVE, S("lim"), ps[0].ap[:, 32:64], [ps[0].t()], [St("lim")])
        self.act(S("dt"), ps[0].ap[:, 64:96], AF.Exp, [ps[0].t()], [St("dt")])

        def tt(out, a, b, op, eng=DVE):
            self.tt(eng, S(out), S(a), S(b), op, [St(a), St(b)], [St(out)])

        if self.stop_after == "s5a":
            return

        def horner(out, y, coefs, last):
            self.ts(DVE, S(out), S(y), coefs[0], None, ALU.mult, None, [St(y)], [St(out)])
            for ck in coefs[1:]:
                self.stt(S(out), S(out), ck, S(y), ALU.add, ALU.mult, [St(out), St(y)], [St(out)])
            self.ts(DVE, S(out), S(out), last, None, ALU.add, None, [St(out)], [St(out)])

        tt("t1", "lre", "dt", ALU.mult)
        self.ts(DVE, S("t2"), S("t1"), 0.25, None, ALU.mult, None, [St("t1")], [St("t2")])
        horner("mag", "t2", [1.0 / 5040, 1.0 / 720, 1.0 / 120, 1.0 / 24, 1.0 / 6, 0.5, 1.0], 1.0)
        tt("mag", "mag", "mag", ALU.mult)
        tt("mag", "mag", "mag", ALU.mult)
        tt("th", "lim", "dt", ALU.mult)
        self.ts(DVE, S("t3"), S("th"), 1.0 / 32.0, None, ALU.mult, None, [St("th")], [St("t3")])
        tt("t2", "t3", "t3", ALU.mult)
        horner("c", "t2", [-1.0 / 3628800, 1.0 / 40320, -1.0 / 720, 1.0 / 24, -0.5], 1.0)
        horner("s", "t2", [1.0 / 362880, -1.0 / 5040, 1.0 / 120, -1.0 / 6], 1.0)
        tt("s", "s", "t3", ALU.mult)

        def csquare(a, b):
            tt("t1", a, a, ALU.mult)
            tt("t2", b, b, ALU.mult)
            self.stt(S(b), S(a), 2.0, S(b), ALU.mult, ALU.mult, [St(a), St(b)], [St(b)])
            tt(a, "t1", "t2", ALU.subtract)

        for _ in range(5):
            csquare("c", "s")
        tt("t1", "c", "c", ALU.mult)
        tt("t2", "s", "s", ALU.mult)
        tt("t1", "t1", "t2", ALU.add)
        self.ts(DVE, S("t1"), S("t1"), -0.5, 1.5, ALU.mult, ALU.add, [St("t1")], [St("t1")])
        tt("c", "c", "t1", ALU.mult)
        tt("s", "s", "t1", ALU.mult)
        tt("abre", "mag", "c", ALU.mult)
        tt("abim", "mag", "s", ALU.mult)
        self.ts(DVE, S("nr"), S("abre"), -1.0, None, ALU.add, None, [St("abre")], [St("nr")])
        tt("t1", "lre", "lre", ALU.mult)
        tt("t2", "lim", "lim", ALU.mult)
        tt("den", "t1", "t2", ALU.add)
        P.op(DVE, lambda e: e.reciprocal(S("den"), S("den")), [St("den")], [St("den")])
        tt("t1", "nr", "lre", ALU.mult)
        tt("t2", "abim", "lim", ALU.mult)
        tt("t1", "t1", "t2", ALU.add)
        tt("fre", "t1", "den", ALU.mult)
        tt("t1", "abim", "lre", ALU.mult)
        tt("t2", "nr", "lim", ALU.mult)
        tt("t1", "t1", "t2", ALU.subtract)
        tt("fim", "t1", "den", ALU.mult)
        tt("rm2", "mag", "mag", ALU.mult)
        P.op(DVE, lambda e: e.reciprocal(S("rm2"), S("rm2")), [St("rm2")], [St("rm2")])
        tt("lire", "abre", "rm2", ALU.mult)
        self.stt(S("liim"), S("abim"), -1.0, S("rm2"), ALU.mult, ALU.mult, [St("abim"), St("rm2")], [St("liim")])
        self.copy(DVE, S("mure"), S("abre"), [St("abre")], [St("mure")])
        self.copy(DVE, S("muim"), S("abim"), [St("abim")], [St("muim")])
        for _ in range(3):
            csquare("mure", "muim")
        coef = self.coef
        self.copy(DVE, S("w2r"), S("mure"), [St("mure")], [St("w2r")])
        self.copy(DVE, S("w2i"), S("muim"), [St("muim")], [St("w2i")])
        for k in range(8):
            self.copy(DVE, coef.ap[:, :, k, 0], S("w2r"), [St("w2r")], [coef.t()])
            self.copy(DVE, coef.ap[:, :, k, 1], S("w2i"), [St("w2i")], [coef.t()])
            self.ts(DVE, coef.ap[:, :, k, 2], S("w2i"), -1.0, None, ALU.mult, None, [St("w2i")], [coef.t()])
            if k < 7:
                csquare("w2r", "w2i")

        if self.stop_after == "s5b":
            return
        for _ in range(3):
            tt("mag", "mag", "mag", ALU.mult)
        self.copy(DVE, self.rho.ap, S("mag"), [St("mag")], [self.rho.t()])
        for _ in range(3):
            csquare("c", "s")
        tt("t1", "c", "c", ALU.mult)
        tt("t2", "s", "s", ALU.mult)
        tt("t1", "t1", "t2", ALU.add)
        self.ts(DVE, S("t1"), S("t1"), -0.5, 1.5, ALU.mult, ALU.add, [St("t1")], [St("t1")])
        tt("c", "c", "t1", ALU.mult)
        tt("s", "s", "t1", ALU.mult)
        tmpt = [tmp.t(i) for i in range(2)]

        def cmul(ore, oim, otoks, are, aim, atoks, zre, zim, ztoks, n):
            ab = lambda nm: S(nm).unsqueeze(2).to_broadcast([128, 32, n])
            t = [tmp.ap[:, i, 0:32 * n].rearrange("p (a h) -> p a h", h=n) for i in range(2)]
            rd = atoks + ztoks
            self.tt(DVE, t[0], zre, ab(are), ALU.mult, rd, [tmpt[0]])
            self.tt(DVE, t[1], zim, ab(aim), ALU.mult, rd, [tmpt[1]])
            self.tt(DVE, ore, t[0], t[1], ALU.subtract, [tmpt[0], tmpt[1]], otoks)
            self.tt(DVE, t[0], zim, ab(are), ALU.mult, rd, [tmpt[0]])
            self.tt(DVE, t[1], zre, ab(aim), ALU.mult, rd, [tmpt[1]])
            self.tt(DVE, oim, t[0], t[1], ALU.add, [tmpt[0], tmpt[1]], otoks)

        fa = [St("fre"), St("fim")]
        cmul(Bb.ap[:, 0], Bb.ap[:, 1], [Bb.t()], "fre", "fim", fa, Bq.ap[:, 0], Bq.ap[:, 1], [Bq.t()], 16)
        la = [St("lire"), St("liim")]
        for j in range(8):
            if j == 0:
                zr, zi, zt = Bb.ap[:, 0], Bb.ap[:, 1], [Bb.t()]
            else:
                zr, zi, zt = X.ap[:, 0, :, j - 1, :], X.ap[:, 1, :, j - 1, :], [X.t(j - 1)]
            cmul(X.ap[:, 0, :, j, :], X.ap[:, 1, :, j, :], [X.t(j)], "lire", "liim", la, zr, zi, zt, 16)
        if self.stop_after == "s5c":
            return
        for r in range(8):
            for ri, nm in enumerate(["s5_c_re", "s5_c_im"]):
                db = (2 * r + ri) % 2
                P.dma(SP, Cin.ap[:, db, 0:64], I[nm][0][8 * r:8 * r + 8].rearrange("g h p -> (g h) p"), writes=[Cin.t(db)])
                pb = ps[1 + db]
                self.tr(pb.ap[0:64, 0:128], Cin.ap[:, db, 0:64], identf.ap, [Cin.t(db), identf.t()], [pb.t()])
                src = pb.ap[0:64, 0:128].rearrange("p (a g h) -> p a g h", g=2, h=16)
                self.copy(ACT, CT.ap[0:64, ri, 4 * r:4 * r + 4, :], src[:, :, 0, :], [pb.t()], [CT.t()])
                self.copy(ACT, CT.ap[64:128, ri, 4 * r:4 * r + 4, :], src[:, :, 1, :], [pb.t()], [CT.t()])
        if self.stop_after == "s5d":
            return
        ab_ = [St("abre"), St("abim")]
        for i in range(8):
            if i == 0:
                zr, zi, zt = CT.ap[:, 0], CT.ap[:, 1], [CT.t()]
            else:
                zr, zi, zt = Wo.ap[:, 0, :, i - 1, :], Wo.ap[:, 1, :, i - 1, :], [Wo.t(i - 1)]
            cmul(Wo.ap[:, 0, :, i, :], Wo.ap[:, 1, :, i, :], [Wo.t(i)], "abre", "abim", ab_, zr, zi, zt, 16)
        ma = [St("mure"), St("muim")]
        for j in range(8):
            cmul(We.ap[:, 0, :, 16 * j:16 * j + 16], We.ap[:, 1, :, 16 * j:16 * j + 16], [We.t(j)], "mure", "muim", ma,
                 X.ap[:, 0, :, j, :], X.ap[:, 1, :, j, :], [X.t(j)], 16)
        allWo = [Wo.t(i) for i in range(8)]
        WoN = Wo.t("neg")
        self.ts(DVE, Wo.ap[:, 1], Wo.ap[:, 1], -1.0, None, ALU.mult, None, allWo, allWo + [WoN])
        def rest():
            self.copy(DVE, Drep.ap, Drows.ap.unsqueeze(1).to_broadcast([64, 8, 16]), [Drows.t()], [Drep.t()])
            self.tr(ps[3].ap[:, 0:64], Drep.ap.rearrange("p j h -> p (j h)"), identf.ap[0:64, 0:64], [Drep.t(), identf.t()], [ps[3].t()])
            self.copy(DVE, dcol.ap, ps[3].ap[:, 0:64], [ps[3].t()], [dcol.t()])
            self.s5_setup_pairs(locals_=dict(We=We, X=X, Wo=Wo, WoN=WoN, allWo=allWo, stage=stage, WoM=WoM, tmpm=tmpm, dcol=dcol, identf=identf))
            P.barrier()
            if self.stop_after is None or not (self.stop_after == "setup" or self.stop_after.startswith("s5") or self.stop_after == "ada"):
                self.issue_prep_loads(0)
            Tr = Buf(X.ap.rearrange("p r a j h -> p (r a j h)").rearrange("p (a c) -> p a c", c=256), "Tr")
            Ti = Buf(Wo.ap.rearrange("p r a j h -> p (r a j h)").rearrange("p (a c) -> p a c", c=256), "Ti")
            tb = [Buf(We.ap[:, i].rearrange("p a n -> p (a n)"), f"tbig{i}") for i in range(2)]
            P.op(DVE, lambda e: e.memset(Tr.ap[:, :, 0:1], 1.0), [], [Tr.t()])
            P.op(DVE, lambda e: e.memset(Ti.ap[:, :, 0:1], 0.0), [], [Ti.t()])
            self.copy(DVE, S("w2r"), S("c"), [St("c")], [St("w2r")])
            self.copy(DVE, S("w2i"), S("s"), [St("s")], [St("w2i")])
            for k in range(8):
                n = 1 << k
                wrb = S("w2r").unsqueeze(2).to_broadcast([128, 32, n])
                wib = S("w2i").unsqueeze(2).to_broadcast([128, 32, n])
                t0 = tb[0].ap[:, 0:32 * n].rearrange("p (a c) -> p a c", c=n)
                t1 = tb[1].ap[:, 0:32 * n].rearrange("p (a c) -> p a c", c=n)
                wt = [St("w2r"), St("w2i")]
                self.tt(DVE, t0, Tr.ap[:, :, 0:n], wrb, ALU.mult, [Tr.t()] + wt, [tb[0].t()])
                self.tt(DVE, t1, Ti.ap[:, :, 0:n], wib, ALU.mult, [Ti.t()] + wt, [tb[1].t()])
                self.tt(DVE, Tr.ap[:, :, n:2 * n], t0, t1, ALU.subtract, [tb[0].t(), tb[1].t()], [Tr.t()])
                self.tt(DVE, t0, Ti.ap[:, :, 0:n], wrb, ALU.mult, [Ti.t()] + wt, [tb[0].t()])
                self.tt(DVE, t1, Tr.ap[:, :, 0:n], wib, ALU.mult, [Tr.t()] + wt, [tb[1].t()])
                self.tt(DVE, Ti.ap[:, :, n:2 * n], t0, t1, ALU.add, [tb[0].t(), tb[1].t()], [Ti.t()])
                if k < 7:
                    csquare("w2r", "w2i")
            P.dma(SP, self.s5tab[:, :, 0, :].rearrange("a p c -> p a c"), Tr.ap, reads=[Tr.t()], writes=[self.s5tab_tok])
            P.dma(SP, self.s5tab[:, :, 1, :].rearrange("a p c -> p a c"), Ti.ap, reads=[Ti.t()], writes=[self.s5tab_tok])
            self.dbg("Tr", Tr.ap, [128, 32, 256], [Tr.t()])
            self.dbg("Ti", Ti.ap, [128, 32, 256], [Ti.t()])
            self.dbg("rho", self.rho.ap, [128, 32], [self.rho.t()])
        return rest

    def s5_setup_pairs(self, locals_):
        P, c, ps = self.P, self.c, self.ps
        We, X, Wo, WoN, allWo, stage, WoM, tmpm, dcol, identf = (locals_[k] for k in
                                                                ["We", "X", "Wo", "WoN", "allWo", "stage", "WoM", "tmpm", "dcol", "identf"])
        allWe = [We.t(j) for j in range(8)]
        allX = [X.t(j) for j in range(8)]

        def phase1(pr):
            slot = (pr // 4) % 2
            p4 = pr % 4
            st_ = stage.t(slot)
            pa = ps[4 + pr % 2]
            pb = ps[6 + pr % 2]
            self.tr(pa.ap[:, 0:128], We.ap[:, 0, pr, :], identf.ap, allWe + [identf.t()], [pa.t()])
            self.tr(pa.ap[:, 128:256], We.ap[:, 1, pr, :], identf.ap, allWe + [identf.t()], [pa.t()])
            self.copy(ACT, stage.ap[:, slot, p4, 0:256], pa.ap[:, 0:256], [pa.t()], [st_])
            self.copy(ACT, stage.ap[:, slot, p4, 256:512].rearrange("p (r n) -> p r n", r=2),
                      Wo.ap[:, :, pr].rearrange("p r i h -> p r (i h)"), [WoN], [st_])
            wm = WoM.ap[:, pr % 2]
            for gl in range(2):
                self.ts(POOL if gl == 0 else DVE, wm[:, gl], Wo.ap[:, :, pr].rearrange("p r i h -> p r (i h)"), c["hm"].ap[:, gl:gl + 1], None, ALU.mult, None,
                        [WoN, c["hm"].t()], [WoM.t(pr % 2)])
            for gl in range(2):
                for ri in range(2):
                    self.mm(pb.ap[:, 128 * gl:128 * gl + 128], X.ap[:, ri, pr].rearrange("p j h -> p (j h)"),
                            wm[:, gl, ri, :], ri == 0, ri == 1, allX + [WoM.t(pr % 2)], [pb.t()])

        def phase2(pr):
            slot = (pr // 4) % 2
            p4 = pr % 4
            st_ = stage.t(slot)
            pb = ps[6 + pr % 2]
            tm = tmpm.ap[:, pr % 2]
            self.tt(DVE, tm, pb.ap[:, 0:256].rearrange("p (g n) -> p g n", g=2),
                    c["bmask_f"].ap.unsqueeze(1).to_broadcast([128, 2, 128]), ALU.mult, [pb.t(), c["bmask_f"].t()], [tmpm.t(pr % 2)])
            for gl in range(2):
                g = 2 * pr + gl
                self.stt(stage.ap[:, slot, p4, 512 + 128 * gl:512 + 128 * gl + 128], identf.ap, dcol.ap[:, g:g + 1], tm[:, gl, :],
                         ALU.mult, ALU.add, [identf.t(), dcol.t(), tmpm.t(pr % 2)], [st_])
            if p4 == 3:
                ftc = pr // 4
                P.dma(SP, self.s5w[4 * ftc:4 * ftc + 4].rearrange("a p n -> p a n"), stage.ap[:, slot], reads=[st_],
                      writes=[self.s5w_tok[ftc]])
                if ftc == 0:
                    self.dbg("s5w0", stage.ap[:, slot], [128, 4, 768], [st_], dt=BF16)
        phase1(0)
        for pr in range(32):
            if pr + 1 < 32:
                phase1(pr + 1)
            phase2(pr)
        self.dbg("coef", self.coef.ap, [128, 32, 8, 3], [self.coef.t()])

    R_RSTD = 16384
    R_XT = 18432
    R_ACTT = R_XT + 65536
    R_TAIL = R_ACTT + 32768

    def seq_pipeline(self, s):
        P = self.P
        A = self.arena
        self.xT, _ = A.view(self.R_XT, [128, 8, 2048], F32, f"xT{s}")
        self.actT, _ = A.view(self.R_ACTT, [128, 8, 2048], BF16, f"actT{s}")
        self.rstd, _ = A.view(self.R_RSTD, [128, 512], F32, f"rstd{s}")
        self.s5_prep(s)
        P.barrier()
        self.s5_core(s)
        P.barrier()
        self.dbg(f"gT{s}", self.actT.ap, [128, 8, 2048], [], dt=BF16)
        self.dbg(f"xT{s}", self.xT.ap, [128, 8, 2048], [])
        if self.stop_after == "xreload":
            return
        self.glu(s)
        P.barrier()
        self.mlp(s, 0, 1)
        P.barrier()
        self.dbg(f"x1T{s}", self.xT.ap, [128, 8, 2048], [])
        if self.stop_after == "layer0":
            return
        self.kv_phase(s)
        P.barrier()
        self.dbg(f"KT{s}", self.KT.ap, [128, 8, 2048], [], dt=BF16)
        self.dbg(f"V{s}", self.V.ap, [128, 16, 1024], [], dt=BF16)
        if self.stop_after == "kv":
            return
        self.attn_phase(s)
        P.barrier()
        self.dbg(f"x2T{s}", self.xT.ap, [128, 8, 2048], [])
        if self.stop_after == "attn":
            return
        self.mlp(s, 1, 4)
        P.barrier()
        self.output_phase(s)
        P.barrier()

    def load_xtok(self, s, half, Xtok):
        src = self.I["x"][s, half * 1024:(half + 1) * 1024, :].rearrange("(c j) f -> c j f", j=8)
        for q in range(4):
            self.P.dma(SP, Xtok.ap[32 * q:32 * q + 32], src[32 * q:32 * q + 32], writes=[Xtok.t()])

    def issue_prep_loads(self, s):
        A = self.arena
        off = self.R_TAIL + 32768
        Xtok, off = A.view(off, [128, 8, 1024], F32, "Xtok")
        Arep, off = A.view(off, [128, 1024], F32, "Arep")
        Brep, off = A.view(off, [128, 1024], F32, "Brep")
        self.P.dma(SP, Arep.ap, self.modrow[s, 0, :].partition_broadcast(128), reads=[self.modrow_tok], writes=[Arep.t()])
        self.P.dma(SP, Brep.ap, self.modrow[s, 1, :].partition_broadcast(128), reads=[self.modrow_tok], writes=[Brep.t()])
        self.load_xtok(s, 0, Xtok)
        self.pre = (s, Xtok, Arep, Brep)

    def s5_prep(self, s):
        P, c, ps = self.P, self.c, self.ps
        A = self.arena
        off = self.R_TAIL
        D, off = A.view(off, [128, 64, 256], BF16, "D")
        self.D = D
        self.s5_tail_off = off
        Xtok, off = A.view(off, [128, 8, 1024], F32, "Xtok")
        Arep, off = A.view(off, [128, 1024], F32, "Arep")
        Brep, off = A.view(off, [128, 1024], F32, "Brep")
        ss, off = A.view(off, [128, 8], F32, "ss")
        rs, off = A.view(off, [128, 8], F32, "rs")
        junk, off = A.view(off, [128, 1024], BF16, "junk")
        o2 = self.R_ACTT
        htok, o2 = A.view(o2, [128, 64, 8, 16], BF16, "htok")
        tmpf, o2 = A.view(o2, [128, 2, 1024], F32, "tmpf")
        xT = self.xT
        pre = getattr(self, "pre", None)
        preloaded = pre is not None and pre[0] == s
        if preloaded:
            _, Xtok, Arep, Brep = pre
            self.pre = None
        else:
            P.dma(SP, Arep.ap, self.modrow[s, 0, :].partition_broadcast(128), reads=[self.modrow_tok], writes=[Arep.t()])
            P.dma(SP, Brep.ap, self.modrow[s, 1, :].partition_broadcast(128), reads=[self.modrow_tok], writes=[Brep.t()])
        n = 0
        for half in range(2):
            if not (preloaded and half == 0):
                self.load_xtok(s, half, Xtok)
            for j in range(8):
                self.act(junk.ap, Xtok.ap[:, j, :], AF.Square, [Xtok.t()], [junk.t(), ss.t()], accum_out=ss.ap[:, j:j + 1])
            self.ts(DVE, rs.ap, ss.ap, 1.0 / 1024.0, EPS, ALU.mult, ALU.add, [ss.t()], [rs.t()])
            self.act(rs.ap, rs.ap, AF.Sqrt, [rs.t()], [rs.t()])
            P.op(DVE, lambda e: e.reciprocal(rs.ap, rs.ap), [rs.t()], [rs.t()])
            for j in range(8):
                tb = j % 2
                self.stt(tmpf.ap[:, tb, :], Xtok.ap[:, j, :], rs.ap[:, j:j + 1], Arep.ap, ALU.mult, ALU.mult,
                         [Xtok.t(), rs.t(), Arep.t()], [tmpf.t(tb)])
                self.tt(POOL, htok.ap[:, :, j, :], tmpf.ap[:, tb, :].rearrange("p (g h) -> p g h", h=16),
                        Brep.ap.rearrange("p (g h) -> p g h", h=16), ALU.add, [tmpf.t(tb), Brep.t()], [htok.t()])
            for ft in range(8):
                for jq in range(2):
                    bank = ps[4 + n % 4]
                    for jj in range(4):
                        j = 4 * jq + jj
                        self.tr(bank.ap[:, 128 * jj:128 * jj + 128], Xtok.ap[:, j, 128 * ft:128 * ft + 128], c["ident_f"].ap,
                                [Xtok.t(), c["ident_f"].t()], [bank.t()], sig=(jj == 3))
                    dst = xT.ap[:, ft, 1024 * half:1024 * half + 1024].rearrange("p (c j) -> p j c", j=8)[:, 4 * jq:4 * jq + 4, :]
                    self.copy(ACT if n % 2 == 0 else DVE, dst, bank.ap.rearrange("p (j c) -> p j c", c=128), [bank.t()], [xT.t(ft)])
                    n += 1
            for g4 in range(16):
                pb = ps[g4 % 4]
                pbv = pb.ap.bitcast(BF16)
                for gi in range(4):
                    g = 4 * g4 + gi
                    self.tr(pbv[:, 128 * gi:128 * gi + 128], htok.ap[:, g].rearrange("p j h -> p (j h)"), c["ident_b"].ap,
                            [htok.t(), c["ident_b"].t()], [pb.t()], sig=(gi == 3))
                self.copy(ACT if g4 % 2 == 0 else DVE, D.ap[:, 4 * g4:4 * g4 + 4, 128 * half:128 * half + 128],
                          pbv[:, 0:512].rearrange("p (g c) -> p g c", c=128), [pb.t()], [D.t()])

    def s5_core(self, s):
        P, c, ps = self.P, self.c, self.ps
        A = self.arena
        D = self.D
        rho = self.rho
        off = self.s5_tail_off
        wch, _ = A.view(0, [128, 2, 4, 768], BF16, "wch")
        tabs, off = A.view(off, [128, 2, 4, 2, 256], F32, "tabs")
        Wk, off = A.view(off, [128, 2, 8, 256], F32, "Wk")
        Sbf, off = A.view(off, [128, 4, 2, 256], BF16, "Sbf")
        Ygb, off = A.view(off, [128, 2, 2, 256], BF16, "Ygb")
        Gtok, off = A.view(off, [128, 2, 2, 8, 128], BF16, "Gtok")
        gT = self.actT
        P.op(POOL, lambda e: e.memset(Sbf.ap, 0.0), [], [Sbf.t(i) for i in range(4)])
        identb = c["ident_b"]

        def wslot_of(pr):
            return (pr // 4) % 2

        def pair_front(pr):
            p4 = pr % 4
            wslot = wslot_of(pr)
            bankE = ps[pr % 4]
            for ri in range(2):
                for gl in range(2):
                    self.mm(bankE.ap[64 * gl:64 * gl + 64, 256 * ri:256 * ri + 256],
                            wch.ap[:, wslot, p4, 128 * ri + 64 * gl:128 * ri + 64 * gl + 64], D.ap[:, 2 * pr + gl, :], True, True,
                            [wch.t(wslot), D.t()], [bankE.t()], sig=(ri == 1 and gl == 1))

        def scan_ops(pr):
            p4 = pr % 4
            wslot = wslot_of(pr)
            sl, w2 = pr % 4, pr % 2
            bankE = ps[pr % 4]
            Er, Ei = bankE.ap[:, 0:256], bankE.ap[:, 256:512]
            cr, sr = tabs.ap[:, wslot, p4, 0, :], tabs.ap[:, wslot, p4, 1, :]
            W = lambda j: Wk.ap[:, w2, j, :]
            wt = lambda j: Wk.t((w2, j))
            tb_, eb = tabs.t(wslot), bankE.t()
            rb = rho.ap[:, pr:pr + 1].to_broadcast([128, 256])
            ops = []
            ops.append(lambda: self.tt(DVE, W(4), Er, cr, ALU.mult, [eb, tb_], [wt(4)]))
            ops.append(lambda: self.tt(DVE, W(5), Ei, sr, ALU.mult, [eb, tb_], [wt(5)]))
            ops.append(lambda: self.tt(DVE, W(6), Ei, cr, ALU.mult, [eb, tb_], [wt(6)]))
            ops.append(lambda: self.tt(DVE, W(7), Er, sr, ALU.mult, [eb, tb_], [wt(7)]))
            ops.append(lambda: self.tt(DVE, W(0), W(4), W(5), ALU.add, [wt(4), wt(5)], [wt(0)]))
            ops.append(lambda: self.tt(DVE, W(1), W(6), W(7), ALU.subtract, [wt(6), wt(7)], [wt(1)]))
            ops.append(lambda: self.P.op(DVE, lambda e: e.tensor_tensor_scan(W(2), rb, W(0), 0.0, ALU.mult, ALU.add),
                                         [wt(0), rho.t()], [wt(2)]))
            ops.append(lambda: self.P.op(DVE, lambda e: e.tensor_tensor_scan(W(3), rb, W(1), 0.0, ALU.mult, ALU.add),
                                         [wt(1), rho.t()], [wt(3)]))
            n = 255
            ops.append(lambda: self.tt(DVE, W(4)[:, 0:n], W(2)[:, 0:n], cr[:, 0:n], ALU.mult, [wt(2), tb_], [wt(4)]))
            ops.append(lambda: self.tt(DVE, W(5)[:, 0:n], W(3)[:, 0:n], sr[:, 0:n], ALU.mult, [wt(3), tb_], [wt(5)]))
            ops.append(lambda: self.tt(DVE, W(6)[:, 0:n], W(3)[:, 0:n], cr[:, 0:n], ALU.mult, [wt(3), tb_], [wt(6)]))
            ops.append(lambda: self.tt(DVE, W(7)[:, 0:n], W(2)[:, 0:n], sr[:, 0:n], ALU.mult, [wt(2), tb_], [wt(7)]))
            ops.append(lambda: self.tt(DVE, Sbf.ap[:, sl, 0, 1:256], W(4)[:, 0:n], W(5)[:, 0:n], ALU.subtract, [wt(4), wt(5)], [Sbf.t(sl)]))
            ops.append(lambda: self.tt(DVE, Sbf.ap[:, sl, 1, 1:256], W(6)[:, 0:n], W(7)[:, 0:n], ALU.add, [wt(6), wt(7)], [Sbf.t(sl)]))
            return ops

        def pair_back(pr):
            ft, p4 = pr // 4, pr % 4
            wslot = wslot_of(pr)
            gslot = ft % 2
            sl = pr % 4
            ys = pr % 2
            bankY = ps[4 + pr % 2]
            for gl in range(2):
                o = bankY.ap[:, 256 * gl:256 * gl + 256]
                hs = slice(64 * gl, 64 * gl + 64)
                self.mm(o, wch.ap[:, wslot, p4, 512 + 128 * gl:512 + 128 * gl + 128], D.ap[:, 2 * pr + gl, :], True, False,
                        [wch.t(wslot), D.t()], [bankY.t()])
                self.mm(o, wch.ap[hs, wslot, p4, 256:384], Sbf.ap[hs, sl, 0, :], False, False, [wch.t(wslot), Sbf.t(sl)], [bankY.t()])
                self.mm(o, wch.ap[hs, wslot, p4, 384:512], Sbf.ap[hs, sl, 1, :], False, True, [wch.t(wslot), Sbf.t(sl)], [bankY.t()],
                        sig=(gl == 1))
            self.act(Ygb.ap[:, ys].rearrange("p g c -> p (g c)"), bankY.ap, AF.Gelu, [bankY.t()], [Ygb.t(ys)])
            pT = ps[6]
            pTv = pT.ap.bitcast(BF16)
            for gl in range(2):
                for half in range(2):
                    q = 2 * gl + half
                    self.tr(pTv[:, 128 * q:128 * q + 128], Ygb.ap[:, ys, gl, 128 * half:128 * half + 128], identb.ap,
                            [Ygb.t(ys), identb.t()], [pT.t()], sig=(q == 3))
            for gl in range(2):
                fo = 32 * p4 + 16 * gl
                self.copy(ACT, Gtok.ap[:, gslot, :, :, fo:fo + 16],
                          pTv[:, 256 * gl:256 * gl + 256].rearrange("p (a i h) -> p a i h", a=2, h=16), [pT.t()], [Gtok.t(gslot)])

        def ft_done(ft):
            gslot = ft % 2
            for half in range(2):
                pT2 = ps[7]
                pv = pT2.ap.bitcast(BF16)
                for i in range(8):
                    self.tr(pv[:, 128 * i:128 * i + 128], Gtok.ap[:, gslot, half, i, :], identb.ap, [Gtok.t(gslot), identb.t()], [pT2.t()],
                            sig=(i == 7))
                self.copy(ACT, gT.ap[:, ft, 1024 * half:1024 * half + 1024].rearrange("p (c i) -> p i c", i=8),
                          pv.rearrange("p (i c) -> p i c", c=128), [pT2.t()], [gT.t(ft)])

        def load_w(ft):
            P.dma(SP, wch.ap[:, ft % 2], self.s5w[4 * ft:4 * ft + 4].rearrange("a p n -> p a n"), reads=[self.s5w_tok[ft]],
                  writes=[wch.t(ft % 2)])
            P.dma(SP, tabs.ap[:, ft % 2], self.s5tab[4 * ft:4 * ft + 4].rearrange("a p r c -> p a r c"), reads=[self.s5tab_tok],
                  writes=[tabs.t(ft % 2)])

        load_w(0)
        load_w(1)
        for p4 in (0, 1):
            pair_front(p4)
        for pp in range(16):
            prs = [2 * pp, 2 * pp + 1]
            oa, ob_ = scan_ops(prs[0]), scan_ops(prs[1])
            for fa, fb in zip(oa, ob_):
                fa()
                fb()
            if pp + 1 < 16:
                for pr in (2 * pp + 2, 2 * pp + 3):
                    pair_front(pr)
            for pr in prs:
                pair_back(pr)
            if pp % 2 == 1:
                ft = pp // 2
                ft_done(ft)
                if ft + 2 < 8:
                    load_w(ft + 2)

    def x_reload(self, s):
        P, c, ps = self.P, self.c, self.ps
        A = self.arena
        Xtok, _ = A.view(self.R_TAIL, [128, 8, 1024], F32, "Xtok2")
        xT = self.xT
        n = 0
        for half in range(2):
            self.load_xtok(s, half, Xtok)
            for ft in range(8):
                for jq in range(2):
                    bank = ps[n % 4]
                    for jj in range(4):
                        j = 4 * jq + jj
                        self.tr(bank.ap[:, 128 * jj:128 * jj + 128], Xtok.ap[:, j, 128 * ft:128 * ft + 128], c["ident_f"].ap,
                                [Xtok.t(), c["ident_f"].t()], [bank.t()], sig=(jj == 3))
                    dst = xT.ap[:, ft, 1024 * half:1024 * half + 1024].rearrange("p (c j) -> p j c", j=8)[:, 4 * jq:4 * jq + 4, :]
                    self.copy(ACT if n % 2 == 0 else DVE, dst, bank.ap.rearrange("p (j c) -> p j c", c=128), [bank.t()], [xT.t(ft)])
                    n += 1


    def take_prefetched(self, key):
        pf = getattr(self, "_prefetched", None)
        if pf is not None and pf[0] == key:
            self._prefetched = None
            return pf[1]
        return None

    def prefetch(self, key, loader):
        self._prefetched = (key, loader())

    def stream(self, loaders, computes, key=None):
        n = len(loaders)
        first = self.take_prefetched(key) if key is not None else None
        slots = {0: first if first is not None else loaders[0]()}
        for i in range(n):
            if i + 1 < n:
                slots[i + 1] = loaders[i + 1]()
            computes[i](slots.pop(i))

    def fm_norm(self, s, n, tt, dst, sq, tmpf, bank):
        c, xT, rstd = self.c, self.xT, self.rstd
        ts_ = slice(512 * tt, 512 * tt + 512)
        for ft in range(8):
            self.act(sq.ap[:, ft, :], xT.ap[:, ft, ts_], AF.Square, [xT.t(ft)], [sq.t(ft)])
        for ft in range(8):
            self.mm(bank.ap, c["onesm_b"].ap, sq.ap[:, ft, :], ft == 0, ft == 7, [c["onesm_b"].t(), sq.t(ft)], [bank.t()])
        self.act(rstd.ap, bank.ap, AF.Ln, [bank.t()], [rstd.t()], bias=self.epsc.ap)
        self.act(rstd.ap, rstd.ap, AF.Exp, [rstd.t()], [rstd.t()], scale=-0.5)
        ms = self.modsc
        for ft in range(8):
            tb = ft % 2
            self.stt(tmpf.ap[:, tb, :], xT.ap[:, ft, ts_], ms.ap[:, s, n, 0, ft:ft + 1], rstd.ap, ALU.mult, ALU.mult,
                     [xT.t(ft), ms.t(), rstd.t()], [tmpf.t(tb)])
            self.act(dst[:, ft, :], tmpf.ap[:, tb, :], AF.Identity, [tmpf.t(tb), ms.t()], [self.cur_h_tok], bias=ms.ap[:, s, n, 1, ft:ft + 1])

    def glu(self, s):
        P, ps = self.P, self.ps
        A = self.arena
        off = self.R_TAIL
        sg, off = A.view(off, [128, 2, 512], F32, "sg")
        mt_, off = A.view(off, [128, 2, 512], F32, "mt")
        gT, xT, adaT = self.actT, self.xT, self.adaT
        w = self.I["s5_w_glu"][0].rearrange("(k p) n -> p k n", p=128)
        allg = [gT.t(ft) for ft in range(8)]
        cnt = [0]

        def loader(ft):
            def f():
                v = lambda a: a[:, 0:2048].rearrange("p (k g n) -> p k g n", g=2, n=128)
                return self.wload([(lambda a: v(a)[:, :, 0, :], w[:, :, 128 * ft:128 * ft + 128]),
                                   (lambda a: v(a)[:, :, 1, :], w[:, :, 1024 + 128 * ft:1024 + 128 * ft + 128])])
            return f

        def compute(ft):
            def f(slot):
                wv = slot.ap[:, 0:2048].rearrange("p (k g n) -> p k g n", g=2, n=128)
                for tt in range(4):
                    n = cnt[0]
                    cnt[0] += 1
                    bv, bg = ps[(2 * n) % 8], ps[(2 * n + 1) % 8]
                    ts_ = slice(512 * tt, 512 * tt + 512)
                    for gi, bank in enumerate([bv, bg]):
                        for k in range(8):
                            self.mm(bank.ap, wv[:, k, gi, :], gT.ap[:, k, ts_], k == 0, k == 7, [slot.t()] + allg, [bank.t()])
                    b2 = n % 2
                    self.act(sg.ap[:, b2, :], bg.ap, AF.Sigmoid, [bg.t()], [sg.t(b2)])
                    self.tt(DVE, mt_.ap[:, b2, :], bv.ap, sg.ap[:, b2, :], ALU.mult, [bv.t(), sg.t(b2)], [mt_.t(b2)])
                    self.stt(xT.ap[:, ft, ts_], mt_.ap[:, b2, :], adaT.ap[:, 16 + ft, s:s + 1], xT.ap[:, ft, ts_], ALU.mult, ALU.add,
                             [mt_.t(b2), adaT.t(), xT.t(ft)], [xT.t(ft)])
            return f

        self.stream([loader(ft) for ft in range(8)], [compute(ft) for ft in range(8)])
        self.prefetch(("w1", 0), lambda: self.wload_k8(self.I["mlp_w1"][0], 0))

    def mlp(self, s, l, nidx):
        P, ps = self.P, self.ps
        A = self.arena
        uT, o_t = A.view(self.R_TAIL, [128, 32, 1024], BF16, "uT")
        htile1, _ = A.view(o_t, [128, 8, 1024], BF16, "htile1")
        off = self.R_ACTT
        htile0, off = A.view(off, [128, 8, 1024], BF16, "htile0")
        sq, off = A.view(off, [128, 8, 512], BF16, "sq")
        tmpf, off = A.view(off, [128, 2, 512], F32, "tmpf2")
        r, off = A.view(off, [128, 2, 1024], BF16, "r")
        htiles = [htile0, htile1]
        xT, adaT = self.xT, self.adaT
        w1 = self.I["mlp_w1"][l]
        w2 = self.I["mlp_w2"][l].rearrange("(k p) n -> p k n", p=128)
        gm0 = 48 * l + 40
        nb = [0]

        def norm(t2):
            for sub in range(2):
                self.cur_h_tok = htiles[t2].t(sub)
                self.fm_norm(s, nidx, 2 * t2 + sub, htiles[t2].ap[:, :, 512 * sub:512 * sub + 512], sq, tmpf, ps[7])

        norm(0)
        for t2 in range(2):
            htile = htiles[t2]
            loaders, computes = [], []
            for ch in range(8):
                loaders.append(lambda ch=ch: self.wload_k8(w1, 512 * ch))

                def c1(sv, ch=ch):
                    slot, wv = sv
                    for m4 in range(4):
                        e = 4 * ch + m4
                        for sub in range(2):
                            bank = ps[nb[0] % 4]
                            nb[0] += 1
                            for k in range(8):
                                self.mm(bank.ap, wv[:, k, 128 * m4:128 * m4 + 128], htile.ap[:, k, 512 * sub:512 * sub + 512], k == 0, k == 7,
                                        [slot.t(), htile.t(sub)], [bank.t()])
                            self.act(r.ap[:, e % 2, 512 * sub:512 * sub + 512], bank.ap, AF.Relu, [bank.t()], [r.t((e % 2, sub))])
                        self.tt(DVE, uT.ap[:, e, :], r.ap[:, e % 2, :], r.ap[:, e % 2, :], ALU.mult,
                                [r.t((e % 2, 0)), r.t((e % 2, 1))], [uT.t(e)])
                    if ch == 7 and t2 == 0:
                        norm(1)
                computes.append(c1)
            allu = [uT.t(e) for e in range(32)]
            for o in range(8):
                def l2(o=o):
                    slot = self.wload([(lambda a: a.rearrange("p (k n) -> p k n", n=128), w2[:, :, 128 * o:128 * o + 128])])
                    return slot, slot.ap.rearrange("p (k n) -> p k n", n=128)
                loaders.append(l2)

                def c2(sv, o=o, t2=t2):
                    slot, wv = sv
                    for sub in range(2):
                        bank = ps[4 + (2 * o + sub) % 3]
                        ts_ = slice(1024 * t2 + 512 * sub, 1024 * t2 + 512 * sub + 512)
                        for k in range(32):
                            self.mm(bank.ap, wv[:, k, :], uT.ap[:, k, 512 * sub:512 * sub + 512], k == 0, k == 31, [slot.t()] + allu, [bank.t()])
                        self.stt(xT.ap[:, o, ts_], bank.ap, adaT.ap[:, gm0 + o, s:s + 1], xT.ap[:, o, ts_], ALU.mult, ALU.add,
                                 [bank.t(), adaT.t(), xT.t(o)], [xT.t(o)])
                computes.append(c2)
            self.stream(loaders, computes, key=("w1", l) if t2 == 0 else None)
        if l == 0:
            self.prefetch(("wkv", 0), lambda: self.wload_k8(self.I["w_kv"], 0))

    def headnorm_batch(self, banks, sbanks, gaincol, dsts, dtoks, sq, tk4):
        c = self.c
        n = len(banks)
        for j in range(n):
            self.act(sq.ap[:, j, :], banks[j].ap, AF.Square, [banks[j].t()], [sq.t(j)])
        for j in range(n):
            self.mm(sbanks[j].ap, c["blk_b"].ap, sq.ap[:, j, :], True, True, [c["blk_b"].t(), sq.t(j)], [sbanks[j].t()])
        for j in range(n):
            self.act(tk4.ap[:, j, :], sbanks[j].ap, AF.Ln, [sbanks[j].t()], [tk4.t(j)], bias=self.epsc.ap)
        for j in range(n):
            self.act(tk4.ap[:, j, :], tk4.ap[:, j, :], AF.Exp, [tk4.t(j)], [tk4.t(j)], scale=-0.5)
        for j in range(n):
            self.stt(dsts[j], banks[j].ap, gaincol, tk4.ap[:, j, :], ALU.mult, ALU.mult, [banks[j].t(), self.qkg.t(), tk4.t(j)], dtoks[j])

    def kv_phase(self, s):
        P, ps = self.P, self.ps
        A = self.arena
        self.KT, _ = A.view(self.R_ACTT, [128, 8, 2048], BF16, "KT")
        off = self.R_TAIL
        self.V, off = A.view(off, [128, 16, 1024], BF16, "V")
        self.attn_off = off
        htile, off = A.view(off, [128, 8, 2048], BF16, "htile_kv")
        sq, off = A.view(off, [128, 8, 512], BF16, "sq_kv")
        tmpf, off = A.view(off, [128, 2, 512], F32, "tmpf_kv")
        tk2, off = A.view(off, [128, 2, 512], F32, "tk2")
        KT, V = self.KT, self.V
        wkv = self.I["w_kv"]
        for tt in range(4):
            self.cur_h_tok = htile.t(tt)
            self.fm_norm(s, 2, tt, htile.ap[:, :, 512 * tt:512 * tt + 512], sq, tmpf, ps[7])
        cnt = [0]
        loaders = [lambda ch=ch: self.wload_k8(wkv, 512 * ch) for ch in range(4)]
        computes = []
        for ch in range(4):
            def cK(sv, ch=ch):
                slot, wv = sv
                for tt in range(4):
                    ts_ = slice(512 * tt, 512 * tt + 512)
                    for hf in range(2):
                        n = cnt[0]
                        cnt[0] += 1
                        banks = [ps[(2 * n) % 4], ps[(2 * n + 1) % 4]]
                        sbanks = [ps[4 + (2 * n) % 4], ps[4 + (2 * n + 1) % 4]]
                        for j in range(2):
                            m4 = 2 * hf + j
                            for k in range(8):
                                self.mm(banks[j].ap, wv[:, k, 128 * m4:128 * m4 + 128], htile.ap[:, k, ts_], k == 0, k == 7,
                                        [slot.t(), htile.t(tt)], [banks[j].t()])
                        prs = [4 * ch + 2 * hf + j for j in range(2)]
                        sqv = Buf(sq.ap[:, 2 * (n % 4):2 * (n % 4) + 2, :], "sqv")
                        sqv.toks = {0: sq.t(2 * (n % 4)), 1: sq.t(2 * (n % 4) + 1)}
                        self.headnorm_batch(banks, sbanks, self.qkg.ap[:, 1:2], [KT.ap[:, p_, ts_] for p_ in prs],
                                            [[KT.t((p_, tt))] for p_ in prs], sqv, tk2)

            def cV(sv, ch=ch):
                slot, wv = sv
                for tb in range(16):
                    n = cnt[0]
                    cnt[0] += 1
                    bank = ps[n % 4]
                    for k in range(8):
                        self.mm(bank.ap, htile.ap[:, k, 128 * tb:128 * tb + 128], wv[:, k, :], k == 0, k == 7,
                                [slot.t(), htile.t(tb // 4)], [bank.t()])
                    self.copy(ACT if n % 2 == 0 else DVE, V.ap[:, tb, 512 * (ch - 2):512 * (ch - 2) + 512], bank.ap,
                              [bank.t()], [V.t(tb)])
            computes.append(cK if ch < 2 else cV)
        self.stream(loaders, computes, key=("wkv", 0))
        self.prefetch(("wq", 0), lambda: self.wload_k8(self.I["sb_w_q"][0], 0))

    def attn_phase(self, s):
        P, ps, c = self.P, self.ps, self.c
        A = self.arena
        KT, V, xT, adaT = self.KT, self.V, self.xT, self.adaT
        off = self.attn_off
        qT, off = A.view(off, [128, 8, 512], BF16, "qT")
        oT, off = A.view(off, [128, 8, 512], BF16, "oT")
        o1 = off
        htile, o1 = A.view(o1, [128, 8, 512], BF16, "htile_q")
        sq, o1 = A.view(o1, [128, 8, 512], BF16, "sq_q")
        tmpf, o1 = A.view(o1, [128, 2, 512], F32, "tmpf_q")
        tk4, o1 = A.view(o1, [128, 4, 512], F32, "tk4_q")
        o2 = off
        Eb, o2 = A.view(o2, [128, 2, 2, 512], F32, "Eb")
        Lb, o2 = A.view(o2, [128, 3, 2, 512], BF16, "Lb")
        Wb, o2 = A.view(o2, [128, 3, 2, 512], BF16, "Wb")
        R32, o2 = A.view(o2, [128, 2, 512], F32, "R32")
        Rbf, o2 = A.view(o2, [128, 3, 2, 512], BF16, "Rbf")
        wq = self.I["sb_w_q"][0]
        wo = self.I["sb_w_o"][0]
        identb, negmask, negtri, negones = c["ident_b"], c["negmask_b"], c["negtri_b"], c["negones_b"]
        zeros = c["zeros512_b"]
        zbank = Buf(self.psall[:, 0:1024].rearrange("p (h t) -> p h t", h=2), "zbank")
        zbank.toks[0] = ps[0].t()
        abank = []
        for j in range(2):
            b_ = Buf(self.psall[:, 1024 + 1024 * j:2048 + 1024 * j].rearrange("p (h t) -> p h t", h=2), f"abank{j}")
            abank.append(b_)
        obank = ps[6]

        def ztoks():
            return [ps[0].t(), ps[1].t()]

        def atoks(j):
            return [ps[2 + 2 * j].t(), ps[3 + 2 * j].t()]

        wq_first = self.take_prefetched(("wq", 0))
        wq_slots = [wq_first if wq_first is not None else self.wload_k8(wq, 0), self.wload_k8(wq, 512)]
        self.cur_h_tok = htile.t()
        self.fm_norm(s, 3, 0, htile.ap, sq, tmpf, ps[7])
        for qt in range(4):
            ts_ = slice(512 * qt, 512 * qt + 512)
            cnt = [0]

            def cQ(sv, ch):
                slot, wv = sv
                banks = [ps[m4] for m4 in range(4)]
                for m4 in range(4):
                    for k in range(8):
                        self.mm(banks[m4].ap, wv[:, k, 128 * m4:128 * m4 + 128], htile.ap[:, k, :], k == 0, k == 7,
                                [slot.t(), htile.t()], [banks[m4].t()])
                self.headnorm_batch(banks, [ps[4 + m4] for m4 in range(4)], self.qkg.ap[:, 0:1],
                                    [qT.ap[:, 4 * ch + m4, :] for m4 in range(4)], [[qT.t(4 * ch + m4)] for m4 in range(4)], sq, tk4)

            for ch in range(2):
                cQ(wq_slots[ch], ch)
            P.barrier()
            wo_slots = [self.wload_k8(wo, 512 * ch) for ch in range(2)]
            tiles = []
            nkb = 4 * qt + 4
            for pair in range(8):
                for ii, kb in enumerate(range(nkb - 1, -1, -1)):
                    r_ = kb - 4 * qt
                    c0 = 128 * r_ if r_ >= 0 else 0
                    tiles.append(dict(pair=pair, kb=kb, first=(ii == 0), last=(kb == 0), diag=(r_ >= 0), c0=c0))
            nt = len(tiles)

            def zmm(bank, btoks, t, close):
                c0 = t["c0"]
                kb = t["kb"]
                rd = [KT.t((t["pair"], kb // 4)), qT.t(t["pair"])]
                for hl in range(2):
                    hs = slice(64 * hl, 64 * hl + 64)
                    self.mm(bank.ap[:, hl, c0:512], KT.ap[hs, t["pair"], 128 * kb:128 * kb + 128], qT.ap[hs, t["pair"], c0:512], True,
                            close and not t["diag"], rd, btoks, sig=(close and not t["diag"] and hl == 1))
                if t["diag"]:
                    for hl in range(2):
                        self.mm(bank.ap[:, hl, c0:c0 + 128], identb.ap, negmask.ap, False, close, [identb.t(), negmask.t()], btoks,
                                sig=(close and hl == 1))

            def stageA1(i):
                t = tiles[i]
                c0 = t["c0"]
                zmm(zbank, ztoks(), t, True)
                self.act(Eb.ap[:, i % 2, :, c0:512], zbank.ap[:, :, c0:512], AF.Exp, ztoks(), [Eb.t(i % 2)])

            def stageA2(i):
                t = tiles[i]
                c0 = t["c0"]
                self.act(Lb.ap[:, i % 3, :, c0:512], Eb.ap[:, i % 2, :, c0:512], AF.Ln, [Eb.t(i % 2)], [Lb.t(i % 3)], bias=self.onec.ap)
                if not t["last"]:
                    if t["first"]:
                        P.op(POOL, lambda e: e.memset(R32.ap, 0.0), [], [R32.t()])
                    self.tt(POOL, R32.ap[:, :, c0:512], R32.ap[:, :, c0:512], Lb.ap[:, i % 3, :, c0:512], ALU.add,
                            [R32.t(), Lb.t(i % 3)], [R32.t()])
                    self.copy(DVE, Rbf.ap[:, i % 3], R32.ap, [R32.t()], [Rbf.t(i % 3)])

            def stageB(i):
                t = tiles[i]
                ab = abank[i % 2]
                at = atoks(i % 2)
                c0 = t["c0"]
                zmm(ab, at, t, False)
                for hl in range(2):
                    self.mm(ab.ap[:, hl, c0:512], negtri.ap, Lb.ap[:, i % 3, hl, c0:512], False, t["first"], [negtri.t(), Lb.t(i % 3)], at,
                            sig=(t["first"] and hl == 1))
                if not t["first"]:
                    for hl in range(2):
                        self.mm(ab.ap[:, hl, c0:512], negones.ap, Rbf.ap[:, (i - 1) % 3, hl, c0:512], False, True,
                                [negones.t(), Rbf.t((i - 1) % 3)], at, sig=(hl == 1))
                self.act(Wb.ap[:, i % 3, :, c0:512], ab.ap[:, :, c0:512], AF.Exp, at, [Wb.t(i % 3)])

            def stageC(i):
                t = tiles[i]
                c0 = t["c0"]
                obank = ps[6 + t["pair"] % 2]
                if t["first"]:
                    self.mm(obank.ap, zeros.ap[:, 0:128], zeros.ap, True, False, [zeros.t()], [obank.t()])
                for hl in range(2):
                    h = 2 * t["pair"] + hl
                    hs = slice(64 * hl, 64 * hl + 64)
                    self.mm(obank.ap[hs, c0:512], V.ap[:, t["kb"], 64 * h:64 * h + 64], Wb.ap[:, i % 3, hl, c0:512], False, t["last"],
                            [V.t(t["kb"]), Wb.t(i % 3)], [obank.t()], sig=(t["last"] and hl == 1))
                if t["last"]:
                    self.copy(DVE, oT.ap[:, t["pair"], :], obank.ap, [obank.t()], [oT.t(t["pair"])])

            for step in range(nt + 3):
                if step < nt:
                    stageA1(step)
                if 0 <= step - 2 < nt:
                    stageB(step - 2)
                if step < nt:
                    stageA2(step)
                if 0 <= step - 3 < nt:
                    stageC(step - 3)
            P.barrier()
            allo = [oT.t(p_) for p_ in range(8)]

            def cO(sv, ch):
                slot, wv = sv
                for m4 in range(4):
                    ft = 4 * ch + m4
                    bank = ps[4 + ft % 2]
                    for k in range(8):
                        self.mm(bank.ap, wv[:, k, 128 * m4:128 * m4 + 128], oT.ap[:, k, :], k == 0, k == 7, [slot.t()] + allo, [bank.t()])
                    self.stt(xT.ap[:, ft, ts_], bank.ap, adaT.ap[:, 48 + 16 + ft, s:s + 1], xT.ap[:, ft, ts_], ALU.mult, ALU.add,
                             [bank.t(), adaT.t(), xT.t(ft)], [xT.t(ft)])

            if qt < 3:
                self.cur_h_tok = htile.t()
                self.fm_norm(s, 3, qt + 1, htile.ap, sq, tmpf, ps[7])
            cO(wo_slots[0], 0)
            if qt < 3:
                wq0 = self.wload_k8(wq, 0)
            cO(wo_slots[1], 1)
            if qt < 3:
                wq_slots = [wq0, self.wload_k8(wq, 512)]
            else:
                self.prefetch(("w1", 1), lambda: self.wload_k8(self.I["mlp_w1"][1], 0))

    def output_phase(self, s):
        P, ps, c = self.P, self.ps, self.c
        A = self.arena
        ost, _ = A.view(self.R_TAIL, [128, 2, 1024], F32, "ostage")
        xT = self.xT
        if s + 1 < self.nseq:
            self.issue_prep_loads(s + 1)
        allx = [xT.t(ft) for ft in range(8)]
        n = 0
        for tb in range(16):
            ob = tb % 2
            for hf in range(2):
                bank = ps[n % 4]
                for jj in range(4):
                    ft = 4 * hf + jj
                    self.tr(bank.ap[:, 128 * jj:128 * jj + 128], xT.ap[:, ft, 128 * tb:128 * tb + 128], c["ident_f"].ap,
                            allx + [c["ident_f"].t()], [bank.t()], sig=(jj == 3))
                self.copy(ACT if n % 2 == 0 else DVE, ost.ap[:, ob, 512 * hf:512 * hf + 512], bank.ap, [bank.t()], [ost.t(ob)])
                n += 1
            otok = Tok(f"out{s}_{tb}")
            P.dma(SP, self.out[s, 128 * tb:128 * tb + 128, :], ost.ap[:, ob, :], reads=[ost.t(ob)], writes=[otok], tok=ost.t(ob))


def build_program(debug=None, stop_after=None, nseq=NS):
    b = Builder(debug=debug, stop_after=stop_after, nseq=nseq)
    nc = b.build()
    if b.P.nfwd:
        print("forward-redirected PE deps:", b.P.nfwd)
    return nc, b


_PARAM_NAMES = ["ada_w", "ada_b", "mix_norm_g", "mlp_norm_g", "mlp_w1", "mlp_w2", "s5_a_re", "s5_a_im", "s5_log_dt",
                "s5_b_re", "s5_b_im", "s5_c_re", "s5_c_im", "s5_d", "s5_w_glu", "kv_ada_w", "kv_ada_b", "kv_norm_g",
                "w_kv", "k_norm_g", "sb_w_q", "q_norm_g", "sb_w_o"]


def kernel(**inputs):
    x = np.ascontiguousarray(np.asarray(inputs["x"], dtype=np.float32))
    c = np.ascontiguousarray(np.asarray(inputs["c"], dtype=np.float32))
    params = {k: np.ascontiguousarray(np.asarray(inputs[k], dtype=np.float32)) for k in _PARAM_NAMES}
    nc, _ = build_program()
    in_maps = []
    for i in range(NCORES):
        m = dict(params)
        m["x"] = np.ascontiguousarray(x[NS * i:NS * i + NS])
        m["c"] = np.ascontiguousarray(c[NS * i:NS * i + NS])
        in_maps.append(m)
    res = run_bass_kernel_spmd(nc, in_maps, core_ids=list(range(NCORES)))
    out = np.concatenate([np.asarray(r["out"]) for r in res.results], axis=0)
    return out.astype(np.float32, copy=False)
```

```python
import math
from contextlib import ExitStack

import numpy as np
import concourse.bass as bass
import concourse.mybir as mybir
from concourse.bass_utils import run_bass_kernel_spmd

F32 = mybir.dt.float32
BF16 = mybir.dt.bfloat16
U8 = mybir.dt.uint8
AF = mybir.ActivationFunctionType
ALU = mybir.AluOpType
PE, DVE, ACT, POOL, SP = "tensor", "vector", "scalar", "gpsimd", "sync"
ENGS = [PE, DVE, ACT, POOL, SP]

D = 1024
T = 2048
NS = 2
FT = 8
TT = 512
NTT = T // TT
DFF = 4096
EPS = 1e-6
NCORES = 8
EPOCH_MAX = 12000


class Tok:
    __slots__ = ("w", "r", "sem", "semcnt", "name")

    def __init__(self, name=""):
        self.w = None
        self.r = []
        self.sem = None
        self.semcnt = 0
        self.name = name


class Op:
    __slots__ = ("eng", "fn", "deps", "dma", "sig", "semref", "semval", "inc", "id", "sigok")


class Prog:
    def __init__(self, nc, stack):
        self.nc = nc
        self.stack = stack
        self.ops = []
        self.last = {e: None for e in ENGS}
        self.barrier_deps = {e: [] for e in ENGS}
        self.dmas = []
        self.nsem = 0

    def new_sem(self, name):
        self.nsem += 1
        return self.stack.enter_context(self.nc.semaphore(f"{name}_{self.nsem}"))

    def op(self, eng, fn, reads=(), writes=(), dma=None, sigok=True):
        o = Op()
        o.sigok = sigok
        o.eng = eng
        o.fn = fn
        o.dma = dma
        o.sig = False
        o.semref = None
        o.semval = 0
        o.inc = 0
        o.id = len(self.ops)
        deps = {}

        def add(d, war=False):
            if d is None:
                return
            if d.dma is None and d.eng == eng:
                if eng == PE:
                    return
            deps[d.id] = d

        for t in reads:
            add(t.w)
        for t in writes:
            add(t.w)
            for r in t.r:
                add(r, war=True)
        for d in self.barrier_deps[eng]:
            add(d)
        self.barrier_deps[eng] = []
        o.deps = list(deps.values())
        for t in reads:
            t.r.append(o)
        for t in writes:
            t.w = o
            t.r = []
        self.ops.append(o)
        self.last[eng] = o
        if dma is not None:
            self.dmas.append(o)
        return o

    def dma(self, eng, out, in_, reads=(), writes=(), tok=None, **kw):
        if tok is None:
            tok = writes[0]

        def fn(e):
            return e.dma_start(out=out, in_=in_, **kw)
        return self.op(eng, fn, reads=reads, writes=writes, dma=tok)

    def barrier(self):
        deps = [o for o in self.last.values() if o is not None] + list(self.dmas)
        self.dmas = []
        for e in ENGS:
            self.barrier_deps[e] = list(self.barrier_deps[e]) + deps

    def emit(self):
        nc = self.nc
        pe_ops = [o for o in self.ops if o.eng == PE and o.dma is None]
        if pe_ops:
            pe_ops[-1].sigok = True
        nxt = {}
        cur = None
        for o in reversed(pe_ops):
            if o.sigok:
                cur = o
            nxt[o.id] = cur
        nfwd = 0
        for o in self.ops:
            nd = {}
            for d in o.deps:
                if d.eng == PE and d.dma is None and not d.sigok:
                    d = nxt[d.id]
                    if d.id > o.id:
                        nfwd += 1
                nd[d.id] = d
            o.deps = list(nd.values())
        self.nfwd = nfwd
        for o in self.ops:
            for d in o.deps:
                d.sig = True
        cnt = {e: 0 for e in ENGS}
        cursem = {e: None for e in ENGS}
        for o in self.ops:
            if o.dma is not None:
                t = o.dma
                if t.sem is None or t.semcnt >= 16 * 3000:
                    t.sem = self.new_sem("d")
                    t.semcnt = 0
                t.semcnt += 16
                o.semref = t.sem
                o.semval = t.semcnt
                o.inc = 16
            elif o.sig:
                e = o.eng
                if cursem[e] is None or cnt[e] >= EPOCH_MAX:
                    cursem[e] = self.new_sem("e" + e[:2])
                    cnt[e] = 0
                cnt[e] += 1
                o.semref = cursem[e]
                o.semval = cnt[e]
                o.inc = 1
        with nc.Block() as block:
            for e in ENGS:
                oplist = [o for o in self.ops if o.eng == e]

                def body(eh, oplist=oplist):
                    waited = {}
                    for o in oplist:
                        need = {}
                        for d in o.deps:
                            k = id(d.semref)
                            if k not in need or need[k][1] < d.semval:
                                need[k] = (d.semref, d.semval)
                        for k, (s, v) in need.items():
                            if waited.get(k, 0) < v:
                                eh.wait_ge(s, v)
                                waited[k] = v
                        if o.fn is not None:
                            inst = o.fn(eh)
                            if o.semref is not None:
                                inst.then_inc(o.semref, o.inc)

                getattr(block, e)(body)


class Buf:
    def __init__(self, ap, name=""):
        self.ap = ap
        self.name = name
        self.toks = {}

    def t(self, key=0):
        if key not in self.toks:
            self.toks[key] = Tok(f"{self.name}{key}")
        return self.toks[key]

    def ts(self, keys):
        return [self.t(k) for k in keys]


class Arena:
    def __init__(self, ap_u8):
        self.ap = ap_u8
        self.size = ap_u8.shape[1]

    def view(self, off, shape, dt, name=""):
        esz = 4 if dt == F32 else 2
        n = 1
        for s in shape[1:]:
            n *= s
        nbytes = n * esz
        assert off % 4 == 0 and off + nbytes <= self.size, (name, off, nbytes, self.size)
        v = self.ap[0:shape[0], off:off + nbytes].bitcast(dt)
        if len(shape) == 3:
            v = v.rearrange("p (a b) -> p a b", b=shape[2])
        elif len(shape) == 4:
            v = v.rearrange("p (a b c) -> p a b c", b=shape[2], c=shape[3])
        elif len(shape) == 5:
            v = v.rearrange("p (a b c d) -> p a b c d", b=shape[2], c=shape[3], d=shape[4])
        return Buf(v, name), off + nbytes


class Builder:
    def __init__(self, debug=None, stop_after=None, nseq=NS):
        self.debug = debug or []
        self.stop_after = stop_after
        self.nseq = nseq
        self.stack = ExitStack()
        self.nc = bass.Bass("TRN2", target_bir_lowering=False)
        self.P = Prog(self.nc, self.stack)
        self.dbg_out = {}

    def dram_in(self, name, shape):
        return self.nc.dram_tensor(name, list(shape), F32, kind="ExternalInput").ap()

    def sb(self, name, shape, dt):
        return self.stack.enter_context(self.nc.sbuf_tensor(name, list(shape), dt))

    def act(self, out, in_, func, reads, writes, bias=None, scale=None, accum_out=None):
        kw = {}
        if bias is not None:
            kw["bias"] = bias
        if scale is not None:
            kw["scale"] = scale
        if accum_out is not None:
            kw["accum_out"] = accum_out
        return self.P.op(ACT, lambda e: e.activation(out=out, in_=in_, func=func, **kw), reads, writes)

    def tt(self, eng, out, in0, in1, op, reads, writes):
        return self.P.op(eng, lambda e: e.tensor_tensor(out, in0, in1, op), reads, writes)

    def stt(self, out, in0, scalar, in1, op0, op1, reads, writes):
        return self.P.op(DVE, lambda e: e.scalar_tensor_tensor(out, in0, scalar, in1, op0, op1), reads, writes)

    def ts(self, eng, out, in0, s1, s2, op0, op1, reads, writes):
        if op1 is None:
            return self.P.op(eng, lambda e: e.tensor_scalar(out, in0, s1, None, op0), reads, writes)
        return self.P.op(eng, lambda e: e.tensor_scalar(out, in0, s1, s2, op0, op1), reads, writes)

    def copy(self, eng, out, in_, reads, writes):
        if eng == ACT:
            return self.P.op(ACT, lambda e: e.activation(out=out, in_=in_, func=AF.Identity), reads, writes)
        return self.P.op(eng, lambda e: e.tensor_copy(out, in_), reads, writes)

    def mm(self, out, lhsT, rhs, start, stop, reads, writes, sig=None):
        return self.P.op(PE, lambda e: e.matmul(out, lhsT=lhsT, rhs=rhs, start=start, stop=stop), reads, writes,
                         sigok=(stop if sig is None else sig))

    def tr(self, out, in_, ident, reads, writes, sig=True):
        return self.P.op(PE, lambda e: e.transpose(out, in_, ident), reads, writes, sigok=sig)

    def dbg(self, name, buf_ap, shape, reads, dt=F32):
        if name not in self.debug:
            return
        o = self.nc.dram_tensor("dbg_" + name, list(shape), dt, kind="ExternalOutput").ap()
        tok = Tok("dbg" + name)
        self.P.dma(SP, o, buf_ap, reads=reads, writes=[tok], tok=tok)
        self.dbg_out[name] = tok

    def build(self):
        nc, P = self.nc, self.P
        I = {}
        I["x"] = self.dram_in("x", [NS, T, D])
        I["c"] = self.dram_in("c", [NS, D])
        I["ada_w"] = self.dram_in("ada_w", [2, D, 6 * D])
        I["ada_b"] = self.dram_in("ada_b", [2, 6 * D])
        I["mix_norm_g"] = self.dram_in("mix_norm_g", [2, D])
        I["mlp_norm_g"] = self.dram_in("mlp_norm_g", [2, D])
        I["mlp_w1"] = self.dram_in("mlp_w1", [2, D, DFF])
        I["mlp_w2"] = self.dram_in("mlp_w2", [2, DFF, D])
        I["s5_a_re"] = self.dram_in("s5_a_re", [1, 64, 64])
        I["s5_a_im"] = self.dram_in("s5_a_im", [1, 64, 64])
        I["s5_log_dt"] = self.dram_in("s5_log_dt", [1, 64])
        I["s5_b_re"] = self.dram_in("s5_b_re", [1, 64, 64, 16])
        I["s5_b_im"] = self.dram_in("s5_b_im", [1, 64, 64, 16])
        I["s5_c_re"] = self.dram_in("s5_c_re", [1, 64, 16, 64])
        I["s5_c_im"] = self.dram_in("s5_c_im", [1, 64, 16, 64])
        I["s5_d"] = self.dram_in("s5_d", [1, D])
        I["s5_w_glu"] = self.dram_in("s5_w_glu", [1, D, 2 * D])
        I["kv_ada_w"] = self.dram_in("kv_ada_w", [D, 2 * D])
        I["kv_ada_b"] = self.dram_in("kv_ada_b", [2 * D])
        I["kv_norm_g"] = self.dram_in("kv_norm_g", [D])
        I["w_kv"] = self.dram_in("w_kv", [D, 2 * D])
        I["k_norm_g"] = self.dram_in("k_norm_g", [64])
        I["sb_w_q"] = self.dram_in("sb_w_q", [1, D, D])
        I["q_norm_g"] = self.dram_in("q_norm_g", [1, 64])
        I["sb_w_o"] = self.dram_in("sb_w_o", [1, D, D])
        self.I = I
        self.out = nc.dram_tensor("out", [NS, T, D], F32, kind="ExternalOutput").ap()
        self.s5w = nc.dram_tensor("s5w_scr", [32, 128, 768], BF16, kind="Internal").ap()
        self.s5w_tok = [Tok(f"s5w{i}") for i in range(8)]
        self.s5tab = nc.dram_tensor("s5tab_scr", [32, 128, 2, 256], F32, kind="Internal").ap()
        self.s5tab_tok = Tok("s5tab")
        self.modrow = nc.dram_tensor("modrow_scr", [NS, 2, D], F32, kind="Internal").ap()
        self.modrow_tok = Tok("modrow")

        self.consts()
        ARENA = 194 * 1024
        self.arena = Arena(self.sb("arena", [128, ARENA], U8)[:])
        self.psall = self.stack.enter_context(nc.psum_tensor("psall", [128, 4096], F32))
        self.ps = [Buf(self.psall[:, 512 * i:512 * i + 512], f"ps{i}") for i in range(8)]

        self.setup_phase()
        if self.stop_after is not None and (self.stop_after == "setup" or self.stop_after.startswith("s5") or self.stop_after == "ada"):
            return self.finish()
        for s in range(self.nseq):
            self.seq_pipeline(s)
        return self.finish()

    def finish(self):
        P = self.P
        outs = [o for o in P.ops if o.dma is not None]
        fin = P.op(SP, None)
        fin.deps = list({o.id: o for o in outs}.values())
        P.emit()
        return self.nc

    def consts(self):
        P = self.P
        c = {}

        def mk(name, shape, dt):
            c[name] = Buf(self.sb("c_" + name, shape, dt)[:], name)
            return c[name]

        ident_f = mk("ident_f", [128, 128], F32)
        ident_b = mk("ident_b", [128, 128], BF16)
        onesm_b = mk("onesm_b", [128, 128], BF16)
        blk_b = mk("blk_b", [128, 128], BF16)
        negtri_b = mk("negtri_b", [128, 128], BF16)
        negones_b = mk("negones_b", [128, 128], BF16)
        negmask_b = mk("negmask_b", [128, 128], BF16)
        zeros_b = mk("zeros_b", [128, 64], BF16)
        onesrow_f = mk("onesrow_f", [1, 128], F32)
        sel2 = mk("sel2", [2, 128], F32)
        bmask_f = mk("bmask_f", [128, 128], F32)
        tmp_f = mk("ctmp_f", [128, 128], F32)
        hm = mk("hm", [128, 2], F32)
        zeros512_b = mk("zeros512_b", [128, 512], BF16)

        def pool(fn, reads, writes):
            return P.op(POOL, fn, reads, writes)

        pool(lambda e: e.memset(ident_f.ap, 1.0), [], [ident_f.t()])
        pool(lambda e: e.affine_select(out=ident_f.ap, in_=ident_f.ap, pattern=[[-1, 128]], compare_op=ALU.is_equal,
                                       fill=0.0, base=0, channel_multiplier=1), [ident_f.t()], [ident_f.t()])
        pool(lambda e: e.tensor_copy(ident_b.ap, ident_f.ap), [ident_f.t()], [ident_b.t()])
        pool(lambda e: e.memset(onesm_b.ap, 1.0 / 1024.0), [], [onesm_b.t()])
        pool(lambda e: e.memset(blk_b.ap, 0.0), [], [blk_b.t()])
        pool(lambda e: e.memset(blk_b.ap[0:64, 0:64], 1.0 / 64.0), [blk_b.t()], [blk_b.t()])
        pool(lambda e: e.memset(blk_b.ap[64:128, 64:128], 1.0 / 64.0), [blk_b.t()], [blk_b.t()])
        pool(lambda e: e.memset(tmp_f.ap, -1.0), [], [tmp_f.t()])
        pool(lambda e: e.affine_select(out=tmp_f.ap, in_=tmp_f.ap, pattern=[[-1, 128]], compare_op=ALU.is_ge,
                                       fill=0.0, base=0, channel_multiplier=1), [tmp_f.t()], [tmp_f.t()])
        pool(lambda e: e.tensor_copy(negtri_b.ap, tmp_f.ap), [tmp_f.t()], [negtri_b.t()])
        pool(lambda e: e.memset(negones_b.ap, -1.0), [], [negones_b.t()])
        pool(lambda e: e.tensor_scalar(negmask_b.ap, negtri_b.ap, 30000.0, None, ALU.mult), [negtri_b.t()], [negmask_b.t()])
        pool(lambda e: e.memset(zeros_b.ap, 0.0), [], [zeros_b.t()])
        pool(lambda e: e.memset(onesrow_f.ap, 1.0), [], [onesrow_f.t()])
        pool(lambda e: e.memset(sel2.ap, 1.0), [], [sel2.t()])
        pool(lambda e: e.affine_select(out=sel2.ap, in_=sel2.ap, pattern=[[1, 128]], compare_op=ALU.is_ge,
                                       fill=0.0, base=0, channel_multiplier=-64), [sel2.t()], [sel2.t()])
        pool(lambda e: e.affine_select(out=sel2.ap, in_=sel2.ap, pattern=[[-1, 128]], compare_op=ALU.is_ge,
                                       fill=0.0, base=63, channel_multiplier=64), [sel2.t()], [sel2.t()])
        pool(lambda e: e.memset(bmask_f.ap, 1.0), [], [bmask_f.t()])
        pool(lambda e: e.affine_select(out=bmask_f.ap.rearrange("p (i h) -> p i h", h=16), in_=bmask_f.ap.rearrange("p (i h) -> p i h", h=16),
                                       pattern=[[16, 8], [0, 16]], compare_op=ALU.is_ge,
                                       fill=0.0, base=15, channel_multiplier=-1), [bmask_f.t()], [bmask_f.t()])
        pool(lambda e: e.memset(zeros512_b.ap, 0.0), [], [zeros512_b.t()])
        pool(lambda e: e.memset(hm.ap, 0.0), [], [hm.t()])
        pool(lambda e: e.memset(hm.ap[0:64, 0:1], 1.0), [hm.t()], [hm.t()])
        pool(lambda e: e.memset(hm.ap[64:128, 1:2], 1.0), [hm.t()], [hm.t()])
        self.c = c
        self.VT = Buf(self.sb("VT", [128, 160], F32)[:], "VT")
        self.adaT = Buf(self.sb("adaT", [128, 112, 2], F32)[:], "adaT")
        self.coef = Buf(self.sb("coef", [128, 32, 8, 3], F32)[:], "coef")
        self.modsc = Buf(self.sb("modsc", [128, NS, 5, 2, 8], F32)[:], "modsc")
        self.rho = Buf(self.sb("rho", [128, 32], F32)[:], "rho")
        self.qkg = Buf(self.sb("qkg", [128, 2], F32)[:], "qkg")
        self.epsc = Buf(self.sb("epsc", [128, 1], F32)[:], "epsc")
        self.onec = Buf(self.sb("onec", [128, 1], F32)[:], "onec")
        pool(lambda e: e.memset(self.epsc.ap, EPS), [], [self.epsc.t()])
        pool(lambda e: e.memset(self.onec.ap, 1.0), [], [self.onec.t()])

    def ring_init(self, off, nslots=2):
        self.ring = []
        for i in range(nslots):
            b, off = self.arena.view(off, [128, 4096], BF16, f"ring{i}")
            self.ring.append(b)
        self.ring_i = 0
        return off

    def wload(self, srcs):
        slot = self.ring[self.ring_i % len(self.ring)]
        self.ring_i += 1
        for dstf, src in srcs:
            self.P.dma(POOL, dstf(slot.ap), src, reads=[], writes=[slot.t()], tok=slot.t())
        return slot

    def wload_k8(self, w2d, col0, ncols=512):
        src = w2d.rearrange("(k p) n -> p k n", p=128)[:, :, col0:col0 + ncols]
        slot = self.wload([(lambda a: a[:, 0:8 * ncols].rearrange("p (k n) -> p k n", n=ncols), src)])
        return slot, slot.ap[:, 0:8 * ncols].rearrange("p (k n) -> p k n", n=ncols)

    def setup_phase(self):
        P, I, c = self.P, self.I, self.c
        A = self.arena
        ps = self.ps
        off = 0
        off = self.ring_init(off, 2)
        self.ring_end = off
        off = A.size - 42 * 1024
        self.s5_limit = off
        vrA, off = A.view(off, [128, 128], F32, "vrA")
        vrB, off = A.view(off, [128, 128], F32, "vrB")
        cs, off = A.view(off, [2, 1024], F32, "cs")
        sT, off = A.view(off, [128, 8, 2], BF16, "sT")
        rowb = []
        for b in range(2):
            r, off = A.view(off, [1, 2048], F32, f"rowb{b}")
            rowb.append(r)
        biasrow, off = A.view(off, [1, 2048], F32, "biasrow")
        grow, off = A.view(off, [1, 1024], F32, "grow")
        abrow, off = A.view(off, [1, 2, 1024], F32, "abrow")
        ld = Tok("setup_ld")

        def ldma(out, in_, wtoks):
            P.dma(SP, out, in_, reads=[], writes=wtoks)

        ldma(vrA.ap[0:16, :], I["mix_norm_g"].rearrange("l (k p) -> (l k) p", p=128), [vrA.t()])
        ldma(vrA.ap[16:32, :], I["mlp_norm_g"].rearrange("l (k p) -> (l k) p", p=128), [vrA.t()])
        ldma(vrA.ap[32:40, :], I["kv_norm_g"].rearrange("(k p) -> k p", p=128), [vrA.t()])
        adab = I["ada_b"].rearrange("l (k p) -> (l k) p", p=128)
        ldma(vrA.ap[40:128, :], adab[0:88, :], [vrA.t()])
        ldma(vrB.ap[0:8, :], adab[88:96, :], [vrB.t()])
        ldma(vrB.ap[8:24, :], I["kv_ada_b"].rearrange("(k p) -> k p", p=128), [vrB.t()])
        for hh in range(2):
            ldma(vrB.ap[24:25, 64 * hh:64 * hh + 64], I["q_norm_g"], [vrB.t()])
            ldma(vrB.ap[25:26, 64 * hh:64 * hh + 64], I["k_norm_g"].rearrange("(o d) -> o d", o=1), [vrB.t()])
        ldma(cs.ap, I["c"], [cs.t()])
        ldma(biasrow.ap, I["ada_b"][0:1, 0:2048], [biasrow.t()])
        ldma(grow.ap, I["mix_norm_g"][0:1, :], [grow.t()])
        VT = self.VT
        self.tr(ps[0].ap[:, 0:128], vrA.ap, c["ident_f"].ap, [vrA.t(), c["ident_f"].t()], [ps[0].t()])
        self.tr(ps[0].ap[:, 128:154], vrB.ap[0:26, :], c["ident_f"].ap[0:26, 0:26], [vrB.t(), c["ident_f"].t()], [ps[0].t()])
        self.copy(DVE, VT.ap[:, 0:154], ps[0].ap[:, 0:154], [ps[0].t()], [VT.t()])
        self.ts(DVE, self.qkg.ap[:, 0:1], VT.ap[:, 152:153], 0.125, None, ALU.mult, None, [VT.t()], [self.qkg.t()])
        self.copy(DVE, self.qkg.ap[:, 1:2], VT.ap[:, 153:154], [VT.t()], [self.qkg.t()])
        self.act(cs.ap, cs.ap, AF.Silu, [cs.t()], [cs.t()])
        for k in range(8):
            self.tr(ps[1].ap[:, 2 * k:2 * k + 2], cs.ap[0:2, 128 * k:128 * k + 128], c["ident_f"].ap[0:2, 0:2],
                    [cs.t(), c["ident_f"].t()], [ps[1].t()])
        self.copy(DVE, sT.ap, ps[1].ap[:, 0:16].rearrange("p (k b) -> p k b", b=2), [ps[1].t()], [sT.t()])
        s5_rest = self.s5_setup_early()
        adaps = ps[5]
        chunks = [(I["ada_w"][0], j * 512) for j in range(12)] + [(I["ada_w"][1], j * 512) for j in range(12)] + \
                 [(I["kv_ada_w"], j * 512) for j in range(4)]
        for ci, (w2d, col0) in enumerate(chunks):
            slot, wv = self.wload_k8(w2d, col0)
            for mt in range(4):
                ot = ci * 4 + mt
                for k in range(8):
                    self.mm(adaps.ap[:, 2 * ot:2 * ot + 2], wv[:, k, 128 * mt:128 * mt + 128], sT.ap[:, k, :],
                            k == 0, k == 7, [slot.t(), sT.t()], [adaps.t()])
            if ci < 4:
                for b in range(2):
                    rp = ps[6 + b]
                    for k in range(8):
                        self.mm(rp.ap[0:1, :], sT.ap[:, k, b:b + 1], wv[:, k, :], k == 0, k == 7, [slot.t(), sT.t()], [rp.t()])
                    self.copy(ACT, rowb[b].ap[0:1, 512 * ci:512 * ci + 512], rp.ap[0:1, :], [rp.t()], [rowb[b].t()])
        self.tt(DVE, self.adaT.ap, adaps.ap[:, 0:224].rearrange("p (o b) -> p o b", b=2),
                VT.ap[:, 40:152].unsqueeze(2).to_broadcast([128, 112, 2]),
                ALU.add, [adaps.t(), VT.t()], [self.adaT.t()])
        adaT = self.adaT
        norms = {1: (16, 24, 32), 2: (32, 96, 104), 3: (8, 48, 56), 4: (24, 72, 80)}
        for s in range(2):
            for n, (gc, sh, sc) in norms.items():
                self.stt(self.modsc.ap[:, s, n, 0, :], adaT.ap[:, sc:sc + 8, s], 1.0, VT.ap[:, gc:gc + 8], ALU.add, ALU.mult,
                         [adaT.t(), VT.t()], [self.modsc.t()])
                self.copy(DVE, self.modsc.ap[:, s, n, 1, :], adaT.ap[:, sh:sh + 8, s], [adaT.t()], [self.modsc.t()])
        for b in range(2):
            self.tt(DVE, rowb[b].ap, rowb[b].ap, biasrow.ap, ALU.add, [rowb[b].t(), biasrow.t()], [rowb[b].t()])
            self.stt(abrow.ap[0:1, 0, :], rowb[b].ap[0:1, 1024:2048], 1.0, grow.ap, ALU.add, ALU.mult,
                     [rowb[b].t(), grow.t()], [abrow.t()])
            self.copy(DVE, abrow.ap[0:1, 1, :], rowb[b].ap[0:1, 0:1024], [rowb[b].t()], [abrow.t()])
            P.dma(SP, self.modrow[b:b + 1], abrow.ap, reads=[abrow.t()], writes=[self.modrow_tok], tok=self.modrow_tok)
        self.dbg("adaT", self.adaT.ap, [128, 112, 2], [self.adaT.t()])
        self.dbg("modsc", self.modsc.ap, [128, NS, 5, 2, 8], [self.modsc.t()])
        self.dbg("VT", self.VT.ap, [128, 160], [self.VT.t()])
        if self.stop_after == "ada":
            return
        s5_rest()
        P.barrier()

    def s5_setup_early(self):
        P, I, c = self.P, self.I, self.c
        A = self.arena
        ps = self.ps
        off = self.ring_end
        lamrows, off = A.view(off, [32, 2, 128], F32, "lamrows")
        ldt2, off = A.view(off, [2, 32], F32, "ldt2")
        NSM = 24
        smb, off = A.view(off, [128, NSM, 32], F32, "sm")
        Bq, off = A.view(off, [128, 2, 32, 16], F32, "Bq")
        Bb, off = A.view(off, [128, 2, 32, 16], F32, "Bb")
        Cin, off = A.view(off, [128, 2, 128], F32, "Cin")
        CT, off = A.view(off, [128, 2, 32, 16], F32, "CT")
        X, off = A.view(off, [128, 2, 32, 8, 16], F32, "X")
        Wo, off = A.view(off, [128, 2, 32, 8, 16], F32, "Wo")
        We, off = A.view(off, [128, 2, 32, 128], F32, "We")
        tmp, off = A.view(off, [128, 2, 512], F32, "s5tmp")
        Drows, off = A.view(off, [64, 16], F32, "Drows")
        Drep, off = A.view(off, [64, 8, 16], F32, "Drep")
        dcol, off = A.view(off, [128, 64], F32, "dcol")
        tmpm, off = A.view(off, [128, 2, 2, 128], F32, "tmpm")
        stage, off = A.view(off, [128, 2, 4, 768], BF16, "stage")
        WoM, off = A.view(off, [128, 2, 2, 2, 128], F32, "WoM")
        assert off <= self.s5_limit, (off, self.s5_limit)
        identf = c["ident_f"]

        names = ["lre", "lim", "dt", "mag", "th", "c", "s", "t1", "t2", "t3", "abre", "abim", "nr", "den",
                 "fre", "fim", "rm2", "lire", "liim", "mure", "muim", "w2r", "w2i"]
        sm = {n: (smb.ap[:, i, :], smb.t(n)) for i, n in enumerate(names)}

        def S(n):
            return sm[n][0]

        def St(n):
            return sm[n][1]

        P.dma(SP, lamrows.ap[:, 0, :], I["s5_a_re"][0].rearrange("(a g) p -> a (g p)", g=2), writes=[lamrows.t()])
        P.dma(SP, lamrows.ap[:, 1, :], I["s5_a_im"][0].rearrange("(a g) p -> a (g p)", g=2), writes=[lamrows.t()])
        P.dma(SP, ldt2.ap, I["s5_log_dt"][0].rearrange("(a g) -> g a", g=2), writes=[ldt2.t()], allow_slow_non_contiguous=True)
        for ri, nm in enumerate(["s5_b_re", "s5_b_im"]):
            src = I[nm][0].rearrange("(a g) p h -> (g p) a h", g=2)
            for q4 in range(4):
                P.dma(SP, Bq.ap[:, ri, 8 * q4:8 * q4 + 8, :], src[:, 8 * q4:8 * q4 + 8, :], writes=[Bq.t()])
        P.dma(SP, Drows.ap, I["s5_d"][0].rearrange("(g h) -> g h", h=16), writes=[Drows.t()])

        self.tr(ps[0].ap[:, 0:32], lamrows.ap[:, 0, :], identf.ap[0:32, 0:32], [lamrows.t(), identf.t()], [ps[0].t()])
        self.tr(ps[0].ap[:, 32:64], lamrows.ap[:, 1, :], identf.ap[0:32, 0:32], [lamrows.t(), identf.t()], [ps[0].t()])
        self.mm(ps[0].ap[:, 64:96], c["sel2"].ap, ldt2.ap, True, True, [c["sel2"].t(), ldt2.t()], [ps[0].t()])
        self.copy(DVE, S("lre"), ps[0].ap[:, 0:32], [ps[0].t()], [St("lre")])
        self.copy(DVE, S("lim"), ps[0].ap[:, 32:64], [ps[0].t()], [St("lim")])
        self.act(S("dt"), ps[0].ap[:, 64:96], AF.Exp, [ps[0].t()], [St("dt")])

        def tt(out, a, b, op, eng=DVE):
            self.tt(eng, S(out), S(a), S(b), op, [St(a), St(b)], [St(out)])

        if self.stop_after == "s5a":
            return

        def horner(out, y, coefs, last):
            self.ts(DVE, S(out), S(y), coefs[0], None, ALU.mult, None, [St(y)], [St(out)])
            for ck in coefs[1:]:
                self.stt(S(out), S(out), ck, S(y), ALU.add, ALU.mult, [St(out), St(y)], [St(out)])
            self.ts(DVE, S(out), S(out), last, None, ALU.add, None, [St(out)], [St(out)])

        tt("t1", "lre", "dt", ALU.mult)
        self.ts(DVE, S("t2"), S("t1"), 0.25, None, ALU.mult, None, [St("t1")], [St("t2")])
        horner("mag", "t2", [1.0 / 5040, 1.0 / 720, 1.0 / 120, 1.0 / 24, 1.0 / 6, 0.5, 1.0], 1.0)
        tt("mag", "mag", "mag", ALU.mult)
        tt("mag", "mag", "mag", ALU.mult)
        tt("th", "lim", "dt", ALU.mult)
        self.ts(DVE, S("t3"), S("th"), 1.0 / 32.0, None, ALU.mult, None, [St("th")], [St("t3")])
        tt("t2", "t3", "t3", ALU.mult)
        horner("c", "t2", [-1.0 / 3628800, 1.0 / 40320, -1.0 / 720, 1.0 / 24, -0.5], 1.0)
        horner("s", "t2", [1.0 / 362880, -1.0 / 5040, 1.0 / 120, -1.0 / 6], 1.0)
        tt("s", "s", "t3", ALU.mult)

        def csquare(a, b):
            tt("t1", a, a, ALU.mult)
            tt("t2", b, b, ALU.mult)
            self.stt(S(b), S(a), 2.0, S(b), ALU.mult, ALU.mult, [St(a), St(b)], [St(b)])
            tt(a, "t1", "t2", ALU.subtract)

        for _ in range(5):
            csquare("c", "s")
        tt("t1", "c", "c", ALU.mult)
        tt("t2", "s", "s", ALU.mult)
        tt("t1", "t1", "t2", ALU.add)
        self.ts(DVE, S("t1"), S("t1"), -0.5, 1.5, ALU.mult, ALU.add, [St("t1")], [St("t1")])
        tt("c", "c", "t1", ALU.mult)
        tt("s", "s", "t1", ALU.mult)
        tt("abre", "mag", "c", ALU.mult)
        tt("abim", "mag", "s", ALU.mult)
        self.ts(DVE, S("nr"), S("abre"), -1.0, None, ALU.add, None, [St("abre")], [St("nr")])
        tt("t1", "lre", "lre", ALU.mult)
        tt("t2", "lim", "lim", ALU.mult)
        tt("den", "t1", "t2", ALU.add)
        P.op(DVE, lambda e: e.reciprocal(S("den"), S("den")), [St("den")], [St("den")])
        tt("t1", "nr", "lre", ALU.mult)
        tt("t2", "abim", "lim", ALU.mult)
        tt("t1", "t1", "t2", ALU.add)
        tt("fre", "t1", "den", ALU.mult)
        tt("t1", "abim", "lre", ALU.mult)
        tt("t2", "nr", "lim", ALU.mult)
        tt("t1", "t1", "t2", ALU.subtract)
        tt("fim", "t1", "den", ALU.mult)
        tt("rm2", "mag", "mag", ALU.mult)
        P.op(DVE, lambda e: e.reciprocal(S("rm2"), S("rm2")), [St("rm2")], [St("rm2")])
        tt("lire", "abre", "rm2", ALU.mult)
        self.stt(S("liim"), S("abim"), -1.0, S("rm2"), ALU.mult, ALU.mult, [St("abim"), St("rm2")], [St("liim")])
        self.copy(DVE, S("mure"), S("abre"), [St("abre")], [St("mure")])
        self.copy(DVE, S("muim"), S("abim"), [St("abim")], [St("muim")])
        for _ in range(3):
            csquare("mure", "muim")
        coef = self.coef
        self.copy(DVE, S("w2r"), S("mure"), [St("mure")], [St("w2r")])
        self.copy(DVE, S("w2i"), S("muim"), [St("muim")], [St("w2i")])
        for k in range(8):
            self.copy(DVE, coef.ap[:, :, k, 0], S("w2r"), [St("w2r")], [coef.t()])
            self.copy(DVE, coef.ap[:, :, k, 1], S("w2i"), [St("w2i")], [coef.t()])
            self.ts(DVE, coef.ap[:, :, k, 2], S("w2i"), -1.0, None, ALU.mult, None, [St("w2i")], [coef.t()])
            if k < 7:
                csquare("w2r", "w2i")

        if self.stop_after == "s5b":
            return
        for _ in range(3):
            tt("mag", "mag", "mag", ALU.mult)
        self.copy(DVE, self.rho.ap, S("mag"), [St("mag")], [self.rho.t()])
        for _ in range(3):
            csquare("c", "s")
        tt("t1", "c", "c", ALU.mult)
        tt("t2", "s", "s", ALU.mult)
        tt("t1", "t1", "t2", ALU.add)
        self.ts(DVE, S("t1"), S("t1"), -0.5, 1.5, ALU.mult, ALU.add, [St("t1")], [St("t1")])
        tt("c", "c", "t1", ALU.mult)
        tt("s", "s", "t1", ALU.mult)
        tmpt = [tmp.t(i) for i in range(2)]

        def cmul(ore, oim, otoks, are, aim, atoks, zre, zim, ztoks, n):
            ab = lambda nm: S(nm).unsqueeze(2).to_broadcast([128, 32, n])
            t = [tmp.ap[:, i, 0:32 * n].rearrange("p (a h) -> p a h", h=n) for i in range(2)]
            rd = atoks + ztoks
            self.tt(DVE, t[0], zre, ab(are), ALU.mult, rd, [tmpt[0]])
            self.tt(DVE, t[1], zim, ab(aim), ALU.mult, rd, [tmpt[1]])
            self.tt(DVE, ore, t[0], t[1], ALU.subtract, [tmpt[0], tmpt[1]], otoks)
            self.tt(DVE, t[0], zim, ab(are), ALU.mult, rd, [tmpt[0]])
            self.tt(DVE, t[1], zre, ab(aim), ALU.mult, rd, [tmpt[1]])
            self.tt(DVE, oim, t[0], t[1], ALU.add, [tmpt[0], tmpt[1]], otoks)

        fa = [St("fre"), St("fim")]
        cmul(Bb.ap[:, 0], Bb.ap[:, 1], [Bb.t()], "fre", "fim", fa, Bq.ap[:, 0], Bq.ap[:, 1], [Bq.t()], 16)
        la = [St("lire"), St("liim")]
        for j in range(8):
            if j == 0:
                zr, zi, zt = Bb.ap[:, 0], Bb.ap[:, 1], [Bb.t()]
            else:
                zr, zi, zt = X.ap[:, 0, :, j - 1, :], X.ap[:, 1, :, j - 1, :], [X.t(j - 1)]
            cmul(X.ap[:, 0, :, j, :], X.ap[:, 1, :, j, :], [X.t(j)], "lire", "liim", la, zr, zi, zt, 16)
        if self.stop_after == "s5c":
            return
        for r in range(8):
            for ri, nm in enumerate(["s5_c_re", "s5_c_im"]):
                db = (2 * r + ri) % 2
                P.dma(SP, Cin.ap[:, db, 0:64], I[nm][0][8 * r:8 * r + 8].rearrange("g h p -> (g h) p"), writes=[Cin.t(db)])
                pb = ps[1 + db]
                self.tr(pb.ap[0:64, 0:128], Cin.ap[:, db, 0:64], identf.ap, [Cin.t(db), identf.t()], [pb.t()])
                src = pb.ap[0:64, 0:128].rearrange("p (a g h) -> p a g h", g=2, h=16)
                self.copy(ACT, CT.ap[0:64, ri, 4 * r:4 * r + 4, :], src[:, :, 0, :], [pb.t()], [CT.t()])
                self.copy(ACT, CT.ap[64:128, ri, 4 * r:4 * r + 4, :], src[:, :, 1, :], [pb.t()], [CT.t()])
        if self.stop_after == "s5d":
            return
        ab_ = [St("abre"), St("abim")]
        for i in range(8):
            if i == 0:
                zr, zi, zt = CT.ap[:, 0], CT.ap[:, 1], [CT.t()]
            else:
                zr, zi, zt = Wo.ap[:, 0, :, i - 1, :], Wo.ap[:, 1, :, i - 1, :], [Wo.t(i - 1)]
            cmul(Wo.ap[:, 0, :, i, :], Wo.ap[:, 1, :, i, :], [Wo.t(i)], "abre", "abim", ab_, zr, zi, zt, 16)
        ma = [St("mure"), St("muim")]
        for j in range(8):
            cmul(We.ap[:, 0, :, 16 * j:16 * j + 16], We.ap[:, 1, :, 16 * j:16 * j + 16], [We.t(j)], "mure", "muim", ma,
                 X.ap[:, 0, :, j, :], X.ap[:, 1, :, j, :], [X.t(j)], 16)
        allWo = [Wo.t(i) for i in range(8)]
        WoN = Wo.t("neg")
        self.ts(DVE, Wo.ap[:, 1], Wo.ap[:, 1], -1.0, None, ALU.mult, None, allWo, allWo + [WoN])
        def rest():
            self.copy(DVE, Drep.ap, Drows.ap.unsqueeze(1).to_broadcast([64, 8, 16]), [Drows.t()], [Drep.t()])
            self.tr(ps[3].ap[:, 0:64], Drep.ap.rearrange("p j h -> p (j h)"), identf.ap[0:64, 0:64], [Drep.t(), identf.t()], [ps[3].t()])
            self.copy(DVE, dcol.ap, ps[3].ap[:, 0:64], [ps[3].t()], [dcol.t()])
            self.s5_setup_pairs(locals_=dict(We=We, X=X, Wo=Wo, WoN=WoN, allWo=allWo, stage=stage, WoM=WoM, tmpm=tmpm, dcol=dcol, identf=identf))
            P.barrier()
            if self.stop_after is None or not (self.stop_after == "setup" or self.stop_after.startswith("s5") or self.stop_after == "ada"):
                self.issue_prep_loads(0)
            Tr = Buf(X.ap.rearrange("p r a j h -> p (r a j h)").rearrange("p (a c) -> p a c", c=256), "Tr")
            Ti = Buf(Wo.ap.rearrange("p r a j h -> p (r a j h)").rearrange("p (a c) -> p a c", c=256), "Ti")
            tb = [Buf(We.ap[:, i].rearrange("p a n -> p (a n)"), f"tbig{i}") for i in range(2)]
            P.op(DVE, lambda e: e.memset(Tr.ap[:, :, 0:1], 1.0), [], [Tr.t()])
            P.op(DVE, lambda e: e.memset(Ti.ap[:, :, 0:1], 0.0), [], [Ti.t()])
            self.copy(DVE, S("w2r"), S("c"), [St("c")], [St("w2r")])
            self.copy(DVE, S("w2i"), S("s"), [St("s")], [St("w2i")])
            for k in range(8):
                n = 1 << k
                wrb = S("w2r").unsqueeze(2).to_broadcast([128, 32, n])
                wib = S("w2i").unsqueeze(2).to_broadcast([128, 32, n])
                t0 = tb[0].ap[:, 0:32 * n].rearrange("p (a c) -> p a c", c=n)
                t1 = tb[1].ap[:, 0:32 * n].rearrange("p (a c) -> p a c", c=n)
                wt = [St("w2r"), St("w2i")]
                self.tt(DVE, t0, Tr.ap[:, :, 0:n], wrb, ALU.mult, [Tr.t()] + wt, [tb[0].t()])
                self.tt(DVE, t1, Ti.ap[:, :, 0:n], wib, ALU.mult, [Ti.t()] + wt, [tb[1].t()])
                self.tt(DVE, Tr.ap[:, :, n:2 * n], t0, t1, ALU.subtract, [tb[0].t(), tb[1].t()], [Tr.t()])
                self.tt(DVE, t0, Ti.ap[:, :, 0:n], wrb, ALU.mult, [Ti.t()] + wt, [tb[0].t()])
                self.tt(DVE, t1, Tr.ap[:, :, 0:n], wib, ALU.mult, [Tr.t()] + wt, [tb[1].t()])
                self.tt(DVE, Ti.ap[:, :, n:2 * n], t0, t1, ALU.add, [tb[0].t(), tb[1].t()], [Ti.t()])
                if k < 7:
                    csquare("w2r", "w2i")
            P.dma(SP, self.s5tab[:, :, 0, :].rearrange("a p c -> p a c"), Tr.ap, reads=[Tr.t()], writes=[self.s5tab_tok])
            P.dma(SP, self.s5tab[:, :, 1, :].rearrange("a p c -> p a c"), Ti.ap, reads=[Ti.t()], writes=[self.s5tab_tok])
            self.dbg("Tr", Tr.ap, [128, 32, 256], [Tr.t()])
            self.dbg("Ti", Ti.ap, [128, 32, 256], [Ti.t()])
            self.dbg("rho", self.rho.ap, [128, 32], [self.rho.t()])
        return rest

    def s5_setup_pairs(self, locals_):
        P, c, ps = self.P, self.c, self.ps
        We, X, Wo, WoN, allWo, stage, WoM, tmpm, dcol, identf = (locals_[k] for k in
                                                                ["We", "X", "Wo", "WoN", "allWo", "stage", "WoM", "tmpm", "dcol", "identf"])
        allWe = [We.t(j) for j in range(8)]
        allX = [X.t(j) for j in range(8)]

        def phase1(pr):
            slot = (pr // 4) % 2
            p4 = pr % 4
            st_ = stage.t(slot)
            pa = ps[4 + pr % 2]
            pb = ps[6 + pr % 2]
            self.tr(pa.ap[:, 0:128], We.ap[:, 0, pr, :], identf.ap, allWe + [identf.t()], [pa.t()])
            self.tr(pa.ap[:, 128:256], We.ap[:, 1, pr, :], identf.ap, allWe + [identf.t()], [pa.t()])
            self.copy(ACT, stage.ap[:, slot, p4, 0:256], pa.ap[:, 0:256], [pa.t()], [st_])
            self.copy(ACT, stage.ap[:, slot, p4, 256:512].rearrange("p (r n) -> p r n", r=2),
                      Wo.ap[:, :, pr].rearrange("p r i h -> p r (i h)"), [WoN], [st_])
            wm = WoM.ap[:, pr % 2]
            for gl in range(2):
                self.ts(POOL if gl == 0 else DVE, wm[:, gl], Wo.ap[:, :, pr].rearrange("p r i h -> p r (i h)"), c["hm"].ap[:, gl:gl + 1], None, ALU.mult, None,
                        [WoN, c["hm"].t()], [WoM.t(pr % 2)])
            for gl in range(2):
                for ri in range(2):
                    self.mm(pb.ap[:, 128 * gl:128 * gl + 128], X.ap[:, ri, pr].rearrange("p j h -> p (j h)"),
                            wm[:, gl, ri, :], ri == 0, ri == 1, allX + [WoM.t(pr % 2)], [pb.t()])

        def phase2(pr):
            slot = (pr // 4) % 2
            p4 = pr % 4
            st_ = stage.t(slot)
            pb = ps[6 + pr % 2]
            tm = tmpm.ap[:, pr % 2]
            self.tt(DVE, tm, pb.ap[:, 0:256].rearrange("p (g n) -> p g n", g=2),
                    c["bmask_f"].ap.unsqueeze(1).to_broadcast([128, 2, 128]), ALU.mult, [pb.t(), c["bmask_f"].t()], [tmpm.t(pr % 2)])
            for gl in range(2):
                g = 2 * pr + gl
                self.stt(stage.ap[:, slot, p4, 512 + 128 * gl:512 + 128 * gl + 128], identf.ap, dcol.ap[:, g:g + 1], tm[:, gl, :],
                         ALU.mult, ALU.add, [identf.t(), dcol.t(), tmpm.t(pr % 2)], [st_])
            if p4 == 3:
                ftc = pr // 4
                P.dma(SP, self.s5w[4 * ftc:4 * ftc + 4].rearrange("a p n -> p a n"), stage.ap[:, slot], reads=[st_],
                      writes=[self.s5w_tok[ftc]])
                if ftc == 0:
                    self.dbg("s5w0", stage.ap[:, slot], [128, 4, 768], [st_], dt=BF16)
        phase1(0)
        for pr in range(32):
            if pr + 1 < 32:
                phase1(pr + 1)
            phase2(pr)
        self.dbg("coef", self.coef.ap, [128, 32, 8, 3], [self.coef.t()])

    R_RSTD = 16384
    R_XT = 18432
    R_ACTT = R_XT + 65536
    R_TAIL = R_ACTT + 32768

    def seq_pipeline(self, s):
        P = self.P
        A = self.arena
        self.xT, _ = A.view(self.R_XT, [128, 8, 2048], F32, f"xT{s}")
        self.actT, _ = A.view(self.R_ACTT, [128, 8, 2048], BF16, f"actT{s}")
        self.rstd, _ = A.view(self.R_RSTD, [128, 512], F32, f"rstd{s}")
        self.s5_prep(s)
        P.barrier()
        self.s5_core(s)
        P.barrier()
        self.dbg(f"gT{s}", self.actT.ap, [128, 8, 2048], [], dt=BF16)
        self.dbg(f"xT{s}", self.xT.ap, [128, 8, 2048], [])
        if self.stop_after == "xreload":
            return
        self.glu(s)
        P.barrier()
        self.mlp(s, 0, 1)
        P.barrier()
        self.dbg(f"x1T{s}", self.xT.ap, [128, 8, 2048], [])
        if self.stop_after == "layer0":
            return
        self.kv_phase(s)
        P.barrier()
        self.dbg(f"KT{s}", self.KT.ap, [128, 8, 2048], [], dt=BF16)
        self.dbg(f"V{s}", self.V.ap, [128, 16, 1024], [], dt=BF16)
        if self.stop_after == "kv":
            return
        self.attn_phase(s)
        P.barrier()
        self.dbg(f"x2T{s}", self.xT.ap, [128, 8, 2048], [])
        if self.stop_after == "attn":
            return
        self.mlp(s, 1, 4)
        P.barrier()
        self.output_phase(s)
        P.barrier()

    def load_xtok(self, s, half, Xtok):
        src = self.I["x"][s, half * 1024:(half + 1) * 1024, :].rearrange("(c j) f -> c j f", j=8)
        for q in range(4):
            self.P.dma(SP, Xtok.ap[32 * q:32 * q + 32], src[32 * q:32 * q + 32], writes=[Xtok.t()])

    def issue_prep_loads(self, s):
        A = self.arena
        off = self.R_TAIL + 32768
        Xtok, off = A.view(off, [128, 8, 1024], F32, "Xtok")
        Arep, off = A.view(off, [128, 1024], F32, "Arep")
        Brep, off = A.view(off, [128, 1024], F32, "Brep")
        self.P.dma(SP, Arep.ap, self.modrow[s, 0, :].partition_broadcast(128), reads=[self.modrow_tok], writes=[Arep.t()])
        self.P.dma(SP, Brep.ap, self.modrow[s, 1, :].partition_broadcast(128), reads=[self.modrow_tok], writes=[Brep.t()])
        self.load_xtok(s, 0, Xtok)
        self.pre = (s, Xtok, Arep, Brep)

    def s5_prep(self, s):
        P, c, ps = self.P, self.c, self.ps
        A = self.arena
        off = self.R_TAIL
        D, off = A.view(off, [128, 64, 256], BF16, "D")
        self.D = D
        self.s5_tail_off = off
        Xtok, off = A.view(off, [128, 8, 1024], F32, "Xtok")
        Arep, off = A.view(off, [128, 1024], F32, "Arep")
        Brep, off = A.view(off, [128, 1024], F32, "Brep")
        ss, off = A.view(off, [128, 8], F32, "ss")
        rs, off = A.view(off, [128, 8], F32, "rs")
        junk, off = A.view(off, [128, 1024], BF16, "junk")
        o2 = self.R_ACTT
        htok, o2 = A.view(o2, [128, 64, 8, 16], BF16, "htok")
        tmpf, o2 = A.view(o2, [128, 2, 1024], F32, "tmpf")
        xT = self.xT
        pre = getattr(self, "pre", None)
        preloaded = pre is not None and pre[0] == s
        if preloaded:
            _, Xtok, Arep, Brep = pre
            self.pre = None
        else:
            P.dma(SP, Arep.ap, self.modrow[s, 0, :].partition_broadcast(128), reads=[self.modrow_tok], writes=[Arep.t()])
            P.dma(SP, Brep.ap, self.modrow[s, 1, :].partition_broadcast(128), reads=[self.modrow_tok], writes=[Brep.t()])
        n = 0
        for half in range(2):
            if not (preloaded and half == 0):
                self.load_xtok(s, half, Xtok)
            for j in range(8):
                self.act(junk.ap, Xtok.ap[:, j, :], AF.Square, [Xtok.t()], [junk.t(), ss.t()], accum_out=ss.ap[:, j:j + 1])
            self.ts(DVE, rs.ap, ss.ap, 1.0 / 1024.0, EPS, ALU.mult, ALU.add, [ss.t()], [rs.t()])
            self.act(rs.ap, rs.ap, AF.Sqrt, [rs.t()], [rs.t()])
            P.op(DVE, lambda e: e.reciprocal(rs.ap, rs.ap), [rs.t()], [rs.t()])
            for j in range(8):
                tb = j % 2
                self.stt(tmpf.ap[:, tb, :], Xtok.ap[:, j, :], rs.ap[:, j:j + 1], Arep.ap, ALU.mult, ALU.mult,
                         [Xtok.t(), rs.t(), Arep.t()], [tmpf.t(tb)])
                self.tt(POOL if j % 2 == 0 else DVE, htok.ap[:, :, j, :], tmpf.ap[:, tb, :].rearrange("p (g h) -> p g h", h=16),
                        Brep.ap.rearrange("p (g h) -> p g h", h=16), ALU.add, [tmpf.t(tb), Brep.t()], [htok.t()])
            for ft in range(8):
                for jq in range(2):
                    bank = ps[4 + n % 4]
                    for jj in range(4):
                        j = 4 * jq + jj
                        self.tr(bank.ap[:, 128 * jj:128 * jj + 128], Xtok.ap[:, j, 128 * ft:128 * ft + 128], c["ident_f"].ap,
                                [Xtok.t(), c["ident_f"].t()], [bank.t()], sig=(jj == 3))
                    dst = xT.ap[:, ft, 1024 * half:1024 * half + 1024].rearrange("p (c j) -> p j c", j=8)[:, 4 * jq:4 * jq + 4, :]
                    self.copy(ACT if n % 2 == 0 else DVE, dst, bank.ap.rearrange("p (j c) -> p j c", c=128), [bank.t()], [xT.t(ft)])
                    n += 1
            for g4 in range(16):
                pb = ps[g4 % 4]
                pbv = pb.ap.bitcast(BF16)
                for gi in range(4):
                    g = 4 * g4 + gi
                    self.tr(pbv[:, 128 * gi:128 * gi + 128], htok.ap[:, g].rearrange("p j h -> p (j h)"), c["ident_b"].ap,
                            [htok.t(), c["ident_b"].t()], [pb.t()], sig=(gi == 3))
                self.copy(ACT if g4 % 2 == 0 else DVE, D.ap[:, 4 * g4:4 * g4 + 4, 128 * half:128 * half + 128],
                          pbv[:, 0:512].rearrange("p (g c) -> p g c", c=128), [pb.t()], [D.t()])

    def s5_core(self, s):
        P, c, ps = self.P, self.c, self.ps
        A = self.arena
        D = self.D
        rho = self.rho
        off = self.s5_tail_off
        wch, _ = A.view(0, [128, 2, 4, 768], BF16, "wch")
        tabs, off = A.view(off, [128, 2, 4, 2, 256], F32, "tabs")
        Wk, off = A.view(off, [128, 2, 8, 256], F32, "Wk")
        Sbf, off = A.view(off, [128, 4, 2, 256], BF16, "Sbf")
        Ygb, off = A.view(off, [128, 2, 2, 256], BF16, "Ygb")
        Gtok, off = A.view(off, [128, 2, 2, 8, 128], BF16, "Gtok")
        gT = self.actT
        P.op(POOL, lambda e: e.memset(Sbf.ap, 0.0), [], [Sbf.t(i) for i in range(4)])
        identb = c["ident_b"]

        def wslot_of(pr):
            return (pr // 4) % 2

        def pair_front(pr):
            p4 = pr % 4
            wslot = wslot_of(pr)
            bankE = ps[pr % 4]
            for ri in range(2):
                for gl in range(2):
                    self.mm(bankE.ap[64 * gl:64 * gl + 64, 256 * ri:256 * ri + 256],
                            wch.ap[:, wslot, p4, 128 * ri + 64 * gl:128 * ri + 64 * gl + 64], D.ap[:, 2 * pr + gl, :], True, True,
                            [wch.t(wslot), D.t()], [bankE.t()], sig=(ri == 1 and gl == 1))

        def scan_ops(pr):
            p4 = pr % 4
            wslot = wslot_of(pr)
            sl, w2 = pr % 4, pr % 2
            bankE = ps[pr % 4]
            Er, Ei = bankE.ap[:, 0:256], bankE.ap[:, 256:512]
            cr, sr = tabs.ap[:, wslot, p4, 0, :], tabs.ap[:, wslot, p4, 1, :]
            W = lambda j: Wk.ap[:, w2, j, :]
            wt = lambda j: Wk.t((w2, j))
            tb_, eb = tabs.t(wslot), bankE.t()
            rb = rho.ap[:, pr:pr + 1].to_broadcast([128, 256])
            ops = []
            ops.append(lambda: self.tt(DVE, W(4), Er, cr, ALU.mult, [eb, tb_], [wt(4)]))
            ops.append(lambda: self.tt(DVE, W(5), Ei, sr, ALU.mult, [eb, tb_], [wt(5)]))
            ops.append(lambda: self.tt(DVE, W(6), Ei, cr, ALU.mult, [eb, tb_], [wt(6)]))
            ops.append(lambda: self.tt(DVE, W(7), Er, sr, ALU.mult, [eb, tb_], [wt(7)]))
            ops.append(lambda: self.tt(DVE, W(0), W(4), W(5), ALU.add, [wt(4), wt(5)], [wt(0)]))
            ops.append(lambda: self.tt(DVE, W(1), W(6), W(7), ALU.subtract, [wt(6), wt(7)], [wt(1)]))
            ops.append(lambda: self.P.op(DVE, lambda e: e.tensor_tensor_scan(W(2), rb, W(0), 0.0, ALU.mult, ALU.add),
                                         [wt(0), rho.t()], [wt(2)]))
            ops.append(lambda: self.P.op(DVE, lambda e: e.tensor_tensor_scan(W(3), rb, W(1), 0.0, ALU.mult, ALU.add),
                                         [wt(1), rho.t()], [wt(3)]))
            n = 255
            ops.append(lambda: self.tt(DVE, W(4)[:, 0:n], W(2)[:, 0:n], cr[:, 0:n], ALU.mult, [wt(2), tb_], [wt(4)]))
            ops.append(lambda: self.tt(DVE, W(5)[:, 0:n], W(3)[:, 0:n], sr[:, 0:n], ALU.mult, [wt(3), tb_], [wt(5)]))
            ops.append(lambda: self.tt(DVE, W(6)[:, 0:n], W(3)[:, 0:n], cr[:, 0:n], ALU.mult, [wt(3), tb_], [wt(6)]))
            ops.append(lambda: self.tt(DVE, W(7)[:, 0:n], W(2)[:, 0:n], sr[:, 0:n], ALU.mult, [wt(2), tb_], [wt(7)]))
            ops.append(lambda: self.tt(DVE, Sbf.ap[:, sl, 0, 1:256], W(4)[:, 0:n], W(5)[:, 0:n], ALU.subtract, [wt(4), wt(5)], [Sbf.t(sl)]))
            ops.append(lambda: self.tt(DVE, Sbf.ap[:, sl, 1, 1:256], W(6)[:, 0:n], W(7)[:, 0:n], ALU.add, [wt(6), wt(7)], [Sbf.t(sl)]))
            return ops

        def pair_back(pr):
            ft, p4 = pr // 4, pr % 4
            wslot = wslot_of(pr)
            gslot = ft % 2
            sl = pr % 4
            ys = pr % 2
            bankY = ps[4 + pr % 2]
            for gl in range(2):
                o = bankY.ap[:, 256 * gl:256 * gl + 256]
                hs = slice(64 * gl, 64 * gl + 64)
                self.mm(o, wch.ap[:, wslot, p4, 512 + 128 * gl:512 + 128 * gl + 128], D.ap[:, 2 * pr + gl, :], True, False,
                        [wch.t(wslot), D.t()], [bankY.t()])
                self.mm(o, wch.ap[hs, wslot, p4, 256:384], Sbf.ap[hs, sl, 0, :], False, False, [wch.t(wslot), Sbf.t(sl)], [bankY.t()])
                self.mm(o, wch.ap[hs, wslot, p4, 384:512], Sbf.ap[hs, sl, 1, :], False, True, [wch.t(wslot), Sbf.t(sl)], [bankY.t()],
                        sig=(gl == 1))
            self.act(Ygb.ap[:, ys].rearrange("p g c -> p (g c)"), bankY.ap, AF.Gelu, [bankY.t()], [Ygb.t(ys)])
            pT = ps[6]
            pTv = pT.ap.bitcast(BF16)
            for gl in range(2):
                for half in range(2):
                    q = 2 * gl + half
                    self.tr(pTv[:, 128 * q:128 * q + 128], Ygb.ap[:, ys, gl, 128 * half:128 * half + 128], identb.ap,
                            [Ygb.t(ys), identb.t()], [pT.t()], sig=(q == 3))
            for gl in range(2):
                fo = 32 * p4 + 16 * gl
                self.copy(ACT, Gtok.ap[:, gslot, :, :, fo:fo + 16],
                          pTv[:, 256 * gl:256 * gl + 256].rearrange("p (a i h) -> p a i h", a=2, h=16), [pT.t()], [Gtok.t(gslot)])

        def ft_done(ft):
            gslot = ft % 2
            for half in range(2):
                pT2 = ps[7]
                pv = pT2.ap.bitcast(BF16)
                for i in range(8):
                    self.tr(pv[:, 128 * i:128 * i + 128], Gtok.ap[:, gslot, half, i, :], identb.ap, [Gtok.t(gslot), identb.t()], [pT2.t()],
                            sig=(i == 7))
                self.copy(ACT, gT.ap[:, ft, 1024 * half:1024 * half + 1024].rearrange("p (c i) -> p i c", i=8),
                          pv.rearrange("p (i c) -> p i c", c=128), [pT2.t()], [gT.t(ft)])

        def load_w(ft):
            P.dma(SP, wch.ap[:, ft % 2], self.s5w[4 * ft:4 * ft + 4].rearrange("a p n -> p a n"), reads=[self.s5w_tok[ft]],
                  writes=[wch.t(ft % 2)])
            P.dma(SP, tabs.ap[:, ft % 2], self.s5tab[4 * ft:4 * ft + 4].rearrange("a p r c -> p a r c"), reads=[self.s5tab_tok],
                  writes=[tabs.t(ft % 2)])

        load_w(0)
        load_w(1)
        for p4 in (0, 1):
            pair_front(p4)
        for pp in range(16):
            prs = [2 * pp, 2 * pp + 1]
            oa, ob_ = scan_ops(prs[0]), scan_ops(prs[1])
            for fa, fb in zip(oa, ob_):
                fa()
                fb()
            if pp + 1 < 16:
                for pr in (2 * pp + 2, 2 * pp + 3):
                    pair_front(pr)
            for pr in prs:
                pair_back(pr)
            if pp % 2 == 1:
                ft = pp // 2
                ft_done(ft)
                if ft + 2 < 8:
                    load_w(ft + 2)

    def x_reload(self, s):
        P, c, ps = self.P, self.c, self.ps
        A = self.arena
        Xtok, _ = A.view(self.R_TAIL, [128, 8, 1024], F32, "Xtok2")
        xT = self.xT
        n = 0
        for half in range(2):
            self.load_xtok(s, half, Xtok)
            for ft in range(8):
                for jq in range(2):
                    bank = ps[n % 4]
                    for jj in range(4):
                        j = 4 * jq + jj
                        self.tr(bank.ap[:, 128 * jj:128 * jj + 128], Xtok.ap[:, j, 128 * ft:128 * ft + 128], c["ident_f"].ap,
                                [Xtok.t(), c["ident_f"].t()], [bank.t()], sig=(jj == 3))
                    dst = xT.ap[:, ft, 1024 * half:1024 * half + 1024].rearrange("p (c j) -> p j c", j=8)[:, 4 * jq:4 * jq + 4, :]
                    self.copy(ACT if n % 2 == 0 else DVE, dst, bank.ap.rearrange("p (j c) -> p j c", c=128), [bank.t()], [xT.t(ft)])
                    n += 1


    def take_prefetched(self, key):
        pf = getattr(self, "_prefetched", None)
        if pf is not None and pf[0] == key:
            self._prefetched = None
            return pf[1]
        return None

    def prefetch(self, key, loader):
        self._prefetched = (key, loader())

    def stream(self, loaders, computes, key=None):
        n = len(loaders)
        first = self.take_prefetched(key) if key is not None else None
        slots = {0: first if first is not None else loaders[0]()}
        for i in range(n):
            if i + 1 < n:
                slots[i + 1] = loaders[i + 1]()
            computes[i](slots.pop(i))

    def fm_norm(self, s, n, tt, dst, sq, tmpf, bank):
        c, xT, rstd = self.c, self.xT, self.rstd
        ts_ = slice(512 * tt, 512 * tt + 512)
        for ft in range(8):
            self.act(sq.ap[:, ft, :], xT.ap[:, ft, ts_], AF.Square, [xT.t(ft)], [sq.t(ft)])
        for ft in range(8):
            self.mm(bank.ap, c["onesm_b"].ap, sq.ap[:, ft, :], ft == 0, ft == 7, [c["onesm_b"].t(), sq.t(ft)], [bank.t()])
        self.act(rstd.ap, bank.ap, AF.Ln, [bank.t()], [rstd.t()], bias=self.epsc.ap)
        self.act(rstd.ap, rstd.ap, AF.Exp, [rstd.t()], [rstd.t()], scale=-0.5)
        ms = self.modsc
        for ft in range(8):
            tb = ft % 2
            self.stt(tmpf.ap[:, tb, :], xT.ap[:, ft, ts_], ms.ap[:, s, n, 0, ft:ft + 1], rstd.ap, ALU.mult, ALU.mult,
                     [xT.t(ft), ms.t(), rstd.t()], [tmpf.t(tb)])
            self.act(dst[:, ft, :], tmpf.ap[:, tb, :], AF.Identity, [tmpf.t(tb), ms.t()], [self.cur_h_tok], bias=ms.ap[:, s, n, 1, ft:ft + 1])

    def glu(self, s):
        P, ps = self.P, self.ps
        A = self.arena
        off = self.R_TAIL
        sg, off = A.view(off, [128, 2, 512], F32, "sg")
        mt_, off = A.view(off, [128, 2, 512], F32, "mt")
        gT, xT, adaT = self.actT, self.xT, self.adaT
        w = self.I["s5_w_glu"][0].rearrange("(k p) n -> p k n", p=128)
        allg = [gT.t(ft) for ft in range(8)]
        cnt = [0]

        def loader(ft):
            def f():
                v = lambda a: a[:, 0:2048].rearrange("p (k g n) -> p k g n", g=2, n=128)
                return self.wload([(lambda a: v(a)[:, :, 0, :], w[:, :, 128 * ft:128 * ft + 128]),
                                   (lambda a: v(a)[:, :, 1, :], w[:, :, 1024 + 128 * ft:1024 + 128 * ft + 128])])
            return f

        def compute(ft):
            def f(slot):
                wv = slot.ap[:, 0:2048].rearrange("p (k g n) -> p k g n", g=2, n=128)
                for tt in range(4):
                    n = cnt[0]
                    cnt[0] += 1
                    bv, bg = ps[(2 * n) % 8], ps[(2 * n + 1) % 8]
                    ts_ = slice(512 * tt, 512 * tt + 512)
                    for gi, bank in enumerate([bv, bg]):
                        for k in range(8):
                            self.mm(bank.ap, wv[:, k, gi, :], gT.ap[:, k, ts_], k == 0, k == 7, [slot.t()] + allg, [bank.t()])
                    b2 = n % 2
                    self.act(sg.ap[:, b2, :], bg.ap, AF.Sigmoid, [bg.t()], [sg.t(b2)])
                    self.tt(DVE, mt_.ap[:, b2, :], bv.ap, sg.ap[:, b2, :], ALU.mult, [bv.t(), sg.t(b2)], [mt_.t(b2)])
                    self.stt(xT.ap[:, ft, ts_], mt_.ap[:, b2, :], adaT.ap[:, 16 + ft, s:s + 1], xT.ap[:, ft, ts_], ALU.mult, ALU.add,
                             [mt_.t(b2), adaT.t(), xT.t(ft)], [xT.t(ft)])
            return f

        self.stream([loader(ft) for ft in range(8)], [compute(ft) for ft in range(8)])
        self.prefetch(("w1", 0), lambda: self.wload_k8(self.I["mlp_w1"][0], 0))

    def mlp(self, s, l, nidx):
        P, ps = self.P, self.ps
        A = self.arena
        uT, o_t = A.view(self.R_TAIL, [128, 32, 1024], BF16, "uT")
        htile1, _ = A.view(o_t, [128, 8, 1024], BF16, "htile1")
        off = self.R_ACTT
        htile0, off = A.view(off, [128, 8, 1024], BF16, "htile0")
        sq, off = A.view(off, [128, 8, 512], BF16, "sq")
        tmpf, off = A.view(off, [128, 2, 512], F32, "tmpf2")
        r, off = A.view(off, [128, 2, 1024], BF16, "r")
        htiles = [htile0, htile1]
        xT, adaT = self.xT, self.adaT
        w1 = self.I["mlp_w1"][l]
        w2 = self.I["mlp_w2"][l].rearrange("(k p) n -> p k n", p=128)
        gm0 = 48 * l + 40
        nb = [0]

        def norm(t2):
            for sub in range(2):
                self.cur_h_tok = htiles[t2].t(sub)
                self.fm_norm(s, nidx, 2 * t2 + sub, htiles[t2].ap[:, :, 512 * sub:512 * sub + 512], sq, tmpf, ps[7])

        norm(0)
        for t2 in range(2):
            htile = htiles[t2]
            loaders, computes = [], []
            for ch in range(8):
                loaders.append(lambda ch=ch: self.wload_k8(w1, 512 * ch))

                def c1(sv, ch=ch):
                    slot, wv = sv
                    for m4 in range(4):
                        e = 4 * ch + m4
                        for sub in range(2):
                            bank = ps[nb[0] % 4]
                            nb[0] += 1
                            for k in range(8):
                                self.mm(bank.ap, wv[:, k, 128 * m4:128 * m4 + 128], htile.ap[:, k, 512 * sub:512 * sub + 512], k == 0, k == 7,
                                        [slot.t(), htile.t(sub)], [bank.t()])
                            self.act(r.ap[:, e % 2, 512 * sub:512 * sub + 512], bank.ap, AF.Relu, [bank.t()], [r.t((e % 2, sub))])
                        self.tt(DVE, uT.ap[:, e, :], r.ap[:, e % 2, :], r.ap[:, e % 2, :], ALU.mult,
                                [r.t((e % 2, 0)), r.t((e % 2, 1))], [uT.t(e)])
                    if ch == 7 and t2 == 0:
                        norm(1)
                computes.append(c1)
            allu = [uT.t(e) for e in range(32)]
            for o in range(8):
                def l2(o=o):
                    slot = self.wload([(lambda a: a.rearrange("p (k n) -> p k n", n=128), w2[:, :, 128 * o:128 * o + 128])])
                    return slot, slot.ap.rearrange("p (k n) -> p k n", n=128)
                loaders.append(l2)

                def c2(sv, o=o, t2=t2):
                    slot, wv = sv
                    for sub in range(2):
                        bank = ps[4 + (2 * o + sub) % 3]
                        ts_ = slice(1024 * t2 + 512 * sub, 1024 * t2 + 512 * sub + 512)
                        for k in range(32):
                            self.mm(bank.ap, wv[:, k, :], uT.ap[:, k, 512 * sub:512 * sub + 512], k == 0, k == 31, [slot.t()] + allu, [bank.t()])
                        self.stt(xT.ap[:, o, ts_], bank.ap, adaT.ap[:, gm0 + o, s:s + 1], xT.ap[:, o, ts_], ALU.mult, ALU.add,
                                 [bank.t(), adaT.t(), xT.t(o)], [xT.t(o)])
                computes.append(c2)
            self.stream(loaders, computes, key=("w1", l) if t2 == 0 else None)
        if l == 0:
            self.prefetch(("wkv", 0), lambda: self.wload_k8(self.I["w_kv"], 0))

    def headnorm_batch(self, banks, sbanks, gaincol, dsts, dtoks, sq, tk4):
        c = self.c
        n = len(banks)
        for j in range(n):
            self.act(sq.ap[:, j, :], banks[j].ap, AF.Square, [banks[j].t()], [sq.t(j)])
        for j in range(n):
            self.mm(sbanks[j].ap, c["blk_b"].ap, sq.ap[:, j, :], True, True, [c["blk_b"].t(), sq.t(j)], [sbanks[j].t()])
        for j in range(n):
            self.act(tk4.ap[:, j, :], sbanks[j].ap, AF.Ln, [sbanks[j].t()], [tk4.t(j)], bias=self.epsc.ap)
        for j in range(n):
            self.act(tk4.ap[:, j, :], tk4.ap[:, j, :], AF.Exp, [tk4.t(j)], [tk4.t(j)], scale=-0.5)
        for j in range(n):
            self.stt(dsts[j], banks[j].ap, gaincol, tk4.ap[:, j, :], ALU.mult, ALU.mult, [banks[j].t(), self.qkg.t(), tk4.t(j)], dtoks[j])

    def kv_phase(self, s):
        P, ps = self.P, self.ps
        A = self.arena
        self.KT, _ = A.view(self.R_ACTT, [128, 8, 2048], BF16, "KT")
        off = self.R_TAIL
        self.V, off = A.view(off, [128, 16, 1024], BF16, "V")
        self.attn_off = off
        htile, off = A.view(off, [128, 8, 2048], BF16, "htile_kv")
        sq, off = A.view(off, [128, 8, 512], BF16, "sq_kv")
        tmpf, off = A.view(off, [128, 2, 512], F32, "tmpf_kv")
        tk2, off = A.view(off, [128, 2, 512], F32, "tk2")
        KT, V = self.KT, self.V
        wkv = self.I["w_kv"]
        for tt in range(4):
            self.cur_h_tok = htile.t(tt)
            self.fm_norm(s, 2, tt, htile.ap[:, :, 512 * tt:512 * tt + 512], sq, tmpf, ps[7])
        cnt = [0]
        loaders = [lambda ch=ch: self.wload_k8(wkv, 512 * ch) for ch in range(4)]
        computes = []
        for ch in range(4):
            def cK(sv, ch=ch):
                slot, wv = sv
                for tt in range(4):
                    ts_ = slice(512 * tt, 512 * tt + 512)
                    for hf in range(2):
                        n = cnt[0]
                        cnt[0] += 1
                        banks = [ps[(2 * n) % 4], ps[(2 * n + 1) % 4]]
                        sbanks = [ps[4 + (2 * n) % 4], ps[4 + (2 * n + 1) % 4]]
                        for j in range(2):
                            m4 = 2 * hf + j
                            for k in range(8):
                                self.mm(banks[j].ap, wv[:, k, 128 * m4:128 * m4 + 128], htile.ap[:, k, ts_], k == 0, k == 7,
                                        [slot.t(), htile.t(tt)], [banks[j].t()])
                        prs = [4 * ch + 2 * hf + j for j in range(2)]
                        sqv = Buf(sq.ap[:, 2 * (n % 4):2 * (n % 4) + 2, :], "sqv")
                        sqv.toks = {0: sq.t(2 * (n % 4)), 1: sq.t(2 * (n % 4) + 1)}
                        self.headnorm_batch(banks, sbanks, self.qkg.ap[:, 1:2], [KT.ap[:, p_, ts_] for p_ in prs],
                                            [[KT.t((p_, tt))] for p_ in prs], sqv, tk2)

            def cV(sv, ch=ch):
                slot, wv = sv
                for tb in range(16):
                    n = cnt[0]
                    cnt[0] += 1
                    bank = ps[n % 4]
                    for k in range(8):
                        self.mm(bank.ap, htile.ap[:, k, 128 * tb:128 * tb + 128], wv[:, k, :], k == 0, k == 7,
                                [slot.t(), htile.t(tb // 4)], [bank.t()])
                    self.copy(ACT if n % 2 == 0 else DVE, V.ap[:, tb, 512 * (ch - 2):512 * (ch - 2) + 512], bank.ap,
                              [bank.t()], [V.t(tb)])
            computes.append(cK if ch < 2 else cV)
        self.stream(loaders, computes, key=("wkv", 0))
        self.prefetch(("wq", 0), lambda: self.wload_k8(self.I["sb_w_q"][0], 0))

    def attn_phase(self, s):
        P, ps, c = self.P, self.ps, self.c
        A = self.arena
        KT, V, xT, adaT = self.KT, self.V, self.xT, self.adaT
        off = self.attn_off
        qT, off = A.view(off, [128, 8, 512], BF16, "qT")
        oT, off = A.view(off, [128, 8, 512], BF16, "oT")
        o1 = off
        htile, o1 = A.view(o1, [128, 8, 512], BF16, "htile_q")
        sq, o1 = A.view(o1, [128, 8, 512], BF16, "sq_q")
        tmpf, o1 = A.view(o1, [128, 2, 512], F32, "tmpf_q")
        tk4, o1 = A.view(o1, [128, 4, 512], F32, "tk4_q")
        o2 = off
        Eb, o2 = A.view(o2, [128, 2, 2, 512], F32, "Eb")
        Lb, o2 = A.view(o2, [128, 3, 2, 512], BF16, "Lb")
        Wb, o2 = A.view(o2, [128, 3, 2, 512], BF16, "Wb")
        R32, o2 = A.view(o2, [128, 2, 512], F32, "R32")
        Rbf, o2 = A.view(o2, [128, 3, 2, 512], BF16, "Rbf")
        wq = self.I["sb_w_q"][0]
        wo = self.I["sb_w_o"][0]
        identb, negmask, negtri, negones = c["ident_b"], c["negmask_b"], c["negtri_b"], c["negones_b"]
        zeros = c["zeros512_b"]
        zbank = Buf(self.psall[:, 0:1024].rearrange("p (h t) -> p h t", h=2), "zbank")
        zbank.toks[0] = ps[0].t()
        abank = []
        for j in range(2):
            b_ = Buf(self.psall[:, 1024 + 1024 * j:2048 + 1024 * j].rearrange("p (h t) -> p h t", h=2), f"abank{j}")
            abank.append(b_)
        obank = ps[6]

        def ztoks():
            return [ps[0].t(), ps[1].t()]

        def atoks(j):
            return [ps[2 + 2 * j].t(), ps[3 + 2 * j].t()]

        wq_first = self.take_prefetched(("wq", 0))
        wq_slots = [wq_first if wq_first is not None else self.wload_k8(wq, 0), self.wload_k8(wq, 512)]
        self.cur_h_tok = htile.t()
        self.fm_norm(s, 3, 0, htile.ap, sq, tmpf, ps[7])
        for qt in range(4):
            ts_ = slice(512 * qt, 512 * qt + 512)
            cnt = [0]

            def cQ(sv, ch):
                slot, wv = sv
                banks = [ps[m4] for m4 in range(4)]
                for m4 in range(4):
                    for k in range(8):
                        self.mm(banks[m4].ap, wv[:, k, 128 * m4:128 * m4 + 128], htile.ap[:, k, :], k == 0, k == 7,
                                [slot.t(), htile.t()], [banks[m4].t()])
                self.headnorm_batch(banks, [ps[4 + m4] for m4 in range(4)], self.qkg.ap[:, 0:1],
                                    [qT.ap[:, 4 * ch + m4, :] for m4 in range(4)], [[qT.t(4 * ch + m4)] for m4 in range(4)], sq, tk4)

            for ch in range(2):
                cQ(wq_slots[ch], ch)
            P.barrier()
            wo_slots = [self.wload_k8(wo, 512 * ch) for ch in range(2)]
            tiles = []
            nkb = 4 * qt + 4
            for pair in range(8):
                for ii, kb in enumerate(range(nkb - 1, -1, -1)):
                    r_ = kb - 4 * qt
                    c0 = 128 * r_ if r_ >= 0 else 0
                    tiles.append(dict(pair=pair, kb=kb, first=(ii == 0), last=(kb == 0), diag=(r_ >= 0), c0=c0))
            nt = len(tiles)

            def zmm(bank, btoks, t, close):
                c0 = t["c0"]
                kb = t["kb"]
                rd = [KT.t((t["pair"], kb // 4)), qT.t(t["pair"])]
                for hl in range(2):
                    hs = slice(64 * hl, 64 * hl + 64)
                    self.mm(bank.ap[:, hl, c0:512], KT.ap[hs, t["pair"], 128 * kb:128 * kb + 128], qT.ap[hs, t["pair"], c0:512], True,
                            close and not t["diag"], rd, btoks, sig=(close and not t["diag"] and hl == 1))
                if t["diag"]:
                    for hl in range(2):
                        self.mm(bank.ap[:, hl, c0:c0 + 128], identb.ap, negmask.ap, False, close, [identb.t(), negmask.t()], btoks,
                                sig=(close and hl == 1))

            def stageA1(i):
                t = tiles[i]
                c0 = t["c0"]
                zmm(zbank, ztoks(), t, True)
                self.act(Eb.ap[:, i % 2, :, c0:512], zbank.ap[:, :, c0:512], AF.Exp, ztoks(), [Eb.t(i % 2)])

            def stageA2(i):
                t = tiles[i]
                c0 = t["c0"]
                self.act(Lb.ap[:, i % 3, :, c0:512], Eb.ap[:, i % 2, :, c0:512], AF.Ln, [Eb.t(i % 2)], [Lb.t(i % 3)], bias=self.onec.ap)
                if not t["last"]:
                    if t["first"]:
                        P.op(POOL, lambda e: e.memset(R32.ap, 0.0), [], [R32.t()])
                    self.tt(POOL, R32.ap[:, :, c0:512], R32.ap[:, :, c0:512], Lb.ap[:, i % 3, :, c0:512], ALU.add,
                            [R32.t(), Lb.t(i % 3)], [R32.t()])
                    self.copy(DVE, Rbf.ap[:, i % 3], R32.ap, [R32.t()], [Rbf.t(i % 3)])

            def stageB(i):
                t = tiles[i]
                ab = abank[i % 2]
                at = atoks(i % 2)
                c0 = t["c0"]
                zmm(ab, at, t, False)
                for hl in range(2):
                    self.mm(ab.ap[:, hl, c0:512], negtri.ap, Lb.ap[:, i % 3, hl, c0:512], False, t["first"], [negtri.t(), Lb.t(i % 3)], at,
                            sig=(t["first"] and hl == 1))
                if not t["first"]:
                    for hl in range(2):
                        self.mm(ab.ap[:, hl, c0:512], negones.ap, Rbf.ap[:, (i - 1) % 3, hl, c0:512], False, True,
                                [negones.t(), Rbf.t((i - 1) % 3)], at, sig=(hl == 1))
                self.act(Wb.ap[:, i % 3, :, c0:512], ab.ap[:, :, c0:512], AF.Exp, at, [Wb.t(i % 3)])

            def stageC(i):
                t = tiles[i]
                c0 = t["c0"]
                obank = ps[6 + t["pair"] % 2]
                if t["first"]:
                    self.mm(obank.ap, zeros.ap[:, 0:128], zeros.ap, True, False, [zeros.t()], [obank.t()])
                for hl in range(2):
                    h = 2 * t["pair"] + hl
                    hs = slice(64 * hl, 64 * hl + 64)
                    self.mm(obank.ap[hs, c0:512], V.ap[:, t["kb"], 64 * h:64 * h + 64], Wb.ap[:, i % 3, hl, c0:512], False, t["last"],
                            [V.t(t["kb"]), Wb.t(i % 3)], [obank.t()], sig=(t["last"] and hl == 1))
                if t["last"]:
                    self.copy(DVE, oT.ap[:, t["pair"], :], obank.ap, [obank.t()], [oT.t(t["pair"])])

            for step in range(nt + 3):
                if step < nt:
                    stageA1(step)
                if 0 <= step - 2 < nt:
                    stageB(step - 2)
                if step < nt:
                    stageA2(step)
                if 0 <= step - 3 < nt:
                    stageC(step - 3)
            P.barrier()
            allo = [oT.t(p_) for p_ in range(8)]

            def cO(sv, ch):
                slot, wv = sv
                for m4 in range(4):
                    ft = 4 * ch + m4
                    bank = ps[4 + ft % 2]
                    for k in range(8):
                        self.mm(bank.ap, wv[:, k, 128 * m4:128 * m4 + 128], oT.ap[:, k, :], k == 0, k == 7, [slot.t()] + allo, [bank.t()])
                    self.stt(xT.ap[:, ft, ts_], bank.ap, adaT.ap[:, 48 + 16 + ft, s:s + 1], xT.ap[:, ft, ts_], ALU.mult, ALU.add,
                             [bank.t(), adaT.t(), xT.t(ft)], [xT.t(ft)])

            if qt < 3:
                self.cur_h_tok = htile.t()
                self.fm_norm(s, 3, qt + 1, htile.ap, sq, tmpf, ps[7])
            cO(wo_slots[0], 0)
            if qt < 3:
                wq0 = self.wload_k8(wq, 0)
            cO(wo_slots[1], 1)
            if qt < 3:
                wq_slots = [wq0, self.wload_k8(wq, 512)]
            else:
                self.prefetch(("w1", 1), lambda: self.wload_k8(self.I["mlp_w1"][1], 0))

    def output_phase(self, s):
        P, ps, c = self.P, self.ps, self.c
        A = self.arena
        ost, _ = A.view(self.R_TAIL, [128, 2, 1024], F32, "ostage")
        xT = self.xT
        if s + 1 < self.nseq:
            self.issue_prep_loads(s + 1)
        allx = [xT.t(ft) for ft in range(8)]
        n = 0
        for tb in range(16):
            ob = tb % 2
            for hf in range(2):
                bank = ps[n % 4]
                for jj in range(4):
                    ft = 4 * hf + jj
                    self.tr(bank.ap[:, 128 * jj:128 * jj + 128], xT.ap[:, ft, 128 * tb:128 * tb + 128], c["ident_f"].ap,
                            allx + [c["ident_f"].t()], [bank.t()], sig=(jj == 3))
                self.copy(ACT if n % 2 == 0 else DVE, ost.ap[:, ob, 512 * hf:512 * hf + 512], bank.ap, [bank.t()], [ost.t(ob)])
                n += 1
            otok = Tok(f"out{s}_{tb}")
            P.dma(SP, self.out[s, 128 * tb:128 * tb + 128, :], ost.ap[:, ob, :], reads=[ost.t(ob)], writes=[otok], tok=ost.t(ob))


def build_program(debug=None, stop_after=None, nseq=NS):
    b = Builder(debug=debug, stop_after=stop_after, nseq=nseq)
    nc = b.build()
    if b.P.nfwd:
        print("forward-redirected PE deps:", b.P.nfwd)
    return nc, b


_PARAM_NAMES = ["ada_w", "ada_b", "mix_norm_g", "mlp_norm_g", "mlp_w1", "mlp_w2", "s5_a_re", "s5_a_im", "s5_log_dt",
                "s5_b_re", "s5_b_im", "s5_c_re", "s5_c_im", "s5_d", "s5_w_glu", "kv_ada_w", "kv_ada_b", "kv_norm_g",
                "w_kv", "k_norm_g", "sb_w_q", "q_norm_g", "sb_w_o"]


def kernel(**inputs):
    x = np.ascontiguousarray(np.asarray(inputs["x"], dtype=np.float32))
    c = np.ascontiguousarray(np.asarray(inputs["c"], dtype=np.float32))
    params = {k: np.ascontiguousarray(np.asarray(inputs[k], dtype=np.float32)) for k in _PARAM_NAMES}
    nc, _ = build_program()
    in_maps = []
    for i in range(NCORES):
        m = dict(params)
        m["x"] = np.ascontiguousarray(x[NS * i:NS * i + NS])
        m["c"] = np.ascontiguousarray(c[NS * i:NS * i + NS])
        in_maps.append(m)
    res = run_bass_kernel_spmd(nc, in_maps, core_ids=list(range(NCORES)))
    out = np.concatenate([np.asarray(r["out"]) for r in res.results], axis=0)
    return out.astype(np.float32, copy=False)
```
